# Optimizing a Trainium2 kernel written in Bass

```python
import math
import jax, jax.numpy as jnp
from jax import lax
import numpy as np

D_MODEL = 1024
BATCH = 4
SEQ = 8192
DEPTH = 2

N_A_LAYERS = DEPTH // 2
N_B_LAYERS = DEPTH - N_A_LAYERS
D_FF = 2816
CONV_WIDTH = 3
N_HEADS = 16
N_KV_GROUPS = 4
HEADS_PER_GROUP = N_HEADS // N_KV_GROUPS
HEAD_DIM = D_MODEL // N_HEADS
N_BRANCH = 3
CMP_BLOCK = 32
CMP_STRIDE = 16
CMP_HIDDEN = 2 * HEAD_DIM
SLC_BLOCK = 64
N_SELECT = 16
WINDOW = 512
Q_BLOCK = 128
N_BUCKETS = 32
MAX_EXACT = N_BUCKETS // 2
MAX_DISTANCE = 4096
EPS = 1e-6
NEG = -1e30
BIG = 1e30

kernel_name = "yoco_shortconv_nsa_macaron"


def rmsnorm(x, g):
    xf = x.astype(jnp.float32)
    y = xf * lax.rsqrt(jnp.mean(xf * xf, axis=-1, keepdims=True) + EPS)
    return (y * g.astype(jnp.float32)).astype(x.dtype)


def swiglu_ffn(x, w_in, w_out):
    gate, up = jnp.split(x @ w_in, 2, axis=-1)
    return (jax.nn.silu(gate) * up) @ w_out


def rel_bucket(rel):
    n = jnp.maximum(rel, 0)
    nf = jnp.maximum(n, 1).astype(jnp.float32)
    large = MAX_EXACT + (jnp.log(nf / MAX_EXACT) / math.log(MAX_DISTANCE / MAX_EXACT)
                         * (N_BUCKETS - MAX_EXACT)).astype(jnp.int32)
    large = jnp.minimum(large, N_BUCKETS - 1)
    return jnp.where(n < MAX_EXACT, n, large)


def short_conv_mixer(x, w_in, conv_w, w_out):
    b_gate, c_gate, v = jnp.split(x @ w_in, 3, axis=-1)
    u = c_gate * v
    y = lax.conv_general_dilated(u, conv_w[:, None, :], window_strides=(1,),
                                 padding=[(CONV_WIDTH - 1, 0)],
                                 dimension_numbers=('NWC', 'WIO', 'NWC'),
                                 feature_group_count=D_MODEL)
    return (b_gate * y) @ w_out


def compress(raw, pe, w1, w2):
    bsz, s = raw.shape[0], raw.shape[1]
    nc = (s - CMP_BLOCK) // CMP_STRIDE + 1
    idx = jnp.arange(nc)[:, None] * CMP_STRIDE + jnp.arange(CMP_BLOCK)[None, :]
    blk = raw[:, idx] + pe[None, None, :, None, :]
    blk = blk.transpose(0, 3, 1, 2, 4).reshape(bsz, N_KV_GROUPS, nc, CMP_BLOCK * HEAD_DIM)
    return jax.nn.gelu(blk @ w1) @ w2


def shared_kv(h, kv_norm, w_kv, k_norm, pe_k, pe_v, w1_k, w2_k, w1_v, w2_v):
    bsz, s, _ = h.shape
    kv = (rmsnorm(h, kv_norm) @ w_kv).reshape(bsz, s, N_BRANCH, 2, N_KV_GROUPS, HEAD_DIM)
    k_c = rmsnorm(compress(kv[:, :, 0, 0], pe_k, w1_k, w2_k), k_norm[0])
    v_c = compress(kv[:, :, 0, 1], pe_v, w1_v, w2_v)
    ns = s // SLC_BLOCK
    k_s = rmsnorm(kv[:, :, 1, 0], k_norm[1]).transpose(0, 2, 1, 3).reshape(bsz, N_KV_GROUPS, ns, SLC_BLOCK, HEAD_DIM)
    v_s = kv[:, :, 1, 1].transpose(0, 2, 1, 3).reshape(bsz, N_KV_GROUPS, ns, SLC_BLOCK, HEAD_DIM)
    pad = ((0, 0), (0, 0), (WINDOW, 0), (0, 0))
    k_w = jnp.pad(rmsnorm(kv[:, :, 2, 0], k_norm[2]).transpose(0, 2, 1, 3), pad)
    v_w = jnp.pad(kv[:, :, 2, 1].transpose(0, 2, 1, 3), pad)
    return k_c, v_c, k_s, v_s, k_w, v_w


def masked_softmax(s, mask):
    return jax.nn.softmax(jnp.where(mask, s.astype(jnp.float32), NEG), axis=-1)


def nsa_mixer(x, w_q, q_norm, w_o, rel_bias, kv):
    k_c, v_c, k_s, v_s, k_w, v_w = kv
    bsz, s, _ = x.shape
    G, HG, dh = N_KV_GROUPS, HEADS_PER_GROUP, HEAD_DIM
    proj = x @ w_q
    q = rmsnorm(proj[..., :N_HEADS * dh].reshape(bsz, s, G, HG, dh), q_norm)
    q = q.transpose(0, 2, 3, 1, 4)
    gates = jax.nn.sigmoid(proj[..., N_HEADS * dh:].astype(jnp.float32))
    gates = gates.reshape(bsz, s, G, HG, N_BRANCH).transpose(0, 2, 3, 1, 4).astype(x.dtype)
    nc, ns = k_c.shape[2], k_s.shape[2]
    n_sel = min(N_SELECT, ns)
    scale = HEAD_DIM ** -0.5
    c_start = jnp.arange(nc) * CMP_STRIDE
    c_end = c_start + CMP_BLOCK - 1
    s_start = jnp.arange(ns) * SLC_BLOCK
    overlap = (jnp.clip(jnp.minimum(c_start[:, None] + CMP_BLOCK, s_start[None, :] + SLC_BLOCK)
                        - jnp.maximum(c_start[:, None], s_start[None, :]), 0) / CMP_STRIDE).astype(jnp.float32)
    bias_grp = rel_bias.reshape(N_BUCKETS, G, HG).transpose(1, 0, 2)
    bi = jnp.arange(bsz)[:, None, None, None]
    gi = jnp.arange(G)[None, :, None, None]
    blk_ids = jnp.arange(ns)

    def head_bias(rel):
        return rel_bias[rel_bucket(rel)].reshape(rel.shape + (G, HG)).transpose(2, 3, 0, 1)

    def block(qb):
        q0 = qb * Q_BLOCK
        qblk = lax.dynamic_slice_in_dim(q, q0, Q_BLOCK, axis=3)
        gblk = lax.dynamic_slice_in_dim(gates, q0, Q_BLOCK, axis=3)
        t = q0 + jnp.arange(Q_BLOCK)
        rel_c = t[:, None] - c_end[None, :]
        s_c = jnp.einsum('bghqd,bgnd->bghqn', qblk, k_c) * scale + head_bias(rel_c)
        p_c = masked_softmax(s_c, rel_c >= 0)
        p_c = jnp.where((t >= CMP_BLOCK - 1)[:, None], p_c, 0.0)
        o_c = jnp.einsum('bghqn,bgnd->bghqd', p_c.astype(v_c.dtype), v_c)
        imp = jnp.einsum('bghqn,nj->bgqj', p_c, overlap)
        blk_t = t // SLC_BLOCK
        valid = blk_ids[None, :] <= blk_t[:, None]
        forced = ((blk_ids[None, :] == 0) | (blk_ids[None, :] == blk_t[:, None])
                  | (blk_ids[None, :] == blk_t[:, None] - 1))
        score = jnp.where(valid, jnp.where(forced, BIG, imp), NEG)
        _, idx = lax.top_k(score, n_sel)
        k_sel = k_s[bi, gi, idx].reshape(bsz, G, Q_BLOCK, n_sel * SLC_BLOCK, dh)
        v_sel = v_s[bi, gi, idx].reshape(bsz, G, Q_BLOCK, n_sel * SLC_BLOCK, dh)
        pos = (idx[..., None] * SLC_BLOCK + jnp.arange(SLC_BLOCK)).reshape(bsz, G, Q_BLOCK, n_sel * SLC_BLOCK)
        rel_s = t[None, None, :, None] - pos
        bias_s = bias_grp[gi, rel_bucket(rel_s)].transpose(0, 1, 4, 2, 3)
        s_s = jnp.einsum('bghqd,bgqkd->bghqk', qblk, k_sel) * scale + bias_s
        p_s = masked_softmax(s_s, (rel_s >= 0)[:, :, None])
        o_s = jnp.einsum('bghqk,bgqkd->bghqd', p_s.astype(v_sel.dtype), v_sel)
        k_wb = lax.dynamic_slice_in_dim(k_w, q0, Q_BLOCK + WINDOW, axis=2)
        v_wb = lax.dynamic_slice_in_dim(v_w, q0, Q_BLOCK + WINDOW, axis=2)
        s_pos = q0 - WINDOW + jnp.arange(Q_BLOCK + WINDOW)
        rel_w = t[:, None] - s_pos[None, :]
        mask_w = (rel_w >= 0) & (rel_w < WINDOW) & (s_pos >= 0)[None, :]
        s_w = jnp.einsum('bghqd,bgkd->bghqk', qblk, k_wb) * scale + head_bias(rel_w)
        p_w = masked_softmax(s_w, mask_w)
        o_w = jnp.einsum('bghqk,bgkd->bghqd', p_w.astype(v_wb.dtype), v_wb)
        return gblk[..., 0:1] * o_c + gblk[..., 1:2] * o_s + gblk[..., 2:3] * o_w

    out = lax.map(block, jnp.arange(s // Q_BLOCK))
    out = out.transpose(1, 0, 4, 2, 3, 5).reshape(bsz, s, N_HEADS * dh)
    return out @ w_o


def setup_inputs(seed: int = 0) -> dict:
    key = jax.random.key(seed)
    ks = jax.random.split(key, 32)

    def w(k, shape, fan_in):
        return jax.random.normal(k, shape, jnp.float32) * (fan_in ** -0.5)

    def gain(k, shape):
        return 1.0 + 0.02 * jax.random.normal(k, shape, jnp.float32)

    q_cols = N_HEADS * HEAD_DIM + N_BRANCH * N_HEADS
    kv_cols = N_BRANCH * 2 * N_KV_GROUPS * HEAD_DIM
    return {
        'x': jax.random.normal(ks[0], (BATCH, SEQ, D_MODEL), jnp.float32),
        'ffn1_norm': gain(ks[1], (DEPTH, D_MODEL)),
        'ffn1_w_in': w(ks[2], (DEPTH, D_MODEL, 2 * D_FF), D_MODEL),
        'ffn1_w_out': w(ks[3], (DEPTH, D_FF, D_MODEL), D_FF),
        'mix_norm': gain(ks[4], (DEPTH, D_MODEL)),
        'ffn2_norm': gain(ks[5], (DEPTH, D_MODEL)),
        'ffn2_w_in': w(ks[6], (DEPTH, D_MODEL, 2 * D_FF), D_MODEL),
        'ffn2_w_out': w(ks[7], (DEPTH, D_FF, D_MODEL), D_FF),
        'conv_w_in': w(ks[8], (N_A_LAYERS, D_MODEL, 3 * D_MODEL), D_MODEL),
        'conv_w': w(ks[9], (N_A_LAYERS, CONV_WIDTH, D_MODEL), CONV_WIDTH),
        'conv_w_out': w(ks[10], (N_A_LAYERS, D_MODEL, D_MODEL), D_MODEL),
        'attn_w_q': w(ks[11], (N_B_LAYERS, D_MODEL, q_cols), D_MODEL),
        'attn_q_norm': gain(ks[12], (N_B_LAYERS, HEAD_DIM)),
        'attn_w_o': w(ks[13], (N_B_LAYERS, N_HEADS * HEAD_DIM, D_MODEL), N_HEADS * HEAD_DIM),
        'kv_norm': gain(ks[14], (D_MODEL,)),
        'w_kv': w(ks[15], (D_MODEL, kv_cols), D_MODEL),
        'k_norm': gain(ks[16], (N_BRANCH, HEAD_DIM)),
        'cmp_pe_k': 0.1 * jax.random.normal(ks[17], (CMP_BLOCK, HEAD_DIM), jnp.float32),
        'cmp_pe_v': 0.1 * jax.random.normal(ks[18], (CMP_BLOCK, HEAD_DIM), jnp.float32),
        'cmp_w1_k': w(ks[19], (CMP_BLOCK * HEAD_DIM, CMP_HIDDEN), CMP_BLOCK * HEAD_DIM),
        'cmp_w2_k': w(ks[20], (CMP_HIDDEN, HEAD_DIM), CMP_HIDDEN),
        'cmp_w1_v': w(ks[21], (CMP_BLOCK * HEAD_DIM, CMP_HIDDEN), CMP_BLOCK * HEAD_DIM),
        'cmp_w2_v': w(ks[22], (CMP_HIDDEN, HEAD_DIM), CMP_HIDDEN),
        'rel_bias': 0.5 * jax.random.normal(ks[23], (N_BUCKETS, N_HEADS), jnp.float32),
    }


def reference(x, ffn1_norm, ffn1_w_in, ffn1_w_out, mix_norm, ffn2_norm, ffn2_w_in, ffn2_w_out,
              conv_w_in, conv_w, conv_w_out, attn_w_q, attn_q_norm, attn_w_o,
              kv_norm, w_kv, k_norm, cmp_pe_k, cmp_pe_v, cmp_w1_k, cmp_w2_k, cmp_w1_v, cmp_w2_v,
              rel_bias):
    h = x
    kv = None
    for i in range(DEPTH):
        if i == N_A_LAYERS:
            kv = shared_kv(h, kv_norm, w_kv, k_norm, cmp_pe_k, cmp_pe_v,
                           cmp_w1_k, cmp_w2_k, cmp_w1_v, cmp_w2_v)
        h = h + 0.5 * swiglu_ffn(rmsnorm(h, ffn1_norm[i]), ffn1_w_in[i], ffn1_w_out[i])
        hn = rmsnorm(h, mix_norm[i])
        if i < N_A_LAYERS:
            h = h + short_conv_mixer(hn, conv_w_in[i], conv_w[i], conv_w_out[i])
        else:
            j = i - N_A_LAYERS
            h = h + nsa_mixer(hn, attn_w_q[j], attn_q_norm[j], attn_w_o[j], rel_bias, kv)
        h = h + 0.5 * swiglu_ffn(rmsnorm(h, ffn2_norm[i]), ffn2_w_in[i], ffn2_w_out[i])
    return h
```

```python
import contextlib
import numpy as np
import concourse.bass as bass
import concourse.mybir as mybir
from concourse.bass_utils import run_bass_kernel_spmd

F32 = mybir.dt.float32
BF16 = mybir.dt.bfloat16
AF = mybir.ActivationFunctionType
ALU = mybir.AluOpType

ENGS = ("pe", "act", "dve", "pool", "sp")
NDSEM = 32
DSEM_Q = {"sp": 24, "pool": 8, "act": 4}
SAME_ENGINE_SYNC = True

D = 1024
DFF = 2816
NJ = 22
EPS = 1e-6
OFFS, OFFW, OFFC = 1023, 1535, 2063
JSAT_S = 2896 + OFFS
W_S = JSAT_S + 512
L_S = W_S + 127
W_W = 2432
L_W = W_W + 127
JSAT_C = 4959
W_C = JSAT_C + 512
L_C = W_C + 2032
MASKV = -30000.0


class Prog:
    def __init__(self, nc):
        self.nc = nc
        self.ops = {e: [] for e in ENGS}
        self.cnt = {e: 0 for e in ENGS}
        self.waited = {e: {} for e in ENGS}
        self.res_w = {}
        self.res_r = {}
        self.ndma = 0
        self.ndma_q = {}

    def _deps(self, reads, writes):
        deps = {}

        def add(d):
            if d is None:
                return
            k, v = d
            if deps.get(k, 0) < v:
                deps[k] = v
        for r in reads:
            add(self.res_w.get(r))
        for w in writes:
            add(self.res_w.get(w))
            for k, v in self.res_r.get(w, {}).items():
                add((k, v))
        return deps

    def _commit(self, reads, writes, me):
        for r in reads:
            d = self.res_r.setdefault(r, {})
            if d.get(me[0], 0) < me[1]:
                d[me[0]] = me[1]
        for w in writes:
            self.res_w[w] = me
            self.res_r[w] = {}

    def op(self, eng, fn, reads=(), writes=()):
        deps = self._deps(reads, writes)
        waits = []
        for k, v in deps.items():
            if k == eng and (eng == "pe" or not SAME_ENGINE_SYNC):
                continue
            if self.waited[eng].get(k, 0) >= v:
                continue
            self.waited[eng][k] = v
            waits.append((k, v))
        self.cnt[eng] += 1
        me = (eng, self.cnt[eng])
        self.ops[eng].append((waits, fn, me))
        self._commit(reads, writes, me)

    def dma(self, q, fn, reads=(), writes=()):
        deps = self._deps(reads, writes)
        k = self.ndma_q.get(q, 0)
        self.ndma_q[q] = k + 1
        self.ndma += 1
        nsl = DSEM_Q[q]
        slot = k % nsl
        val = 16 * (k // nsl + 1)
        key = ("d", q, slot)
        if val > 16:
            deps[key] = max(deps.get(key, 0), val - 16)
        waits = []
        for kk, v in deps.items():
            if self.waited[q].get(kk, 0) >= v:
                continue
            self.waited[q][kk] = v
            waits.append((kk, v))
        me = (key, val)
        self.ops[q].append((waits, fn, me))
        self._commit(reads, writes, me)

    def barrier(self):
        tgt = [(e, self.cnt[e]) for e in ENGS if self.cnt[e] > 0]
        for q, n in self.ndma_q.items():
            nsl = DSEM_Q[q]
            for k in range(max(0, n - nsl), n):
                tgt.append((("d", q, k % nsl), 16 * (k // nsl + 1)))
        for e in ENGS:
            waits = []
            for k, v in tgt:
                if k == e:
                    continue
                if self.waited[e].get(k, 0) >= v:
                    continue
                self.waited[e][k] = v
                waits.append((k, v))
            if waits:
                self.ops[e].append((waits, None, None))

    def run(self):
        nc = self.nc
        def _phase1():
            with contextlib.ExitStack() as st:
                sems = {}
                for e in ENGS:
                    sems[e] = st.enter_context(nc.semaphore("s_" + e))
                for q_, n_ in DSEM_Q.items():
                    for i in range(n_):
                        sems[("d", q_, i)] = st.enter_context(nc.semaphore("d_%s_%d" % (q_, i)))
                block = st.enter_context(nc.Block())

                def body(ename):
                    def f(eng):
                        for waits, fn, me in self.ops[ename]:
                            for k, v in waits:
                                eng.wait_ge(sems[k], v)
                            if fn is None:
                                continue
                            ins = fn(eng)
                            ins.then_inc(sems[me[0]], 16 if isinstance(me[0], tuple) else 1)
                    return f
                block.tensor(body("pe"))
                block.scalar(body("act"))
                block.vector(body("dve"))
                block.gpsimd(body("pool"))
                block.sync(body("sp"))


        _phase1()
def build(S, dbg=()):
    NT = S // 512
    NO = NT // 2
    SO = S // 2
    NKT = S // 128
    NSB = S // 64
    NCP = S // 16
    NCK = NCP // 128
    NCV = NCP - 1
    nc = bass.Bass("TRN2", target_bir_lowering=False)
    IN = {}

    def inp(name, shape):
        IN[name] = nc.dram_tensor(name, list(shape), F32, kind="ExternalInput")
        return IN[name]

    def scr(name, shape, dt):
        kind = "ExternalOutput" if name in dbg else "Internal"
        return nc.dram_tensor(name, list(shape), dt, kind=kind)

    xT = inp("xT", [D, S])
    gains = inp("gains", [128, 56])
    par = inp("par", [128, 2])
    wins = [inp("win%d" % i, [NJ, 128, 2048]) for i in range(4)]
    wouts = [inp("wout%d" % i, [8, 128, NJ * 128]) for i in range(4)]
    cin = inp("cin", [8, 128, 3072])
    cout = inp("cout", [8, 128, 1024])
    cwk = inp("cwk", [128, 24])
    wkf = inp("wkf", [12, 128, 1024])
    wvt = inp("wvt", [1, 128, 4096])
    wq = inp("wq", [16, 128, 512])
    wg = inp("wg", [1, 128, 384])
    wo = inp("wo", [16, 64, 1024])
    kn = inp("kn", [128, 3])
    qn = inp("qn", [128, 1])
    w1k = inp("w1k", [1, 128, 4096])
    w1v = inp("w1v", [1, 128, 4096])
    w2k = inp("w2k", [128, 64])
    w2v = inp("w2v", [128, 64])
    peT = inp("peT", [128, 64])
    bvs = inp("bvs", [16, L_S])
    bvw = inp("bvw", [16, L_W])
    bvc = inp("bvc", [16, L_C])
    expd = inp("expd", [1, NSB, S])
    ovm = inp("ovm", [128, NCK * (NSB + 1)])
    selg = inp("selg", [1, 48, 48 * 128])
    selA = inp("selA", [NO * 4, 128, NSB])
    selB = inp("selB", [NO * 4, 128, NSB])
    ident_in = inp("ident", [128, 128])
    outT = nc.dram_tensor("outT", [D, SO], F32, kind="ExternalOutput")

    def bfc(name, src):
        return scr(name + "_bf", list(src.shape), BF16)
    wins_b = [bfc("win%d" % i, wins[i]) for i in range(4)]
    wouts_b = [bfc("wout%d" % i, wouts[i]) for i in range(4)]
    cin_b, cout_b, wkf_b, wvt_b = bfc("cin", cin), bfc("cout", cout), bfc("wkf", wkf), bfc("wvt", wvt)
    wq_b, wg_b, wo_b = bfc("wq", wq), bfc("wg", wg), bfc("wo", wo)
    w1k_b, w1v_b, expd_b, selg_b = bfc("w1k", w1k), bfc("w1v", w1v), bfc("expd", expd), bfc("selg", selg)

    h1T = scr("h1T", [D, S], F32)
    h2T = scr("h2T", [D, SO], F32)
    rawT = scr("rawT", [4, 128, S], BF16)
    ksT = scr("ksT", [4, 64, S], BF16)
    kwT = scr("kwT", [4, 64, S], BF16)
    vsw = scr("vsw", [NKT, 128, 512], BF16)
    kcT = scr("kcT", [4, 64, NCP], BF16)
    vcS = scr("vcS", [4, 128, NCK * 64], BF16)
    qTs = scr("qTs", [16, 64, SO], BF16)
    gsT = scr("gsT", [48, SO], BF16)
    ocT = scr("ocT", [16, 64, SO], BF16)
    rep_s = scr("rep_s", [16, 128, L_S], BF16)
    rep_w = scr("rep_w", [16, 128, L_W], BF16)
    rep_c = scr("rep_c", [16, 128, L_C], BF16)

    p = Prog(nc)
    HN = ["h.%d" % c for c in range(8)]
    sb = nc.sbuf_tensor
    uid = [0]

    def T(st, shape, dt, nm="t"):
        uid[0] += 1
        return st.enter_context(sb("%s_%d" % (nm, uid[0]), list(shape), dt))

    with contextlib.ExitStack() as top:
        PS = [top.enter_context(nc.psum_tensor("ps%d" % i, [128, 512], F32)) for i in range(8)]
        PSN = ["ps%d" % i for i in range(8)]
        gains_t = T(top, [128, 56], F32, "gains")
        par_t = T(top, [128, 2], F32, "par")
        cwk_t = T(top, [128, 24], F32, "cwk")
        kn_t = T(top, [128, 3], F32, "kn")
        qn_t = T(top, [128, 1], F32, "qn")
        eps_t = T(top, [128, 1], F32, "eps")
        ones_bf = T(top, [128, 128], BF16, "ones")
        ones_f = T(top, [128, 128], F32, "onesf")
        ident_t = T(top, [128, 128], F32, "ident")
        for dst, src, nm in ((gains_t, gains, "gains"), (par_t, par, "par"), (cwk_t, cwk, "cwk"),
                             (kn_t, kn, "kn"), (qn_t, qn, "qn"), (ident_t, ident_in, "ident")):
            p.dma("sp", lambda e, dst=dst, src=src: e.dma_start(out=dst[:], in_=src.ap()), writes=[nm])
        p.op("dve", lambda e: e.memset(eps_t[:], EPS), writes=["eps"])
        p.op("dve", lambda e: e.memset(ones_bf[:], 1.0), writes=["ones"])
        p.op("dve", lambda e: e.memset(ones_f[:], 1.0), writes=["onesf"])

        def precast(src, dst, nm):
            n0 = src.shape[0]
            for i in range(n0):
                p.dma("pool", lambda e, i=i: e.dma_start(out=dst.ap()[i], in_=src.ap()[i]), writes=["%s.%d" % (nm, i)])
        order = [(wins[0], wins_b[0], "win0"), (wouts[0], wouts_b[0], "wout0"), (cin, cin_b, "cin"), (cout, cout_b, "cout"),
                 (wins[1], wins_b[1], "win1"), (wouts[1], wouts_b[1], "wout1"), (wkf, wkf_b, "wkf"), (wvt, wvt_b, "wvt"),
                 (w1k, w1k_b, "w1k"), (w1v, w1v_b, "w1v"),
                 (wins[2], wins_b[2], "win2"), (wouts[2], wouts_b[2], "wout2"), (wq, wq_b, "wq"), (wg, wg_b, "wg"),
                 (expd, expd_b, "expd"), (selg, selg_b, "selg"), (wo, wo_b, "wo"),
                 (wins[3], wins_b[3], "win3"), (wouts[3], wouts_b[3], "wout3")]
        for s_, d_, n_ in order:
            precast(s_, d_, n_)

        def _phase2():
            with contextlib.ExitStack() as st:
                vrow = [T(st, [1, L_C], F32, "vrow") for _ in range(2)]
                ev = [T(st, [128, L_C], BF16, "ev") for _ in range(2)]
                k = 0
                for src, rep, L, nm in ((bvs, rep_s, L_S, "rep_s"), (bvw, rep_w, L_W, "rep_w"), (bvc, rep_c, L_C, "rep_c")):
                    for h in range(16):
                        b = k % 2
                        k += 1
                        p.dma("sp", lambda e, b=b, h=h, src=src, L=L: e.dma_start(out=vrow[b][0:1, 0:L], in_=src.ap()[h:h + 1, :]),
                              writes=["vrow%d" % b])
                        for c0 in range(0, L, 512):
                            c1 = min(L, c0 + 512)
                            pb = (c0 // 512) % 2
                            p.op("pe", lambda e, b=b, c0=c0, c1=c1, pb=pb: e.matmul(PS[pb][:, 0:c1 - c0], ones_f[0:1, :], vrow[b][0:1, c0:c1],
                                                                                  start=True, stop=True),
                                 reads=["vrow%d" % b, "onesf"], writes=[PSN[pb]])
                            p.op("act", lambda e, b=b, c0=c0, c1=c1, pb=pb: e.activation(out=ev[b][:, c0:c1], in_=PS[pb][:, 0:c1 - c0], func=AF.Exp),
                                 reads=[PSN[pb]], writes=["ev%d" % b])
                        p.dma("sp", lambda e, b=b, h=h, rep=rep, L=L: e.dma_start(out=rep.ap()[h], in_=ev[b][:, 0:L]),
                              reads=["ev%d" % b], writes=["%s.%d" % (nm, h)])
        _phase2()
        p.barrier()

        def rstd_from(ps_i, n, rs, rsn, rows=128):
            p.op("act", lambda e: e.activation(out=rs[0:rows, :], in_=PS[ps_i][0:rows, :], func=AF.Sqrt, bias=eps_t[0:rows, 0:1], scale=1.0 / n),
                 reads=[PSN[ps_i], "eps"], writes=[rsn])
            p.op("dve", lambda e: e.reciprocal(out=rs[0:rows, :], in_=rs[0:rows, :]), reads=[rsn], writes=[rsn])

        def rms_to_xn(h, hn, gcol, sq, xn, rs):
            for c in range(8):
                p.op("act", lambda e, c=c: e.activation(out=sq[:, c, :], in_=h[:, c, :], func=AF.Square), reads=["%s.%d" % (hn, c)], writes=["sq"])

            def mm(e):
                for c in range(8):
                    ins = e.matmul(PS[7][:, :], ones_bf[:, :], sq[:, c, :], start=(c == 0), stop=(c == 7))
                return ins
            p.op("pe", mm, reads=["sq", "ones"], writes=[PSN[7]])
            rstd_from(7, D, rs, "rs")
            for c in range(8):
                p.op("dve", lambda e, c=c: e.scalar_tensor_tensor(out=xn[:, c, :], in0=h[:, c, :], scalar=gains_t[:, gcol + c:gcol + c + 1],
                                                                 in1=rs[:, :], op0=ALU.mult, op1=ALU.mult),
                     reads=["%s.%d" % (hn, c), "rs", "gains"], writes=["xn"])

        def ffn(h, hn, li, gcol, bufs):
            sq, xn, rs, act, sg, wpi, wpo = bufs
            rms_to_xn(h, hn, gcol, sq, xn, rs)
            for j in range(NJ):
                b = j % 2
                wb = j % len(wpi)
                p.dma("sp", lambda e, j=j, wb=wb: e.dma_start(out=wpi[wb][:, :], in_=wins_b[li].ap()[j]),
                      reads=["win%d.%d" % (li, j)], writes=["wpi%d" % wb])
                for s_ in range(2):
                    def mm(e, s_=s_, b=b, wb=wb):
                        for c in range(8):
                            ins = e.matmul(PS[b * 2 + s_][:, :], wpi[wb][:, (s_ * 8 + c) * 128:(s_ * 8 + c + 1) * 128], xn[:, c, :],
                                           start=(c == 0), stop=(c == 7))
                        return ins
                    p.op("pe", mm, reads=["xn", "wpi%d" % wb], writes=[PSN[b * 2 + s_]])
                p.op("act", lambda e, b=b: e.activation(out=sg[b][:, :], in_=PS[b * 2][:, :], func=AF.Silu), reads=[PSN[b * 2]], writes=["sg%d" % b])
                p.op("dve", lambda e, b=b, j=j: e.tensor_tensor(out=act[:, j, :], in0=sg[b][:, :], in1=PS[b * 2 + 1][:, :], op=ALU.mult),
                     reads=["sg%d" % b, PSN[b * 2 + 1]], writes=["act.%d" % j])
            for d in range(8):
                b = d % 2
                wb = d % len(wpo)
                p.dma("sp", lambda e, d=d, wb=wb: e.dma_start(out=wpo[wb][:, :], in_=wouts_b[li].ap()[d]),
                      reads=["wout%d.%d" % (li, d)], writes=["wpo%d" % wb])

                def mm(e, b=b, wb=wb):
                    for j in range(NJ):
                        ins = e.matmul(PS[4 + b][:, :], wpo[wb][:, j * 128:(j + 1) * 128], act[:, j, :], start=(j == 0), stop=(j == NJ - 1))
                    return ins
                p.op("pe", mm, reads=["wpo%d" % wb] + ["act.%d" % j for j in range(NJ)], writes=[PSN[4 + b]])
                p.op("dve", lambda e, d=d, b=b: e.scalar_tensor_tensor(out=h[:, d, :], in0=PS[4 + b][:, :], scalar=0.5, in1=h[:, d, :],
                                                                      op0=ALU.mult, op1=ALU.add),
                     reads=[PSN[4 + b], "%s.%d" % (hn, d)], writes=["%s.%d" % (hn, d)])

        def _phase3():
            with contextlib.ExitStack() as st:
                h = T(st, [128, 8, 512], F32, "h")
                sq = T(st, [128, 8, 512], BF16, "sq")
                xn = T(st, [128, 8, 512], BF16, "xn")
                rs = T(st, [128, 512], F32, "rs")
                act = T(st, [128, NJ, 512], BF16, "act")
                sg = [T(st, [128, 512], F32, "sg") for _ in range(2)]
                wpi = [T(st, [128, 2048], BF16, "wpi") for _ in range(4)]
                wpo = [T(st, [128, NJ * 128], BF16, "wpo") for _ in range(3)]
                bufs = (sq, xn, rs, act, sg, wpi, wpo)
                wci = [T(st, [128, 3072], BF16, "wci") for _ in range(2)]
                wsq = [T(st, [128, 1024], BF16, "wsq") for _ in range(2)]
                u = [T(st, [128, 514], F32, "u") for _ in range(2)]
                ucar = T(st, [128, 8, 2], F32, "ucar")
                yb = [T(st, [128, 512], F32, "yb") for _ in range(2)]
                bgy = T(st, [128, 8, 512], BF16, "bgy")
                wvt_t = T(st, [128, 4096], BF16, "wvt")
                bd64 = T(st, [128, 128], BF16, "bd64")
                ksb = [T(st, [128, 512], BF16, "ksb") for _ in range(4)]
                sgk = [T(st, [128, 512], F32, "sgk") for _ in range(4)]
                vtb = [T(st, [128, 512], BF16, "vtb") for _ in range(2)]
                p.op("dve", lambda e: e.memset(ucar[:], 0.0), writes=["ucar"])
                p.op("dve", lambda e: e.memset(bd64[:], 0.0), writes=["bd64"])
                p.op("dve", lambda e: e.memset(bd64[0:64, 0:64], 1.0), writes=["bd64"])
                p.op("dve", lambda e: e.memset(bd64[64:128, 64:128], 1.0), writes=["bd64"])
                p.dma("sp", lambda e: e.dma_start(out=wvt_t[:, :], in_=wvt_b.ap()[0]), reads=["wvt.0"], writes=["wvtt"])
                xTv = xT.ap().rearrange("(c p) s -> p c s", p=128)
                h1v = h1T.ap().rearrange("(c p) s -> p c s", p=128)
                for ti in range(NT):
                    t0 = ti * 512
                    p.dma("sp", lambda e, t0=t0: e.dma_start(out=h[:, :, :], in_=xTv[:, :, t0:t0 + 512]), writes=HN)
                    ffn(h, "h", 0, 0, bufs)
                    rms_to_xn(h, "h", 8, sq, xn, rs)
                    for i in range(8):
                        b = i % 2
                        p.dma("sp", lambda e, i=i, b=b: e.dma_start(out=wci[b][:, :], in_=cin_b.ap()[i]), reads=["cin.%d" % i], writes=["wci%d" % b])
                        cb = (0, 1, 2) if i % 2 == 0 else (3, 6, 7)
                        for s_ in range(3):
                            def mm(e, s_=s_, b=b, cb=cb):
                                for c in range(8):
                                    ins = e.matmul(PS[cb[s_]][:, :], wci[b][:, (s_ * 8 + c) * 128:(s_ * 8 + c + 1) * 128], xn[:, c, :],
                                                   start=(c == 0), stop=(c == 7))
                                return ins
                            p.op("pe", mm, reads=["xn", "wci%d" % b], writes=[PSN[cb[s_]]])
                        p.op("dve", lambda e, i=i, b=b: e.tensor_copy(out=u[b][:, 0:2], in_=ucar[:, i, :]), reads=["ucar"], writes=["u%d" % b])
                        p.op("act", lambda e, b=b, cb=cb: e.activation(out=yb[b][:, :], in_=PS[cb[1]][:, :], func=AF.Copy), reads=[PSN[cb[1]]], writes=["yb%d" % b])
                        p.op("dve", lambda e, b=b, cb=cb: e.tensor_tensor(out=u[b][:, 2:514], in0=yb[b][:, :], in1=PS[cb[2]][:, :], op=ALU.mult),
                             reads=["yb%d" % b, PSN[cb[2]]], writes=["u%d" % b])
                        p.op("dve", lambda e, i=i, b=b: e.tensor_copy(out=ucar[:, i, :], in_=u[b][:, 512:514]), reads=["u%d" % b], writes=["ucar"])
                        p.op("dve", lambda e, i=i, b=b: e.tensor_scalar(out=yb[b][:, :], in0=u[b][:, 2:514], scalar1=cwk_t[:, i * 3 + 2:i * 3 + 3], scalar2=None,
                                                                      op0=ALU.mult), reads=["u%d" % b, "cwk"], writes=["yb%d" % b])
                        for w_ in (1, 0):
                            p.op("dve", lambda e, i=i, b=b, w_=w_: e.scalar_tensor_tensor(out=yb[b][:, :], in0=u[b][:, w_:w_ + 512],
                                                                                         scalar=cwk_t[:, i * 3 + w_:i * 3 + w_ + 1], in1=yb[b][:, :],
                                                                                         op0=ALU.mult, op1=ALU.add),
                                 reads=["u%d" % b, "cwk", "yb%d" % b], writes=["yb%d" % b])
                        p.op("dve", lambda e, i=i, b=b, cb=cb: e.tensor_tensor(out=bgy[:, i, :], in0=yb[b][:, :], in1=PS[cb[0]][:, :], op=ALU.mult),
                             reads=["yb%d" % b, PSN[cb[0]]], writes=["bgy.%d" % i])
                    for d in range(8):
                        b = d % 2
                        p.dma("sp", lambda e, d=d, b=b: e.dma_start(out=wsq[b][:, :], in_=cout_b.ap()[d]), reads=["cout.%d" % d], writes=["wsq%d" % b])

                        def mm(e, b=b):
                            for c in range(8):
                                ins = e.matmul(PS[4 + b][:, :], wsq[b][:, c * 128:(c + 1) * 128], bgy[:, c, :], start=(c == 0), stop=(c == 7))
                            return ins
                        p.op("pe", mm, reads=["wsq%d" % b] + ["bgy.%d" % c for c in range(8)], writes=[PSN[4 + b]])
                        p.op("dve", lambda e, d=d, b=b: e.tensor_tensor(out=h[:, d, :], in0=PS[4 + b][:, :], in1=h[:, d, :], op=ALU.add),
                             reads=[PSN[4 + b], "h.%d" % d], writes=["h.%d" % d])
                    ffn(h, "h", 1, 16, bufs)
                    p.dma("sp", lambda e, t0=t0: e.dma_start(out=h1v[:, :, t0:t0 + 512], in_=h[:, :, :]), reads=HN, writes=["h1T"])
                    rms_to_xn(h, "h", 24, sq, xn, rs)
                    for un in range(12):
                        b = un % 2
                        k4 = un % 4
                        p.dma("sp", lambda e, un=un, b=b: e.dma_start(out=wsq[b][:, :], in_=wkf_b.ap()[un]), reads=["wkf.%d" % un], writes=["wsq%d" % b])

                        def mm(e, b=b, k4=k4):
                            for c in range(8):
                                ins = e.matmul(PS[k4][:, :], wsq[b][:, c * 128:(c + 1) * 128], xn[:, c, :], start=(c == 0), stop=(c == 7))
                            return ins
                        p.op("pe", mm, reads=["xn", "wsq%d" % b], writes=[PSN[k4]])
                        if un < 4:
                            p.op("act", lambda e, k4=k4: e.activation(out=ksb[k4][:, :], in_=PS[k4][:, :], func=AF.Copy), reads=[PSN[k4]], writes=["ksb%d" % k4])
                            p.dma("sp", lambda e, un=un, k4=k4, t0=t0: e.dma_start(out=rawT.ap()[un][:, t0:t0 + 512], in_=ksb[k4][:, :]),
                                  reads=["ksb%d" % k4], writes=["rawT"])
                        else:
                            br = 1 if un < 8 else 2
                            g = (un - 4) % 4
                            dst = ksT if br == 1 else kwT
                            p.op("act", lambda e, k4=k4: e.activation(out=sgk[k4][:, :], in_=PS[k4][:, :], func=AF.Square), reads=[PSN[k4]], writes=["sgk%d" % k4])
                            p.op("dve", lambda e, k4=k4: e.tensor_copy(out=ksb[k4][:, :], in_=sgk[k4][:, :]), reads=["sgk%d" % k4], writes=["ksb%d" % k4])
                            p.op("pe", lambda e, b=b, k4=k4: e.matmul(PS[4 + b][:, :], bd64[:, :], ksb[k4][:, :], start=True, stop=True),
                                 reads=["ksb%d" % k4, "bd64"], writes=[PSN[4 + b]])
                            rstd_from(4 + b, 64, sgk[k4], "sgk%d" % k4)
                            p.op("dve", lambda e, k4=k4, br=br: e.scalar_tensor_tensor(out=ksb[k4][:, :], in0=PS[k4][:, :], scalar=kn_t[:, br:br + 1],
                                                                                      in1=sgk[k4][:, :], op0=ALU.mult, op1=ALU.mult),
                                 reads=[PSN[k4], "sgk%d" % k4, "kn"], writes=["ksb%d" % k4])
                            p.dma("sp", lambda e, g=g, k4=k4, t0=t0, dst=dst: e.dma_start(out=dst.ap()[g][:, t0:t0 + 512], in_=ksb[k4][0:64, :]),
                                  reads=["ksb%d" % k4], writes=["ksT" if br == 1 else "kwT"])
                    for tb in range(4):
                        b = tb % 2

                        def mm(e, tb=tb, b=b):
                            for c in range(8):
                                ins = e.matmul(PS[6 + b][:, :], xn[:, c, tb * 128:(tb + 1) * 128], wvt_t[:, c * 512:(c + 1) * 512],
                                               start=(c == 0), stop=(c == 7))
                            return ins
                        p.op("pe", mm, reads=["xn", "wvtt"], writes=[PSN[6 + b]])
                        p.op("act", lambda e, b=b: e.activation(out=vtb[b][:, :], in_=PS[6 + b][:, :], func=AF.Copy), reads=[PSN[6 + b]], writes=["vtb%d" % b])
                        p.dma("sp", lambda e, tb=tb, b=b, ti=ti: e.dma_start(out=vsw.ap()[ti * 4 + tb], in_=vtb[b][:, :]), reads=["vtb%d" % b], writes=["vsw"])
        _phase3()
        p.barrier()

        def _phase4():
            with contextlib.ExitStack() as st:
                raw = T(st, [128, S], BF16, "raw")
                w1t = T(st, [128, 4096], BF16, "w1t")
                w2kt = T(st, [128, 64], BF16, "w2kt")
                w2vt = T(st, [128, 64], BF16, "w2vt")
                w2f = T(st, [128, 128], F32, "w2f")
                peTt = T(st, [128, 64], BF16, "peTt")
                peTf = T(st, [128, 64], F32, "peTf")
                c1 = T(st, [128, 2], F32, "c1")
                xx = T(st, [128, 512], F32, "xx")
                tt = T(st, [128, 512], F32, "tt")
                hid = T(st, [128, NCP], BF16, "hid")
                kc_f = T(st, [64, 512], F32, "kc_f")
                kc_b = T(st, [64, 512], BF16, "kc_b")
                vc_b = T(st, [128, 64], BF16, "vc_b")
                bdh = T(st, [64, 64], BF16, "bdh")
                p.dma("sp", lambda e: e.dma_start(out=w2f[:, 0:64], in_=w2k.ap()), writes=["w2f"])
                p.dma("sp", lambda e: e.dma_start(out=w2f[:, 64:128], in_=w2v.ap()), writes=["w2f"])
                p.dma("sp", lambda e: e.dma_start(out=peTf[:, :], in_=peT.ap()), writes=["peTf"])
                p.op("dve", lambda e: e.tensor_copy(out=w2kt[:, :], in_=w2f[:, 0:64]), reads=["w2f"], writes=["w2kt"])
                p.op("dve", lambda e: e.tensor_copy(out=w2vt[:, :], in_=w2f[:, 64:128]), reads=["w2f"], writes=["w2vt"])
                p.op("dve", lambda e: e.tensor_copy(out=peTt[:, :], in_=peTf[:, :]), reads=["peTf"], writes=["peTt"])
                p.op("dve", lambda e: e.memset(bdh[:, :], 1.0), writes=["bdh"])
                p.op("dve", lambda e: e.memset(hid[:, :], 0.0), writes=["hid"])
                for kv in range(2):
                    p.dma("sp", lambda e, kv=kv: e.dma_start(out=w1t[:, :], in_=(w1k_b if kv == 0 else w1v_b).ap()[0]),
                          reads=["w1k.0", "w1v.0"], writes=["w1t"])
                    def mmc(e, kv=kv):
                        for l in range(32):
                            ins = e.matmul(PS[6][:, 0:1], w1t[0:64, l * 128:(l + 1) * 128], peTt[0:64, kv * 32 + l:kv * 32 + l + 1], start=(l == 0), stop=(l == 31))
                        return ins
                    p.op("pe", mmc, reads=["w1t", "peTt"], writes=[PSN[6]])
                    p.op("act", lambda e, kv=kv: e.activation(out=c1[:, kv:kv + 1], in_=PS[6][:, 0:1], func=AF.Copy), reads=[PSN[6]], writes=["c1"])
                    for ch in range(2):
                        p.dma("sp", lambda e, kv=kv, ch=ch: e.dma_start(out=raw[:, :], in_=rawT.ap()[kv * 2 + ch]), reads=["rawT"], writes=["raw"])
                        for gg in range(2):
                            g = ch * 2 + gg
                            r0 = gg * 64
                            for n0 in range(0, NCV, 512):
                                n1 = min(NCV, n0 + 512)
                                nn = n1 - n0

                                def mm(e, r0=r0, n0=n0, nn=nn):
                                    for l in range(32):
                                        a0 = l + 16 * n0
                                        ins = e.matmul(PS[0][:, 0:nn], w1t[r0:r0 + 64, l * 128:(l + 1) * 128], raw[r0:r0 + 64, a0:a0 + 16 * (nn - 1) + 1:16],
                                                       start=(l == 0), stop=(l == 31))
                                    return ins
                                p.op("pe", mm, reads=["raw", "w1t"], writes=[PSN[0]])
                                p.op("act", lambda e, nn=nn, kv=kv: e.activation(out=xx[:, 0:nn], in_=PS[0][:, 0:nn], func=AF.Identity, bias=c1[:, kv:kv + 1], scale=1.0),
                                     reads=[PSN[0], "c1"], writes=["xx"])
                                p.op("dve", lambda e, nn=nn: e.tensor_tensor(out=tt[:, 0:nn], in0=xx[:, 0:nn], in1=xx[:, 0:nn], op=ALU.mult), reads=["xx"], writes=["tt"])
                                p.op("dve", lambda e, nn=nn: e.tensor_scalar(out=tt[:, 0:nn], in0=tt[:, 0:nn], scalar1=0.044715, scalar2=1.0, op0=ALU.mult, op1=ALU.add),
                                     reads=["tt"], writes=["tt"])
                                p.op("dve", lambda e, nn=nn: e.tensor_tensor(out=tt[:, 0:nn], in0=tt[:, 0:nn], in1=xx[:, 0:nn], op=ALU.mult), reads=["tt", "xx"], writes=["tt"])
                                p.op("act", lambda e, nn=nn: e.activation(out=tt[:, 0:nn], in_=tt[:, 0:nn], func=AF.Sigmoid, scale=1.5957691216057308),
                                     reads=["tt"], writes=["tt"])
                                p.op("dve", lambda e, nn=nn, n0=n0: e.tensor_tensor(out=hid[:, n0:n0 + nn], in0=tt[:, 0:nn], in1=xx[:, 0:nn], op=ALU.mult),
                                     reads=["tt", "xx"], writes=["hid"])
                            if kv == 0:
                                CW = min(512, NCP)
                                for n0 in range(0, NCP, CW):
                                    p.op("pe", lambda e, n0=n0: e.matmul(PS[1][0:64, 0:CW], w2kt[:, :], hid[:, n0:n0 + CW], start=True, stop=True),
                                         reads=["hid", "w2kt"], writes=[PSN[1]])
                                    p.op("act", lambda e: e.activation(out=kc_f[:, 0:CW], in_=PS[1][0:64, 0:CW], func=AF.Square), reads=[PSN[1]], writes=["kc_f"])
                                    p.op("dve", lambda e: e.tensor_copy(out=kc_b[:, 0:CW], in_=kc_f[:, 0:CW]), reads=["kc_f"], writes=["kc_b"])
                                    p.op("pe", lambda e: e.matmul(PS[2][0:64, 0:CW], bdh[:, :], kc_b[:, 0:CW], start=True, stop=True), reads=["kc_b", "bdh"], writes=[PSN[2]])
                                    p.op("act", lambda e: e.activation(out=kc_f[:, 0:CW], in_=PS[2][0:64, 0:CW], func=AF.Sqrt, bias=eps_t[0:64, 0:1], scale=1.0 / 64),
                                         reads=[PSN[2], "eps"], writes=["kc_f"])
                                    p.op("dve", lambda e: e.reciprocal(out=kc_f[:, 0:CW], in_=kc_f[:, 0:CW]), reads=["kc_f"], writes=["kc_f"])
                                    p.op("dve", lambda e: e.scalar_tensor_tensor(out=kc_b[:, 0:CW], in0=PS[1][0:64, 0:CW], scalar=kn_t[0:64, 0:1], in1=kc_f[:, 0:CW],
                                                                                op0=ALU.mult, op1=ALU.mult), reads=[PSN[1], "kc_f", "kn"], writes=["kc_b"])
                                    p.dma("sp", lambda e, g=g, n0=n0: e.dma_start(out=kcT.ap()[g][:, n0:n0 + CW], in_=kc_b[:, 0:CW]), reads=["kc_b"], writes=["kcT"])
                            else:
                                for nb in range(NCK):
                                    p.op("pe", lambda e, nb=nb: e.matmul(PS[1][:, 0:64], hid[:, nb * 128:(nb + 1) * 128], w2vt[:, :], start=True, stop=True),
                                         reads=["hid", "w2vt"], writes=[PSN[1]])
                                    p.op("act", lambda e: e.activation(out=vc_b[:, :], in_=PS[1][:, 0:64], func=AF.Copy), reads=[PSN[1]], writes=["vc_b"])
                                    p.dma("sp", lambda e, g=g, nb=nb: e.dma_start(out=vcS.ap()[g][:, nb * 64:(nb + 1) * 64], in_=vc_b[:, :]), reads=["vc_b"], writes=["vcS"])
        _phase4()
        p.barrier()

        def _phase5():
            with contextlib.ExitStack() as st:
                h = T(st, [128, 8, 512], F32, "h")
                hb = T(st, [128, 8, 512], F32, "hb")
                sq = T(st, [128, 8, 512], BF16, "sq")
                xn = T(st, [128, 8, 512], BF16, "xn")
                rs = T(st, [128, 512], F32, "rs")
                act = T(st, [128, NJ, 512], BF16, "act")
                sg = [T(st, [128, 512], F32, "sg") for _ in range(2)]
                wpi = [T(st, [128, 2048], BF16, "wpi") for _ in range(4)]
                wpo = [T(st, [128, NJ * 128], BF16, "wpo") for _ in range(3)]
                bufs = (sq, xn, rs, act, sg, wpi, wpo)
                wqt = [T(st, [128, 512], BF16, "wqt") for _ in range(2)]
                wgt = T(st, [128, 384], BF16, "wgt")
                qb_ = [T(st, [64, 512], BF16, "qb") for _ in range(2)]
                bdh = T(st, [64, 64], BF16, "bdh")
                gsb = T(st, [48, 512], BF16, "gsb")
                p.op("dve", lambda e: e.memset(bdh[:, :], 1.0), writes=["bdh"])
                p.dma("sp", lambda e: e.dma_start(out=wgt[:, :], in_=wg_b.ap()[0]), reads=["wg.0"], writes=["wgt"])
                h1v = h1T.ap().rearrange("(c p) s -> p c s", p=128)
                h2v = h2T.ap().rearrange("(c p) s -> p c s", p=128)
                for i in range(NO):
                    ta, tb_ = (2 * i) * 512, (2 * i + 1) * 512
                    p.dma("sp", lambda e, ta=ta: e.dma_start(out=h[:, :, :], in_=h1v[:, :, ta:ta + 512]), reads=["h1T"], writes=HN)
                    p.dma("sp", lambda e, tb_=tb_: e.dma_start(out=hb[:, :, :], in_=h1v[:, :, tb_:tb_ + 512]), reads=["h1T"], writes=["hb"])
                    p.op("dve", lambda e: e.tensor_scalar(out=h[:, :, :], in0=h[:, :, :], scalar1=par_t[:, 1:2], scalar2=None, op0=ALU.mult),
                         reads=HN + ["par"], writes=HN)
                    p.op("dve", lambda e: e.scalar_tensor_tensor(out=h[:, :, :], in0=hb[:, :, :], scalar=par_t[:, 0:1], in1=h[:, :, :], op0=ALU.mult, op1=ALU.add),
                         reads=HN + ["hb", "par"], writes=HN)
                    ffn(h, "h", 2, 32, bufs)
                    p.dma("sp", lambda e, i=i: e.dma_start(out=h2v[:, :, i * 512:(i + 1) * 512], in_=h[:, :, :]), reads=HN, writes=["h2T"])
                    rms_to_xn(h, "h", 40, sq, xn, rs)
                    if "dbg_xn" in dbg and i == 0:
                        dx = nc.dram_tensor("dbg_xn", [128, 4096], BF16, kind="ExternalOutput")
                        p.dma("sp", lambda e: e.dma_start(out=dx.ap(), in_=xn[:, :, :].rearrange("p c s -> p (c s)")), reads=["xn"], writes=["dbg_xn"])
                        dr = nc.dram_tensor("dbg_rs", [128, 512], F32, kind="ExternalOutput")
                        p.dma("sp", lambda e: e.dma_start(out=dr.ap(), in_=rs[:, :]), reads=["rs"], writes=["dbg_rs"])
                    for hd in range(16):
                        b = hd % 2
                        p.dma("sp", lambda e, hd=hd, b=b: e.dma_start(out=wqt[b][:, :], in_=wq_b.ap()[hd]), reads=["wq.%d" % hd], writes=["wqt%d" % b])

                        def mm(e, b=b):
                            for c in range(8):
                                ins = e.matmul(PS[b][0:64, :], wqt[b][:, c * 64:(c + 1) * 64], xn[:, c, :], start=(c == 0), stop=(c == 7))
                            return ins
                        p.op("pe", mm, reads=["xn", "wqt%d" % b], writes=[PSN[b]])
                        p.op("act", lambda e, b=b: e.activation(out=sg[b][0:64, :], in_=PS[b][0:64, :], func=AF.Square), reads=[PSN[b]], writes=["sg%d" % b])
                        p.op("dve", lambda e, b=b: e.tensor_copy(out=qb_[b][:, :], in_=sg[b][0:64, :]), reads=["sg%d" % b], writes=["qb%d" % b])
                        p.op("pe", lambda e, b=b: e.matmul(PS[2 + b][0:64, :], bdh[:, :], qb_[b][:, :], start=True, stop=True), reads=["qb%d" % b, "bdh"], writes=[PSN[2 + b]])
                        rstd_from(2 + b, 64, sg[b], "sg%d" % b, rows=64)
                        p.op("dve", lambda e, b=b: e.scalar_tensor_tensor(out=qb_[b][:, :], in0=PS[b][0:64, :], scalar=qn_t[0:64, 0:1], in1=sg[b][0:64, :],
                                                                         op0=ALU.mult, op1=ALU.mult), reads=[PSN[b], "sg%d" % b, "qn"], writes=["qb%d" % b])
                        p.dma("sp", lambda e, hd=hd, b=b, i=i: e.dma_start(out=qTs.ap()[hd][:, i * 512:(i + 1) * 512], in_=qb_[b][:, :]),
                              reads=["qb%d" % b], writes=["qTs"])

                    def mmg(e):
                        for c in range(8):
                            ins = e.matmul(PS[4][0:48, :], wgt[:, c * 48:(c + 1) * 48], xn[:, c, :], start=(c == 0), stop=(c == 7))
                        return ins
                    p.op("pe", mmg, reads=["xn", "wgt"], writes=[PSN[4]])
                    p.op("act", lambda e: e.activation(out=gsb[:, :], in_=PS[4][0:48, :], func=AF.Sigmoid), reads=[PSN[4]], writes=["gsb"])
                    p.dma("sp", lambda e, i=i: e.dma_start(out=gsT.ap()[:, i * 512:(i + 1) * 512], in_=gsb[:, :]), reads=["gsb"], writes=["gsT"])
        _phase5()
        p.barrier()

        def _phase6():
            with contextlib.ExitStack() as st:
                ks_t = T(st, [64, S], BF16, "ks_t")
                kw_t = T(st, [64, S], BF16, "kw_t")
                kc_t = T(st, [64, NCP], BF16, "kc_t")
                vsw_t = T(st, [128, NKT, 256], BF16, "vsw_t")
                vc_t = T(st, [128, NCK, 128], BF16, "vc_t")
                expd_t = T(st, [NSB, S], BF16, "expd_t")
                ov_t = T(st, [128, NCK * (NSB + 1)], BF16, "ov_t")
                ov_f = T(st, [128, NCK * (NSB + 1)], F32, "ov_f")
                selg_t = T(st, [48, 48 * 128], BF16, "selg_t")
                negT = T(st, [NSB, SO], BF16, "negT")
                impacc = T(st, [128, NO * 4, NSB], F32, "impacc")
                q_t = [T(st, [64, 512], BF16, "q_t") for _ in range(2)]
                gs_t = T(st, [48, 512], BF16, "gs_t")
                ms_t = T(st, [128, W_S], BF16, "ms_t")
                mw_t = T(st, [128, W_W], BF16, "mw_t")
                mc_t = [T(st, [128, 512], BF16, "mc_t") for _ in range(4)]
                e_t = [T(st, [128, 512], BF16, "e_t") for _ in range(4)]
                p_t = [T(st, [128, 512], BF16, "p_t") for _ in range(4)]
                rl = T(st, [128, 4], F32, "rl")
                sA = T(st, [128, NSB], F32, "sA")
                sB = T(st, [128, NSB], F32, "sB")
                sc = T(st, [128, NSB], F32, "sc")
                sc2 = T(st, [128, NSB], F32, "sc2")
                m8a = T(st, [128, 8], F32, "m8a")
                m8b = T(st, [128, 8], F32, "m8b")
                gbb = [T(st, [128, 3, 512], F32, "gb") for _ in range(2)]
                rr2 = T(st, [128, 512], F32, "rr2")
                rr = T(st, [64, 512], F32, "rr")
                acc = T(st, [64, 512], F32, "acc")
                ocb = T(st, [64, 512], BF16, "ocb")
                ones64 = ones_bf[:, 0:64]
                p.dma("sp", lambda e: e.dma_start(out=expd_t[:, :], in_=expd_b.ap()[0]), reads=["expd.0"], writes=["expd_t"])
                p.dma("sp", lambda e: e.dma_start(out=selg_t[:, :], in_=selg_b.ap()[0]), reads=["selg.0"], writes=["selg_t"])
                p.dma("sp", lambda e: e.dma_start(out=ov_f[:, :], in_=ovm.ap()), writes=["ov_f"])
                p.op("dve", lambda e: e.tensor_copy(out=ov_t[:, :], in_=ov_f[:, :]), reads=["ov_f"], writes=["ov_t"])
                cnt = [0]
                p.op("dve", lambda e: e.memset(vsw_t[:, :, :], 1.0), writes=["kgrp"])
                p.op("dve", lambda e: e.memset(vc_t[:, :, :], 1.0), writes=["kgrp"])
                p.op("dve", lambda e: e.memset(rr2[:, :], 0.0), writes=["rr2"])

                def c_chunks(i):
                    return [nk for nk in range(NCK) if (2 * i + 2) * 512 - 1 >= 2048 * nk + 31]

                mcc = [0]

                def load_mc(hd, i, nk):
                    b = mcc[0] % 4
                    mcc[0] += 1
                    D_ = 1024 * i - 2048 * nk
                    j0 = min(D_, JSAT_C)
                    src = bass.AP(tensor=rep_c, offset=hd * 128 * L_C + 2032 + j0, ap=[[L_C - 16, 128], [1, 512]])
                    p.dma("sp", lambda e, b=b, src=src: e.dma_start(out=mc_t[b][:, :], in_=src), reads=["rep_c.%d" % hd], writes=["mc_t%d" % b])
                    return b

                def score_unit(hd, kmat, kcol0, qbuf, qn_, mask_ap, mask_res, neg_cols=None):
                    b = cnt[0] % 4
                    cnt[0] += 1
                    pb = b

                    def mm(e):
                        ins = e.matmul(PS[pb][:, :], kmat[:, kcol0:kcol0 + 128], qbuf[:, :], start=True, stop=(neg_cols is None))
                        if neg_cols is not None:
                            ins = e.matmul(PS[pb][:, :], expd_t[:, kcol0:kcol0 + 128], negT[:, neg_cols:neg_cols + 512], start=False, stop=True)
                        return ins
                    p.op("pe", mm, reads=["kgrp", qn_, "expd_t", "negT"], writes=[PSN[pb]])
                    p.op("act", lambda e: e.activation(out=e_t[b][:, :], in_=PS[pb][:, :], func=AF.Exp, scale=0.125), reads=[PSN[pb]], writes=["e_t%d" % b])
                    eng = "dve" if b % 2 == 0 else "pool"
                    p.op(eng, lambda e: e.tensor_tensor(out=p_t[b][:, :], in0=e_t[b][:, :], in1=mask_ap, op=ALU.mult),
                         reads=["e_t%d" % b] + mask_res, writes=["p_t%d" % b])
                    return b

                for g in range(4):
                    p.dma("sp", lambda e, g=g: e.dma_start(out=ks_t[:, :], in_=ksT.ap()[g]), reads=["ksT"], writes=["kgrp"])
                    p.dma("sp", lambda e, g=g: e.dma_start(out=kw_t[:, :], in_=kwT.ap()[g]), reads=["kwT"], writes=["kgrp"])
                    p.dma("sp", lambda e, g=g: e.dma_start(out=kc_t[:, :], in_=kcT.ap()[g]), reads=["kcT"], writes=["kgrp"])
                    p.dma("sp", lambda e, g=g: e.dma_start(out=vc_t[:, :, 0:64], in_=vcS.ap()[g].rearrange("p (k c) -> p k c", c=64)), reads=["vcS"], writes=["kgrp"])
                    vsrc = vsw.ap().rearrange("k p c -> p k c")
                    for k0 in range(0, NKT, 8):
                        p.dma("sp", lambda e, g=g, k0=k0: e.dma_start(out=vsw_t[:, k0:k0 + 8, 0:64], in_=vsrc[:, k0:k0 + 8, g * 64:(g + 1) * 64]),
                              reads=["vsw"], writes=["kgrp"])
                        p.dma("sp", lambda e, g=g, k0=k0: e.dma_start(out=vsw_t[:, k0:k0 + 8, 128:192], in_=vsrc[:, k0:k0 + 8, 256 + g * 64:256 + (g + 1) * 64]),
                              reads=["vsw"], writes=["kgrp"])
                    for i in range(NO):
                        chunks = c_chunks(i)
                        for hh in range(4):
                            hd = g * 4 + hh
                            qb = hh % 2
                            p.dma("sp", lambda e, hd=hd, i=i, qb=qb: e.dma_start(out=q_t[qb][:, :], in_=qTs.ap()[hd][:, i * 512:(i + 1) * 512]),
                                  reads=["qTs"], writes=["q_t%d" % qb])
                            pbs = []
                            for nk in chunks:
                                mb = load_mc(hd, i, nk)
                                pbs.append(score_unit(hd, kc_t, nk * 128, q_t[qb], "q_t%d" % qb, mc_t[mb][:, :], ["mc_t%d" % mb]))
                            for qq in range(4):
                                bank = 4 + qq // 2
                                col = (qq % 2) * (NSB + 1)

                                def mm(e, qq=qq, bank=bank, col=col, pbs=pbs, chunks=chunks):
                                    for ii, nk in enumerate(chunks):
                                        ins = e.matmul(PS[bank][:, col:col + NSB + 1], p_t[pbs[ii]][:, qq * 128:(qq + 1) * 128],
                                                       ov_t[:, nk * (NSB + 1):(nk + 1) * (NSB + 1)], start=(ii == 0), stop=(ii == len(chunks) - 1))
                                    return ins
                                p.op("pe", mm, reads=["p_t%d" % b_ for b_ in pbs] + ["ov_t"], writes=[PSN[bank]])
                                p.op("dve", lambda e, qq=qq, bank=bank, col=col: e.tensor_scalar(out=rl[:, qq:qq + 1], in0=PS[bank][:, col + NSB:col + NSB + 1],
                                                                                                scalar1=1e-30, scalar2=None, op0=ALU.add),
                                     reads=[PSN[bank]], writes=["rl"])
                                p.op("dve", lambda e, qq=qq: e.reciprocal(out=rl[:, qq:qq + 1], in_=rl[:, qq:qq + 1]), reads=["rl"], writes=["rl"])
                                if hh == 0:
                                    p.op("dve", lambda e, qq=qq, bank=bank, col=col, i=i: e.tensor_scalar(out=impacc[:, i * 4 + qq, :], in0=PS[bank][:, col:col + NSB],
                                                                                                         scalar1=rl[:, qq:qq + 1], scalar2=None, op0=ALU.mult),
                                         reads=[PSN[bank], "rl"], writes=["impacc"])
                                else:
                                    p.op("dve", lambda e, qq=qq, bank=bank, col=col, i=i: e.scalar_tensor_tensor(out=impacc[:, i * 4 + qq, :], in0=PS[bank][:, col:col + NSB],
                                                                                                                scalar=rl[:, qq:qq + 1], in1=impacc[:, i * 4 + qq, :],
                                                                                                                op0=ALU.mult, op1=ALU.add),
                                         reads=[PSN[bank], "rl", "impacc"], writes=["impacc"])
                        for qq in range(4):
                            qi = i * 4 + qq
                            p.dma("sp", lambda e, qi=qi: e.dma_start(out=sA[:, :], in_=selA.ap()[qi]), writes=["sA"])
                            p.dma("sp", lambda e, qi=qi: e.dma_start(out=sB[:, :], in_=selB.ap()[qi]), writes=["sB"])
                            p.op("dve", lambda e, qi=qi: e.tensor_tensor(out=sc[:, :], in0=impacc[:, qi, :], in1=sA[:, :], op=ALU.mult), reads=["impacc", "sA"], writes=["sc"])
                            p.op("dve", lambda e: e.tensor_tensor(out=sc[:, :], in0=sc[:, :], in1=sB[:, :], op=ALU.add), reads=["sc", "sB"], writes=["sc"])
                            p.op("dve", lambda e: e.max(out=m8a[:, :], in_=sc[:, :]), reads=["sc"], writes=["m8a"])
                            p.op("dve", lambda e: e.match_replace(out=sc2[:, :], in_to_replace=m8a[:, :], in_values=sc[:, :], imm_value=-3e38),
                                 reads=["sc", "m8a"], writes=["sc2"])
                            p.op("dve", lambda e: e.max(out=m8b[:, :], in_=sc2[:, :]), reads=["sc2"], writes=["m8b"])
                            p.op("dve", lambda e: e.tensor_scalar(out=sc2[:, :], in0=sc[:, :], scalar1=m8b[:, 7:8], scalar2=-MASKV, op0=ALU.is_ge, op1=ALU.mult),
                                 reads=["sc", "m8b"], writes=["sc2"])
                            p.op("dve", lambda e: e.tensor_scalar(out=sc2[:, :], in0=sc2[:, :], scalar1=MASKV, scalar2=None, op0=ALU.add), reads=["sc2"], writes=["sc2"])
                            p.op("pe", lambda e: e.transpose(out=PS[6][0:NSB, 0:128], in_=sc2[:, :], identity=ident_t[:, :]), reads=["sc2", "ident"], writes=[PSN[6]])
                            p.op("act", lambda e, qi=qi: e.activation(out=negT[:, qi * 128:(qi + 1) * 128], in_=PS[6][0:NSB, 0:128], func=AF.Copy),
                                 reads=[PSN[6]], writes=["negT"])
                    LA = 3
                    for hh in range(4):
                        hd = g * 4 + hh
                        src_s = bass.AP(tensor=rep_s, offset=hd * 128 * L_S + 127, ap=[[L_S - 1, 128], [1, W_S]])
                        src_w = bass.AP(tensor=rep_w, offset=hd * 128 * L_W + 127, ap=[[L_W - 1, 128], [1, W_W]])
                        p.dma("sp", lambda e, src_s=src_s: e.dma_start(out=ms_t[:, :], in_=src_s), reads=["rep_s.%d" % hd], writes=["ms_t"])
                        p.dma("sp", lambda e, src_w=src_w: e.dma_start(out=mw_t[:, :], in_=src_w), reads=["rep_w.%d" % hd], writes=["mw_t"])
                        pend = []

                        def emit_pv(un, b):
                            ob = 4 + un["br"]
                            first, last, vap, br, gi, hd_, i_ = un["first"], un["last"], un["vap"], un["br"], un["gi"], un["hd"], un["i"]
                            p.op("pe", lambda e: e.matmul(PS[ob][:, :], vap, p_t[b][:, :], start=first, stop=last),
                                 reads=["p_t%d" % b, "kgrp"], writes=[PSN[ob]])
                            if not last:
                                return
                            gbn = "gb%d" % gi
                            p.op("dve", lambda e: e.tensor_scalar(out=rr2[64:128, :], in0=PS[ob][64:128, :], scalar1=1e-30, scalar2=None, op0=ALU.add), reads=[PSN[ob]], writes=["rr2"])
                            p.op("dve", lambda e: e.reciprocal(out=rr2[64:128, :], in_=rr2[64:128, :]), reads=["rr2"], writes=["rr2"])
                            p.op("dve", lambda e: e.tensor_tensor(out=rr2[64:128, :], in0=rr2[64:128, :], in1=gbb[gi][64:128, br, :], op=ALU.mult), reads=["rr2", gbn], writes=["rr2"])
                            p.op("pe", lambda e: e.matmul(PS[7][0:64, :], ident_t[:, 64:128], rr2[:, :], start=True, stop=True), reads=["rr2", "ident"], writes=[PSN[7]])
                            p.op("act", lambda e: e.activation(out=rr[:, :], in_=PS[7][0:64, :], func=AF.Copy), reads=[PSN[7]], writes=["rr"])
                            if br == 0:
                                p.op("dve", lambda e: e.tensor_tensor(out=acc[:, :], in0=PS[ob][0:64, :], in1=rr[:, :], op=ALU.mult), reads=[PSN[ob], "rr"], writes=["acc"])
                            else:
                                p.op("dve", lambda e: e.tensor_tensor(out=rr[:, :], in0=PS[ob][0:64, :], in1=rr[:, :], op=ALU.mult), reads=[PSN[ob], "rr"], writes=["rr"])
                                p.op("dve", lambda e: e.tensor_tensor(out=acc[:, :], in0=acc[:, :], in1=rr[:, :], op=ALU.add), reads=["acc", "rr"], writes=["acc"])
                            if br == 2:
                                p.op("act", lambda e: e.activation(out=ocb[:, :], in_=acc[:, :], func=AF.Copy), reads=["acc"], writes=["ocb"])
                                p.dma("sp", lambda e: e.dma_start(out=ocT.ap()[hd_][:, i_ * 512:(i_ + 1) * 512], in_=ocb[:, :]), reads=["ocb"], writes=["ocT"])

                        for i in range(NO):
                            qb = i % 2
                            gi = i % 2
                            p.dma("sp", lambda e, hd=hd, i=i, qb=qb: e.dma_start(out=q_t[qb][:, :], in_=qTs.ap()[hd][:, i * 512:(i + 1) * 512]),
                                  reads=["qTs"], writes=["q_t%d" % qb])
                            p.dma("sp", lambda e, i=i: e.dma_start(out=gs_t[:, :], in_=gsT.ap()[:, i * 512:(i + 1) * 512]), reads=["gsT"], writes=["gs_t"])
                            for br in range(3):
                                p.op("pe", lambda e, br=br, hd=hd: e.matmul(PS[7][:, :], selg_t[:, (hd * 3 + br) * 128:(hd * 3 + br + 1) * 128], gs_t[:, :],
                                                                           start=True, stop=True), reads=["selg_t", "gs_t"], writes=[PSN[7]])
                                p.op("act", lambda e, br=br, gi=gi: e.activation(out=gbb[gi][64:128, br, :], in_=PS[7][64:128, :], func=AF.Copy), reads=[PSN[7]], writes=["gb%d" % gi])
                            units = []
                            for br in range(3):
                                if br == 0:
                                    kts = c_chunks(i)
                                elif br == 1:
                                    kts = list(range((2 * i + 2) * 4))
                                else:
                                    kts = list(range(max(0, (2 * i - 1) * 4), (2 * i + 2) * 4))
                                for ui, kt in enumerate(kts):
                                    units.append(dict(br=br, kt=kt, first=(ui == 0), last=(ui == len(kts) - 1), gi=gi, hd=hd, i=i))
                            for un in units:
                                kt, br = un["kt"], un["br"]
                                if br == 0:
                                    mb = load_mc(hd, i, kt)
                                    b = score_unit(hd, kc_t, kt * 128, q_t[qb], "q_t%d" % qb, mc_t[mb][:, :], ["mc_t%d" % mb])
                                    un["vap"] = vc_t[:, kt, :]
                                elif br == 1:
                                    j0 = min(1024 * i - 128 * kt + OFFS - 127, JSAT_S)
                                    b = score_unit(hd, ks_t, kt * 128, q_t[qb], "q_t%d" % qb, ms_t[:, j0:j0 + 512], ["ms_t"], neg_cols=i * 512)
                                    un["vap"] = vsw_t[:, kt, 0:128]
                                else:
                                    j0 = 1024 * i - 128 * kt + OFFW - 127
                                    b = score_unit(hd, kw_t, kt * 128, q_t[qb], "q_t%d" % qb, mw_t[:, j0:j0 + 512], ["mw_t"])
                                    un["vap"] = vsw_t[:, kt, 128:256]
                                pend.append((un, b))
                                if len(pend) > LA:
                                    emit_pv(*pend.pop(0))
                        while pend:
                            emit_pv(*pend.pop(0))
        _phase6()
        p.barrier()

        def _phase7():
            with contextlib.ExitStack() as st:
                h = T(st, [128, 8, 512], F32, "h")
                sq = T(st, [128, 8, 512], BF16, "sq")
                xn = T(st, [128, 8, 512], BF16, "xn")
                rs = T(st, [128, 512], F32, "rs")
                act = T(st, [128, NJ, 512], BF16, "act")
                sg = [T(st, [128, 512], F32, "sg") for _ in range(2)]
                wpi = [T(st, [128, 2048], BF16, "wpi") for _ in range(4)]
                wpo = [T(st, [128, NJ * 128], BF16, "wpo") for _ in range(3)]
                bufs = (sq, xn, rs, act, sg, wpi, wpo)
                wo_t = T(st, [64, 16, 1024], BF16, "wo_t")
                oc_t = T(st, [64, 16, 512], BF16, "oc_t")
                p.dma("sp", lambda e: e.dma_start(out=wo_t[:, :, :], in_=wo_b.ap().rearrange("h p n -> p h n")), reads=["wo.%d" % i for i in range(16)], writes=["wo_t"])
                h2v = h2T.ap().rearrange("(c p) s -> p c s", p=128)
                ov = outT.ap().rearrange("(c p) s -> p c s", p=128)
                for i in range(NO):
                    p.dma("sp", lambda e, i=i: e.dma_start(out=h[:, :, :], in_=h2v[:, :, i * 512:(i + 1) * 512]), reads=["h2T"], writes=HN)
                    p.dma("sp", lambda e, i=i: e.dma_start(out=oc_t[:, :, :], in_=ocT.ap().rearrange("h p s -> p h s")[:, :, i * 512:(i + 1) * 512]),
                          reads=["ocT"], writes=["oc_t"])
                    for d in range(8):
                        b = d % 2

                        def mm(e, d=d, b=b):
                            for hd in range(16):
                                ins = e.matmul(PS[b][:, :], wo_t[:, hd, d * 128:(d + 1) * 128], oc_t[:, hd, :], start=(hd == 0), stop=(hd == 15))
                            return ins
                        p.op("pe", mm, reads=["wo_t", "oc_t"], writes=[PSN[b]])
                        p.op("dve", lambda e, d=d, b=b: e.tensor_tensor(out=h[:, d, :], in0=PS[b][:, :], in1=h[:, d, :], op=ALU.add), reads=[PSN[b], "h.%d" % d], writes=["h.%d" % d])
                    ffn(h, "h", 3, 48, bufs)
                    p.dma("sp", lambda e, i=i: e.dma_start(out=ov[:, :, i * 512:(i + 1) * 512], in_=h[:, :, :]), reads=HN, writes=["outT"])
        _phase7()
        p.barrier()
        p.run()
    return nc


def _rel_bucket_table(n):
    import jax
    import jax.numpy as jnp
    import math
    with jax.default_device(jax.devices("cpu")[0]):
        rel = jnp.arange(n)
        nn = jnp.maximum(rel, 0)
        nf = jnp.maximum(nn, 1).astype(jnp.float32)
        large = 16 + (jnp.log(nf / 16) / math.log(4096 / 16) * 16).astype(jnp.int32)
        large = jnp.minimum(large, 31)
        return np.asarray(jnp.where(nn < 16, nn, large))


def prep_shared(S, inputs):
    f = lambda a: np.ascontiguousarray(a, dtype=np.float32)
    NSB = S // 64
    NCP = S // 16
    NCK = NCP // 128
    sh = {}

    def col8(g):
        return g.reshape(8, 128).T
    norms = [inputs["ffn1_norm"][0], inputs["mix_norm"][0], inputs["ffn2_norm"][0], inputs["kv_norm"],
             inputs["ffn1_norm"][1], inputs["mix_norm"][1], inputs["ffn2_norm"][1]]
    sh["gains"] = f(np.concatenate([col8(np.asarray(g)) for g in norms], axis=1))

    def win_r(w):
        w = np.asarray(w).reshape(8, 128, 2, NJ, 128)
        return f(w.transpose(3, 1, 2, 0, 4).reshape(NJ, 128, 2048))

    def wout_r(w):
        w = np.asarray(w).reshape(NJ, 128, 8, 128)
        return f(w.transpose(2, 1, 0, 3).reshape(8, 128, NJ * 128))
    ffn_list = [("ffn1_w_in", "ffn1_w_out", 0), ("ffn2_w_in", "ffn2_w_out", 0), ("ffn1_w_in", "ffn1_w_out", 1), ("ffn2_w_in", "ffn2_w_out", 1)]
    for i, (a, b, l) in enumerate(ffn_list):
        sh["win%d" % i] = win_r(inputs[a][l])
        sh["wout%d" % i] = wout_r(inputs[b][l])

    def sq_r(w, ncol):
        w = np.asarray(w).reshape(8, 128, ncol, 128)
        return w.transpose(2, 1, 0, 3)
    ci = np.asarray(inputs["conv_w_in"][0]).reshape(8, 128, 3, 8, 128)
    sh["cin"] = f(ci.transpose(3, 1, 2, 0, 4).reshape(8, 128, 3072))
    sh["cout"] = f(sq_r(inputs["conv_w_out"][0], 8).reshape(8, 128, 1024))
    cw = np.asarray(inputs["conv_w"][0])
    sh["cwk"] = f(cw.reshape(3, 8, 128).transpose(2, 1, 0).reshape(128, 24))
    wkv = np.asarray(inputs["w_kv"])
    units = []
    for c0 in (0, 128, 256, 384):
        units.append(wkv[:, c0:c0 + 128])
    for br in (1, 2):
        for g in range(4):
            cc = wkv[:, br * 512 + g * 64: br * 512 + (g + 1) * 64]
            units.append(np.concatenate([cc, cc], axis=1))
    sh["wkf"] = f(np.stack([u.reshape(8, 128, 128).transpose(1, 0, 2).reshape(128, 1024) for u in units]))
    wv = np.concatenate([wkv[:, 768:1024], wkv[:, 1280:1536]], axis=1)
    sh["wvt"] = f(wv.reshape(8, 128, 512).transpose(1, 0, 2).reshape(1, 128, 4096))
    wqf = np.asarray(inputs["attn_w_q"][0])
    sh["wq"] = f(np.stack([wqf[:, h * 64:(h + 1) * 64].reshape(8, 128, 64).transpose(1, 0, 2).reshape(128, 512) for h in range(16)]))
    sh["wg"] = f(wqf[:, 1024:1072].reshape(8, 128, 48).transpose(1, 0, 2).reshape(1, 128, 384))
    sh["wo"] = f(np.asarray(inputs["attn_w_o"][0]).reshape(16, 64, 1024))
    knm = np.asarray(inputs["k_norm"])
    sh["kn"] = f(np.concatenate([knm.T, knm.T], axis=0))
    qnm = np.asarray(inputs["attn_q_norm"][0])
    sh["qn"] = f(np.concatenate([qnm, qnm])[:, None])

    def w1_r(w):
        w = np.asarray(w).reshape(32, 64, 128).transpose(1, 0, 2).reshape(64, 4096)
        return f(np.concatenate([w, w], axis=0)[None])
    sh["w1k"] = w1_r(inputs["cmp_w1_k"])
    sh["w1v"] = w1_r(inputs["cmp_w1_v"])
    sh["w2k"] = f(inputs["cmp_w2_k"])
    sh["w2v"] = f(inputs["cmp_w2_v"])
    pk = np.asarray(inputs["cmp_pe_k"]).T
    pv = np.asarray(inputs["cmp_pe_v"]).T
    pe = np.concatenate([pk, pv], axis=1)
    sh["peT"] = f(np.concatenate([pe, pe], axis=0))
    ex = np.zeros((1, NSB, S), np.float32)
    for j in range(NSB):
        ex[0, j, j * 64:(j + 1) * 64] = 1.0
    sh["expd"] = ex
    n = np.arange(NCP)
    cs = n * 16
    ss = np.arange(NSB) * 64
    ovl = np.clip(np.minimum(cs[:, None] + 32, ss[None, :] + 64) - np.maximum(cs[:, None], ss[None, :]), 0, None) / 16.0
    ovl[NCP - 1, :] = 0.0
    ovl1 = np.concatenate([ovl, np.ones((NCP, 1))], axis=1)
    sh["ovm"] = f(ovl1.reshape(NCK, 128, NSB + 1).transpose(1, 0, 2).reshape(128, NCK * (NSB + 1)))
    sg_ = np.zeros((1, 48, 48 * 128), np.float32)
    for c in range(48):
        sg_[0, c, c * 128 + 64:(c + 1) * 128] = 1.0
    sh["selg"] = sg_
    sh["ident"] = np.eye(128, dtype=np.float32)
    return sh


def prep_parity(S, inputs, par):
    f = lambda a: np.ascontiguousarray(a, dtype=np.float32)
    NSB = S // 64
    NO = S // 1024
    pp = {}
    pp["par"] = f(np.tile(np.array([[par, 1 - par]], np.float32), (128, 1)))
    rb = np.asarray(inputs["rel_bias"])
    bt = _rel_bucket_table(max(S, 4096) + 8192)

    def vec(L, off, lo, hi):
        rel = np.arange(L) - off + 512 * par
        ok = (rel >= lo) & (rel < hi)
        idx = bt[np.clip(rel, 0, len(bt) - 1)]
        v = rb[idx, :].T.copy()
        v[:, ~ok] = MASKV
        return f(v)
    pp["bvs"] = vec(L_S, OFFS, 0, 1 << 30)
    pp["bvw"] = vec(L_W, OFFW, 0, 512)
    pp["bvc"] = vec(L_C, OFFC, 0, 1 << 30)
    A = np.zeros((NO * 4, 128, NSB), np.float32)
    B = np.zeros((NO * 4, 128, NSB), np.float32)
    jj = np.arange(NSB)[None, :]
    for i in range(NO):
        for qq in range(4):
            t = (2 * i + par) * 512 + qq * 128 + np.arange(128)
            bt_ = (t // 64)[:, None]
            valid = jj <= bt_
            forced = (jj == 0) | (jj == bt_) | (jj == bt_ - 1)
            A[i * 4 + qq] = (valid & ~forced)
            B[i * 4 + qq] = np.where(valid, np.where(forced, 1e30, 0.0), -1e30)
    pp["selA"] = A
    pp["selB"] = B
    return pp


_CACHE = {}


def run_model(S, inputs, x, dbg=()):
    B = x.shape[0]
    key = (S, tuple(dbg))
    if key not in _CACHE:
        _CACHE[key] = build(S, dbg)
    nc = _CACHE[key]
    sh = prep_shared(S, inputs)
    pps = [prep_parity(S, inputs, 0), prep_parity(S, inputs, 1)]
    in_maps = []
    for core in range(2 * B):
        b, par = core // 2, core % 2
        m = dict(sh)
        m.update(pps[par])
        m["xT"] = np.ascontiguousarray(np.asarray(x[b], dtype=np.float32).T)
        in_maps.append(m)
    res = run_bass_kernel_spmd(nc, in_maps, core_ids=list(range(2 * B)))
    out = np.zeros((B, S, D), np.float32)
    for core in range(2 * B):
        b, par = core // 2, core % 2
        oT = np.asarray(res.results[core]["outT"])
        o = oT.T.reshape(S // 1024, 512, D)
        for i in range(S // 1024):
            t0 = (2 * i + par) * 512
            out[b, t0:t0 + 512] = o[i]
    return out, res


def kernel(**inputs):
    x = np.asarray(inputs["x"])
    out, _ = run_model(x.shape[1], inputs, x)
    return out
```

```python
import contextlib
import numpy as np
import concourse.bass as bass
import concourse.mybir as mybir
from concourse.bass_utils import run_bass_kernel_spmd

F32 = mybir.dt.float32
BF16 = mybir.dt.bfloat16
AF = mybir.ActivationFunctionType
ALU = mybir.AluOpType

ENGS = ("pe", "act", "dve", "pool", "sp")
NDSEM = 32
DSEM_Q = {"sp": 24, "pool": 8, "act": 4}
SAME_ENGINE_SYNC = True

D = 1024
DFF = 2816
NJ = 22
EPS = 1e-6
OFFS, OFFW, OFFC = 1023, 1535, 2063
JSAT_S = 2896 + OFFS
W_S = JSAT_S + 512
L_S = W_S + 127
W_W = 2432
L_W = W_W + 127
JSAT_C = 4959
W_C = JSAT_C + 512
L_C = W_C + 2032
MASKV = -30000.0


class Prog:
    def __init__(self, nc):
        self.nc = nc
        self.ops = {e: [] for e in ENGS}
        self.cnt = {e: 0 for e in ENGS}
        self.waited = {e: {} for e in ENGS}
        self.res_w = {}
        self.res_r = {}
        self.ndma = 0
        self.ndma_q = {}

    def _deps(self, reads, writes):
        deps = {}

        def add(d):
            if d is None:
                return
            k, v = d
            if deps.get(k, 0) < v:
                deps[k] = v
        for r in reads:
            add(self.res_w.get(r))
        for w in writes:
            add(self.res_w.get(w))
            for k, v in self.res_r.get(w, {}).items():
                add((k, v))
        return deps

    def _commit(self, reads, writes, me):
        for r in reads:
            d = self.res_r.setdefault(r, {})
            if d.get(me[0], 0) < me[1]:
                d[me[0]] = me[1]
        for w in writes:
            self.res_w[w] = me
            self.res_r[w] = {}

    def op(self, eng, fn, reads=(), writes=()):
        deps = self._deps(reads, writes)
        waits = []
        for k, v in deps.items():
            if k == eng and (eng == "pe" or not SAME_ENGINE_SYNC):
                continue
            if self.waited[eng].get(k, 0) >= v:
                continue
            self.waited[eng][k] = v
            waits.append((k, v))
        self.cnt[eng] += 1
        me = (eng, self.cnt[eng])
        self.ops[eng].append((waits, fn, me))
        self._commit(reads, writes, me)

    def dma(self, q, fn, reads=(), writes=()):
        deps = self._deps(reads, writes)
        k = self.ndma_q.get(q, 0)
        self.ndma_q[q] = k + 1
        self.ndma += 1
        nsl = DSEM_Q[q]
        slot = k % nsl
        val = 16 * (k // nsl + 1)
        key = ("d", q, slot)
        if val > 16:
            deps[key] = max(deps.get(key, 0), val - 16)
        waits = []
        for kk, v in deps.items():
            if self.waited[q].get(kk, 0) >= v:
                continue
            self.waited[q][kk] = v
            waits.append((kk, v))
        me = (key, val)
        self.ops[q].append((waits, fn, me))
        self._commit(reads, writes, me)

    def barrier(self):
        tgt = [(e, self.cnt[e]) for e in ENGS if self.cnt[e] > 0]
        for q, n in self.ndma_q.items():
            nsl = DSEM_Q[q]
            for k in range(max(0, n - nsl), n):
                tgt.append((("d", q, k % nsl), 16 * (k // nsl + 1)))
        for e in ENGS:
            waits = []
            for k, v in tgt:
                if k == e:
                    continue
                if self.waited[e].get(k, 0) >= v:
                    continue
                self.waited[e][k] = v
                waits.append((k, v))
            if waits:
                self.ops[e].append((waits, None, None))

    def run(self):
        nc = self.nc
        def _phase1():
            with contextlib.ExitStack() as st:
                sems = {}
                for e in ENGS:
                    sems[e] = st.enter_context(nc.semaphore("s_" + e))
                for q_, n_ in DSEM_Q.items():
                    for i in range(n_):
                        sems[("d", q_, i)] = st.enter_context(nc.semaphore("d_%s_%d" % (q_, i)))
                block = st.enter_context(nc.Block())

                def body(ename):
                    def f(eng):
                        for waits, fn, me in self.ops[ename]:
                            for k, v in waits:
                                eng.wait_ge(sems[k], v)
                            if fn is None:
                                continue
                            ins = fn(eng)
                            ins.then_inc(sems[me[0]], 16 if isinstance(me[0], tuple) else 1)
                    return f
                block.tensor(body("pe"))
                block.scalar(body("act"))
                block.vector(body("dve"))
                block.gpsimd(body("pool"))
                block.sync(body("sp"))


        _phase1()
def build(S, dbg=()):
    NT = S // 512
    NO = NT // 2
    SO = S // 2
    NKT = S // 128
    NSB = S // 64
    NCP = S // 16
    NCK = NCP // 128
    NCV = NCP - 1
    nc = bass.Bass("TRN2", target_bir_lowering=False)
    IN = {}

    def inp(name, shape):
        IN[name] = nc.dram_tensor(name, list(shape), F32, kind="ExternalInput")
        return IN[name]

    def scr(name, shape, dt):
        kind = "ExternalOutput" if name in dbg else "Internal"
        return nc.dram_tensor(name, list(shape), dt, kind=kind)

    xT = inp("xT", [D, S])
    gains = inp("gains", [128, 56])
    par = inp("par", [128, 2])
    wins = [inp("win%d" % i, [NJ, 128, 2048]) for i in range(4)]
    wouts = [inp("wout%d" % i, [8, 128, NJ * 128]) for i in range(4)]
    cin = inp("cin", [8, 128, 3072])
    cout = inp("cout", [8, 128, 1024])
    cwk = inp("cwk", [128, 24])
    wkf = inp("wkf", [12, 128, 1024])
    wvt = inp("wvt", [1, 128, 4096])
    wq = inp("wq", [16, 128, 512])
    wg = inp("wg", [1, 128, 384])
    wo = inp("wo", [16, 64, 1024])
    kn = inp("kn", [128, 3])
    qn = inp("qn", [128, 1])
    w1k = inp("w1k", [1, 128, 4096])
    w1v = inp("w1v", [1, 128, 4096])
    w2k = inp("w2k", [128, 64])
    w2v = inp("w2v", [128, 64])
    peT = inp("peT", [128, 64])
    bvs = inp("bvs", [16, L_S])
    bvw = inp("bvw", [16, L_W])
    bvc = inp("bvc", [16, L_C])
    expd = inp("expd", [1, NSB, S])
    ovm = inp("ovm", [128, NCK * (NSB + 1)])
    selg = inp("selg", [1, 48, 48 * 128])
    selA = inp("selA", [NO * 4, 128, NSB])
    selB = inp("selB", [NO * 4, 128, NSB])
    ident_in = inp("ident", [128, 128])
    outT = nc.dram_tensor("outT", [D, SO], F32, kind="ExternalOutput")

    def bfc(name, src):
        return scr(name + "_bf", list(src.shape), BF16)
    wins_b = [bfc("win%d" % i, wins[i]) for i in range(4)]
    wouts_b = [bfc("wout%d" % i, wouts[i]) for i in range(4)]
    cin_b, cout_b, wkf_b, wvt_b = bfc("cin", cin), bfc("cout", cout), bfc("wkf", wkf), bfc("wvt", wvt)
    wq_b, wg_b, wo_b = bfc("wq", wq), bfc("wg", wg), bfc("wo", wo)
    w1k_b, w1v_b, expd_b, selg_b = bfc("w1k", w1k), bfc("w1v", w1v), bfc("expd", expd), bfc("selg", selg)

    h1T = scr("h1T", [D, S], F32)
    h2T = scr("h2T", [D, SO], F32)
    rawT = scr("rawT", [4, 128, S], BF16)
    ksT = scr("ksT", [4, 64, S], BF16)
    kwT = scr("kwT", [4, 64, S], BF16)
    vsw = scr("vsw", [NKT, 128, 512], BF16)
    kcT = scr("kcT", [4, 64, NCP], BF16)
    vcS = scr("vcS", [4, 128, NCK * 64], BF16)
    qTs = scr("qTs", [16, 64, SO], BF16)
    gsT = scr("gsT", [48, SO], BF16)
    ocT = scr("ocT", [16, 64, SO], BF16)
    rep_s = scr("rep_s", [16, 128, L_S], BF16)
    rep_w = scr("rep_w", [16, 128, L_W], BF16)
    rep_c = scr("rep_c", [16, 128, L_C], BF16)

    p = Prog(nc)
    HN = ["h.%d" % c for c in range(8)]
    sb = nc.sbuf_tensor
    uid = [0]

    def T(st, shape, dt, nm="t"):
        uid[0] += 1
        return st.enter_context(sb("%s_%d" % (nm, uid[0]), list(shape), dt))

    with contextlib.ExitStack() as top:
        PS = [top.enter_context(nc.psum_tensor("ps%d" % i, [128, 512], F32)) for i in range(8)]
        PSN = ["ps%d" % i for i in range(8)]
        gains_t = T(top, [128, 56], F32, "gains")
        par_t = T(top, [128, 2], F32, "par")
        cwk_t = T(top, [128, 24], F32, "cwk")
        kn_t = T(top, [128, 3], F32, "kn")
        qn_t = T(top, [128, 1], F32, "qn")
        eps_t = T(top, [128, 1], F32, "eps")
        ones_bf = T(top, [128, 128], BF16, "ones")
        ones_f = T(top, [128, 128], F32, "onesf")
        ident_t = T(top, [128, 128], F32, "ident")
        for dst, src, nm in ((gains_t, gains, "gains"), (par_t, par, "par"), (cwk_t, cwk, "cwk"),
                             (kn_t, kn, "kn"), (qn_t, qn, "qn"), (ident_t, ident_in, "ident")):
            p.dma("sp", lambda e, dst=dst, src=src: e.dma_start(out=dst[:], in_=src.ap()), writes=[nm])
        p.op("dve", lambda e: e.memset(eps_t[:], EPS), writes=["eps"])
        p.op("dve", lambda e: e.memset(ones_bf[:], 1.0), writes=["ones"])
        p.op("dve", lambda e: e.memset(ones_f[:], 1.0), writes=["onesf"])

        def precast(src, dst, nm):
            n0 = src.shape[0]
            for i in range(n0):
                p.dma("pool", lambda e, i=i: e.dma_start(out=dst.ap()[i], in_=src.ap()[i]), writes=["%s.%d" % (nm, i)])
        order = [(wins[0], wins_b[0], "win0"), (wouts[0], wouts_b[0], "wout0"), (cin, cin_b, "cin"), (cout, cout_b, "cout"),
                 (wins[1], wins_b[1], "win1"), (wouts[1], wouts_b[1], "wout1"), (wkf, wkf_b, "wkf"), (wvt, wvt_b, "wvt"),
                 (w1k, w1k_b, "w1k"), (w1v, w1v_b, "w1v"),
                 (wins[2], wins_b[2], "win2"), (wouts[2], wouts_b[2], "wout2"), (wq, wq_b, "wq"), (wg, wg_b, "wg"),
                 (expd, expd_b, "expd"), (selg, selg_b, "selg"), (wo, wo_b, "wo"),
                 (wins[3], wins_b[3], "win3"), (wouts[3], wouts_b[3], "wout3")]
        early, late = order[:10], order[10:]
        for s_, d_, n_ in early:
            precast(s_, d_, n_)
        late_items = []
        for s_, d_, n_ in late:
            for i_ in range(s_.shape[0]):
                late_items.append((s_, d_, n_, i_))

        def _phase2():
            with contextlib.ExitStack() as st:
                vrow = [T(st, [1, L_C], F32, "vrow") for _ in range(2)]
                ev = [T(st, [128, L_C], BF16, "ev") for _ in range(2)]
                k = 0
                for src, rep, L, nm in ((bvs, rep_s, L_S, "rep_s"), (bvw, rep_w, L_W, "rep_w"), (bvc, rep_c, L_C, "rep_c")):
                    for h in range(16):
                        b = k % 2
                        k += 1
                        p.dma("sp", lambda e, b=b, h=h, src=src, L=L: e.dma_start(out=vrow[b][0:1, 0:L], in_=src.ap()[h:h + 1, :]),
                              writes=["vrow%d" % b])
                        for c0 in range(0, L, 512):
                            c1 = min(L, c0 + 512)
                            pb = (c0 // 512) % 2
                            p.op("pe", lambda e, b=b, c0=c0, c1=c1, pb=pb: e.matmul(PS[pb][:, 0:c1 - c0], ones_f[0:1, :], vrow[b][0:1, c0:c1],
                                                                                  start=True, stop=True),
                                 reads=["vrow%d" % b, "onesf"], writes=[PSN[pb]])
                            p.op("act", lambda e, b=b, c0=c0, c1=c1, pb=pb: e.activation(out=ev[b][:, c0:c1], in_=PS[pb][:, 0:c1 - c0], func=AF.Exp),
                                 reads=[PSN[pb]], writes=["ev%d" % b])
                        p.dma("sp", lambda e, b=b, h=h, rep=rep, L=L: e.dma_start(out=rep.ap()[h], in_=ev[b][:, 0:L]),
                              reads=["ev%d" % b], writes=["%s.%d" % (nm, h)])
        _phase2()
        p.barrier()

        def rstd_from(ps_i, n, rs, rsn, rows=128):
            p.op("act", lambda e: e.activation(out=rs[0:rows, :], in_=PS[ps_i][0:rows, :], func=AF.Sqrt, bias=eps_t[0:rows, 0:1], scale=1.0 / n),
                 reads=[PSN[ps_i], "eps"], writes=[rsn])
            p.op("dve", lambda e: e.reciprocal(out=rs[0:rows, :], in_=rs[0:rows, :]), reads=[rsn], writes=[rsn])

        def rms_to_xn(h, hn, gcol, sq, xn, rs):
            for c in range(8):
                p.op("act", lambda e, c=c: e.activation(out=sq[:, c, :], in_=h[:, c, :], func=AF.Square), reads=["%s.%d" % (hn, c)], writes=["sq"])

            def mm(e):
                for c in range(8):
                    ins = e.matmul(PS[7][:, :], ones_bf[:, :], sq[:, c, :], start=(c == 0), stop=(c == 7))
                return ins
            p.op("pe", mm, reads=["sq", "ones"], writes=[PSN[7]])
            rstd_from(7, D, rs, "rs")
            for c in range(8):
                p.op("dve", lambda e, c=c: e.scalar_tensor_tensor(out=xn[:, c, :], in0=h[:, c, :], scalar=gains_t[:, gcol + c:gcol + c + 1],
                                                                 in1=rs[:, :], op0=ALU.mult, op1=ALU.mult),
                     reads=["%s.%d" % (hn, c), "rs", "gains"], writes=["xn"])

        def ffn(h, hn, li, gcol, bufs):
            sq, xn, rs, act, sg, wpi, wpo = bufs
            rms_to_xn(h, hn, gcol, sq, xn, rs)
            for j in range(NJ):
                b = j % 2
                wb = j % len(wpi)
                p.dma("sp", lambda e, j=j, wb=wb: e.dma_start(out=wpi[wb][:, :], in_=wins_b[li].ap()[j]),
                      reads=["win%d.%d" % (li, j)], writes=["wpi%d" % wb])
                for s_ in range(2):
                    def mm(e, s_=s_, b=b, wb=wb):
                        for c in range(8):
                            ins = e.matmul(PS[b * 2 + s_][:, :], wpi[wb][:, (s_ * 8 + c) * 128:(s_ * 8 + c + 1) * 128], xn[:, c, :],
                                           start=(c == 0), stop=(c == 7))
                        return ins
                    p.op("pe", mm, reads=["xn", "wpi%d" % wb], writes=[PSN[b * 2 + s_]])
                p.op("act", lambda e, b=b: e.activation(out=sg[b][:, :], in_=PS[b * 2][:, :], func=AF.Silu), reads=[PSN[b * 2]], writes=["sg%d" % b])
                p.op("dve", lambda e, b=b, j=j: e.tensor_tensor(out=act[:, j, :], in0=sg[b][:, :], in1=PS[b * 2 + 1][:, :], op=ALU.mult),
                     reads=["sg%d" % b, PSN[b * 2 + 1]], writes=["act.%d" % j])
            for d in range(8):
                b = d % 2
                wb = d % len(wpo)
                p.dma("sp", lambda e, d=d, wb=wb: e.dma_start(out=wpo[wb][:, :], in_=wouts_b[li].ap()[d]),
                      reads=["wout%d.%d" % (li, d)], writes=["wpo%d" % wb])

                def mm(e, b=b, wb=wb):
                    for j in range(NJ):
                        ins = e.matmul(PS[4 + b][:, :], wpo[wb][:, j * 128:(j + 1) * 128], act[:, j, :], start=(j == 0), stop=(j == NJ - 1))
                    return ins
                p.op("pe", mm, reads=["wpo%d" % wb] + ["act.%d" % j for j in range(NJ)], writes=[PSN[4 + b]])
                p.op("dve", lambda e, d=d, b=b: e.scalar_tensor_tensor(out=h[:, d, :], in0=PS[4 + b][:, :], scalar=0.5, in1=h[:, d, :],
                                                                      op0=ALU.mult, op1=ALU.add),
                     reads=[PSN[4 + b], "%s.%d" % (hn, d)], writes=["%s.%d" % (hn, d)])

        def _phase3():
            with contextlib.ExitStack() as st:
                h = T(st, [128, 8, 512], F32, "h")
                sq = T(st, [128, 8, 512], BF16, "sq")
                xn = T(st, [128, 8, 512], BF16, "xn")
                rs = T(st, [128, 512], F32, "rs")
                act = T(st, [128, NJ, 512], BF16, "act")
                sg = [T(st, [128, 512], F32, "sg") for _ in range(2)]
                wpi = [T(st, [128, 2048], BF16, "wpi") for _ in range(4)]
                wpo = [T(st, [128, NJ * 128], BF16, "wpo") for _ in range(3)]
                bufs = (sq, xn, rs, act, sg, wpi, wpo)
                wci = [T(st, [128, 3072], BF16, "wci") for _ in range(2)]
                wsq = [T(st, [128, 1024], BF16, "wsq") for _ in range(2)]
                u = [T(st, [128, 514], F32, "u") for _ in range(2)]
                ucar = T(st, [128, 8, 2], F32, "ucar")
                yb = [T(st, [128, 512], F32, "yb") for _ in range(2)]
                bgy = T(st, [128, 8, 512], BF16, "bgy")
                wvt_t = T(st, [128, 4096], BF16, "wvt")
                wkf_t = T(st, [128, 12, 1024], BF16, "wkf_t")
                bd64 = T(st, [128, 128], BF16, "bd64")
                ksb = [T(st, [128, 512], BF16, "ksb") for _ in range(4)]
                sgk = [T(st, [128, 512], F32, "sgk") for _ in range(4)]
                vtb = [T(st, [128, 512], BF16, "vtb") for _ in range(2)]
                p.op("dve", lambda e: e.memset(ucar[:], 0.0), writes=["ucar"])
                p.op("dve", lambda e: e.memset(bd64[:], 0.0), writes=["bd64"])
                p.op("dve", lambda e: e.memset(bd64[0:64, 0:64], 1.0), writes=["bd64"])
                p.op("dve", lambda e: e.memset(bd64[64:128, 64:128], 1.0), writes=["bd64"])
                p.dma("sp", lambda e: e.dma_start(out=wvt_t[:, :], in_=wvt_b.ap()[0]), reads=["wvt.0"], writes=["wvtt"])
                p.dma("sp", lambda e: e.dma_start(out=wkf_t[:, :, :], in_=wkf_b.ap().rearrange("u p n -> p u n")), reads=["wkf.%d" % u_ for u_ in range(12)], writes=["wkf_t"])
                xTv = xT.ap().rearrange("(c p) s -> p c s", p=128)
                h1v = h1T.ap().rearrange("(c p) s -> p c s", p=128)
                for ti in range(NT):
                    t0 = ti * 512
                    p.dma("sp", lambda e, t0=t0: e.dma_start(out=h[:, :, :], in_=xTv[:, :, t0:t0 + 512]), writes=HN)
                    if ti >= 1 and late_items:
                        nper = (len(late_items) + max(1, NT - 2) - 1) // max(1, NT - 2) if ti == 1 else nper_keep[0]
                        nper_keep[0] = nper
                        for _ in range(min(nper, len(late_items))):
                            s_, d_, n_, i_ = late_items.pop(0)
                            p.dma("pool", lambda e, s_=s_, d_=d_, i_=i_: e.dma_start(out=d_.ap()[i_], in_=s_.ap()[i_]), writes=["%s.%d" % (n_, i_)])
                    ffn(h, "h", 0, 0, bufs)
                    rms_to_xn(h, "h", 8, sq, xn, rs)
                    for i in range(8):
                        b = i % 2
                        p.dma("sp", lambda e, i=i, b=b: e.dma_start(out=wci[b][:, :], in_=cin_b.ap()[i]), reads=["cin.%d" % i], writes=["wci%d" % b])
                        cb = (0, 1, 2) if i % 2 == 0 else (3, 6, 7)
                        for s_ in range(3):
                            def mm(e, s_=s_, b=b, cb=cb):
                                for c in range(8):
                                    ins = e.matmul(PS[cb[s_]][:, :], wci[b][:, (s_ * 8 + c) * 128:(s_ * 8 + c + 1) * 128], xn[:, c, :],
                                                   start=(c == 0), stop=(c == 7))
                                return ins
                            p.op("pe", mm, reads=["xn", "wci%d" % b], writes=[PSN[cb[s_]]])
                        p.op("dve", lambda e, i=i, b=b: e.tensor_copy(out=u[b][:, 0:2], in_=ucar[:, i, :]), reads=["ucar"], writes=["u%d" % b])
                        p.op("act", lambda e, b=b, cb=cb: e.activation(out=yb[b][:, :], in_=PS[cb[1]][:, :], func=AF.Copy), reads=[PSN[cb[1]]], writes=["yb%d" % b])
                        p.op("dve", lambda e, b=b, cb=cb: e.tensor_tensor(out=u[b][:, 2:514], in0=yb[b][:, :], in1=PS[cb[2]][:, :], op=ALU.mult),
                             reads=["yb%d" % b, PSN[cb[2]]], writes=["u%d" % b])
                        p.op("dve", lambda e, i=i, b=b: e.tensor_copy(out=ucar[:, i, :], in_=u[b][:, 512:514]), reads=["u%d" % b], writes=["ucar"])
                        p.op("dve", lambda e, i=i, b=b: e.tensor_scalar(out=yb[b][:, :], in0=u[b][:, 2:514], scalar1=cwk_t[:, i * 3 + 2:i * 3 + 3], scalar2=None,
                                                                      op0=ALU.mult), reads=["u%d" % b, "cwk"], writes=["yb%d" % b])
                        for w_ in (1, 0):
                            p.op("dve", lambda e, i=i, b=b, w_=w_: e.scalar_tensor_tensor(out=yb[b][:, :], in0=u[b][:, w_:w_ + 512],
                                                                                         scalar=cwk_t[:, i * 3 + w_:i * 3 + w_ + 1], in1=yb[b][:, :],
                                                                                         op0=ALU.mult, op1=ALU.add),
                                 reads=["u%d" % b, "cwk", "yb%d" % b], writes=["yb%d" % b])
                        p.op("dve", lambda e, i=i, b=b, cb=cb: e.tensor_tensor(out=bgy[:, i, :], in0=yb[b][:, :], in1=PS[cb[0]][:, :], op=ALU.mult),
                             reads=["yb%d" % b, PSN[cb[0]]], writes=["bgy.%d" % i])
                    for d in range(8):
                        b = d % 2
                        p.dma("sp", lambda e, d=d, b=b: e.dma_start(out=wsq[b][:, :], in_=cout_b.ap()[d]), reads=["cout.%d" % d], writes=["wsq%d" % b])

                        def mm(e, b=b):
                            for c in range(8):
                                ins = e.matmul(PS[4 + b][:, :], wsq[b][:, c * 128:(c + 1) * 128], bgy[:, c, :], start=(c == 0), stop=(c == 7))
                            return ins
                        p.op("pe", mm, reads=["wsq%d" % b] + ["bgy.%d" % c for c in range(8)], writes=[PSN[4 + b]])
                        p.op("dve", lambda e, d=d, b=b: e.tensor_tensor(out=h[:, d, :], in0=PS[4 + b][:, :], in1=h[:, d, :], op=ALU.add),
                             reads=[PSN[4 + b], "h.%d" % d], writes=["h.%d" % d])
                    ffn(h, "h", 1, 16, bufs)
                    p.dma("pool", lambda e, t0=t0: e.dma_start(out=h1v[:, :, t0:t0 + 512], in_=h[:, :, :]), reads=HN, writes=["h1T"])
                    rms_to_xn(h, "h", 24, sq, xn, rs)
                    for un in range(12):
                        b = un % 2
                        k4 = un % 4
                        def mm(e, un=un, k4=k4):
                            for c in range(8):
                                ins = e.matmul(PS[k4][:, :], wkf_t[:, un, c * 128:(c + 1) * 128], xn[:, c, :], start=(c == 0), stop=(c == 7))
                            return ins
                        p.op("pe", mm, reads=["xn", "wkf_t"], writes=[PSN[k4]])
                        if un < 4:
                            p.op("act", lambda e, k4=k4: e.activation(out=ksb[k4][:, :], in_=PS[k4][:, :], func=AF.Copy), reads=[PSN[k4]], writes=["ksb%d" % k4])
                            p.dma("pool", lambda e, un=un, k4=k4, t0=t0: e.dma_start(out=rawT.ap()[un][:, t0:t0 + 512], in_=ksb[k4][:, :]),
                                  reads=["ksb%d" % k4], writes=["rawT"])
                        else:
                            br = 1 if un < 8 else 2
                            g = (un - 4) % 4
                            dst = ksT if br == 1 else kwT
                            p.op("act", lambda e, k4=k4: e.activation(out=sgk[k4][:, :], in_=PS[k4][:, :], func=AF.Square), reads=[PSN[k4]], writes=["sgk%d" % k4])
                            p.op("dve", lambda e, k4=k4: e.tensor_copy(out=ksb[k4][:, :], in_=sgk[k4][:, :]), reads=["sgk%d" % k4], writes=["ksb%d" % k4])
                            p.op("pe", lambda e, b=b, k4=k4: e.matmul(PS[4 + b][:, :], bd64[:, :], ksb[k4][:, :], start=True, stop=True),
                                 reads=["ksb%d" % k4, "bd64"], writes=[PSN[4 + b]])
                            rstd_from(4 + b, 64, sgk[k4], "sgk%d" % k4)
                            p.op("dve", lambda e, k4=k4, br=br: e.scalar_tensor_tensor(out=ksb[k4][:, :], in0=PS[k4][:, :], scalar=kn_t[:, br:br + 1],
                                                                                      in1=sgk[k4][:, :], op0=ALU.mult, op1=ALU.mult),
                                 reads=[PSN[k4], "sgk%d" % k4, "kn"], writes=["ksb%d" % k4])
                            p.dma("pool", lambda e, g=g, k4=k4, t0=t0, dst=dst: e.dma_start(out=dst.ap()[g][:, t0:t0 + 512], in_=ksb[k4][0:64, :]),
                                  reads=["ksb%d" % k4], writes=["ksT" if br == 1 else "kwT"])
                    for tb in range(4):
                        b = tb % 2

                        def mm(e, tb=tb, b=b):
                            for c in range(8):
                                ins = e.matmul(PS[6 + b][:, :], xn[:, c, tb * 128:(tb + 1) * 128], wvt_t[:, c * 512:(c + 1) * 512],
                                               start=(c == 0), stop=(c == 7))
                            return ins
                        p.op("pe", mm, reads=["xn", "wvtt"], writes=[PSN[6 + b]])
                        p.op("act", lambda e, b=b: e.activation(out=vtb[b][:, :], in_=PS[6 + b][:, :], func=AF.Copy), reads=[PSN[6 + b]], writes=["vtb%d" % b])
                        p.dma("pool", lambda e, tb=tb, b=b, ti=ti: e.dma_start(out=vsw.ap()[ti * 4 + tb], in_=vtb[b][:, :]), reads=["vtb%d" % b], writes=["vsw"])
        nper_keep = [1]
        _phase3()
        while late_items:
            s_, d_, n_, i_ = late_items.pop(0)
            p.dma("pool", lambda e, s_=s_, d_=d_, i_=i_: e.dma_start(out=d_.ap()[i_], in_=s_.ap()[i_]), writes=["%s.%d" % (n_, i_)])
        p.barrier()

        def _phase4():
            with contextlib.ExitStack() as st:
                raw = T(st, [128, S], BF16, "raw")
                w1t = T(st, [128, 4096], BF16, "w1t")
                w2kt = T(st, [128, 64], BF16, "w2kt")
                w2vt = T(st, [128, 64], BF16, "w2vt")
                w2f = T(st, [128, 128], F32, "w2f")
                peTt = T(st, [128, 64], BF16, "peTt")
                peTf = T(st, [128, 64], F32, "peTf")
                c1 = T(st, [128, 2], F32, "c1")
                xx = T(st, [128, 512], F32, "xx")
                tt = T(st, [128, 512], F32, "tt")
                hid = T(st, [128, NCP], BF16, "hid")
                kc_f = T(st, [64, 512], F32, "kc_f")
                kc_b = T(st, [64, 512], BF16, "kc_b")
                vc_b = T(st, [128, 64], BF16, "vc_b")
                bdh = T(st, [64, 64], BF16, "bdh")
                p.dma("sp", lambda e: e.dma_start(out=w2f[:, 0:64], in_=w2k.ap()), writes=["w2f"])
                p.dma("sp", lambda e: e.dma_start(out=w2f[:, 64:128], in_=w2v.ap()), writes=["w2f"])
                p.dma("sp", lambda e: e.dma_start(out=peTf[:, :], in_=peT.ap()), writes=["peTf"])
                p.op("dve", lambda e: e.tensor_copy(out=w2kt[:, :], in_=w2f[:, 0:64]), reads=["w2f"], writes=["w2kt"])
                p.op("dve", lambda e: e.tensor_copy(out=w2vt[:, :], in_=w2f[:, 64:128]), reads=["w2f"], writes=["w2vt"])
                p.op("dve", lambda e: e.tensor_copy(out=peTt[:, :], in_=peTf[:, :]), reads=["peTf"], writes=["peTt"])
                p.op("dve", lambda e: e.memset(bdh[:, :], 1.0), writes=["bdh"])
                p.op("dve", lambda e: e.memset(hid[:, :], 0.0), writes=["hid"])
                for kv in range(2):
                    p.dma("sp", lambda e, kv=kv: e.dma_start(out=w1t[:, :], in_=(w1k_b if kv == 0 else w1v_b).ap()[0]),
                          reads=["w1k.0", "w1v.0"], writes=["w1t"])
                    def mmc(e, kv=kv):
                        for l in range(32):
                            ins = e.matmul(PS[6][:, 0:1], w1t[0:64, l * 128:(l + 1) * 128], peTt[0:64, kv * 32 + l:kv * 32 + l + 1], start=(l == 0), stop=(l == 31))
                        return ins
                    p.op("pe", mmc, reads=["w1t", "peTt"], writes=[PSN[6]])
                    p.op("act", lambda e, kv=kv: e.activation(out=c1[:, kv:kv + 1], in_=PS[6][:, 0:1], func=AF.Copy), reads=[PSN[6]], writes=["c1"])
                    for ch in range(2):
                        p.dma("sp", lambda e, kv=kv, ch=ch: e.dma_start(out=raw[:, :], in_=rawT.ap()[kv * 2 + ch]), reads=["rawT"], writes=["raw"])
                        for gg in range(2):
                            g = ch * 2 + gg
                            r0 = gg * 64
                            for n0 in range(0, NCV, 512):
                                n1 = min(NCV, n0 + 512)
                                nn = n1 - n0

                                def mm(e, r0=r0, n0=n0, nn=nn):
                                    for l in range(32):
                                        a0 = l + 16 * n0
                                        ins = e.matmul(PS[0][:, 0:nn], w1t[r0:r0 + 64, l * 128:(l + 1) * 128], raw[r0:r0 + 64, a0:a0 + 16 * (nn - 1) + 1:16],
                                                       start=(l == 0), stop=(l == 31))
                                    return ins
                                p.op("pe", mm, reads=["raw", "w1t"], writes=[PSN[0]])
                                p.op("act", lambda e, nn=nn, kv=kv: e.activation(out=xx[:, 0:nn], in_=PS[0][:, 0:nn], func=AF.Identity, bias=c1[:, kv:kv + 1], scale=1.0),
                                     reads=[PSN[0], "c1"], writes=["xx"])
                                p.op("dve", lambda e, nn=nn: e.tensor_tensor(out=tt[:, 0:nn], in0=xx[:, 0:nn], in1=xx[:, 0:nn], op=ALU.mult), reads=["xx"], writes=["tt"])
                                p.op("dve", lambda e, nn=nn: e.tensor_scalar(out=tt[:, 0:nn], in0=tt[:, 0:nn], scalar1=0.044715, scalar2=1.0, op0=ALU.mult, op1=ALU.add),
                                     reads=["tt"], writes=["tt"])
                                p.op("dve", lambda e, nn=nn: e.tensor_tensor(out=tt[:, 0:nn], in0=tt[:, 0:nn], in1=xx[:, 0:nn], op=ALU.mult), reads=["tt", "xx"], writes=["tt"])
                                p.op("act", lambda e, nn=nn: e.activation(out=tt[:, 0:nn], in_=tt[:, 0:nn], func=AF.Sigmoid, scale=1.5957691216057308),
                                     reads=["tt"], writes=["tt"])
                                p.op("dve", lambda e, nn=nn, n0=n0: e.tensor_tensor(out=hid[:, n0:n0 + nn], in0=tt[:, 0:nn], in1=xx[:, 0:nn], op=ALU.mult),
                                     reads=["tt", "xx"], writes=["hid"])
                            if kv == 0:
                                CW = min(512, NCP)
                                for n0 in range(0, NCP, CW):
                                    p.op("pe", lambda e, n0=n0: e.matmul(PS[1][0:64, 0:CW], w2kt[:, :], hid[:, n0:n0 + CW], start=True, stop=True),
                                         reads=["hid", "w2kt"], writes=[PSN[1]])
                                    p.op("act", lambda e: e.activation(out=kc_f[:, 0:CW], in_=PS[1][0:64, 0:CW], func=AF.Square), reads=[PSN[1]], writes=["kc_f"])
                                    p.op("dve", lambda e: e.tensor_copy(out=kc_b[:, 0:CW], in_=kc_f[:, 0:CW]), reads=["kc_f"], writes=["kc_b"])
                                    p.op("pe", lambda e: e.matmul(PS[2][0:64, 0:CW], bdh[:, :], kc_b[:, 0:CW], start=True, stop=True), reads=["kc_b", "bdh"], writes=[PSN[2]])
                                    p.op("act", lambda e: e.activation(out=kc_f[:, 0:CW], in_=PS[2][0:64, 0:CW], func=AF.Sqrt, bias=eps_t[0:64, 0:1], scale=1.0 / 64),
                                         reads=[PSN[2], "eps"], writes=["kc_f"])
                                    p.op("dve", lambda e: e.reciprocal(out=kc_f[:, 0:CW], in_=kc_f[:, 0:CW]), reads=["kc_f"], writes=["kc_f"])
                                    p.op("dve", lambda e: e.scalar_tensor_tensor(out=kc_b[:, 0:CW], in0=PS[1][0:64, 0:CW], scalar=kn_t[0:64, 0:1], in1=kc_f[:, 0:CW],
                                                                                op0=ALU.mult, op1=ALU.mult), reads=[PSN[1], "kc_f", "kn"], writes=["kc_b"])
                                    p.dma("sp", lambda e, g=g, n0=n0: e.dma_start(out=kcT.ap()[g][:, n0:n0 + CW], in_=kc_b[:, 0:CW]), reads=["kc_b"], writes=["kcT"])
                            else:
                                for nb in range(NCK):
                                    p.op("pe", lambda e, nb=nb: e.matmul(PS[1][:, 0:64], hid[:, nb * 128:(nb + 1) * 128], w2vt[:, :], start=True, stop=True),
                                         reads=["hid", "w2vt"], writes=[PSN[1]])
                                    p.op("act", lambda e: e.activation(out=vc_b[:, :], in_=PS[1][:, 0:64], func=AF.Copy), reads=[PSN[1]], writes=["vc_b"])
                                    p.dma("sp", lambda e, g=g, nb=nb: e.dma_start(out=vcS.ap()[g][:, nb * 64:(nb + 1) * 64], in_=vc_b[:, :]), reads=["vc_b"], writes=["vcS"])
        _phase4()
        p.barrier()

        def _phase5():
            with contextlib.ExitStack() as st:
                h = T(st, [128, 8, 512], F32, "h")
                hb = T(st, [128, 8, 512], F32, "hb")
                sq = T(st, [128, 8, 512], BF16, "sq")
                xn = T(st, [128, 8, 512], BF16, "xn")
                rs = T(st, [128, 512], F32, "rs")
                act = T(st, [128, NJ, 512], BF16, "act")
                sg = [T(st, [128, 512], F32, "sg") for _ in range(2)]
                wpi = [T(st, [128, 2048], BF16, "wpi") for _ in range(4)]
                wpo = [T(st, [128, NJ * 128], BF16, "wpo") for _ in range(3)]
                bufs = (sq, xn, rs, act, sg, wpi, wpo)
                wqt = [T(st, [128, 512], BF16, "wqt") for _ in range(2)]
                wgt = T(st, [128, 384], BF16, "wgt")
                qb_ = [T(st, [64, 512], BF16, "qb") for _ in range(2)]
                bdh = T(st, [64, 64], BF16, "bdh")
                gsb = T(st, [48, 512], BF16, "gsb")
                p.op("dve", lambda e: e.memset(bdh[:, :], 1.0), writes=["bdh"])
                p.dma("sp", lambda e: e.dma_start(out=wgt[:, :], in_=wg_b.ap()[0]), reads=["wg.0"], writes=["wgt"])
                h1v = h1T.ap().rearrange("(c p) s -> p c s", p=128)
                h2v = h2T.ap().rearrange("(c p) s -> p c s", p=128)
                for i in range(NO):
                    ta, tb_ = (2 * i) * 512, (2 * i + 1) * 512
                    p.dma("sp", lambda e, ta=ta: e.dma_start(out=h[:, :, :], in_=h1v[:, :, ta:ta + 512]), reads=["h1T"], writes=HN)
                    p.dma("sp", lambda e, tb_=tb_: e.dma_start(out=hb[:, :, :], in_=h1v[:, :, tb_:tb_ + 512]), reads=["h1T"], writes=["hb"])
                    p.op("dve", lambda e: e.tensor_scalar(out=h[:, :, :], in0=h[:, :, :], scalar1=par_t[:, 1:2], scalar2=None, op0=ALU.mult),
                         reads=HN + ["par"], writes=HN)
                    p.op("dve", lambda e: e.scalar_tensor_tensor(out=h[:, :, :], in0=hb[:, :, :], scalar=par_t[:, 0:1], in1=h[:, :, :], op0=ALU.mult, op1=ALU.add),
                         reads=HN + ["hb", "par"], writes=HN)
                    ffn(h, "h", 2, 32, bufs)
                    p.dma("pool", lambda e, i=i: e.dma_start(out=h2v[:, :, i * 512:(i + 1) * 512], in_=h[:, :, :]), reads=HN, writes=["h2T"])
                    rms_to_xn(h, "h", 40, sq, xn, rs)
                    if "dbg_xn" in dbg and i == 0:
                        dx = nc.dram_tensor("dbg_xn", [128, 4096], BF16, kind="ExternalOutput")
                        p.dma("sp", lambda e: e.dma_start(out=dx.ap(), in_=xn[:, :, :].rearrange("p c s -> p (c s)")), reads=["xn"], writes=["dbg_xn"])
                        dr = nc.dram_tensor("dbg_rs", [128, 512], F32, kind="ExternalOutput")
                        p.dma("sp", lambda e: e.dma_start(out=dr.ap(), in_=rs[:, :]), reads=["rs"], writes=["dbg_rs"])
                    for hd in range(16):
                        b = hd % 2
                        p.dma("sp", lambda e, hd=hd, b=b: e.dma_start(out=wqt[b][:, :], in_=wq_b.ap()[hd]), reads=["wq.%d" % hd], writes=["wqt%d" % b])

                        def mm(e, b=b):
                            for c in range(8):
                                ins = e.matmul(PS[b][0:64, :], wqt[b][:, c * 64:(c + 1) * 64], xn[:, c, :], start=(c == 0), stop=(c == 7))
                            return ins
                        p.op("pe", mm, reads=["xn", "wqt%d" % b], writes=[PSN[b]])
                        p.op("act", lambda e, b=b: e.activation(out=sg[b][0:64, :], in_=PS[b][0:64, :], func=AF.Square), reads=[PSN[b]], writes=["sg%d" % b])
                        p.op("dve", lambda e, b=b: e.tensor_copy(out=qb_[b][:, :], in_=sg[b][0:64, :]), reads=["sg%d" % b], writes=["qb%d" % b])
                        p.op("pe", lambda e, b=b: e.matmul(PS[2 + b][0:64, :], bdh[:, :], qb_[b][:, :], start=True, stop=True), reads=["qb%d" % b, "bdh"], writes=[PSN[2 + b]])
                        rstd_from(2 + b, 64, sg[b], "sg%d" % b, rows=64)
                        p.op("dve", lambda e, b=b: e.scalar_tensor_tensor(out=qb_[b][:, :], in0=PS[b][0:64, :], scalar=qn_t[0:64, 0:1], in1=sg[b][0:64, :],
                                                                         op0=ALU.mult, op1=ALU.mult), reads=[PSN[b], "sg%d" % b, "qn"], writes=["qb%d" % b])
                        p.dma("pool", lambda e, hd=hd, b=b, i=i: e.dma_start(out=qTs.ap()[hd][:, i * 512:(i + 1) * 512], in_=qb_[b][:, :]),
                              reads=["qb%d" % b], writes=["qTs"])

                    def mmg(e):
                        for c in range(8):
                            ins = e.matmul(PS[4][0:48, :], wgt[:, c * 48:(c + 1) * 48], xn[:, c, :], start=(c == 0), stop=(c == 7))
                        return ins
                    p.op("pe", mmg, reads=["xn", "wgt"], writes=[PSN[4]])
                    p.op("act", lambda e: e.activation(out=gsb[:, :], in_=PS[4][0:48, :], func=AF.Sigmoid), reads=[PSN[4]], writes=["gsb"])
                    p.dma("pool", lambda e, i=i: e.dma_start(out=gsT.ap()[:, i * 512:(i + 1) * 512], in_=gsb[:, :]), reads=["gsb"], writes=["gsT"])
        _phase5()
        p.barrier()

        def _phase6():
            with contextlib.ExitStack() as st:
                ks_t = T(st, [64, S], BF16, "ks_t")
                kw_t = T(st, [64, S], BF16, "kw_t")
                kc_t = T(st, [64, NCP], BF16, "kc_t")
                vsw_t = T(st, [128, NKT, 256], BF16, "vsw_t")
                vc_t = T(st, [128, NCK, 128], BF16, "vc_t")
                expd_t = T(st, [NSB, S], BF16, "expd_t")
                ov_t = T(st, [128, NCK * (NSB + 1)], BF16, "ov_t")
                ov_f = T(st, [128, NCK * (NSB + 1)], F32, "ov_f")
                selg_t = T(st, [48, 48 * 128], BF16, "selg_t")
                negT = T(st, [NSB, SO], BF16, "negT")
                impacc = T(st, [128, NO * 4, NSB], F32, "impacc")
                q_t = [T(st, [64, 512], BF16, "q_t") for _ in range(2)]
                gs_t = T(st, [48, 512], BF16, "gs_t")
                ms_t = T(st, [128, W_S], BF16, "ms_t")
                mw_t = T(st, [128, W_W], BF16, "mw_t")
                mc_t = [T(st, [128, 512], BF16, "mc_t") for _ in range(4)]
                e_t = [T(st, [128, 512], BF16, "e_t") for _ in range(8)]
                p_t = [T(st, [128, 512], BF16, "p_t") for _ in range(8)]
                rl = T(st, [128, 4], F32, "rl")
                sA = T(st, [128, NSB], F32, "sA")
                sB = T(st, [128, NSB], F32, "sB")
                sc = T(st, [128, NSB], F32, "sc")
                sc2 = T(st, [128, NSB], F32, "sc2")
                m8a = T(st, [128, 8], F32, "m8a")
                m8b = T(st, [128, 8], F32, "m8b")
                gbb = [T(st, [128, 3, 512], F32, "gb") for _ in range(2)]
                rr2b = [T(st, [128, 512], F32, "rr2") for _ in range(3)]
                rr = T(st, [64, 512], F32, "rr")
                acc = T(st, [64, 512], F32, "acc")
                ocb = T(st, [64, 512], BF16, "ocb")
                ones64 = ones_bf[:, 0:64]
                p.dma("sp", lambda e: e.dma_start(out=expd_t[:, :], in_=expd_b.ap()[0]), reads=["expd.0"], writes=["expd_t"])
                p.dma("sp", lambda e: e.dma_start(out=selg_t[:, :], in_=selg_b.ap()[0]), reads=["selg.0"], writes=["selg_t"])
                p.dma("sp", lambda e: e.dma_start(out=ov_f[:, :], in_=ovm.ap()), writes=["ov_f"])
                p.op("dve", lambda e: e.tensor_copy(out=ov_t[:, :], in_=ov_f[:, :]), reads=["ov_f"], writes=["ov_t"])
                cnt = [0]
                p.op("dve", lambda e: e.memset(vsw_t[:, :, :], 1.0), writes=["kgrp"])
                p.op("dve", lambda e: e.memset(vc_t[:, :, :], 1.0), writes=["kgrp"])
                for k_ in range(3):
                    p.op("dve", lambda e, k_=k_: e.memset(rr2b[k_][:, :], 0.0), writes=["rr2.%d" % k_])

                def c_chunks(i):
                    return [nk for nk in range(NCK) if (2 * i + 2) * 512 - 1 >= 2048 * nk + 31]

                mcc = [0]

                def load_mc(hd, i, nk):
                    b = mcc[0] % 4
                    mcc[0] += 1
                    D_ = 1024 * i - 2048 * nk
                    j0 = min(D_, JSAT_C)
                    src = bass.AP(tensor=rep_c, offset=hd * 128 * L_C + 2032 + j0, ap=[[L_C - 16, 128], [1, 512]])
                    p.dma("sp", lambda e, b=b, src=src: e.dma_start(out=mc_t[b][:, :], in_=src), reads=["rep_c.%d" % hd], writes=["mc_t%d" % b])
                    return b

                def score_unit(hd, kmat, kcol0, qbuf, qn_, mask_ap, mask_res, neg_cols=None):
                    b = cnt[0] % 8
                    pb = cnt[0] % 4
                    u3 = cnt[0] % 3
                    cnt[0] += 1

                    def mm(e):
                        ins = e.matmul(PS[pb][:, :], kmat[:, kcol0:kcol0 + 128], qbuf[:, :], start=True, stop=(neg_cols is None))
                        if neg_cols is not None:
                            ins = e.matmul(PS[pb][:, :], expd_t[:, kcol0:kcol0 + 128], negT[:, neg_cols:neg_cols + 512], start=False, stop=True)
                        return ins
                    p.op("pe", mm, reads=["kgrp", qn_, "expd_t", "negT"], writes=[PSN[pb]])
                    p.op("act", lambda e: e.activation(out=e_t[b][:, :], in_=PS[pb][:, :], func=AF.Exp, scale=0.125), reads=[PSN[pb]], writes=["e_t%d" % b])
                    eng = "pool" if u3 == 2 else "dve"
                    p.op(eng, lambda e: e.tensor_tensor(out=p_t[b][:, :], in0=e_t[b][:, :], in1=mask_ap, op=ALU.mult),
                         reads=["e_t%d" % b] + mask_res, writes=["p_t%d" % b])
                    return b

                for g in range(4):
                    p.dma("sp", lambda e, g=g: e.dma_start(out=ks_t[:, :], in_=ksT.ap()[g]), reads=["ksT"], writes=["kgrp"])
                    p.dma("sp", lambda e, g=g: e.dma_start(out=kw_t[:, :], in_=kwT.ap()[g]), reads=["kwT"], writes=["kgrp"])
                    p.dma("sp", lambda e, g=g: e.dma_start(out=kc_t[:, :], in_=kcT.ap()[g]), reads=["kcT"], writes=["kgrp"])
                    p.dma("sp", lambda e, g=g: e.dma_start(out=vc_t[:, :, 0:64], in_=vcS.ap()[g].rearrange("p (k c) -> p k c", c=64)), reads=["vcS"], writes=["kgrp"])
                    vsrc = vsw.ap().rearrange("k p c -> p k c")
                    for k0 in range(0, NKT, 8):
                        p.dma("sp", lambda e, g=g, k0=k0: e.dma_start(out=vsw_t[:, k0:k0 + 8, 0:64], in_=vsrc[:, k0:k0 + 8, g * 64:(g + 1) * 64]),
                              reads=["vsw"], writes=["kgrp"])
                        p.dma("sp", lambda e, g=g, k0=k0: e.dma_start(out=vsw_t[:, k0:k0 + 8, 128:192], in_=vsrc[:, k0:k0 + 8, 256 + g * 64:256 + (g + 1) * 64]),
                              reads=["vsw"], writes=["kgrp"])
                    for i in range(NO):
                        chunks = c_chunks(i)
                        for hh in range(4):
                            hd = g * 4 + hh
                            qb = hh % 2
                            p.dma("sp", lambda e, hd=hd, i=i, qb=qb: e.dma_start(out=q_t[qb][:, :], in_=qTs.ap()[hd][:, i * 512:(i + 1) * 512]),
                                  reads=["qTs"], writes=["q_t%d" % qb])
                            pbs = []
                            for nk in chunks:
                                mb = load_mc(hd, i, nk)
                                pbs.append(score_unit(hd, kc_t, nk * 128, q_t[qb], "q_t%d" % qb, mc_t[mb][:, :], ["mc_t%d" % mb]))
                            for qq in range(4):
                                bank = 4 + qq // 2
                                col = (qq % 2) * (NSB + 1)

                                def mm(e, qq=qq, bank=bank, col=col, pbs=pbs, chunks=chunks):
                                    for ii, nk in enumerate(chunks):
                                        ins = e.matmul(PS[bank][:, col:col + NSB + 1], p_t[pbs[ii]][:, qq * 128:(qq + 1) * 128],
                                                       ov_t[:, nk * (NSB + 1):(nk + 1) * (NSB + 1)], start=(ii == 0), stop=(ii == len(chunks) - 1))
                                    return ins
                                p.op("pe", mm, reads=["p_t%d" % b_ for b_ in pbs] + ["ov_t"], writes=[PSN[bank]])
                                p.op("dve", lambda e, qq=qq, bank=bank, col=col: e.tensor_scalar(out=rl[:, qq:qq + 1], in0=PS[bank][:, col + NSB:col + NSB + 1],
                                                                                                scalar1=1e-30, scalar2=None, op0=ALU.add),
                                     reads=[PSN[bank]], writes=["rl"])
                                p.op("dve", lambda e, qq=qq: e.reciprocal(out=rl[:, qq:qq + 1], in_=rl[:, qq:qq + 1]), reads=["rl"], writes=["rl"])
                                if hh == 0:
                                    p.op("dve", lambda e, qq=qq, bank=bank, col=col, i=i: e.tensor_scalar(out=impacc[:, i * 4 + qq, :], in0=PS[bank][:, col:col + NSB],
                                                                                                         scalar1=rl[:, qq:qq + 1], scalar2=None, op0=ALU.mult),
                                         reads=[PSN[bank], "rl"], writes=["impacc"])
                                else:
                                    p.op("dve", lambda e, qq=qq, bank=bank, col=col, i=i: e.scalar_tensor_tensor(out=impacc[:, i * 4 + qq, :], in0=PS[bank][:, col:col + NSB],
                                                                                                                scalar=rl[:, qq:qq + 1], in1=impacc[:, i * 4 + qq, :],
                                                                                                                op0=ALU.mult, op1=ALU.add),
                                         reads=[PSN[bank], "rl", "impacc"], writes=["impacc"])
                        for qq in range(4):
                            qi = i * 4 + qq
                            p.dma("sp", lambda e, qi=qi: e.dma_start(out=sA[:, :], in_=selA.ap()[qi]), writes=["sA"])
                            p.dma("sp", lambda e, qi=qi: e.dma_start(out=sB[:, :], in_=selB.ap()[qi]), writes=["sB"])
                            p.op("dve", lambda e, qi=qi: e.tensor_tensor(out=sc[:, :], in0=impacc[:, qi, :], in1=sA[:, :], op=ALU.mult), reads=["impacc", "sA"], writes=["sc"])
                            p.op("dve", lambda e: e.tensor_tensor(out=sc[:, :], in0=sc[:, :], in1=sB[:, :], op=ALU.add), reads=["sc", "sB"], writes=["sc"])
                            p.op("dve", lambda e: e.max(out=m8a[:, :], in_=sc[:, :]), reads=["sc"], writes=["m8a"])
                            p.op("dve", lambda e: e.match_replace(out=sc2[:, :], in_to_replace=m8a[:, :], in_values=sc[:, :], imm_value=-3e38),
                                 reads=["sc", "m8a"], writes=["sc2"])
                            p.op("dve", lambda e: e.max(out=m8b[:, :], in_=sc2[:, :]), reads=["sc2"], writes=["m8b"])
                            p.op("dve", lambda e: e.tensor_scalar(out=sc2[:, :], in0=sc[:, :], scalar1=m8b[:, 7:8], scalar2=-MASKV, op0=ALU.is_ge, op1=ALU.mult),
                                 reads=["sc", "m8b"], writes=["sc2"])
                            p.op("dve", lambda e: e.tensor_scalar(out=sc2[:, :], in0=sc2[:, :], scalar1=MASKV, scalar2=None, op0=ALU.add), reads=["sc2"], writes=["sc2"])
                            p.op("pe", lambda e: e.transpose(out=PS[6][0:NSB, 0:128], in_=sc2[:, :], identity=ident_t[:, :]), reads=["sc2", "ident"], writes=[PSN[6]])
                            p.op("act", lambda e, qi=qi: e.activation(out=negT[:, qi * 128:(qi + 1) * 128], in_=PS[6][0:NSB, 0:128], func=AF.Copy),
                                 reads=[PSN[6]], writes=["negT"])
                    LA = 6
                    DEFER = 6
                    epi = [0]
                    for hh in range(4):
                        hd = g * 4 + hh
                        src_s = bass.AP(tensor=rep_s, offset=hd * 128 * L_S + 127, ap=[[L_S - 1, 128], [1, W_S]])
                        src_w = bass.AP(tensor=rep_w, offset=hd * 128 * L_W + 127, ap=[[L_W - 1, 128], [1, W_W]])
                        p.dma("sp", lambda e, src_s=src_s: e.dma_start(out=ms_t[:, :], in_=src_s), reads=["rep_s.%d" % hd], writes=["ms_t"])
                        p.dma("sp", lambda e, src_w=src_w: e.dma_start(out=mw_t[:, :], in_=src_w), reads=["rep_w.%d" % hd], writes=["mw_t"])
                        pend = []

                        pend2 = []

                        def emit_pv(un, b):
                            ob = 4 + un["br"]
                            first, last, vap, br, gi, hd_, i_ = un["first"], un["last"], un["vap"], un["br"], un["gi"], un["hd"], un["i"]
                            p.op("pe", lambda e: e.matmul(PS[ob][:, :], vap, p_t[b][:, :], start=first, stop=last),
                                 reads=["p_t%d" % b, "kgrp"], writes=[PSN[ob]])
                            if last:
                                gbn = "gb%d" % gi
                                k_ = epi[0] % 3
                                epi[0] += 1
                                rr2 = rr2b[k_]
                                rn = "rr2.%d" % k_
                                p.op("dve", lambda e: e.tensor_scalar(out=rr2[64:128, :], in0=PS[ob][64:128, :], scalar1=1e-30, scalar2=None, op0=ALU.add), reads=[PSN[ob]], writes=[rn])
                                p.op("dve", lambda e: e.reciprocal(out=rr2[64:128, :], in_=rr2[64:128, :]), reads=[rn], writes=[rn])
                                p.op("dve", lambda e: e.tensor_tensor(out=rr2[64:128, :], in0=rr2[64:128, :], in1=gbb[gi][64:128, br, :], op=ALU.mult), reads=[rn, gbn], writes=[rn])

                                def stage2():
                                    p.op("pe", lambda e: e.matmul(PS[7][0:64, :], ident_t[:, 64:128], rr2[:, :], start=True, stop=True), reads=[rn, "ident"], writes=[PSN[7]])
                                    p.op("act", lambda e: e.activation(out=rr[:, :], in_=PS[7][0:64, :], func=AF.Copy), reads=[PSN[7]], writes=["rr"])
                                    if br == 0:
                                        p.op("dve", lambda e: e.tensor_tensor(out=acc[:, :], in0=PS[ob][0:64, :], in1=rr[:, :], op=ALU.mult), reads=[PSN[ob], "rr"], writes=["acc"])
                                    else:
                                        p.op("dve", lambda e: e.tensor_tensor(out=rr[:, :], in0=PS[ob][0:64, :], in1=rr[:, :], op=ALU.mult), reads=[PSN[ob], "rr"], writes=["rr"])
                                        p.op("dve", lambda e: e.tensor_tensor(out=acc[:, :], in0=acc[:, :], in1=rr[:, :], op=ALU.add), reads=["acc", "rr"], writes=["acc"])
                                    if br == 2:
                                        p.op("act", lambda e: e.activation(out=ocb[:, :], in_=acc[:, :], func=AF.Copy), reads=["acc"], writes=["ocb"])
                                        p.dma("sp", lambda e: e.dma_start(out=ocT.ap()[hd_][:, i_ * 512:(i_ + 1) * 512], in_=ocb[:, :]), reads=["ocb"], writes=["ocT"])
                                pend2.append([DEFER + 1, stage2])
                            for it in pend2:
                                it[0] -= 1
                            while pend2 and pend2[0][0] <= 0:
                                pend2.pop(0)[1]()

                        for i in range(NO):
                            qb = i % 2
                            gi = i % 2
                            p.dma("sp", lambda e, hd=hd, i=i, qb=qb: e.dma_start(out=q_t[qb][:, :], in_=qTs.ap()[hd][:, i * 512:(i + 1) * 512]),
                                  reads=["qTs"], writes=["q_t%d" % qb])
                            p.dma("sp", lambda e, i=i: e.dma_start(out=gs_t[:, :], in_=gsT.ap()[:, i * 512:(i + 1) * 512]), reads=["gsT"], writes=["gs_t"])
                            for br in range(3):
                                p.op("pe", lambda e, br=br, hd=hd: e.matmul(PS[7][:, :], selg_t[:, (hd * 3 + br) * 128:(hd * 3 + br + 1) * 128], gs_t[:, :],
                                                                           start=True, stop=True), reads=["selg_t", "gs_t"], writes=[PSN[7]])
                                p.op("act", lambda e, br=br, gi=gi: e.activation(out=gbb[gi][64:128, br, :], in_=PS[7][64:128, :], func=AF.Copy), reads=[PSN[7]], writes=["gb%d" % gi])
                            units = []
                            for br in range(3):
                                if br == 0:
                                    kts = c_chunks(i)
                                elif br == 1:
                                    kts = list(range((2 * i + 2) * 4))
                                else:
                                    kts = list(range(max(0, (2 * i - 1) * 4), (2 * i + 2) * 4))
                                for ui, kt in enumerate(kts):
                                    units.append(dict(br=br, kt=kt, first=(ui == 0), last=(ui == len(kts) - 1), gi=gi, hd=hd, i=i))
                            for un in units:
                                kt, br = un["kt"], un["br"]
                                if br == 0:
                                    mb = load_mc(hd, i, kt)
                                    b = score_unit(hd, kc_t, kt * 128, q_t[qb], "q_t%d" % qb, mc_t[mb][:, :], ["mc_t%d" % mb])
                                    un["vap"] = vc_t[:, kt, :]
                                elif br == 1:
                                    j0 = min(1024 * i - 128 * kt + OFFS - 127, JSAT_S)
                                    b = score_unit(hd, ks_t, kt * 128, q_t[qb], "q_t%d" % qb, ms_t[:, j0:j0 + 512], ["ms_t"], neg_cols=i * 512)
                                    un["vap"] = vsw_t[:, kt, 0:128]
                                else:
                                    j0 = 1024 * i - 128 * kt + OFFW - 127
                                    b = score_unit(hd, kw_t, kt * 128, q_t[qb], "q_t%d" % qb, mw_t[:, j0:j0 + 512], ["mw_t"])
                                    un["vap"] = vsw_t[:, kt, 128:256]
                                pend.append((un, b))
                                if len(pend) > LA:
                                    emit_pv(*pend.pop(0))
                        while pend:
                            emit_pv(*pend.pop(0))
                        while pend2:
                            pend2.pop(0)[1]()
        _phase6()
        p.barrier()

        def _phase7():
            with contextlib.ExitStack() as st:
                h = T(st, [128, 8, 512], F32, "h")
                sq = T(st, [128, 8, 512], BF16, "sq")
                xn = T(st, [128, 8, 512], BF16, "xn")
                rs = T(st, [128, 512], F32, "rs")
                act = T(st, [128, NJ, 512], BF16, "act")
                sg = [T(st, [128, 512], F32, "sg") for _ in range(2)]
                wpi = [T(st, [128, 2048], BF16, "wpi") for _ in range(4)]
                wpo = [T(st, [128, NJ * 128], BF16, "wpo") for _ in range(3)]
                bufs = (sq, xn, rs, act, sg, wpi, wpo)
                wo_t = T(st, [64, 16, 1024], BF16, "wo_t")
                oc_t = T(st, [64, 16, 512], BF16, "oc_t")
                p.dma("sp", lambda e: e.dma_start(out=wo_t[:, :, :], in_=wo_b.ap().rearrange("h p n -> p h n")), reads=["wo.%d" % i for i in range(16)], writes=["wo_t"])
                h2v = h2T.ap().rearrange("(c p) s -> p c s", p=128)
                ov = outT.ap().rearrange("(c p) s -> p c s", p=128)
                for i in range(NO):
                    p.dma("sp", lambda e, i=i: e.dma_start(out=h[:, :, :], in_=h2v[:, :, i * 512:(i + 1) * 512]), reads=["h2T"], writes=HN)
                    p.dma("sp", lambda e, i=i: e.dma_start(out=oc_t[:, :, :], in_=ocT.ap().rearrange("h p s -> p h s")[:, :, i * 512:(i + 1) * 512]),
                          reads=["ocT"], writes=["oc_t"])
                    for d in range(8):
                        b = d % 2

                        def mm(e, d=d, b=b):
                            for hd in range(16):
                                ins = e.matmul(PS[b][:, :], wo_t[:, hd, d * 128:(d + 1) * 128], oc_t[:, hd, :], start=(hd == 0), stop=(hd == 15))
                            return ins
                        p.op("pe", mm, reads=["wo_t", "oc_t"], writes=[PSN[b]])
                        p.op("dve", lambda e, d=d, b=b: e.tensor_tensor(out=h[:, d, :], in0=PS[b][:, :], in1=h[:, d, :], op=ALU.add), reads=[PSN[b], "h.%d" % d], writes=["h.%d" % d])
                    ffn(h, "h", 3, 48, bufs)
                    p.dma("pool", lambda e, i=i: e.dma_start(out=ov[:, :, i * 512:(i + 1) * 512], in_=h[:, :, :]), reads=HN, writes=["outT"])
        _phase7()
        p.barrier()
        p.run()
    return nc


def _rel_bucket_table(n):
    import jax
    import jax.numpy as jnp
    import math
    with jax.default_device(jax.devices("cpu")[0]):
        rel = jnp.arange(n)
        nn = jnp.maximum(rel, 0)
        nf = jnp.maximum(nn, 1).astype(jnp.float32)
        large = 16 + (jnp.log(nf / 16) / math.log(4096 / 16) * 16).astype(jnp.int32)
        large = jnp.minimum(large, 31)
        return np.asarray(jnp.where(nn < 16, nn, large))


def prep_shared(S, inputs):
    f = lambda a: np.ascontiguousarray(a, dtype=np.float32)
    NSB = S // 64
    NCP = S // 16
    NCK = NCP // 128
    sh = {}

    def col8(g):
        return g.reshape(8, 128).T
    norms = [inputs["ffn1_norm"][0], inputs["mix_norm"][0], inputs["ffn2_norm"][0], inputs["kv_norm"],
             inputs["ffn1_norm"][1], inputs["mix_norm"][1], inputs["ffn2_norm"][1]]
    sh["gains"] = f(np.concatenate([col8(np.asarray(g)) for g in norms], axis=1))

    def win_r(w):
        w = np.asarray(w).reshape(8, 128, 2, NJ, 128)
        return f(w.transpose(3, 1, 2, 0, 4).reshape(NJ, 128, 2048))

    def wout_r(w):
        w = np.asarray(w).reshape(NJ, 128, 8, 128)
        return f(w.transpose(2, 1, 0, 3).reshape(8, 128, NJ * 128))
    ffn_list = [("ffn1_w_in", "ffn1_w_out", 0), ("ffn2_w_in", "ffn2_w_out", 0), ("ffn1_w_in", "ffn1_w_out", 1), ("ffn2_w_in", "ffn2_w_out", 1)]
    for i, (a, b, l) in enumerate(ffn_list):
        sh["win%d" % i] = win_r(inputs[a][l])
        sh["wout%d" % i] = wout_r(inputs[b][l])

    def sq_r(w, ncol):
        w = np.asarray(w).reshape(8, 128, ncol, 128)
        return w.transpose(2, 1, 0, 3)
    ci = np.asarray(inputs["conv_w_in"][0]).reshape(8, 128, 3, 8, 128)
    sh["cin"] = f(ci.transpose(3, 1, 2, 0, 4).reshape(8, 128, 3072))
    sh["cout"] = f(sq_r(inputs["conv_w_out"][0], 8).reshape(8, 128, 1024))
    cw = np.asarray(inputs["conv_w"][0])
    sh["cwk"] = f(cw.reshape(3, 8, 128).transpose(2, 1, 0).reshape(128, 24))
    wkv = np.asarray(inputs["w_kv"])
    units = []
    for c0 in (0, 128, 256, 384):
        units.append(wkv[:, c0:c0 + 128])
    for br in (1, 2):
        for g in range(4):
            cc = wkv[:, br * 512 + g * 64: br * 512 + (g + 1) * 64]
            units.append(np.concatenate([cc, cc], axis=1))
    sh["wkf"] = f(np.stack([u.reshape(8, 128, 128).transpose(1, 0, 2).reshape(128, 1024) for u in units]))
    wv = np.concatenate([wkv[:, 768:1024], wkv[:, 1280:1536]], axis=1)
    sh["wvt"] = f(wv.reshape(8, 128, 512).transpose(1, 0, 2).reshape(1, 128, 4096))
    wqf = np.asarray(inputs["attn_w_q"][0])
    sh["wq"] = f(np.stack([wqf[:, h * 64:(h + 1) * 64].reshape(8, 128, 64).transpose(1, 0, 2).reshape(128, 512) for h in range(16)]))
    sh["wg"] = f(wqf[:, 1024:1072].reshape(8, 128, 48).transpose(1, 0, 2).reshape(1, 128, 384))
    sh["wo"] = f(np.asarray(inputs["attn_w_o"][0]).reshape(16, 64, 1024))
    knm = np.asarray(inputs["k_norm"])
    sh["kn"] = f(np.concatenate([knm.T, knm.T], axis=0))
    qnm = np.asarray(inputs["attn_q_norm"][0])
    sh["qn"] = f(np.concatenate([qnm, qnm])[:, None])

    def w1_r(w):
        w = np.asarray(w).reshape(32, 64, 128).transpose(1, 0, 2).reshape(64, 4096)
        return f(np.concatenate([w, w], axis=0)[None])
    sh["w1k"] = w1_r(inputs["cmp_w1_k"])
    sh["w1v"] = w1_r(inputs["cmp_w1_v"])
    sh["w2k"] = f(inputs["cmp_w2_k"])
    sh["w2v"] = f(inputs["cmp_w2_v"])
    pk = np.asarray(inputs["cmp_pe_k"]).T
    pv = np.asarray(inputs["cmp_pe_v"]).T
    pe = np.concatenate([pk, pv], axis=1)
    sh["peT"] = f(np.concatenate([pe, pe], axis=0))
    ex = np.zeros((1, NSB, S), np.float32)
    for j in range(NSB):
        ex[0, j, j * 64:(j + 1) * 64] = 1.0
    sh["expd"] = ex
    n = np.arange(NCP)
    cs = n * 16
    ss = np.arange(NSB) * 64
    ovl = np.clip(np.minimum(cs[:, None] + 32, ss[None, :] + 64) - np.maximum(cs[:, None], ss[None, :]), 0, None) / 16.0
    ovl[NCP - 1, :] = 0.0
    ovl1 = np.concatenate([ovl, np.ones((NCP, 1))], axis=1)
    sh["ovm"] = f(ovl1.reshape(NCK, 128, NSB + 1).transpose(1, 0, 2).reshape(128, NCK * (NSB + 1)))
    sg_ = np.zeros((1, 48, 48 * 128), np.float32)
    for c in range(48):
        sg_[0, c, c * 128 + 64:(c + 1) * 128] = 1.0
    sh["selg"] = sg_
    sh["ident"] = np.eye(128, dtype=np.float32)
    return sh


def prep_parity(S, inputs, par):
    f = lambda a: np.ascontiguousarray(a, dtype=np.float32)
    NSB = S // 64
    NO = S // 1024
    pp = {}
    pp["par"] = f(np.tile(np.array([[par, 1 - par]], np.float32), (128, 1)))
    rb = np.asarray(inputs["rel_bias"])
    bt = _rel_bucket_table(max(S, 4096) + 8192)

    def vec(L, off, lo, hi):
        rel = np.arange(L) - off + 512 * par
        ok = (rel >= lo) & (rel < hi)
        idx = bt[np.clip(rel, 0, len(bt) - 1)]
        v = rb[idx, :].T.copy()
        v[:, ~ok] = MASKV
        return f(v)
    pp["bvs"] = vec(L_S, OFFS, 0, 1 << 30)
    pp["bvw"] = vec(L_W, OFFW, 0, 512)
    pp["bvc"] = vec(L_C, OFFC, 0, 1 << 30)
    A = np.zeros((NO * 4, 128, NSB), np.float32)
    B = np.zeros((NO * 4, 128, NSB), np.float32)
    jj = np.arange(NSB)[None, :]
    for i in range(NO):
        for qq in range(4):
            t = (2 * i + par) * 512 + qq * 128 + np.arange(128)
            bt_ = (t // 64)[:, None]
            valid = jj <= bt_
            forced = (jj == 0) | (jj == bt_) | (jj == bt_ - 1)
            A[i * 4 + qq] = (valid & ~forced)
            B[i * 4 + qq] = np.where(valid, np.where(forced, 1e30, 0.0), -1e30)
    pp["selA"] = A
    pp["selB"] = B
    return pp


_CACHE = {}


def run_model(S, inputs, x, dbg=()):
    B = x.shape[0]
    key = (S, tuple(dbg))
    if key not in _CACHE:
        _CACHE[key] = build(S, dbg)
    nc = _CACHE[key]
    sh = prep_shared(S, inputs)
    pps = [prep_parity(S, inputs, 0), prep_parity(S, inputs, 1)]
    in_maps = []
    for core in range(2 * B):
        b, par = core // 2, core % 2
        m = dict(sh)
        m.update(pps[par])
        m["xT"] = np.ascontiguousarray(np.asarray(x[b], dtype=np.float32).T)
        in_maps.append(m)
    res = run_bass_kernel_spmd(nc, in_maps, core_ids=list(range(2 * B)))
    out = np.zeros((B, S, D), np.float32)
    for core in range(2 * B):
        b, par = core // 2, core % 2
        oT = np.asarray(res.results[core]["outT"])
        o = oT.T.reshape(S // 1024, 512, D)
        for i in range(S // 1024):
            t0 = (2 * i + par) * 512
            out[b, t0:t0 + 512] = o[i]
    return out, res


def kernel(**inputs):
    x = np.asarray(inputs["x"])
    out, _ = run_model(x.shape[1], inputs, x)
    return out
```

```python
import contextlib
import numpy as np
import concourse.bass as bass
import concourse.mybir as mybir
from concourse.bass_utils import run_bass_kernel_spmd

F32 = mybir.dt.float32
BF16 = mybir.dt.bfloat16
AF = mybir.ActivationFunctionType
ALU = mybir.AluOpType

ENGS = ("pe", "act", "dve", "pool", "sp")
NDSEM = 32
DSEM_Q = {"sp": 24, "pool": 8, "act": 4}
SAME_ENGINE_SYNC = True

D = 1024
DFF = 2816
NJ = 22
EPS = 1e-6
OFFS, OFFW, OFFC = 1023, 1535, 2063
JSAT_S = 2896 + OFFS
W_S = JSAT_S + 512
L_S = W_S + 127
W_W = 2432
L_W = W_W + 127
JSAT_C = 4959
W_C = JSAT_C + 512
L_C = W_C + 2032
MASKV = -30000.0


class Prog:
    def __init__(self, nc):
        self.nc = nc
        self.ops = {e: [] for e in ENGS}
        self.cnt = {e: 0 for e in ENGS}
        self.waited = {e: {} for e in ENGS}
        self.res_w = {}
        self.res_r = {}
        self.ndma = 0
        self.ndma_q = {}

    def _deps(self, reads, writes):
        deps = {}

        def add(d):
            if d is None:
                return
            k, v = d
            if deps.get(k, 0) < v:
                deps[k] = v
        for r in reads:
            add(self.res_w.get(r))
        for w in writes:
            add(self.res_w.get(w))
            for k, v in self.res_r.get(w, {}).items():
                add((k, v))
        return deps

    def _commit(self, reads, writes, me):
        for r in reads:
            d = self.res_r.setdefault(r, {})
            if d.get(me[0], 0) < me[1]:
                d[me[0]] = me[1]
        for w in writes:
            self.res_w[w] = me
            self.res_r[w] = {}

    def op(self, eng, fn, reads=(), writes=()):
        deps = self._deps(reads, writes)
        waits = []
        for k, v in deps.items():
            if k == eng and (eng == "pe" or not SAME_ENGINE_SYNC):
                continue
            if self.waited[eng].get(k, 0) >= v:
                continue
            self.waited[eng][k] = v
            waits.append((k, v))
        self.cnt[eng] += 1
        me = (eng, self.cnt[eng])
        self.ops[eng].append((waits, fn, me))
        self._commit(reads, writes, me)

    def dma(self, q, fn, reads=(), writes=()):
        deps = self._deps(reads, writes)
        k = self.ndma_q.get(q, 0)
        self.ndma_q[q] = k + 1
        self.ndma += 1
        nsl = DSEM_Q[q]
        slot = k % nsl
        val = 16 * (k // nsl + 1)
        key = ("d", q, slot)
        if val > 16:
            deps[key] = max(deps.get(key, 0), val - 16)
        waits = []
        for kk, v in deps.items():
            if self.waited[q].get(kk, 0) >= v:
                continue
            self.waited[q][kk] = v
            waits.append((kk, v))
        me = (key, val)
        self.ops[q].append((waits, fn, me))
        self._commit(reads, writes, me)

    def barrier(self):
        tgt = [(e, self.cnt[e]) for e in ENGS if self.cnt[e] > 0]
        for q, n in self.ndma_q.items():
            nsl = DSEM_Q[q]
            for k in range(max(0, n - nsl), n):
                tgt.append((("d", q, k % nsl), 16 * (k // nsl + 1)))
        for e in ENGS:
            waits = []
            for k, v in tgt:
                if k == e:
                    continue
                if self.waited[e].get(k, 0) >= v:
                    continue
                self.waited[e][k] = v
                waits.append((k, v))
            if waits:
                self.ops[e].append((waits, None, None))

    def run(self):
        nc = self.nc
        def _phase1():
            with contextlib.ExitStack() as st:
                sems = {}
                for e in ENGS:
                    sems[e] = st.enter_context(nc.semaphore("s_" + e))
                for q_, n_ in DSEM_Q.items():
                    for i in range(n_):
                        sems[("d", q_, i)] = st.enter_context(nc.semaphore("d_%s_%d" % (q_, i)))
                block = st.enter_context(nc.Block())

                def body(ename):
                    def f(eng):
                        for waits, fn, me in self.ops[ename]:
                            for k, v in waits:
                                eng.wait_ge(sems[k], v)
                            if fn is None:
                                continue
                            ins = fn(eng)
                            ins.then_inc(sems[me[0]], 16 if isinstance(me[0], tuple) else 1)
                    return f
                block.tensor(body("pe"))
                block.scalar(body("act"))
                block.vector(body("dve"))
                block.gpsimd(body("pool"))
                block.sync(body("sp"))


        _phase1()
def build(S, dbg=()):
    NT = S // 512
    NO = NT // 2
    SO = S // 2
    NKT = S // 128
    NSB = S // 64
    NCP = S // 16
    NCK = NCP // 128
    NCV = NCP - 1
    nc = bass.Bass("TRN2", target_bir_lowering=False)
    IN = {}

    def inp(name, shape):
        IN[name] = nc.dram_tensor(name, list(shape), F32, kind="ExternalInput")
        return IN[name]

    def scr(name, shape, dt):
        kind = "ExternalOutput" if name in dbg else "Internal"
        return nc.dram_tensor(name, list(shape), dt, kind=kind)

    xT = inp("xT", [D, S])
    gains = inp("gains", [128, 56])
    par = inp("par", [128, 2])
    wins = [inp("win%d" % i, [NJ, 128, 2048]) for i in range(4)]
    wouts = [inp("wout%d" % i, [8, 128, NJ * 128]) for i in range(4)]
    cin = inp("cin", [8, 128, 3072])
    cout = inp("cout", [8, 128, 1024])
    cwk = inp("cwk", [128, 24])
    wkf = inp("wkf", [12, 128, 1024])
    wvt = inp("wvt", [1, 128, 4096])
    wq = inp("wq", [16, 128, 512])
    wg = inp("wg", [1, 128, 384])
    wo = inp("wo", [16, 64, 1024])
    kn = inp("kn", [128, 3])
    qn = inp("qn", [128, 1])
    w1k = inp("w1k", [1, 128, 4096])
    w1v = inp("w1v", [1, 128, 4096])
    w2k = inp("w2k", [128, 64])
    w2v = inp("w2v", [128, 64])
    peT = inp("peT", [128, 64])
    bvs = inp("bvs", [16, L_S])
    bvw = inp("bvw", [16, L_W])
    bvc = inp("bvc", [16, L_C])
    expd = inp("expd", [1, NSB, S])
    ovm = inp("ovm", [128, NCK * (NSB + 1)])
    selg = inp("selg", [1, 48, 48 * 128])
    selA = inp("selA", [NO * 4, 128, NSB])
    selB = inp("selB", [NO * 4, 128, NSB])
    ident_in = inp("ident", [128, 128])
    outT = nc.dram_tensor("outT", [D, SO], F32, kind="ExternalOutput")

    def bfc(name, src):
        return scr(name + "_bf", list(src.shape), BF16)
    wins_b = [bfc("win%d" % i, wins[i]) for i in range(4)]
    wouts_b = [bfc("wout%d" % i, wouts[i]) for i in range(4)]
    cin_b, cout_b, wkf_b, wvt_b = bfc("cin", cin), bfc("cout", cout), bfc("wkf", wkf), bfc("wvt", wvt)
    wq_b, wg_b, wo_b = bfc("wq", wq), bfc("wg", wg), bfc("wo", wo)
    w1k_b, w1v_b, expd_b, selg_b = bfc("w1k", w1k), bfc("w1v", w1v), bfc("expd", expd), bfc("selg", selg)

    h1T = scr("h1T", [D, S], F32)
    h2T = scr("h2T", [D, SO], F32)
    rawT = scr("rawT", [4, 128, S], BF16)
    ksT = scr("ksT", [4, 64, S], BF16)
    kwT = scr("kwT", [4, 64, S], BF16)
    vsw = scr("vsw", [NKT, 128, 512], BF16)
    kcT = scr("kcT", [4, 64, NCP], BF16)
    vcS = scr("vcS", [4, 128, NCK * 64], BF16)
    qTs = scr("qTs", [16, 64, SO], BF16)
    gsT = scr("gsT", [48, SO], BF16)
    ocT = scr("ocT", [16, 64, SO], BF16)
    rep_s = scr("rep_s", [16, 128, L_S], BF16)
    rep_w = scr("rep_w", [16, 128, L_W], BF16)
    rep_c = scr("rep_c", [16, 128, L_C], BF16)

    p = Prog(nc)
    HN = ["h.%d" % c for c in range(8)]
    sb = nc.sbuf_tensor
    uid = [0]

    def T(st, shape, dt, nm="t"):
        uid[0] += 1
        return st.enter_context(sb("%s_%d" % (nm, uid[0]), list(shape), dt))

    with contextlib.ExitStack() as top:
        PS = [top.enter_context(nc.psum_tensor("ps%d" % i, [128, 512], F32)) for i in range(8)]
        PSN = ["ps%d" % i for i in range(8)]
        gains_t = T(top, [128, 56], F32, "gains")
        par_t = T(top, [128, 2], F32, "par")
        cwk_t = T(top, [128, 24], F32, "cwk")
        kn_t = T(top, [128, 3], F32, "kn")
        qn_t = T(top, [128, 1], F32, "qn")
        eps_t = T(top, [128, 1], F32, "eps")
        ones_bf = T(top, [128, 128], BF16, "ones")
        ones_f = T(top, [128, 128], F32, "onesf")
        ident_t = T(top, [128, 128], F32, "ident")
        for dst, src, nm in ((gains_t, gains, "gains"), (par_t, par, "par"), (cwk_t, cwk, "cwk"),
                             (kn_t, kn, "kn"), (qn_t, qn, "qn"), (ident_t, ident_in, "ident")):
            p.dma("sp", lambda e, dst=dst, src=src: e.dma_start(out=dst[:], in_=src.ap()), writes=[nm])
        p.op("dve", lambda e: e.memset(eps_t[:], EPS), writes=["eps"])
        p.op("dve", lambda e: e.memset(ones_bf[:], 1.0), writes=["ones"])
        p.op("dve", lambda e: e.memset(ones_f[:], 1.0), writes=["onesf"])

        def precast(src, dst, nm):
            n0 = src.shape[0]
            for i in range(n0):
                p.dma("pool", lambda e, i=i: e.dma_start(out=dst.ap()[i], in_=src.ap()[i]), writes=["%s.%d" % (nm, i)])
        order = [(wins[0], wins_b[0], "win0"), (wouts[0], wouts_b[0], "wout0"), (cin, cin_b, "cin"), (cout, cout_b, "cout"),
                 (wins[1], wins_b[1], "win1"), (wouts[1], wouts_b[1], "wout1"), (wkf, wkf_b, "wkf"), (wvt, wvt_b, "wvt"),
                 (w1k, w1k_b, "w1k"), (w1v, w1v_b, "w1v"),
                 (wins[2], wins_b[2], "win2"), (wouts[2], wouts_b[2], "wout2"), (wq, wq_b, "wq"), (wg, wg_b, "wg"),
                 (expd, expd_b, "expd"), (selg, selg_b, "selg"), (wo, wo_b, "wo"),
                 (wins[3], wins_b[3], "win3"), (wouts[3], wouts_b[3], "wout3")]
        early, late = order[:10], order[10:]
        for s_, d_, n_ in early:
            precast(s_, d_, n_)
        late_items = []
        for s_, d_, n_ in late:
            for i_ in range(s_.shape[0]):
                late_items.append((s_, d_, n_, i_))

        def _phase2():
            with contextlib.ExitStack() as st:
                vrow = [T(st, [1, L_C], F32, "vrow") for _ in range(2)]
                ev = [T(st, [128, L_C], BF16, "ev") for _ in range(2)]
                k = 0
                for src, rep, L, nm in ((bvs, rep_s, L_S, "rep_s"), (bvw, rep_w, L_W, "rep_w"), (bvc, rep_c, L_C, "rep_c")):
                    for h in range(16):
                        b = k % 2
                        k += 1
                        p.dma("sp", lambda e, b=b, h=h, src=src, L=L: e.dma_start(out=vrow[b][0:1, 0:L], in_=src.ap()[h:h + 1, :]),
                              writes=["vrow%d" % b])
                        for c0 in range(0, L, 512):
                            c1 = min(L, c0 + 512)
                            pb = (c0 // 512) % 2
                            p.op("pe", lambda e, b=b, c0=c0, c1=c1, pb=pb: e.matmul(PS[pb][:, 0:c1 - c0], ones_f[0:1, :], vrow[b][0:1, c0:c1],
                                                                                  start=True, stop=True),
                                 reads=["vrow%d" % b, "onesf"], writes=[PSN[pb]])
                            p.op("act", lambda e, b=b, c0=c0, c1=c1, pb=pb: e.activation(out=ev[b][:, c0:c1], in_=PS[pb][:, 0:c1 - c0], func=AF.Exp),
                                 reads=[PSN[pb]], writes=["ev%d" % b])
                        p.dma("sp", lambda e, b=b, h=h, rep=rep, L=L: e.dma_start(out=rep.ap()[h], in_=ev[b][:, 0:L]),
                              reads=["ev%d" % b], writes=["%s.%d" % (nm, h)])
        _phase2()
        p.barrier()

        def rstd_from(ps_i, n, rs, rsn, rows=128):
            p.op("act", lambda e: e.activation(out=rs[0:rows, :], in_=PS[ps_i][0:rows, :], func=AF.Sqrt, bias=eps_t[0:rows, 0:1], scale=1.0 / n),
                 reads=[PSN[ps_i], "eps"], writes=[rsn])
            p.op("dve", lambda e: e.reciprocal(out=rs[0:rows, :], in_=rs[0:rows, :]), reads=[rsn], writes=[rsn])

        def rms_to_xn(h, hn, gcol, sq, xn, rs):
            for c in range(8):
                p.op("act", lambda e, c=c: e.activation(out=sq[:, c, :], in_=h[:, c, :], func=AF.Square), reads=["%s.%d" % (hn, c)], writes=["sq"])

            def mm(e):
                for c in range(8):
                    ins = e.matmul(PS[7][:, :], ones_bf[:, :], sq[:, c, :], start=(c == 0), stop=(c == 7))
                return ins
            p.op("pe", mm, reads=["sq", "ones"], writes=[PSN[7]])
            rstd_from(7, D, rs, "rs")
            for c in range(8):
                p.op("dve", lambda e, c=c: e.scalar_tensor_tensor(out=xn[:, c, :], in0=h[:, c, :], scalar=gains_t[:, gcol + c:gcol + c + 1],
                                                                 in1=rs[:, :], op0=ALU.mult, op1=ALU.mult),
                     reads=["%s.%d" % (hn, c), "rs", "gains"], writes=["xn"])

        def ffn(h, hn, li, gcol, bufs):
            sq, xn, rs, act, sg, wpi, wpo = bufs
            rms_to_xn(h, hn, gcol, sq, xn, rs)
            for j in range(NJ):
                b = j % 2
                wb = j % len(wpi)
                p.dma("sp", lambda e, j=j, wb=wb: e.dma_start(out=wpi[wb][:, :], in_=wins_b[li].ap()[j]),
                      reads=["win%d.%d" % (li, j)], writes=["wpi%d" % wb])
                for s_ in range(2):
                    def mm(e, s_=s_, b=b, wb=wb):
                        for c in range(8):
                            ins = e.matmul(PS[b * 2 + s_][:, :], wpi[wb][:, (s_ * 8 + c) * 128:(s_ * 8 + c + 1) * 128], xn[:, c, :],
                                           start=(c == 0), stop=(c == 7))
                        return ins
                    p.op("pe", mm, reads=["xn", "wpi%d" % wb], writes=[PSN[b * 2 + s_]])
                p.op("act", lambda e, b=b: e.activation(out=sg[b][:, :], in_=PS[b * 2][:, :], func=AF.Silu), reads=[PSN[b * 2]], writes=["sg%d" % b])
                p.op("dve", lambda e, b=b, j=j: e.tensor_tensor(out=act[:, j, :], in0=sg[b][:, :], in1=PS[b * 2 + 1][:, :], op=ALU.mult),
                     reads=["sg%d" % b, PSN[b * 2 + 1]], writes=["act.%d" % j])
            for d in range(8):
                b = d % 2
                wb = d % len(wpo)
                p.dma("sp", lambda e, d=d, wb=wb: e.dma_start(out=wpo[wb][:, :], in_=wouts_b[li].ap()[d]),
                      reads=["wout%d.%d" % (li, d)], writes=["wpo%d" % wb])

                def mm(e, b=b, wb=wb):
                    for j in range(NJ):
                        ins = e.matmul(PS[4 + b][:, :], wpo[wb][:, j * 128:(j + 1) * 128], act[:, j, :], start=(j == 0), stop=(j == NJ - 1))
                    return ins
                p.op("pe", mm, reads=["wpo%d" % wb] + ["act.%d" % j for j in range(NJ)], writes=[PSN[4 + b]])
                p.op("dve", lambda e, d=d, b=b: e.scalar_tensor_tensor(out=h[:, d, :], in0=PS[4 + b][:, :], scalar=0.5, in1=h[:, d, :],
                                                                      op0=ALU.mult, op1=ALU.add),
                     reads=[PSN[4 + b], "%s.%d" % (hn, d)], writes=["%s.%d" % (hn, d)])

        def _phase3():
            with contextlib.ExitStack() as st:
                h = T(st, [128, 8, 512], F32, "h")
                sq = T(st, [128, 8, 512], BF16, "sq")
                xn = T(st, [128, 8, 512], BF16, "xn")
                rs = T(st, [128, 512], F32, "rs")
                act = T(st, [128, NJ, 512], BF16, "act")
                sg = [T(st, [128, 512], F32, "sg") for _ in range(2)]
                wpi = [T(st, [128, 2048], BF16, "wpi") for _ in range(4)]
                wpo = [T(st, [128, NJ * 128], BF16, "wpo") for _ in range(3)]
                bufs = (sq, xn, rs, act, sg, wpi, wpo)
                wci = [T(st, [128, 3072], BF16, "wci") for _ in range(2)]
                wsq = [T(st, [128, 1024], BF16, "wsq") for _ in range(2)]
                u = [T(st, [128, 514], F32, "u") for _ in range(2)]
                ucar = T(st, [128, 8, 2], F32, "ucar")
                yb = [T(st, [128, 512], F32, "yb") for _ in range(2)]
                bgy = T(st, [128, 8, 512], BF16, "bgy")
                wvt_t = T(st, [128, 4096], BF16, "wvt")
                wkf_t = T(st, [128, 12, 1024], BF16, "wkf_t")
                bd64 = T(st, [128, 128], BF16, "bd64")
                ksb = [T(st, [128, 512], BF16, "ksb") for _ in range(4)]
                sgk = [T(st, [128, 512], F32, "sgk") for _ in range(4)]
                vtb = [T(st, [128, 512], BF16, "vtb") for _ in range(2)]
                p.op("dve", lambda e: e.memset(ucar[:], 0.0), writes=["ucar"])
                p.op("dve", lambda e: e.memset(bd64[:], 0.0), writes=["bd64"])
                p.op("dve", lambda e: e.memset(bd64[0:64, 0:64], 1.0), writes=["bd64"])
                p.op("dve", lambda e: e.memset(bd64[64:128, 64:128], 1.0), writes=["bd64"])
                p.dma("sp", lambda e: e.dma_start(out=wvt_t[:, :], in_=wvt_b.ap()[0]), reads=["wvt.0"], writes=["wvtt"])
                p.dma("sp", lambda e: e.dma_start(out=wkf_t[:, :, :], in_=wkf_b.ap().rearrange("u p n -> p u n")), reads=["wkf.%d" % u_ for u_ in range(12)], writes=["wkf_t"])
                xTv = xT.ap().rearrange("(c p) s -> p c s", p=128)
                h1v = h1T.ap().rearrange("(c p) s -> p c s", p=128)
                for ti in range(NT):
                    t0 = ti * 512
                    p.dma("sp", lambda e, t0=t0: e.dma_start(out=h[:, :, :], in_=xTv[:, :, t0:t0 + 512]), writes=HN)
                    if ti >= 1 and late_items:
                        nper = (len(late_items) + max(1, NT - 2) - 1) // max(1, NT - 2) if ti == 1 else nper_keep[0]
                        nper_keep[0] = nper
                        for _ in range(min(nper, len(late_items))):
                            s_, d_, n_, i_ = late_items.pop(0)
                            p.dma("pool", lambda e, s_=s_, d_=d_, i_=i_: e.dma_start(out=d_.ap()[i_], in_=s_.ap()[i_]), writes=["%s.%d" % (n_, i_)])
                    ffn(h, "h", 0, 0, bufs)
                    rms_to_xn(h, "h", 8, sq, xn, rs)
                    for i in range(8):
                        b = i % 2
                        p.dma("sp", lambda e, i=i, b=b: e.dma_start(out=wci[b][:, :], in_=cin_b.ap()[i]), reads=["cin.%d" % i], writes=["wci%d" % b])
                        cb = (0, 1, 2) if i % 2 == 0 else (3, 6, 7)
                        for s_ in range(3):
                            def mm(e, s_=s_, b=b, cb=cb):
                                for c in range(8):
                                    ins = e.matmul(PS[cb[s_]][:, :], wci[b][:, (s_ * 8 + c) * 128:(s_ * 8 + c + 1) * 128], xn[:, c, :],
                                                   start=(c == 0), stop=(c == 7))
                                return ins
                            p.op("pe", mm, reads=["xn", "wci%d" % b], writes=[PSN[cb[s_]]])
                        p.op("dve", lambda e, i=i, b=b: e.tensor_copy(out=u[b][:, 0:2], in_=ucar[:, i, :]), reads=["ucar"], writes=["u%d" % b])
                        p.op("act", lambda e, b=b, cb=cb: e.activation(out=yb[b][:, :], in_=PS[cb[1]][:, :], func=AF.Copy), reads=[PSN[cb[1]]], writes=["yb%d" % b])
                        p.op("dve", lambda e, b=b, cb=cb: e.tensor_tensor(out=u[b][:, 2:514], in0=yb[b][:, :], in1=PS[cb[2]][:, :], op=ALU.mult),
                             reads=["yb%d" % b, PSN[cb[2]]], writes=["u%d" % b])
                        p.op("dve", lambda e, i=i, b=b: e.tensor_copy(out=ucar[:, i, :], in_=u[b][:, 512:514]), reads=["u%d" % b], writes=["ucar"])
                        p.op("dve", lambda e, i=i, b=b: e.tensor_scalar(out=yb[b][:, :], in0=u[b][:, 2:514], scalar1=cwk_t[:, i * 3 + 2:i * 3 + 3], scalar2=None,
                                                                      op0=ALU.mult), reads=["u%d" % b, "cwk"], writes=["yb%d" % b])
                        for w_ in (1, 0):
                            p.op("dve", lambda e, i=i, b=b, w_=w_: e.scalar_tensor_tensor(out=yb[b][:, :], in0=u[b][:, w_:w_ + 512],
                                                                                         scalar=cwk_t[:, i * 3 + w_:i * 3 + w_ + 1], in1=yb[b][:, :],
                                                                                         op0=ALU.mult, op1=ALU.add),
                                 reads=["u%d" % b, "cwk", "yb%d" % b], writes=["yb%d" % b])
                        p.op("dve", lambda e, i=i, b=b, cb=cb: e.tensor_tensor(out=bgy[:, i, :], in0=yb[b][:, :], in1=PS[cb[0]][:, :], op=ALU.mult),
                             reads=["yb%d" % b, PSN[cb[0]]], writes=["bgy.%d" % i])
                    for d in range(8):
                        b = d % 2
                        p.dma("sp", lambda e, d=d, b=b: e.dma_start(out=wsq[b][:, :], in_=cout_b.ap()[d]), reads=["cout.%d" % d], writes=["wsq%d" % b])

                        def mm(e, b=b):
                            for c in range(8):
                                ins = e.matmul(PS[4 + b][:, :], wsq[b][:, c * 128:(c + 1) * 128], bgy[:, c, :], start=(c == 0), stop=(c == 7))
                            return ins
                        p.op("pe", mm, reads=["wsq%d" % b] + ["bgy.%d" % c for c in range(8)], writes=[PSN[4 + b]])
                        p.op("dve", lambda e, d=d, b=b: e.tensor_tensor(out=h[:, d, :], in0=PS[4 + b][:, :], in1=h[:, d, :], op=ALU.add),
                             reads=[PSN[4 + b], "h.%d" % d], writes=["h.%d" % d])
                    ffn(h, "h", 1, 16, bufs)
                    p.dma("pool", lambda e, t0=t0: e.dma_start(out=h1v[:, :, t0:t0 + 512], in_=h[:, :, :]), reads=HN, writes=["h1T"])
                    rms_to_xn(h, "h", 24, sq, xn, rs)
                    for un in range(12):
                        b = un % 2
                        k4 = un % 4
                        def mm(e, un=un, k4=k4):
                            for c in range(8):
                                ins = e.matmul(PS[k4][:, :], wkf_t[:, un, c * 128:(c + 1) * 128], xn[:, c, :], start=(c == 0), stop=(c == 7))
                            return ins
                        p.op("pe", mm, reads=["xn", "wkf_t"], writes=[PSN[k4]])
                        if un < 4:
                            p.op("act", lambda e, k4=k4: e.activation(out=ksb[k4][:, :], in_=PS[k4][:, :], func=AF.Copy), reads=[PSN[k4]], writes=["ksb%d" % k4])
                            p.dma("pool", lambda e, un=un, k4=k4, t0=t0: e.dma_start(out=rawT.ap()[un][:, t0:t0 + 512], in_=ksb[k4][:, :]),
                                  reads=["ksb%d" % k4], writes=["rawT"])
                        else:
                            br = 1 if un < 8 else 2
                            g = (un - 4) % 4
                            dst = ksT if br == 1 else kwT
                            p.op("act", lambda e, k4=k4: e.activation(out=sgk[k4][:, :], in_=PS[k4][:, :], func=AF.Square), reads=[PSN[k4]], writes=["sgk%d" % k4])
                            p.op("dve", lambda e, k4=k4: e.tensor_copy(out=ksb[k4][:, :], in_=sgk[k4][:, :]), reads=["sgk%d" % k4], writes=["ksb%d" % k4])
                            p.op("pe", lambda e, b=b, k4=k4: e.matmul(PS[4 + b][:, :], bd64[:, :], ksb[k4][:, :], start=True, stop=True),
                                 reads=["ksb%d" % k4, "bd64"], writes=[PSN[4 + b]])
                            rstd_from(4 + b, 64, sgk[k4], "sgk%d" % k4)
                            p.op("dve", lambda e, k4=k4, br=br: e.scalar_tensor_tensor(out=ksb[k4][:, :], in0=PS[k4][:, :], scalar=kn_t[:, br:br + 1],
                                                                                      in1=sgk[k4][:, :], op0=ALU.mult, op1=ALU.mult),
                                 reads=[PSN[k4], "sgk%d" % k4, "kn"], writes=["ksb%d" % k4])
                            p.dma("pool", lambda e, g=g, k4=k4, t0=t0, dst=dst: e.dma_start(out=dst.ap()[g][:, t0:t0 + 512], in_=ksb[k4][0:64, :]),
                                  reads=["ksb%d" % k4], writes=["ksT" if br == 1 else "kwT"])
                    for tb in range(4):
                        b = tb % 2

                        def mm(e, tb=tb, b=b):
                            for c in range(8):
                                ins = e.matmul(PS[6 + b][:, :], xn[:, c, tb * 128:(tb + 1) * 128], wvt_t[:, c * 512:(c + 1) * 512],
                                               start=(c == 0), stop=(c == 7))
                            return ins
                        p.op("pe", mm, reads=["xn", "wvtt"], writes=[PSN[6 + b]])
                        p.op("act", lambda e, b=b: e.activation(out=vtb[b][:, :], in_=PS[6 + b][:, :], func=AF.Copy), reads=[PSN[6 + b]], writes=["vtb%d" % b])
                        p.dma("pool", lambda e, tb=tb, b=b, ti=ti: e.dma_start(out=vsw.ap()[ti * 4 + tb], in_=vtb[b][:, :]), reads=["vtb%d" % b], writes=["vsw"])
        nper_keep = [1]
        _phase3()
        while late_items:
            s_, d_, n_, i_ = late_items.pop(0)
            p.dma("pool", lambda e, s_=s_, d_=d_, i_=i_: e.dma_start(out=d_.ap()[i_], in_=s_.ap()[i_]), writes=["%s.%d" % (n_, i_)])
        p.barrier()

        def _phase4():
            with contextlib.ExitStack() as st:
                raw = T(st, [128, S], BF16, "raw")
                w1t = T(st, [128, 4096], BF16, "w1t")
                w2kt = T(st, [128, 64], BF16, "w2kt")
                w2vt = T(st, [128, 64], BF16, "w2vt")
                w2f = T(st, [128, 128], F32, "w2f")
                peTt = T(st, [128, 64], BF16, "peTt")
                peTf = T(st, [128, 64], F32, "peTf")
                c1 = T(st, [128, 2], F32, "c1")
                xx = T(st, [128, 512], F32, "xx")
                tt = T(st, [128, 512], F32, "tt")
                hid = T(st, [128, NCP], BF16, "hid")
                kc_f = T(st, [64, 512], F32, "kc_f")
                kc_b = T(st, [64, 512], BF16, "kc_b")
                vc_b = T(st, [128, 64], BF16, "vc_b")
                bdh = T(st, [64, 64], BF16, "bdh")
                p.dma("sp", lambda e: e.dma_start(out=w2f[:, 0:64], in_=w2k.ap()), writes=["w2f"])
                p.dma("sp", lambda e: e.dma_start(out=w2f[:, 64:128], in_=w2v.ap()), writes=["w2f"])
                p.dma("sp", lambda e: e.dma_start(out=peTf[:, :], in_=peT.ap()), writes=["peTf"])
                p.op("dve", lambda e: e.tensor_copy(out=w2kt[:, :], in_=w2f[:, 0:64]), reads=["w2f"], writes=["w2kt"])
                p.op("dve", lambda e: e.tensor_copy(out=w2vt[:, :], in_=w2f[:, 64:128]), reads=["w2f"], writes=["w2vt"])
                p.op("dve", lambda e: e.tensor_copy(out=peTt[:, :], in_=peTf[:, :]), reads=["peTf"], writes=["peTt"])
                p.op("dve", lambda e: e.memset(bdh[:, :], 1.0), writes=["bdh"])
                p.op("dve", lambda e: e.memset(hid[:, :], 0.0), writes=["hid"])
                for kv in range(2):
                    p.dma("sp", lambda e, kv=kv: e.dma_start(out=w1t[:, :], in_=(w1k_b if kv == 0 else w1v_b).ap()[0]),
                          reads=["w1k.0", "w1v.0"], writes=["w1t"])
                    def mmc(e, kv=kv):
                        for l in range(32):
                            ins = e.matmul(PS[6][:, 0:1], w1t[0:64, l * 128:(l + 1) * 128], peTt[0:64, kv * 32 + l:kv * 32 + l + 1], start=(l == 0), stop=(l == 31))
                        return ins
                    p.op("pe", mmc, reads=["w1t", "peTt"], writes=[PSN[6]])
                    p.op("act", lambda e, kv=kv: e.activation(out=c1[:, kv:kv + 1], in_=PS[6][:, 0:1], func=AF.Copy), reads=[PSN[6]], writes=["c1"])
                    for ch in range(2):
                        p.dma("sp", lambda e, kv=kv, ch=ch: e.dma_start(out=raw[:, :], in_=rawT.ap()[kv * 2 + ch]), reads=["rawT"], writes=["raw"])
                        for gg in range(2):
                            g = ch * 2 + gg
                            r0 = gg * 64
                            for n0 in range(0, NCV, 512):
                                n1 = min(NCV, n0 + 512)
                                nn = n1 - n0

                                def mm(e, r0=r0, n0=n0, nn=nn):
                                    for l in range(32):
                                        a0 = l + 16 * n0
                                        ins = e.matmul(PS[0][:, 0:nn], w1t[r0:r0 + 64, l * 128:(l + 1) * 128], raw[r0:r0 + 64, a0:a0 + 16 * (nn - 1) + 1:16],
                                                       start=(l == 0), stop=(l == 31))
                                    return ins
                                p.op("pe", mm, reads=["raw", "w1t"], writes=[PSN[0]])
                                p.op("act", lambda e, nn=nn, kv=kv: e.activation(out=xx[:, 0:nn], in_=PS[0][:, 0:nn], func=AF.Identity, bias=c1[:, kv:kv + 1], scale=1.0),
                                     reads=[PSN[0], "c1"], writes=["xx"])
                                p.op("dve", lambda e, nn=nn: e.tensor_tensor(out=tt[:, 0:nn], in0=xx[:, 0:nn], in1=xx[:, 0:nn], op=ALU.mult), reads=["xx"], writes=["tt"])
                                p.op("dve", lambda e, nn=nn: e.tensor_scalar(out=tt[:, 0:nn], in0=tt[:, 0:nn], scalar1=0.044715, scalar2=1.0, op0=ALU.mult, op1=ALU.add),
                                     reads=["tt"], writes=["tt"])
                                p.op("dve", lambda e, nn=nn: e.tensor_tensor(out=tt[:, 0:nn], in0=tt[:, 0:nn], in1=xx[:, 0:nn], op=ALU.mult), reads=["tt", "xx"], writes=["tt"])
                                p.op("act", lambda e, nn=nn: e.activation(out=tt[:, 0:nn], in_=tt[:, 0:nn], func=AF.Sigmoid, scale=1.5957691216057308),
                                     reads=["tt"], writes=["tt"])
                                p.op("dve", lambda e, nn=nn, n0=n0: e.tensor_tensor(out=hid[:, n0:n0 + nn], in0=tt[:, 0:nn], in1=xx[:, 0:nn], op=ALU.mult),
                                     reads=["tt", "xx"], writes=["hid"])
                            if kv == 0:
                                CW = min(512, NCP)
                                for n0 in range(0, NCP, CW):
                                    p.op("pe", lambda e, n0=n0: e.matmul(PS[1][0:64, 0:CW], w2kt[:, :], hid[:, n0:n0 + CW], start=True, stop=True),
                                         reads=["hid", "w2kt"], writes=[PSN[1]])
                                    p.op("act", lambda e: e.activation(out=kc_f[:, 0:CW], in_=PS[1][0:64, 0:CW], func=AF.Square), reads=[PSN[1]], writes=["kc_f"])
                                    p.op("dve", lambda e: e.tensor_copy(out=kc_b[:, 0:CW], in_=kc_f[:, 0:CW]), reads=["kc_f"], writes=["kc_b"])
                                    p.op("pe", lambda e: e.matmul(PS[2][0:64, 0:CW], bdh[:, :], kc_b[:, 0:CW], start=True, stop=True), reads=["kc_b", "bdh"], writes=[PSN[2]])
                                    p.op("act", lambda e: e.activation(out=kc_f[:, 0:CW], in_=PS[2][0:64, 0:CW], func=AF.Sqrt, bias=eps_t[0:64, 0:1], scale=1.0 / 64),
                                         reads=[PSN[2], "eps"], writes=["kc_f"])
                                    p.op("dve", lambda e: e.reciprocal(out=kc_f[:, 0:CW], in_=kc_f[:, 0:CW]), reads=["kc_f"], writes=["kc_f"])
                                    p.op("dve", lambda e: e.scalar_tensor_tensor(out=kc_b[:, 0:CW], in0=PS[1][0:64, 0:CW], scalar=kn_t[0:64, 0:1], in1=kc_f[:, 0:CW],
                                                                                op0=ALU.mult, op1=ALU.mult), reads=[PSN[1], "kc_f", "kn"], writes=["kc_b"])
                                    p.dma("sp", lambda e, g=g, n0=n0: e.dma_start(out=kcT.ap()[g][:, n0:n0 + CW], in_=kc_b[:, 0:CW]), reads=["kc_b"], writes=["kcT"])
                            else:
                                for nb in range(NCK):
                                    p.op("pe", lambda e, nb=nb: e.matmul(PS[1][:, 0:64], hid[:, nb * 128:(nb + 1) * 128], w2vt[:, :], start=True, stop=True),
                                         reads=["hid", "w2vt"], writes=[PSN[1]])
                                    p.op("act", lambda e: e.activation(out=vc_b[:, :], in_=PS[1][:, 0:64], func=AF.Copy), reads=[PSN[1]], writes=["vc_b"])
                                    p.dma("sp", lambda e, g=g, nb=nb: e.dma_start(out=vcS.ap()[g][:, nb * 64:(nb + 1) * 64], in_=vc_b[:, :]), reads=["vc_b"], writes=["vcS"])
        _phase4()
        p.barrier()

        def _phase5():
            with contextlib.ExitStack() as st:
                h = T(st, [128, 8, 512], F32, "h")
                hb = T(st, [128, 8, 512], F32, "hb")
                sq = T(st, [128, 8, 512], BF16, "sq")
                xn = T(st, [128, 8, 512], BF16, "xn")
                rs = T(st, [128, 512], F32, "rs")
                act = T(st, [128, NJ, 512], BF16, "act")
                sg = [T(st, [128, 512], F32, "sg") for _ in range(2)]
                wpi = [T(st, [128, 2048], BF16, "wpi") for _ in range(4)]
                wpo = [T(st, [128, NJ * 128], BF16, "wpo") for _ in range(3)]
                bufs = (sq, xn, rs, act, sg, wpi, wpo)
                wqt = [T(st, [128, 512], BF16, "wqt") for _ in range(2)]
                wgt = T(st, [128, 384], BF16, "wgt")
                qb_ = [T(st, [64, 512], BF16, "qb") for _ in range(2)]
                bdh = T(st, [64, 64], BF16, "bdh")
                gsb = T(st, [48, 512], BF16, "gsb")
                p.op("dve", lambda e: e.memset(bdh[:, :], 1.0), writes=["bdh"])
                p.dma("sp", lambda e: e.dma_start(out=wgt[:, :], in_=wg_b.ap()[0]), reads=["wg.0"], writes=["wgt"])
                h1v = h1T.ap().rearrange("(c p) s -> p c s", p=128)
                h2v = h2T.ap().rearrange("(c p) s -> p c s", p=128)
                for i in range(NO):
                    ta, tb_ = (2 * i) * 512, (2 * i + 1) * 512
                    p.dma("sp", lambda e, ta=ta: e.dma_start(out=h[:, :, :], in_=h1v[:, :, ta:ta + 512]), reads=["h1T"], writes=HN)
                    p.dma("sp", lambda e, tb_=tb_: e.dma_start(out=hb[:, :, :], in_=h1v[:, :, tb_:tb_ + 512]), reads=["h1T"], writes=["hb"])
                    p.op("dve", lambda e: e.tensor_scalar(out=h[:, :, :], in0=h[:, :, :], scalar1=par_t[:, 1:2], scalar2=None, op0=ALU.mult),
                         reads=HN + ["par"], writes=HN)
                    p.op("dve", lambda e: e.scalar_tensor_tensor(out=h[:, :, :], in0=hb[:, :, :], scalar=par_t[:, 0:1], in1=h[:, :, :], op0=ALU.mult, op1=ALU.add),
                         reads=HN + ["hb", "par"], writes=HN)
                    ffn(h, "h", 2, 32, bufs)
                    p.dma("pool", lambda e, i=i: e.dma_start(out=h2v[:, :, i * 512:(i + 1) * 512], in_=h[:, :, :]), reads=HN, writes=["h2T"])
                    rms_to_xn(h, "h", 40, sq, xn, rs)
                    if "dbg_xn" in dbg and i == 0:
                        dx = nc.dram_tensor("dbg_xn", [128, 4096], BF16, kind="ExternalOutput")
                        p.dma("sp", lambda e: e.dma_start(out=dx.ap(), in_=xn[:, :, :].rearrange("p c s -> p (c s)")), reads=["xn"], writes=["dbg_xn"])
                        dr = nc.dram_tensor("dbg_rs", [128, 512], F32, kind="ExternalOutput")
                        p.dma("sp", lambda e: e.dma_start(out=dr.ap(), in_=rs[:, :]), reads=["rs"], writes=["dbg_rs"])
                    for hd in range(16):
                        b = hd % 2
                        p.dma("sp", lambda e, hd=hd, b=b: e.dma_start(out=wqt[b][:, :], in_=wq_b.ap()[hd]), reads=["wq.%d" % hd], writes=["wqt%d" % b])

                        def mm(e, b=b):
                            for c in range(8):
                                ins = e.matmul(PS[b][0:64, :], wqt[b][:, c * 64:(c + 1) * 64], xn[:, c, :], start=(c == 0), stop=(c == 7))
                            return ins
                        p.op("pe", mm, reads=["xn", "wqt%d" % b], writes=[PSN[b]])
                        p.op("act", lambda e, b=b: e.activation(out=sg[b][0:64, :], in_=PS[b][0:64, :], func=AF.Square), reads=[PSN[b]], writes=["sg%d" % b])
                        p.op("dve", lambda e, b=b: e.tensor_copy(out=qb_[b][:, :], in_=sg[b][0:64, :]), reads=["sg%d" % b], writes=["qb%d" % b])
                        p.op("pe", lambda e, b=b: e.matmul(PS[2 + b][0:64, :], bdh[:, :], qb_[b][:, :], start=True, stop=True), reads=["qb%d" % b, "bdh"], writes=[PSN[2 + b]])
                        rstd_from(2 + b, 64, sg[b], "sg%d" % b, rows=64)
                        p.op("dve", lambda e, b=b: e.scalar_tensor_tensor(out=qb_[b][:, :], in0=PS[b][0:64, :], scalar=qn_t[0:64, 0:1], in1=sg[b][0:64, :],
                                                                         op0=ALU.mult, op1=ALU.mult), reads=[PSN[b], "sg%d" % b, "qn"], writes=["qb%d" % b])
                        p.dma("pool", lambda e, hd=hd, b=b, i=i: e.dma_start(out=qTs.ap()[hd][:, i * 512:(i + 1) * 512], in_=qb_[b][:, :]),
                              reads=["qb%d" % b], writes=["qTs"])

                    def mmg(e):
                        for c in range(8):
                            ins = e.matmul(PS[4][0:48, :], wgt[:, c * 48:(c + 1) * 48], xn[:, c, :], start=(c == 0), stop=(c == 7))
                        return ins
                    p.op("pe", mmg, reads=["xn", "wgt"], writes=[PSN[4]])
                    p.op("act", lambda e: e.activation(out=gsb[:, :], in_=PS[4][0:48, :], func=AF.Sigmoid), reads=[PSN[4]], writes=["gsb"])
                    p.dma("pool", lambda e, i=i: e.dma_start(out=gsT.ap()[:, i * 512:(i + 1) * 512], in_=gsb[:, :]), reads=["gsb"], writes=["gsT"])
        _phase5()
        p.barrier()

        def _phase6():
            with contextlib.ExitStack() as st:
                ks_t = T(st, [64, S], BF16, "ks_t")
                kw_t = T(st, [64, S], BF16, "kw_t")
                kc_t = T(st, [64, NCP], BF16, "kc_t")
                vsw_t = T(st, [128, NKT, 256], BF16, "vsw_t")
                vc_t = T(st, [128, NCK, 128], BF16, "vc_t")
                expd_t = T(st, [NSB, S], BF16, "expd_t")
                ov_t = T(st, [128, NCK * (NSB + 1)], BF16, "ov_t")
                ov_f = T(st, [128, NCK * (NSB + 1)], F32, "ov_f")
                selg_t = T(st, [48, 48 * 128], BF16, "selg_t")
                negT = T(st, [NSB, SO], BF16, "negT")
                impacc = T(st, [128, NO * 4, NSB], F32, "impacc")
                q_t = [T(st, [64, 512], BF16, "q_t") for _ in range(2)]
                gs_t = T(st, [48, 512], BF16, "gs_t")
                ms_t = T(st, [128, W_S], BF16, "ms_t")
                mw_t = T(st, [128, W_W], BF16, "mw_t")
                mc_t = [T(st, [128, 512], BF16, "mc_t") for _ in range(8)]
                e_t = [T(st, [128, 512], BF16, "e_t") for _ in range(8)]
                p_t = [T(st, [128, 512], BF16, "p_t") for _ in range(8)]
                rl = T(st, [128, 4], F32, "rl")
                sA = T(st, [128, NSB], F32, "sA")
                sB = T(st, [128, NSB], F32, "sB")
                sc = T(st, [128, NSB], F32, "sc")
                sc2 = T(st, [128, NSB], F32, "sc2")
                m8a = T(st, [128, 8], F32, "m8a")
                m8b = T(st, [128, 8], F32, "m8b")
                gbb = [T(st, [128, 3, 512], F32, "gb") for _ in range(2)]
                rr2b = [T(st, [128, 512], F32, "rr2") for _ in range(3)]
                rr = T(st, [64, 512], F32, "rr")
                acc = T(st, [64, 512], F32, "acc")
                ocb = T(st, [64, 512], BF16, "ocb")
                ones64 = ones_bf[:, 0:64]
                p.dma("sp", lambda e: e.dma_start(out=expd_t[:, :], in_=expd_b.ap()[0]), reads=["expd.0"], writes=["expd_t"])
                p.dma("sp", lambda e: e.dma_start(out=selg_t[:, :], in_=selg_b.ap()[0]), reads=["selg.0"], writes=["selg_t"])
                p.dma("sp", lambda e: e.dma_start(out=ov_f[:, :], in_=ovm.ap()), writes=["ov_f"])
                p.op("dve", lambda e: e.tensor_copy(out=ov_t[:, :], in_=ov_f[:, :]), reads=["ov_f"], writes=["ov_t"])
                cnt = [0]
                p.op("dve", lambda e: e.memset(vsw_t[:, :, :], 1.0), writes=["kgrp"])
                p.op("dve", lambda e: e.memset(vc_t[:, :, :], 1.0), writes=["kgrp"])
                for k_ in range(3):
                    p.op("dve", lambda e, k_=k_: e.memset(rr2b[k_][:, :], 0.0), writes=["rr2.%d" % k_])

                def c_chunks(i):
                    return [nk for nk in range(NCK) if (2 * i + 2) * 512 - 1 >= 2048 * nk + 31]

                mcc = [0]

                def load_mc(hd, i, nk):
                    b = mcc[0] % 8
                    mcc[0] += 1
                    D_ = 1024 * i - 2048 * nk
                    j0 = min(D_, JSAT_C)
                    src = bass.AP(tensor=rep_c, offset=hd * 128 * L_C + 2032 + j0, ap=[[L_C - 16, 128], [1, 512]])
                    p.dma("sp", lambda e, b=b, src=src: e.dma_start(out=mc_t[b][:, :], in_=src), reads=["rep_c.%d" % hd], writes=["mc_t%d" % b])
                    return b

                def score_unit(hd, kmat, kcol0, qbuf, qn_, mask_ap, mask_res, neg_cols=None):
                    b = cnt[0] % 8
                    pb = cnt[0] % 4
                    u3 = cnt[0] % 3
                    cnt[0] += 1

                    def mm(e):
                        ins = e.matmul(PS[pb][:, :], kmat[:, kcol0:kcol0 + 128], qbuf[:, :], start=True, stop=(neg_cols is None))
                        if neg_cols is not None:
                            ins = e.matmul(PS[pb][:, :], expd_t[:, kcol0:kcol0 + 128], negT[:, neg_cols:neg_cols + 512], start=False, stop=True)
                        return ins
                    p.op("pe", mm, reads=["kgrp", qn_, "expd_t", "negT"], writes=[PSN[pb]])
                    p.op("act", lambda e: e.activation(out=e_t[b][:, :], in_=PS[pb][:, :], func=AF.Exp, scale=0.125), reads=[PSN[pb]], writes=["e_t%d" % b])
                    eng = "pool" if u3 == 2 else "dve"
                    p.op(eng, lambda e: e.tensor_tensor(out=p_t[b][:, :], in0=e_t[b][:, :], in1=mask_ap, op=ALU.mult),
                         reads=["e_t%d" % b] + mask_res, writes=["p_t%d" % b])
                    return b

                for g in range(4):
                    p.dma("sp", lambda e, g=g: e.dma_start(out=ks_t[:, :], in_=ksT.ap()[g]), reads=["ksT"], writes=["kgrp"])
                    p.dma("sp", lambda e, g=g: e.dma_start(out=kw_t[:, :], in_=kwT.ap()[g]), reads=["kwT"], writes=["kgrp"])
                    p.dma("sp", lambda e, g=g: e.dma_start(out=kc_t[:, :], in_=kcT.ap()[g]), reads=["kcT"], writes=["kgrp"])
                    p.dma("sp", lambda e, g=g: e.dma_start(out=vc_t[:, :, 0:64], in_=vcS.ap()[g].rearrange("p (k c) -> p k c", c=64)), reads=["vcS"], writes=["kgrp"])
                    vsrc = vsw.ap().rearrange("k p c -> p k c")
                    for k0 in range(0, NKT, 8):
                        p.dma("sp", lambda e, g=g, k0=k0: e.dma_start(out=vsw_t[:, k0:k0 + 8, 0:64], in_=vsrc[:, k0:k0 + 8, g * 64:(g + 1) * 64]),
                              reads=["vsw"], writes=["kgrp"])
                        p.dma("sp", lambda e, g=g, k0=k0: e.dma_start(out=vsw_t[:, k0:k0 + 8, 128:192], in_=vsrc[:, k0:k0 + 8, 256 + g * 64:256 + (g + 1) * 64]),
                              reads=["vsw"], writes=["kgrp"])
                    for i in range(NO):
                        chunks = c_chunks(i)
                        for hp in range(2):
                            pbs_h = {}
                            for hh in (2 * hp, 2 * hp + 1):
                                hd = g * 4 + hh
                                qb = hh % 2
                                p.dma("sp", lambda e, hd=hd, i=i, qb=qb: e.dma_start(out=q_t[qb][:, :], in_=qTs.ap()[hd][:, i * 512:(i + 1) * 512]),
                                      reads=["qTs"], writes=["q_t%d" % qb])
                                pbs = []
                                for nk in chunks:
                                    mb = load_mc(hd, i, nk)
                                    pbs.append(score_unit(hd, kc_t, nk * 128, q_t[qb], "q_t%d" % qb, mc_t[mb][:, :], ["mc_t%d" % mb]))
                                pbs_h[hh] = pbs
                            for hh in (2 * hp, 2 * hp + 1):
                                hd = g * 4 + hh
                                pbs = pbs_h[hh]
                                for qq in range(4):
                                    bank = 4 + qq // 2
                                    col = (qq % 2) * (NSB + 1)

                                    def mm(e, qq=qq, bank=bank, col=col, pbs=pbs, chunks=chunks):
                                        for ii, nk in enumerate(chunks):
                                            ins = e.matmul(PS[bank][:, col:col + NSB + 1], p_t[pbs[ii]][:, qq * 128:(qq + 1) * 128],
                                                           ov_t[:, nk * (NSB + 1):(nk + 1) * (NSB + 1)], start=(ii == 0), stop=(ii == len(chunks) - 1))
                                        return ins
                                    p.op("pe", mm, reads=["p_t%d" % b_ for b_ in pbs] + ["ov_t"], writes=[PSN[bank]])
                                    p.op("dve", lambda e, qq=qq, bank=bank, col=col: e.tensor_scalar(out=rl[:, qq:qq + 1], in0=PS[bank][:, col + NSB:col + NSB + 1],
                                                                                                    scalar1=1e-30, scalar2=None, op0=ALU.add),
                                         reads=[PSN[bank]], writes=["rl"])
                                    p.op("dve", lambda e, qq=qq: e.reciprocal(out=rl[:, qq:qq + 1], in_=rl[:, qq:qq + 1]), reads=["rl"], writes=["rl"])
                                    if hh == 0:
                                        p.op("dve", lambda e, qq=qq, bank=bank, col=col, i=i: e.tensor_scalar(out=impacc[:, i * 4 + qq, :], in0=PS[bank][:, col:col + NSB],
                                                                                                             scalar1=rl[:, qq:qq + 1], scalar2=None, op0=ALU.mult),
                                             reads=[PSN[bank], "rl"], writes=["impacc"])
                                    else:
                                        p.op("dve", lambda e, qq=qq, bank=bank, col=col, i=i: e.scalar_tensor_tensor(out=impacc[:, i * 4 + qq, :], in0=PS[bank][:, col:col + NSB],
                                                                                                                    scalar=rl[:, qq:qq + 1], in1=impacc[:, i * 4 + qq, :],
                                                                                                                    op0=ALU.mult, op1=ALU.add),
                                             reads=[PSN[bank], "rl", "impacc"], writes=["impacc"])

                        for qq in range(4):
                            qi = i * 4 + qq
                            p.dma("sp", lambda e, qi=qi: e.dma_start(out=sA[:, :], in_=selA.ap()[qi]), writes=["sA"])
                            p.dma("sp", lambda e, qi=qi: e.dma_start(out=sB[:, :], in_=selB.ap()[qi]), writes=["sB"])
                            p.op("dve", lambda e, qi=qi: e.tensor_tensor(out=sc[:, :], in0=impacc[:, qi, :], in1=sA[:, :], op=ALU.mult), reads=["impacc", "sA"], writes=["sc"])
                            p.op("dve", lambda e: e.tensor_tensor(out=sc[:, :], in0=sc[:, :], in1=sB[:, :], op=ALU.add), reads=["sc", "sB"], writes=["sc"])
                            p.op("dve", lambda e: e.max(out=m8a[:, :], in_=sc[:, :]), reads=["sc"], writes=["m8a"])
                            p.op("dve", lambda e: e.match_replace(out=sc2[:, :], in_to_replace=m8a[:, :], in_values=sc[:, :], imm_value=-3e38),
                                 reads=["sc", "m8a"], writes=["sc2"])
                            p.op("dve", lambda e: e.max(out=m8b[:, :], in_=sc2[:, :]), reads=["sc2"], writes=["m8b"])
                            p.op("dve", lambda e: e.tensor_scalar(out=sc2[:, :], in0=sc[:, :], scalar1=m8b[:, 7:8], scalar2=-MASKV, op0=ALU.is_ge, op1=ALU.mult),
                                 reads=["sc", "m8b"], writes=["sc2"])
                            p.op("dve", lambda e: e.tensor_scalar(out=sc2[:, :], in0=sc2[:, :], scalar1=MASKV, scalar2=None, op0=ALU.add), reads=["sc2"], writes=["sc2"])
                            p.op("pe", lambda e: e.transpose(out=PS[6][0:NSB, 0:128], in_=sc2[:, :], identity=ident_t[:, :]), reads=["sc2", "ident"], writes=[PSN[6]])
                            p.op("act", lambda e, qi=qi: e.activation(out=negT[:, qi * 128:(qi + 1) * 128], in_=PS[6][0:NSB, 0:128], func=AF.Copy),
                                 reads=[PSN[6]], writes=["negT"])
                    LA = 6
                    DEFER = 6
                    epi = [0]
                    for hh in range(4):
                        hd = g * 4 + hh
                        src_s = bass.AP(tensor=rep_s, offset=hd * 128 * L_S + 127, ap=[[L_S - 1, 128], [1, W_S]])
                        src_w = bass.AP(tensor=rep_w, offset=hd * 128 * L_W + 127, ap=[[L_W - 1, 128], [1, W_W]])
                        p.dma("sp", lambda e, src_s=src_s: e.dma_start(out=ms_t[:, :], in_=src_s), reads=["rep_s.%d" % hd], writes=["ms_t"])
                        p.dma("sp", lambda e, src_w=src_w: e.dma_start(out=mw_t[:, :], in_=src_w), reads=["rep_w.%d" % hd], writes=["mw_t"])
                        pend = []

                        pend2 = []

                        def emit_pv(un, b):
                            ob = 4 + un["br"]
                            first, last, vap, br, gi, hd_, i_ = un["first"], un["last"], un["vap"], un["br"], un["gi"], un["hd"], un["i"]
                            p.op("pe", lambda e: e.matmul(PS[ob][:, :], vap, p_t[b][:, :], start=first, stop=last),
                                 reads=["p_t%d" % b, "kgrp"], writes=[PSN[ob]])
                            if last:
                                gbn = "gb%d" % gi
                                k_ = epi[0] % 3
                                epi[0] += 1
                                rr2 = rr2b[k_]
                                rn = "rr2.%d" % k_
                                p.op("dve", lambda e: e.tensor_scalar(out=rr2[64:128, :], in0=PS[ob][64:128, :], scalar1=1e-30, scalar2=None, op0=ALU.add), reads=[PSN[ob]], writes=[rn])
                                p.op("dve", lambda e: e.reciprocal(out=rr2[64:128, :], in_=rr2[64:128, :]), reads=[rn], writes=[rn])
                                p.op("dve", lambda e: e.tensor_tensor(out=rr2[64:128, :], in0=rr2[64:128, :], in1=gbb[gi][64:128, br, :], op=ALU.mult), reads=[rn, gbn], writes=[rn])

                                def stage2():
                                    p.op("pe", lambda e: e.matmul(PS[7][0:64, :], ident_t[:, 64:128], rr2[:, :], start=True, stop=True), reads=[rn, "ident"], writes=[PSN[7]])
                                    p.op("act", lambda e: e.activation(out=rr[:, :], in_=PS[7][0:64, :], func=AF.Copy), reads=[PSN[7]], writes=["rr"])
                                    if br == 0:
                                        p.op("dve", lambda e: e.tensor_tensor(out=acc[:, :], in0=PS[ob][0:64, :], in1=rr[:, :], op=ALU.mult), reads=[PSN[ob], "rr"], writes=["acc"])
                                    else:
                                        p.op("dve", lambda e: e.tensor_tensor(out=rr[:, :], in0=PS[ob][0:64, :], in1=rr[:, :], op=ALU.mult), reads=[PSN[ob], "rr"], writes=["rr"])
                                        p.op("dve", lambda e: e.tensor_tensor(out=acc[:, :], in0=acc[:, :], in1=rr[:, :], op=ALU.add), reads=["acc", "rr"], writes=["acc"])
                                    if br == 2:
                                        p.op("act", lambda e: e.activation(out=ocb[:, :], in_=acc[:, :], func=AF.Copy), reads=["acc"], writes=["ocb"])
                                        p.dma("sp", lambda e: e.dma_start(out=ocT.ap()[hd_][:, i_ * 512:(i_ + 1) * 512], in_=ocb[:, :]), reads=["ocb"], writes=["ocT"])
                                pend2.append([DEFER + 1, stage2])
                            for it in pend2:
                                it[0] -= 1
                            while pend2 and pend2[0][0] <= 0:
                                pend2.pop(0)[1]()

                        for i in range(NO):
                            qb = i % 2
                            gi = i % 2
                            p.dma("sp", lambda e, hd=hd, i=i, qb=qb: e.dma_start(out=q_t[qb][:, :], in_=qTs.ap()[hd][:, i * 512:(i + 1) * 512]),
                                  reads=["qTs"], writes=["q_t%d" % qb])
                            p.dma("sp", lambda e, i=i: e.dma_start(out=gs_t[:, :], in_=gsT.ap()[:, i * 512:(i + 1) * 512]), reads=["gsT"], writes=["gs_t"])
                            for br in range(3):
                                p.op("pe", lambda e, br=br, hd=hd: e.matmul(PS[7][:, :], selg_t[:, (hd * 3 + br) * 128:(hd * 3 + br + 1) * 128], gs_t[:, :],
                                                                           start=True, stop=True), reads=["selg_t", "gs_t"], writes=[PSN[7]])
                                p.op("act", lambda e, br=br, gi=gi: e.activation(out=gbb[gi][64:128, br, :], in_=PS[7][64:128, :], func=AF.Copy), reads=[PSN[7]], writes=["gb%d" % gi])
                            units = []
                            for br in range(3):
                                if br == 0:
                                    kts = c_chunks(i)
                                elif br == 1:
                                    kts = list(range((2 * i + 2) * 4))
                                else:
                                    kts = list(range(max(0, (2 * i - 1) * 4), (2 * i + 2) * 4))
                                for ui, kt in enumerate(kts):
                                    units.append(dict(br=br, kt=kt, first=(ui == 0), last=(ui == len(kts) - 1), gi=gi, hd=hd, i=i))
                            for un in units:
                                kt, br = un["kt"], un["br"]
                                if br == 0:
                                    mb = load_mc(hd, i, kt)
                                    b = score_unit(hd, kc_t, kt * 128, q_t[qb], "q_t%d" % qb, mc_t[mb][:, :], ["mc_t%d" % mb])
                                    un["vap"] = vc_t[:, kt, :]
                                elif br == 1:
                                    j0 = min(1024 * i - 128 * kt + OFFS - 127, JSAT_S)
                                    b = score_unit(hd, ks_t, kt * 128, q_t[qb], "q_t%d" % qb, ms_t[:, j0:j0 + 512], ["ms_t"], neg_cols=i * 512)
                                    un["vap"] = vsw_t[:, kt, 0:128]
                                else:
                                    j0 = 1024 * i - 128 * kt + OFFW - 127
                                    b = score_unit(hd, kw_t, kt * 128, q_t[qb], "q_t%d" % qb, mw_t[:, j0:j0 + 512], ["mw_t"])
                                    un["vap"] = vsw_t[:, kt, 128:256]
                                pend.append((un, b))
                                if len(pend) > LA:
                                    emit_pv(*pend.pop(0))
                        while pend:
                            emit_pv(*pend.pop(0))
                        while pend2:
                            pend2.pop(0)[1]()
        _phase6()
        p.barrier()

        def _phase7():
            with contextlib.ExitStack() as st:
                h = T(st, [128, 8, 512], F32, "h")
                sq = T(st, [128, 8, 512], BF16, "sq")
                xn = T(st, [128, 8, 512], BF16, "xn")
                rs = T(st, [128, 512], F32, "rs")
                act = T(st, [128, NJ, 512], BF16, "act")
                sg = [T(st, [128, 512], F32, "sg") for _ in range(2)]
                wpi = [T(st, [128, 2048], BF16, "wpi") for _ in range(4)]
                wpo = [T(st, [128, NJ * 128], BF16, "wpo") for _ in range(3)]
                bufs = (sq, xn, rs, act, sg, wpi, wpo)
                wo_t = T(st, [64, 16, 1024], BF16, "wo_t")
                oc_t = T(st, [64, 16, 512], BF16, "oc_t")
                p.dma("sp", lambda e: e.dma_start(out=wo_t[:, :, :], in_=wo_b.ap().rearrange("h p n -> p h n")), reads=["wo.%d" % i for i in range(16)], writes=["wo_t"])
                h2v = h2T.ap().rearrange("(c p) s -> p c s", p=128)
                ov = outT.ap().rearrange("(c p) s -> p c s", p=128)
                for i in range(NO):
                    p.dma("sp", lambda e, i=i: e.dma_start(out=h[:, :, :], in_=h2v[:, :, i * 512:(i + 1) * 512]), reads=["h2T"], writes=HN)
                    p.dma("sp", lambda e, i=i: e.dma_start(out=oc_t[:, :, :], in_=ocT.ap().rearrange("h p s -> p h s")[:, :, i * 512:(i + 1) * 512]),
                          reads=["ocT"], writes=["oc_t"])
                    for d in range(8):
                        b = d % 2

                        def mm(e, d=d, b=b):
                            for hd in range(16):
                                ins = e.matmul(PS[b][:, :], wo_t[:, hd, d * 128:(d + 1) * 128], oc_t[:, hd, :], start=(hd == 0), stop=(hd == 15))
                            return ins
                        p.op("pe", mm, reads=["wo_t", "oc_t"], writes=[PSN[b]])
                        p.op("dve", lambda e, d=d, b=b: e.tensor_tensor(out=h[:, d, :], in0=PS[b][:, :], in1=h[:, d, :], op=ALU.add), reads=[PSN[b], "h.%d" % d], writes=["h.%d" % d])
                    ffn(h, "h", 3, 48, bufs)
                    p.dma("pool", lambda e, i=i: e.dma_start(out=ov[:, :, i * 512:(i + 1) * 512], in_=h[:, :, :]), reads=HN, writes=["outT"])
        _phase7()
        p.barrier()
        p.run()
    return nc


def _rel_bucket_table(n):
    import jax
    import jax.numpy as jnp
    import math
    with jax.default_device(jax.devices("cpu")[0]):
        rel = jnp.arange(n)
        nn = jnp.maximum(rel, 0)
        nf = jnp.maximum(nn, 1).astype(jnp.float32)
        large = 16 + (jnp.log(nf / 16) / math.log(4096 / 16) * 16).astype(jnp.int32)
        large = jnp.minimum(large, 31)
        return np.asarray(jnp.where(nn < 16, nn, large))


def prep_shared(S, inputs):
    f = lambda a: np.ascontiguousarray(a, dtype=np.float32)
    NSB = S // 64
    NCP = S // 16
    NCK = NCP // 128
    sh = {}

    def col8(g):
        return g.reshape(8, 128).T
    norms = [inputs["ffn1_norm"][0], inputs["mix_norm"][0], inputs["ffn2_norm"][0], inputs["kv_norm"],
             inputs["ffn1_norm"][1], inputs["mix_norm"][1], inputs["ffn2_norm"][1]]
    sh["gains"] = f(np.concatenate([col8(np.asarray(g)) for g in norms], axis=1))

    def win_r(w):
        w = np.asarray(w).reshape(8, 128, 2, NJ, 128)
        return f(w.transpose(3, 1, 2, 0, 4).reshape(NJ, 128, 2048))

    def wout_r(w):
        w = np.asarray(w).reshape(NJ, 128, 8, 128)
        return f(w.transpose(2, 1, 0, 3).reshape(8, 128, NJ * 128))
    ffn_list = [("ffn1_w_in", "ffn1_w_out", 0), ("ffn2_w_in", "ffn2_w_out", 0), ("ffn1_w_in", "ffn1_w_out", 1), ("ffn2_w_in", "ffn2_w_out", 1)]
    for i, (a, b, l) in enumerate(ffn_list):
        sh["win%d" % i] = win_r(inputs[a][l])
        sh["wout%d" % i] = wout_r(inputs[b][l])

    def sq_r(w, ncol):
        w = np.asarray(w).reshape(8, 128, ncol, 128)
        return w.transpose(2, 1, 0, 3)
    ci = np.asarray(inputs["conv_w_in"][0]).reshape(8, 128, 3, 8, 128)
    sh["cin"] = f(ci.transpose(3, 1, 2, 0, 4).reshape(8, 128, 3072))
    sh["cout"] = f(sq_r(inputs["conv_w_out"][0], 8).reshape(8, 128, 1024))
    cw = np.asarray(inputs["conv_w"][0])
    sh["cwk"] = f(cw.reshape(3, 8, 128).transpose(2, 1, 0).reshape(128, 24))
    wkv = np.asarray(inputs["w_kv"])
    units = []
    for c0 in (0, 128, 256, 384):
        units.append(wkv[:, c0:c0 + 128])
    for br in (1, 2):
        for g in range(4):
            cc = wkv[:, br * 512 + g * 64: br * 512 + (g + 1) * 64]
            units.append(np.concatenate([cc, cc], axis=1))
    sh["wkf"] = f(np.stack([u.reshape(8, 128, 128).transpose(1, 0, 2).reshape(128, 1024) for u in units]))
    wv = np.concatenate([wkv[:, 768:1024], wkv[:, 1280:1536]], axis=1)
    sh["wvt"] = f(wv.reshape(8, 128, 512).transpose(1, 0, 2).reshape(1, 128, 4096))
    wqf = np.asarray(inputs["attn_w_q"][0])
    sh["wq"] = f(np.stack([wqf[:, h * 64:(h + 1) * 64].reshape(8, 128, 64).transpose(1, 0, 2).reshape(128, 512) for h in range(16)]))
    sh["wg"] = f(wqf[:, 1024:1072].reshape(8, 128, 48).transpose(1, 0, 2).reshape(1, 128, 384))
    sh["wo"] = f(np.asarray(inputs["attn_w_o"][0]).reshape(16, 64, 1024))
    knm = np.asarray(inputs["k_norm"])
    sh["kn"] = f(np.concatenate([knm.T, knm.T], axis=0))
    qnm = np.asarray(inputs["attn_q_norm"][0])
    sh["qn"] = f(np.concatenate([qnm, qnm])[:, None])

    def w1_r(w):
        w = np.asarray(w).reshape(32, 64, 128).transpose(1, 0, 2).reshape(64, 4096)
        return f(np.concatenate([w, w], axis=0)[None])
    sh["w1k"] = w1_r(inputs["cmp_w1_k"])
    sh["w1v"] = w1_r(inputs["cmp_w1_v"])
    sh["w2k"] = f(inputs["cmp_w2_k"])
    sh["w2v"] = f(inputs["cmp_w2_v"])
    pk = np.asarray(inputs["cmp_pe_k"]).T
    pv = np.asarray(inputs["cmp_pe_v"]).T
    pe = np.concatenate([pk, pv], axis=1)
    sh["peT"] = f(np.concatenate([pe, pe], axis=0))
    ex = np.zeros((1, NSB, S), np.float32)
    for j in range(NSB):
        ex[0, j, j * 64:(j + 1) * 64] = 1.0
    sh["expd"] = ex
    n = np.arange(NCP)
    cs = n * 16
    ss = np.arange(NSB) * 64
    ovl = np.clip(np.minimum(cs[:, None] + 32, ss[None, :] + 64) - np.maximum(cs[:, None], ss[None, :]), 0, None) / 16.0
    ovl[NCP - 1, :] = 0.0
    ovl1 = np.concatenate([ovl, np.ones((NCP, 1))], axis=1)
    sh["ovm"] = f(ovl1.reshape(NCK, 128, NSB + 1).transpose(1, 0, 2).reshape(128, NCK * (NSB + 1)))
    sg_ = np.zeros((1, 48, 48 * 128), np.float32)
    for c in range(48):
        sg_[0, c, c * 128 + 64:(c + 1) * 128] = 1.0
    sh["selg"] = sg_
    sh["ident"] = np.eye(128, dtype=np.float32)
    return sh


def prep_parity(S, inputs, par):
    f = lambda a: np.ascontiguousarray(a, dtype=np.float32)
    NSB = S // 64
    NO = S // 1024
    pp = {}
    pp["par"] = f(np.tile(np.array([[par, 1 - par]], np.float32), (128, 1)))
    rb = np.asarray(inputs["rel_bias"])
    bt = _rel_bucket_table(max(S, 4096) + 8192)

    def vec(L, off, lo, hi):
        rel = np.arange(L) - off + 512 * par
        ok = (rel >= lo) & (rel < hi)
        idx = bt[np.clip(rel, 0, len(bt) - 1)]
        v = rb[idx, :].T.copy()
        v[:, ~ok] = MASKV
        return f(v)
    pp["bvs"] = vec(L_S, OFFS, 0, 1 << 30)
    pp["bvw"] = vec(L_W, OFFW, 0, 512)
    pp["bvc"] = vec(L_C, OFFC, 0, 1 << 30)
    A = np.zeros((NO * 4, 128, NSB), np.float32)
    B = np.zeros((NO * 4, 128, NSB), np.float32)
    jj = np.arange(NSB)[None, :]
    for i in range(NO):
        for qq in range(4):
            t = (2 * i + par) * 512 + qq * 128 + np.arange(128)
            bt_ = (t // 64)[:, None]
            valid = jj <= bt_
            forced = (jj == 0) | (jj == bt_) | (jj == bt_ - 1)
            A[i * 4 + qq] = (valid & ~forced)
            B[i * 4 + qq] = np.where(valid, np.where(forced, 1e30, 0.0), -1e30)
    pp["selA"] = A
    pp["selB"] = B
    return pp


_CACHE = {}


def run_model(S, inputs, x, dbg=()):
    B = x.shape[0]
    key = (S, tuple(dbg))
    if key not in _CACHE:
        _CACHE[key] = build(S, dbg)
    nc = _CACHE[key]
    sh = prep_shared(S, inputs)
    pps = [prep_parity(S, inputs, 0), prep_parity(S, inputs, 1)]
    in_maps = []
    for core in range(2 * B):
        b, par = core // 2, core % 2
        m = dict(sh)
        m.update(pps[par])
        m["xT"] = np.ascontiguousarray(np.asarray(x[b], dtype=np.float32).T)
        in_maps.append(m)
    res = run_bass_kernel_spmd(nc, in_maps, core_ids=list(range(2 * B)))
    out = np.zeros((B, S, D), np.float32)
    for core in range(2 * B):
        b, par = core // 2, core % 2
        oT = np.asarray(res.results[core]["outT"])
        o = oT.T.reshape(S // 1024, 512, D)
        for i in range(S // 1024):
            t0 = (2 * i + par) * 512
            out[b, t0:t0 + 512] = o[i]
    return out, res


def kernel(**inputs):
    x = np.asarray(inputs["x"])
    out, _ = run_model(x.shape[1], inputs, x)
    return out
```

```python
import contextlib
import numpy as np
import concourse.bass as bass
import concourse.mybir as mybir
from concourse.bass_utils import run_bass_kernel_spmd

F32 = mybir.dt.float32
BF16 = mybir.dt.bfloat16
AF = mybir.ActivationFunctionType
ALU = mybir.AluOpType

ENGS = ("pe", "act", "dve", "pool", "sp")
NDSEM = 32
DSEM_Q = {"sp": 24, "pool": 8, "act": 4}
SAME_ENGINE_SYNC = True

D = 1024
DFF = 2816
NJ = 22
EPS = 1e-6
OFFS, OFFW, OFFC = 1023, 1535, 2063
JSAT_S = 2896 + OFFS
W_S = JSAT_S + 512
L_S = W_S + 127
W_W = 2432
L_W = W_W + 127
JSAT_C = 4959
W_C = JSAT_C + 512
L_C = W_C + 2032
MASKV = -30000.0


class Prog:
    def __init__(self, nc):
        self.nc = nc
        self.ops = {e: [] for e in ENGS}
        self.cnt = {e: 0 for e in ENGS}
        self.waited = {e: {} for e in ENGS}
        self.res_w = {}
        self.res_r = {}
        self.ndma = 0
        self.ndma_q = {}

    def _deps(self, reads, writes):
        deps = {}

        def add(d):
            if d is None:
                return
            k, v = d
            if deps.get(k, 0) < v:
                deps[k] = v
        for r in reads:
            add(self.res_w.get(r))
        for w in writes:
            add(self.res_w.get(w))
            for k, v in self.res_r.get(w, {}).items():
                add((k, v))
        return deps

    def _commit(self, reads, writes, me):
        for r in reads:
            d = self.res_r.setdefault(r, {})
            if d.get(me[0], 0) < me[1]:
                d[me[0]] = me[1]
        for w in writes:
            self.res_w[w] = me
            self.res_r[w] = {}

    def op(self, eng, fn, reads=(), writes=()):
        deps = self._deps(reads, writes)
        waits = []
        for k, v in deps.items():
            if k == eng and (eng == "pe" or not SAME_ENGINE_SYNC):
                continue
            if self.waited[eng].get(k, 0) >= v:
                continue
            self.waited[eng][k] = v
            waits.append((k, v))
        self.cnt[eng] += 1
        me = (eng, self.cnt[eng])
        self.ops[eng].append((waits, fn, me))
        self._commit(reads, writes, me)

    def dma(self, q, fn, reads=(), writes=()):
        deps = self._deps(reads, writes)
        k = self.ndma_q.get(q, 0)
        self.ndma_q[q] = k + 1
        self.ndma += 1
        nsl = DSEM_Q[q]
        slot = k % nsl
        val = 16 * (k // nsl + 1)
        key = ("d", q, slot)
        if val > 16:
            deps[key] = max(deps.get(key, 0), val - 16)
        waits = []
        for kk, v in deps.items():
            if self.waited[q].get(kk, 0) >= v:
                continue
            self.waited[q][kk] = v
            waits.append((kk, v))
        me = (key, val)
        self.ops[q].append((waits, fn, me))
        self._commit(reads, writes, me)

    def barrier(self):
        tgt = [(e, self.cnt[e]) for e in ENGS if self.cnt[e] > 0]
        for q, n in self.ndma_q.items():
            nsl = DSEM_Q[q]
            for k in range(max(0, n - nsl), n):
                tgt.append((("d", q, k % nsl), 16 * (k // nsl + 1)))
        for e in ENGS:
            waits = []
            for k, v in tgt:
                if k == e:
                    continue
                if self.waited[e].get(k, 0) >= v:
                    continue
                self.waited[e][k] = v
                waits.append((k, v))
            if waits:
                self.ops[e].append((waits, None, None))

    def run(self):
        nc = self.nc
        def _phase1():
            with contextlib.ExitStack() as st:
                sems = {}
                for e in ENGS:
                    sems[e] = st.enter_context(nc.semaphore("s_" + e))
                for q_, n_ in DSEM_Q.items():
                    for i in range(n_):
                        sems[("d", q_, i)] = st.enter_context(nc.semaphore("d_%s_%d" % (q_, i)))
                block = st.enter_context(nc.Block())

                def body(ename):
                    def f(eng):
                        for waits, fn, me in self.ops[ename]:
                            for k, v in waits:
                                eng.wait_ge(sems[k], v)
                            if fn is None:
                                continue
                            ins = fn(eng)
                            ins.then_inc(sems[me[0]], 16 if isinstance(me[0], tuple) else 1)
                    return f
                block.tensor(body("pe"))
                block.scalar(body("act"))
                block.vector(body("dve"))
                block.gpsimd(body("pool"))
                block.sync(body("sp"))


        _phase1()
def build(S, dbg=()):
    NT = S // 512
    NO = NT // 2
    SO = S // 2
    NKT = S // 128
    NSB = S // 64
    NCP = S // 16
    NCK = NCP // 128
    NCV = NCP - 1
    nc = bass.Bass("TRN2", target_bir_lowering=False)
    IN = {}

    def inp(name, shape):
        IN[name] = nc.dram_tensor(name, list(shape), F32, kind="ExternalInput")
        return IN[name]

    def scr(name, shape, dt):
        kind = "ExternalOutput" if name in dbg else "Internal"
        return nc.dram_tensor(name, list(shape), dt, kind=kind)

    xT = inp("xT", [D, S])
    gains = inp("gains", [128, 56])
    par = inp("par", [128, 2])
    wins = [inp("win%d" % i, [NJ, 128, 2048]) for i in range(4)]
    wouts = [inp("wout%d" % i, [8, 128, NJ * 128]) for i in range(4)]
    cin = inp("cin", [8, 128, 3072])
    cout = inp("cout", [8, 128, 1024])
    cwk = inp("cwk", [128, 24])
    wkf = inp("wkf", [12, 128, 1024])
    wvt = inp("wvt", [1, 128, 4096])
    wq = inp("wq", [16, 128, 512])
    wg = inp("wg", [1, 128, 384])
    wo = inp("wo", [16, 64, 1024])
    kn = inp("kn", [128, 3])
    qn = inp("qn", [128, 1])
    w1k = inp("w1k", [1, 128, 4096])
    w1v = inp("w1v", [1, 128, 4096])
    w2k = inp("w2k", [128, 64])
    w2v = inp("w2v", [128, 64])
    peT = inp("peT", [128, 64])
    bvs = inp("bvs", [16, L_S])
    bvw = inp("bvw", [16, L_W])
    bvc = inp("bvc", [16, L_C])
    expd = inp("expd", [1, 64, S])
    ovm = inp("ovm", [128, NCK * (NSB + 1)])
    selg = inp("selg", [1, 48, 48 * 128])
    selA = inp("selA", [NO * 4, 128, NSB])
    selB = inp("selB", [NO * 4, 128, NSB])
    ident_in = inp("ident", [128, 128])
    outT = nc.dram_tensor("outT", [D, SO], F32, kind="ExternalOutput")

    def bfc(name, src):
        return scr(name + "_bf", list(src.shape), BF16)
    wins_b = [bfc("win%d" % i, wins[i]) for i in range(4)]
    wouts_b = [bfc("wout%d" % i, wouts[i]) for i in range(4)]
    cin_b, cout_b, wkf_b, wvt_b = bfc("cin", cin), bfc("cout", cout), bfc("wkf", wkf), bfc("wvt", wvt)
    wq_b, wg_b, wo_b = bfc("wq", wq), bfc("wg", wg), bfc("wo", wo)
    w1k_b, w1v_b, expd_b, selg_b = bfc("w1k", w1k), bfc("w1v", w1v), bfc("expd", expd), bfc("selg", selg)

    h1T = scr("h1T", [D, S], F32)
    h2T = scr("h2T", [D, SO], F32)
    rawT = scr("rawT", [4, 128, S], BF16)
    ksT = scr("ksT", [4, 64, S], BF16)
    kwT = scr("kwT", [4, 64, S], BF16)
    vsw = scr("vsw", [NKT, 128, 512], BF16)
    kcT = scr("kcT", [4, 64, NCP], BF16)
    vcS = scr("vcS", [4, 128, NCK * 64], BF16)
    qTs = scr("qTs", [16, 64, SO], BF16)
    gsT = scr("gsT", [48, SO], BF16)
    ocT = scr("ocT", [16, 64, SO], BF16)
    rep_s = scr("rep_s", [16, 128, L_S], BF16)
    rep_w = scr("rep_w", [16, 128, L_W], BF16)
    rep_c = scr("rep_c", [16, 128, L_C], BF16)

    p = Prog(nc)
    HN = ["h.%d" % c for c in range(8)]
    sb = nc.sbuf_tensor
    uid = [0]

    def T(st, shape, dt, nm="t"):
        uid[0] += 1
        return st.enter_context(sb("%s_%d" % (nm, uid[0]), list(shape), dt))

    with contextlib.ExitStack() as top:
        PS = [top.enter_context(nc.psum_tensor("ps%d" % i, [128, 512], F32)) for i in range(8)]
        PSN = ["ps%d" % i for i in range(8)]
        gains_t = T(top, [128, 56], F32, "gains")
        par_t = T(top, [128, 2], F32, "par")
        cwk_t = T(top, [128, 24], F32, "cwk")
        kn_t = T(top, [128, 3], F32, "kn")
        qn_t = T(top, [128, 1], F32, "qn")
        eps_t = T(top, [128, 1], F32, "eps")
        ones_bf = T(top, [128, 128], BF16, "ones")
        ones_f = T(top, [128, 128], F32, "onesf")
        ident_t = T(top, [128, 128], F32, "ident")
        for dst, src, nm in ((gains_t, gains, "gains"), (par_t, par, "par"), (cwk_t, cwk, "cwk"),
                             (kn_t, kn, "kn"), (qn_t, qn, "qn"), (ident_t, ident_in, "ident")):
            p.dma("sp", lambda e, dst=dst, src=src: e.dma_start(out=dst[:], in_=src.ap()), writes=[nm])
        p.op("dve", lambda e: e.memset(eps_t[:], EPS), writes=["eps"])
        p.op("dve", lambda e: e.memset(ones_bf[:], 1.0), writes=["ones"])
        p.op("dve", lambda e: e.memset(ones_f[:], 1.0), writes=["onesf"])

        def precast(src, dst, nm):
            n0 = src.shape[0]
            for i in range(n0):
                p.dma("pool", lambda e, i=i: e.dma_start(out=dst.ap()[i], in_=src.ap()[i]), writes=["%s.%d" % (nm, i)])
        order = [(wins[0], wins_b[0], "win0"), (wouts[0], wouts_b[0], "wout0"), (cin, cin_b, "cin"), (cout, cout_b, "cout"),
                 (wins[1], wins_b[1], "win1"), (wouts[1], wouts_b[1], "wout1"), (wkf, wkf_b, "wkf"), (wvt, wvt_b, "wvt"),
                 (w1k, w1k_b, "w1k"), (w1v, w1v_b, "w1v"),
                 (wins[2], wins_b[2], "win2"), (wouts[2], wouts_b[2], "wout2"), (wq, wq_b, "wq"), (wg, wg_b, "wg"),
                 (expd, expd_b, "expd"), (selg, selg_b, "selg"), (wo, wo_b, "wo"),
                 (wins[3], wins_b[3], "win3"), (wouts[3], wouts_b[3], "wout3")]
        early, late = order[:10], order[10:]
        for s_, d_, n_ in early:
            precast(s_, d_, n_)
        late_items = []
        for s_, d_, n_ in late:
            for i_ in range(s_.shape[0]):
                late_items.append((s_, d_, n_, i_))

        def _phase2():
            with contextlib.ExitStack() as st:
                vrow = [T(st, [1, L_C], F32, "vrow") for _ in range(2)]
                ev = [T(st, [128, L_C], BF16, "ev") for _ in range(2)]
                k = 0
                for src, rep, L, nm in ((bvs, rep_s, L_S, "rep_s"), (bvw, rep_w, L_W, "rep_w"), (bvc, rep_c, L_C, "rep_c")):
                    for h in range(16):
                        b = k % 2
                        k += 1
                        p.dma("sp", lambda e, b=b, h=h, src=src, L=L: e.dma_start(out=vrow[b][0:1, 0:L], in_=src.ap()[h:h + 1, :]),
                              writes=["vrow%d" % b])
                        for c0 in range(0, L, 512):
                            c1 = min(L, c0 + 512)
                            pb = (c0 // 512) % 2
                            p.op("pe", lambda e, b=b, c0=c0, c1=c1, pb=pb: e.matmul(PS[pb][:, 0:c1 - c0], ones_f[0:1, :], vrow[b][0:1, c0:c1],
                                                                                  start=True, stop=True),
                                 reads=["vrow%d" % b, "onesf"], writes=[PSN[pb]])
                            p.op("act", lambda e, b=b, c0=c0, c1=c1, pb=pb: e.activation(out=ev[b][:, c0:c1], in_=PS[pb][:, 0:c1 - c0], func=AF.Exp),
                                 reads=[PSN[pb]], writes=["ev%d" % b])
                        p.dma("sp", lambda e, b=b, h=h, rep=rep, L=L: e.dma_start(out=rep.ap()[h], in_=ev[b][:, 0:L]),
                              reads=["ev%d" % b], writes=["%s.%d" % (nm, h)])
        _phase2()
        p.barrier()

        def rstd_from(ps_i, n, rs, rsn, rows=128):
            p.op("act", lambda e: e.activation(out=rs[0:rows, :], in_=PS[ps_i][0:rows, :], func=AF.Sqrt, bias=eps_t[0:rows, 0:1], scale=1.0 / n),
                 reads=[PSN[ps_i], "eps"], writes=[rsn])
            p.op("dve", lambda e: e.reciprocal(out=rs[0:rows, :], in_=rs[0:rows, :]), reads=[rsn], writes=[rsn])

        def rms_to_xn(h, hn, gcol, sq, xn, rs):
            for c in range(8):
                p.op("act", lambda e, c=c: e.activation(out=sq[:, c, :], in_=h[:, c, :], func=AF.Square), reads=["%s.%d" % (hn, c)], writes=["sq"])

            def mm(e):
                for c in range(8):
                    ins = e.matmul(PS[7][:, :], ones_bf[:, :], sq[:, c, :], start=(c == 0), stop=(c == 7))
                return ins
            p.op("pe", mm, reads=["sq", "ones"], writes=[PSN[7]])
            rstd_from(7, D, rs, "rs")
            for c in range(8):
                p.op("dve", lambda e, c=c: e.scalar_tensor_tensor(out=xn[:, c, :], in0=h[:, c, :], scalar=gains_t[:, gcol + c:gcol + c + 1],
                                                                 in1=rs[:, :], op0=ALU.mult, op1=ALU.mult),
                     reads=["%s.%d" % (hn, c), "rs", "gains"], writes=["xn"])

        def ffn(h, hn, li, gcol, bufs):
            sq, xn, rs, act, sg, wpi, wpo = bufs
            rms_to_xn(h, hn, gcol, sq, xn, rs)
            for j in range(NJ):
                b = j % 2
                wb = j % len(wpi)
                p.dma("sp", lambda e, j=j, wb=wb: e.dma_start(out=wpi[wb][:, :], in_=wins_b[li].ap()[j]),
                      reads=["win%d.%d" % (li, j)], writes=["wpi%d" % wb])
                for s_ in range(2):
                    def mm(e, s_=s_, b=b, wb=wb):
                        for c in range(8):
                            ins = e.matmul(PS[b * 2 + s_][:, :], wpi[wb][:, (s_ * 8 + c) * 128:(s_ * 8 + c + 1) * 128], xn[:, c, :],
                                           start=(c == 0), stop=(c == 7))
                        return ins
                    p.op("pe", mm, reads=["xn", "wpi%d" % wb], writes=[PSN[b * 2 + s_]])
                p.op("act", lambda e, b=b: e.activation(out=sg[b][:, :], in_=PS[b * 2][:, :], func=AF.Silu), reads=[PSN[b * 2]], writes=["sg%d" % b])
                p.op("dve", lambda e, b=b, j=j: e.tensor_tensor(out=act[:, j, :], in0=sg[b][:, :], in1=PS[b * 2 + 1][:, :], op=ALU.mult),
                     reads=["sg%d" % b, PSN[b * 2 + 1]], writes=["act.%d" % j])
            for d in range(8):
                b = d % 2
                wb = d % len(wpo)
                p.dma("sp", lambda e, d=d, wb=wb: e.dma_start(out=wpo[wb][:, :], in_=wouts_b[li].ap()[d]),
                      reads=["wout%d.%d" % (li, d)], writes=["wpo%d" % wb])

                def mm(e, b=b, wb=wb):
                    for j in range(NJ):
                        ins = e.matmul(PS[4 + b][:, :], wpo[wb][:, j * 128:(j + 1) * 128], act[:, j, :], start=(j == 0), stop=(j == NJ - 1))
                    return ins
                p.op("pe", mm, reads=["wpo%d" % wb] + ["act.%d" % j for j in range(NJ)], writes=[PSN[4 + b]])
                p.op("dve", lambda e, d=d, b=b: e.scalar_tensor_tensor(out=h[:, d, :], in0=PS[4 + b][:, :], scalar=0.5, in1=h[:, d, :],
                                                                      op0=ALU.mult, op1=ALU.add),
                     reads=[PSN[4 + b], "%s.%d" % (hn, d)], writes=["%s.%d" % (hn, d)])

        def _phase3():
            with contextlib.ExitStack() as st:
                h = T(st, [128, 8, 512], F32, "h")
                sq = T(st, [128, 8, 512], BF16, "sq")
                xn = T(st, [128, 8, 512], BF16, "xn")
                rs = T(st, [128, 512], F32, "rs")
                act = T(st, [128, NJ, 512], BF16, "act")
                sg = [T(st, [128, 512], F32, "sg") for _ in range(2)]
                wpi = [T(st, [128, 2048], BF16, "wpi") for _ in range(4)]
                wpo = [T(st, [128, NJ * 128], BF16, "wpo") for _ in range(3)]
                bufs = (sq, xn, rs, act, sg, wpi, wpo)
                wci = [T(st, [128, 3072], BF16, "wci") for _ in range(2)]
                wsq = [T(st, [128, 1024], BF16, "wsq") for _ in range(2)]
                u = [T(st, [128, 514], F32, "u") for _ in range(2)]
                ucar = T(st, [128, 8, 2], F32, "ucar")
                yb = [T(st, [128, 512], F32, "yb") for _ in range(2)]
                bgy = T(st, [128, 8, 512], BF16, "bgy")
                wvt_t = T(st, [128, 4096], BF16, "wvt")
                wkf_t = T(st, [128, 12, 1024], BF16, "wkf_t")
                bd64 = T(st, [128, 128], BF16, "bd64")
                ksb = [T(st, [128, 512], BF16, "ksb") for _ in range(4)]
                sgk = [T(st, [128, 512], F32, "sgk") for _ in range(4)]
                vtb = [T(st, [128, 512], BF16, "vtb") for _ in range(2)]
                p.op("dve", lambda e: e.memset(ucar[:], 0.0), writes=["ucar"])
                p.op("dve", lambda e: e.memset(bd64[:], 0.0), writes=["bd64"])
                p.op("dve", lambda e: e.memset(bd64[0:64, 0:64], 1.0), writes=["bd64"])
                p.op("dve", lambda e: e.memset(bd64[64:128, 64:128], 1.0), writes=["bd64"])
                p.dma("sp", lambda e: e.dma_start(out=wvt_t[:, :], in_=wvt_b.ap()[0]), reads=["wvt.0"], writes=["wvtt"])
                p.dma("sp", lambda e: e.dma_start(out=wkf_t[:, :, :], in_=wkf_b.ap().rearrange("u p n -> p u n")), reads=["wkf.%d" % u_ for u_ in range(12)], writes=["wkf_t"])
                xTv = xT.ap().rearrange("(c p) s -> p c s", p=128)
                h1v = h1T.ap().rearrange("(c p) s -> p c s", p=128)
                for ti in range(NT):
                    t0 = ti * 512
                    p.dma("sp", lambda e, t0=t0: e.dma_start(out=h[:, :, :], in_=xTv[:, :, t0:t0 + 512]), writes=HN)
                    if ti >= 1 and late_items:
                        nper = (len(late_items) + max(1, NT - 2) - 1) // max(1, NT - 2) if ti == 1 else nper_keep[0]
                        nper_keep[0] = nper
                        for _ in range(min(nper, len(late_items))):
                            s_, d_, n_, i_ = late_items.pop(0)
                            p.dma("pool", lambda e, s_=s_, d_=d_, i_=i_: e.dma_start(out=d_.ap()[i_], in_=s_.ap()[i_]), writes=["%s.%d" % (n_, i_)])
                    ffn(h, "h", 0, 0, bufs)
                    rms_to_xn(h, "h", 8, sq, xn, rs)
                    for i in range(8):
                        b = i % 2
                        p.dma("sp", lambda e, i=i, b=b: e.dma_start(out=wci[b][:, :], in_=cin_b.ap()[i]), reads=["cin.%d" % i], writes=["wci%d" % b])
                        cb = (0, 1, 2) if i % 2 == 0 else (3, 6, 7)
                        for s_ in range(3):
                            def mm(e, s_=s_, b=b, cb=cb):
                                for c in range(8):
                                    ins = e.matmul(PS[cb[s_]][:, :], wci[b][:, (s_ * 8 + c) * 128:(s_ * 8 + c + 1) * 128], xn[:, c, :],
                                                   start=(c == 0), stop=(c == 7))
                                return ins
                            p.op("pe", mm, reads=["xn", "wci%d" % b], writes=[PSN[cb[s_]]])
                        p.op("dve", lambda e, i=i, b=b: e.tensor_copy(out=u[b][:, 0:2], in_=ucar[:, i, :]), reads=["ucar"], writes=["u%d" % b])
                        p.op("act", lambda e, b=b, cb=cb: e.activation(out=yb[b][:, :], in_=PS[cb[1]][:, :], func=AF.Copy), reads=[PSN[cb[1]]], writes=["yb%d" % b])
                        p.op("dve", lambda e, b=b, cb=cb: e.tensor_tensor(out=u[b][:, 2:514], in0=yb[b][:, :], in1=PS[cb[2]][:, :], op=ALU.mult),
                             reads=["yb%d" % b, PSN[cb[2]]], writes=["u%d" % b])
                        p.op("dve", lambda e, i=i, b=b: e.tensor_copy(out=ucar[:, i, :], in_=u[b][:, 512:514]), reads=["u%d" % b], writes=["ucar"])
                        p.op("dve", lambda e, i=i, b=b: e.tensor_scalar(out=yb[b][:, :], in0=u[b][:, 2:514], scalar1=cwk_t[:, i * 3 + 2:i * 3 + 3], scalar2=None,
                                                                      op0=ALU.mult), reads=["u%d" % b, "cwk"], writes=["yb%d" % b])
                        for w_ in (1, 0):
                            p.op("dve", lambda e, i=i, b=b, w_=w_: e.scalar_tensor_tensor(out=yb[b][:, :], in0=u[b][:, w_:w_ + 512],
                                                                                         scalar=cwk_t[:, i * 3 + w_:i * 3 + w_ + 1], in1=yb[b][:, :],
                                                                                         op0=ALU.mult, op1=ALU.add),
                                 reads=["u%d" % b, "cwk", "yb%d" % b], writes=["yb%d" % b])
                        p.op("dve", lambda e, i=i, b=b, cb=cb: e.tensor_tensor(out=bgy[:, i, :], in0=yb[b][:, :], in1=PS[cb[0]][:, :], op=ALU.mult),
                             reads=["yb%d" % b, PSN[cb[0]]], writes=["bgy.%d" % i])
                    for d in range(8):
                        b = d % 2
                        p.dma("sp", lambda e, d=d, b=b: e.dma_start(out=wsq[b][:, :], in_=cout_b.ap()[d]), reads=["cout.%d" % d], writes=["wsq%d" % b])

                        def mm(e, b=b):
                            for c in range(8):
                                ins = e.matmul(PS[4 + b][:, :], wsq[b][:, c * 128:(c + 1) * 128], bgy[:, c, :], start=(c == 0), stop=(c == 7))
                            return ins
                        p.op("pe", mm, reads=["wsq%d" % b] + ["bgy.%d" % c for c in range(8)], writes=[PSN[4 + b]])
                        p.op("dve", lambda e, d=d, b=b: e.tensor_tensor(out=h[:, d, :], in0=PS[4 + b][:, :], in1=h[:, d, :], op=ALU.add),
                             reads=[PSN[4 + b], "h.%d" % d], writes=["h.%d" % d])
                    ffn(h, "h", 1, 16, bufs)
                    p.dma("pool", lambda e, t0=t0: e.dma_start(out=h1v[:, :, t0:t0 + 512], in_=h[:, :, :]), reads=HN, writes=["h1T"])
                    rms_to_xn(h, "h", 24, sq, xn, rs)
                    for un in range(12):
                        b = un % 2
                        k4 = un % 4
                        def mm(e, un=un, k4=k4):
                            for c in range(8):
                                ins = e.matmul(PS[k4][:, :], wkf_t[:, un, c * 128:(c + 1) * 128], xn[:, c, :], start=(c == 0), stop=(c == 7))
                            return ins
                        p.op("pe", mm, reads=["xn", "wkf_t"], writes=[PSN[k4]])
                        if un < 4:
                            p.op("act", lambda e, k4=k4: e.activation(out=ksb[k4][:, :], in_=PS[k4][:, :], func=AF.Copy), reads=[PSN[k4]], writes=["ksb%d" % k4])
                            p.dma("pool", lambda e, un=un, k4=k4, t0=t0: e.dma_start(out=rawT.ap()[un][:, t0:t0 + 512], in_=ksb[k4][:, :]),
                                  reads=["ksb%d" % k4], writes=["rawT"])
                        else:
                            br = 1 if un < 8 else 2
                            g = (un - 4) % 4
                            dst = ksT if br == 1 else kwT
                            p.op("act", lambda e, k4=k4: e.activation(out=sgk[k4][:, :], in_=PS[k4][:, :], func=AF.Square), reads=[PSN[k4]], writes=["sgk%d" % k4])
                            p.op("dve", lambda e, k4=k4: e.tensor_copy(out=ksb[k4][:, :], in_=sgk[k4][:, :]), reads=["sgk%d" % k4], writes=["ksb%d" % k4])
                            p.op("pe", lambda e, b=b, k4=k4: e.matmul(PS[4 + b][:, :], bd64[:, :], ksb[k4][:, :], start=True, stop=True),
                                 reads=["ksb%d" % k4, "bd64"], writes=[PSN[4 + b]])
                            rstd_from(4 + b, 64, sgk[k4], "sgk%d" % k4)
                            p.op("dve", lambda e, k4=k4, br=br: e.scalar_tensor_tensor(out=ksb[k4][:, :], in0=PS[k4][:, :], scalar=kn_t[:, br:br + 1],
                                                                                      in1=sgk[k4][:, :], op0=ALU.mult, op1=ALU.mult),
                                 reads=[PSN[k4], "sgk%d" % k4, "kn"], writes=["ksb%d" % k4])
                            p.dma("pool", lambda e, g=g, k4=k4, t0=t0, dst=dst: e.dma_start(out=dst.ap()[g][:, t0:t0 + 512], in_=ksb[k4][0:64, :]),
                                  reads=["ksb%d" % k4], writes=["ksT" if br == 1 else "kwT"])
                    for tb in range(4):
                        b = tb % 2

                        def mm(e, tb=tb, b=b):
                            for c in range(8):
                                ins = e.matmul(PS[6 + b][:, :], xn[:, c, tb * 128:(tb + 1) * 128], wvt_t[:, c * 512:(c + 1) * 512],
                                               start=(c == 0), stop=(c == 7))
                            return ins
                        p.op("pe", mm, reads=["xn", "wvtt"], writes=[PSN[6 + b]])
                        p.op("act", lambda e, b=b: e.activation(out=vtb[b][:, :], in_=PS[6 + b][:, :], func=AF.Copy), reads=[PSN[6 + b]], writes=["vtb%d" % b])
                        p.dma("pool", lambda e, tb=tb, b=b, ti=ti: e.dma_start(out=vsw.ap()[ti * 4 + tb], in_=vtb[b][:, :]), reads=["vtb%d" % b], writes=["vsw"])
        nper_keep = [1]
        _phase3()
        while late_items:
            s_, d_, n_, i_ = late_items.pop(0)
            p.dma("pool", lambda e, s_=s_, d_=d_, i_=i_: e.dma_start(out=d_.ap()[i_], in_=s_.ap()[i_]), writes=["%s.%d" % (n_, i_)])
        p.barrier()

        def _phase4():
            with contextlib.ExitStack() as st:
                raw = T(st, [128, S], BF16, "raw")
                w1t = T(st, [128, 4096], BF16, "w1t")
                w2kt = T(st, [128, 64], BF16, "w2kt")
                w2vt = T(st, [128, 64], BF16, "w2vt")
                w2f = T(st, [128, 128], F32, "w2f")
                peTt = T(st, [128, 64], BF16, "peTt")
                peTf = T(st, [128, 64], F32, "peTf")
                c1 = T(st, [128, 2], F32, "c1")
                xx = T(st, [128, 512], F32, "xx")
                tt = T(st, [128, 512], F32, "tt")
                hid = T(st, [128, NCP], BF16, "hid")
                kc_f = T(st, [64, 512], F32, "kc_f")
                kc_b = T(st, [64, 512], BF16, "kc_b")
                vc_b = T(st, [128, 64], BF16, "vc_b")
                bdh = T(st, [64, 64], BF16, "bdh")
                p.dma("sp", lambda e: e.dma_start(out=w2f[:, 0:64], in_=w2k.ap()), writes=["w2f"])
                p.dma("sp", lambda e: e.dma_start(out=w2f[:, 64:128], in_=w2v.ap()), writes=["w2f"])
                p.dma("sp", lambda e: e.dma_start(out=peTf[:, :], in_=peT.ap()), writes=["peTf"])
                p.op("dve", lambda e: e.tensor_copy(out=w2kt[:, :], in_=w2f[:, 0:64]), reads=["w2f"], writes=["w2kt"])
                p.op("dve", lambda e: e.tensor_copy(out=w2vt[:, :], in_=w2f[:, 64:128]), reads=["w2f"], writes=["w2vt"])
                p.op("dve", lambda e: e.tensor_copy(out=peTt[:, :], in_=peTf[:, :]), reads=["peTf"], writes=["peTt"])
                p.op("dve", lambda e: e.memset(bdh[:, :], 1.0), writes=["bdh"])
                p.op("dve", lambda e: e.memset(hid[:, :], 0.0), writes=["hid"])
                for kv in range(2):
                    p.dma("sp", lambda e, kv=kv: e.dma_start(out=w1t[:, :], in_=(w1k_b if kv == 0 else w1v_b).ap()[0]),
                          reads=["w1k.0", "w1v.0"], writes=["w1t"])
                    def mmc(e, kv=kv):
                        for l in range(32):
                            ins = e.matmul(PS[6][:, 0:1], w1t[0:64, l * 128:(l + 1) * 128], peTt[0:64, kv * 32 + l:kv * 32 + l + 1], start=(l == 0), stop=(l == 31))
                        return ins
                    p.op("pe", mmc, reads=["w1t", "peTt"], writes=[PSN[6]])
                    p.op("act", lambda e, kv=kv: e.activation(out=c1[:, kv:kv + 1], in_=PS[6][:, 0:1], func=AF.Copy), reads=[PSN[6]], writes=["c1"])
                    for ch in range(2):
                        p.dma("sp", lambda e, kv=kv, ch=ch: e.dma_start(out=raw[:, :], in_=rawT.ap()[kv * 2 + ch]), reads=["rawT"], writes=["raw"])
                        for gg in range(2):
                            g = ch * 2 + gg
                            r0 = gg * 64
                            for n0 in range(0, NCV, 512):
                                n1 = min(NCV, n0 + 512)
                                nn = n1 - n0

                                def mm(e, r0=r0, n0=n0, nn=nn):
                                    for l in range(32):
                                        a0 = l + 16 * n0
                                        ins = e.matmul(PS[0][:, 0:nn], w1t[r0:r0 + 64, l * 128:(l + 1) * 128], raw[r0:r0 + 64, a0:a0 + 16 * (nn - 1) + 1:16],
                                                       start=(l == 0), stop=(l == 31))
                                    return ins
                                p.op("pe", mm, reads=["raw", "w1t"], writes=[PSN[0]])
                                p.op("act", lambda e, nn=nn, kv=kv: e.activation(out=xx[:, 0:nn], in_=PS[0][:, 0:nn], func=AF.Identity, bias=c1[:, kv:kv + 1], scale=1.0),
                                     reads=[PSN[0], "c1"], writes=["xx"])
                                p.op("dve", lambda e, nn=nn: e.tensor_tensor(out=tt[:, 0:nn], in0=xx[:, 0:nn], in1=xx[:, 0:nn], op=ALU.mult), reads=["xx"], writes=["tt"])
                                p.op("dve", lambda e, nn=nn: e.tensor_scalar(out=tt[:, 0:nn], in0=tt[:, 0:nn], scalar1=0.044715, scalar2=1.0, op0=ALU.mult, op1=ALU.add),
                                     reads=["tt"], writes=["tt"])
                                p.op("dve", lambda e, nn=nn: e.tensor_tensor(out=tt[:, 0:nn], in0=tt[:, 0:nn], in1=xx[:, 0:nn], op=ALU.mult), reads=["tt", "xx"], writes=["tt"])
                                p.op("act", lambda e, nn=nn: e.activation(out=tt[:, 0:nn], in_=tt[:, 0:nn], func=AF.Sigmoid, scale=1.5957691216057308),
                                     reads=["tt"], writes=["tt"])
                                p.op("dve", lambda e, nn=nn, n0=n0: e.tensor_tensor(out=hid[:, n0:n0 + nn], in0=tt[:, 0:nn], in1=xx[:, 0:nn], op=ALU.mult),
                                     reads=["tt", "xx"], writes=["hid"])
                            if kv == 0:
                                CW = min(512, NCP)
                                for n0 in range(0, NCP, CW):
                                    p.op("pe", lambda e, n0=n0: e.matmul(PS[1][0:64, 0:CW], w2kt[:, :], hid[:, n0:n0 + CW], start=True, stop=True),
                                         reads=["hid", "w2kt"], writes=[PSN[1]])
                                    p.op("act", lambda e: e.activation(out=kc_f[:, 0:CW], in_=PS[1][0:64, 0:CW], func=AF.Square), reads=[PSN[1]], writes=["kc_f"])
                                    p.op("dve", lambda e: e.tensor_copy(out=kc_b[:, 0:CW], in_=kc_f[:, 0:CW]), reads=["kc_f"], writes=["kc_b"])
                                    p.op("pe", lambda e: e.matmul(PS[2][0:64, 0:CW], bdh[:, :], kc_b[:, 0:CW], start=True, stop=True), reads=["kc_b", "bdh"], writes=[PSN[2]])
                                    p.op("act", lambda e: e.activation(out=kc_f[:, 0:CW], in_=PS[2][0:64, 0:CW], func=AF.Sqrt, bias=eps_t[0:64, 0:1], scale=1.0 / 64),
                                         reads=[PSN[2], "eps"], writes=["kc_f"])
                                    p.op("dve", lambda e: e.reciprocal(out=kc_f[:, 0:CW], in_=kc_f[:, 0:CW]), reads=["kc_f"], writes=["kc_f"])
                                    p.op("dve", lambda e: e.scalar_tensor_tensor(out=kc_b[:, 0:CW], in0=PS[1][0:64, 0:CW], scalar=kn_t[0:64, 0:1], in1=kc_f[:, 0:CW],
                                                                                op0=ALU.mult, op1=ALU.mult), reads=[PSN[1], "kc_f", "kn"], writes=["kc_b"])
                                    p.dma("sp", lambda e, g=g, n0=n0: e.dma_start(out=kcT.ap()[g][:, n0:n0 + CW], in_=kc_b[:, 0:CW]), reads=["kc_b"], writes=["kcT"])
                            else:
                                for nb in range(NCK):
                                    p.op("pe", lambda e, nb=nb: e.matmul(PS[1][:, 0:64], hid[:, nb * 128:(nb + 1) * 128], w2vt[:, :], start=True, stop=True),
                                         reads=["hid", "w2vt"], writes=[PSN[1]])
                                    p.op("act", lambda e: e.activation(out=vc_b[:, :], in_=PS[1][:, 0:64], func=AF.Copy), reads=[PSN[1]], writes=["vc_b"])
                                    p.dma("sp", lambda e, g=g, nb=nb: e.dma_start(out=vcS.ap()[g][:, nb * 64:(nb + 1) * 64], in_=vc_b[:, :]), reads=["vc_b"], writes=["vcS"])
        _phase4()
        p.barrier()

        def _phase5():
            with contextlib.ExitStack() as st:
                h = T(st, [128, 8, 512], F32, "h")
                hb = T(st, [128, 8, 512], F32, "hb")
                sq = T(st, [128, 8, 512], BF16, "sq")
                xn = T(st, [128, 8, 512], BF16, "xn")
                rs = T(st, [128, 512], F32, "rs")
                act = T(st, [128, NJ, 512], BF16, "act")
                sg = [T(st, [128, 512], F32, "sg") for _ in range(2)]
                wpi = [T(st, [128, 2048], BF16, "wpi") for _ in range(4)]
                wpo = [T(st, [128, NJ * 128], BF16, "wpo") for _ in range(3)]
                bufs = (sq, xn, rs, act, sg, wpi, wpo)
                wqt = [T(st, [128, 512], BF16, "wqt") for _ in range(2)]
                wgt = T(st, [128, 384], BF16, "wgt")
                qb_ = [T(st, [64, 512], BF16, "qb") for _ in range(2)]
                bdh = T(st, [64, 64], BF16, "bdh")
                gsb = T(st, [48, 512], BF16, "gsb")
                p.op("dve", lambda e: e.memset(bdh[:, :], 1.0), writes=["bdh"])
                p.dma("sp", lambda e: e.dma_start(out=wgt[:, :], in_=wg_b.ap()[0]), reads=["wg.0"], writes=["wgt"])
                h1v = h1T.ap().rearrange("(c p) s -> p c s", p=128)
                h2v = h2T.ap().rearrange("(c p) s -> p c s", p=128)
                for i in range(NO):
                    ta, tb_ = (2 * i) * 512, (2 * i + 1) * 512
                    p.dma("sp", lambda e, ta=ta: e.dma_start(out=h[:, :, :], in_=h1v[:, :, ta:ta + 512]), reads=["h1T"], writes=HN)
                    p.dma("sp", lambda e, tb_=tb_: e.dma_start(out=hb[:, :, :], in_=h1v[:, :, tb_:tb_ + 512]), reads=["h1T"], writes=["hb"])
                    p.op("dve", lambda e: e.tensor_scalar(out=h[:, :, :], in0=h[:, :, :], scalar1=par_t[:, 1:2], scalar2=None, op0=ALU.mult),
                         reads=HN + ["par"], writes=HN)
                    p.op("dve", lambda e: e.scalar_tensor_tensor(out=h[:, :, :], in0=hb[:, :, :], scalar=par_t[:, 0:1], in1=h[:, :, :], op0=ALU.mult, op1=ALU.add),
                         reads=HN + ["hb", "par"], writes=HN)
                    ffn(h, "h", 2, 32, bufs)
                    p.dma("pool", lambda e, i=i: e.dma_start(out=h2v[:, :, i * 512:(i + 1) * 512], in_=h[:, :, :]), reads=HN, writes=["h2T"])
                    rms_to_xn(h, "h", 40, sq, xn, rs)
                    if "dbg_xn" in dbg and i == 0:
                        dx = nc.dram_tensor("dbg_xn", [128, 4096], BF16, kind="ExternalOutput")
                        p.dma("sp", lambda e: e.dma_start(out=dx.ap(), in_=xn[:, :, :].rearrange("p c s -> p (c s)")), reads=["xn"], writes=["dbg_xn"])
                        dr = nc.dram_tensor("dbg_rs", [128, 512], F32, kind="ExternalOutput")
                        p.dma("sp", lambda e: e.dma_start(out=dr.ap(), in_=rs[:, :]), reads=["rs"], writes=["dbg_rs"])
                    for hd in range(16):
                        b = hd % 2
                        p.dma("sp", lambda e, hd=hd, b=b: e.dma_start(out=wqt[b][:, :], in_=wq_b.ap()[hd]), reads=["wq.%d" % hd], writes=["wqt%d" % b])

                        def mm(e, b=b):
                            for c in range(8):
                                ins = e.matmul(PS[b][0:64, :], wqt[b][:, c * 64:(c + 1) * 64], xn[:, c, :], start=(c == 0), stop=(c == 7))
                            return ins
                        p.op("pe", mm, reads=["xn", "wqt%d" % b], writes=[PSN[b]])
                        p.op("act", lambda e, b=b: e.activation(out=sg[b][0:64, :], in_=PS[b][0:64, :], func=AF.Square), reads=[PSN[b]], writes=["sg%d" % b])
                        p.op("dve", lambda e, b=b: e.tensor_copy(out=qb_[b][:, :], in_=sg[b][0:64, :]), reads=["sg%d" % b], writes=["qb%d" % b])
                        p.op("pe", lambda e, b=b: e.matmul(PS[2 + b][0:64, :], bdh[:, :], qb_[b][:, :], start=True, stop=True), reads=["qb%d" % b, "bdh"], writes=[PSN[2 + b]])
                        rstd_from(2 + b, 64, sg[b], "sg%d" % b, rows=64)
                        p.op("dve", lambda e, b=b: e.scalar_tensor_tensor(out=qb_[b][:, :], in0=PS[b][0:64, :], scalar=qn_t[0:64, 0:1], in1=sg[b][0:64, :],
                                                                         op0=ALU.mult, op1=ALU.mult), reads=[PSN[b], "sg%d" % b, "qn"], writes=["qb%d" % b])
                        p.dma("pool", lambda e, hd=hd, b=b, i=i: e.dma_start(out=qTs.ap()[hd][:, i * 512:(i + 1) * 512], in_=qb_[b][:, :]),
                              reads=["qb%d" % b], writes=["qTs"])

                    def mmg(e):
                        for c in range(8):
                            ins = e.matmul(PS[4][0:48, :], wgt[:, c * 48:(c + 1) * 48], xn[:, c, :], start=(c == 0), stop=(c == 7))
                        return ins
                    p.op("pe", mmg, reads=["xn", "wgt"], writes=[PSN[4]])
                    p.op("act", lambda e: e.activation(out=gsb[:, :], in_=PS[4][0:48, :], func=AF.Sigmoid), reads=[PSN[4]], writes=["gsb"])
                    p.dma("pool", lambda e, i=i: e.dma_start(out=gsT.ap()[:, i * 512:(i + 1) * 512], in_=gsb[:, :]), reads=["gsb"], writes=["gsT"])
        _phase5()
        p.barrier()

        def _phase6():
            with contextlib.ExitStack() as st:
                ks_t = T(st, [128, S], BF16, "ks_t")
                kw_t = T(st, [64, S], BF16, "kw_t")
                kc_t = T(st, [64, NCP], BF16, "kc_t")
                vsw_t = T(st, [128, NKT, 256], BF16, "vsw_t")
                vc_t = T(st, [128, NCK, 128], BF16, "vc_t")
                ov_t = T(st, [128, NCK * (NSB + 1)], BF16, "ov_t")
                ov_f = T(st, [128, NCK * (NSB + 1)], F32, "ov_f")
                selg_t = T(st, [48, 48 * 128], BF16, "selg_t")
                negT = T(st, [128, SO], BF16, "negT")
                negS = T(st, [128, SO], BF16, "negS")
                sc2w = T(st, [128, 256], F32, "sc2w")
                impacc = T(st, [128, 4, NSB], F32, "impacc")
                qa = [[T(st, [128, 512], BF16, "qa") for _ in range(2)] for _ in range(2)]
                gs_t = T(st, [48, 512], BF16, "gs_t")
                ms_t = T(st, [128, W_S], BF16, "ms_t")
                mw_t = T(st, [128, W_W], BF16, "mw_t")
                mc_t = [T(st, [128, 512], BF16, "mc_t") for _ in range(8)]
                e_t = [T(st, [128, 512], BF16, "e_t") for _ in range(8)]
                p_t = [T(st, [128, 512], BF16, "p_t") for _ in range(8)]
                rl = T(st, [128, 4], F32, "rl")
                sA = T(st, [128, NSB], F32, "sA")
                sB = T(st, [128, NSB], F32, "sB")
                sc = T(st, [128, NSB], F32, "sc")
                sc2 = T(st, [128, NSB], F32, "sc2")
                m8a = T(st, [128, 8], F32, "m8a")
                m8b = T(st, [128, 8], F32, "m8b")
                gbb = [T(st, [128, 3, 512], F32, "gb") for _ in range(2)]
                rr2b = [T(st, [128, 512], F32, "rr2") for _ in range(3)]
                rscr2 = T(st, [128, 512], F32, "rscr2")
                rr = T(st, [64, 512], F32, "rr")
                acc = T(st, [64, 512], F32, "acc")
                ocb = T(st, [64, 512], BF16, "ocb")
                ones64 = ones_bf[:, 0:64]
                p.dma("sp", lambda e: e.dma_start(out=ks_t[64:128, :], in_=expd_b.ap()[0]), reads=["expd.0"], writes=["kpat"])
                p.dma("sp", lambda e: e.dma_start(out=selg_t[:, :], in_=selg_b.ap()[0]), reads=["selg.0"], writes=["selg_t"])
                p.dma("sp", lambda e: e.dma_start(out=ov_f[:, :], in_=ovm.ap()), writes=["ov_f"])
                p.op("dve", lambda e: e.tensor_copy(out=ov_t[:, :], in_=ov_f[:, :]), reads=["ov_f"], writes=["ov_t"])
                cnt = [0]
                p.op("dve", lambda e: e.memset(vsw_t[:, :, :], 1.0), writes=["kgrp"])
                p.op("dve", lambda e: e.memset(vc_t[:, :, :], 1.0), writes=["kgrp"])
                for k_ in range(3):
                    p.op("dve", lambda e, k_=k_: e.memset(rr2b[k_][:, :], 0.0), writes=["rr2.%d" % k_])

                def c_chunks(i):
                    return [nk for nk in range(NCK) if (2 * i + 2) * 512 - 1 >= 2048 * nk + 31]

                mcc = [0]

                def load_mc(hd, i, nk):
                    b = mcc[0] % 8
                    mcc[0] += 1
                    D_ = 1024 * i - 2048 * nk
                    j0 = min(D_, JSAT_C)
                    src = bass.AP(tensor=rep_c, offset=hd * 128 * L_C + 2032 + j0, ap=[[L_C - 16, 128], [1, 512]])
                    p.dma("sp", lambda e, b=b, src=src: e.dma_start(out=mc_t[b][:, :], in_=src), reads=["rep_c.%d" % hd], writes=["mc_t%d" % b])
                    return b

                def score_unit(hd, kmat, kcol0, qbuf, qn_, mask_ap, mask_res, neg_cols=None):
                    b = cnt[0] % 8
                    pb = cnt[0] % 4
                    u3 = cnt[0] % 3
                    cnt[0] += 1

                    p.op("pe", lambda e: e.matmul(PS[pb][:, :], kmat[:, kcol0:kcol0 + 128], qbuf, start=True, stop=True),
                         reads=["kgrp", "kpat", qn_], writes=[PSN[pb]])
                    p.op("act", lambda e: e.activation(out=e_t[b][:, :], in_=PS[pb][:, :], func=AF.Exp, scale=0.125), reads=[PSN[pb]], writes=["e_t%d" % b])
                    eng = "pool" if cnt[0] % 2 == 0 else "dve"
                    p.op(eng, lambda e: e.tensor_tensor(out=p_t[b][:, :], in0=e_t[b][:, :], in1=mask_ap, op=ALU.mult),
                         reads=["e_t%d" % b] + mask_res, writes=["p_t%d" % b])
                    return b

                for g in range(4):
                    p.dma("sp", lambda e, g=g: e.dma_start(out=ks_t[0:64, :], in_=ksT.ap()[g]), reads=["ksT"], writes=["kgrp"])
                    p.dma("sp", lambda e, g=g: e.dma_start(out=kw_t[:, :], in_=kwT.ap()[g]), reads=["kwT"], writes=["kgrp"])
                    p.dma("sp", lambda e, g=g: e.dma_start(out=kc_t[:, :], in_=kcT.ap()[g]), reads=["kcT"], writes=["kgrp"])
                    p.dma("sp", lambda e, g=g: e.dma_start(out=vc_t[:, :, 0:64], in_=vcS.ap()[g].rearrange("p (k c) -> p k c", c=64)), reads=["vcS"], writes=["kgrp"])
                    vsrc = vsw.ap().rearrange("k p c -> p k c")
                    for k0 in range(0, NKT, 8):
                        p.dma("sp", lambda e, g=g, k0=k0: e.dma_start(out=vsw_t[:, k0:k0 + 8, 0:64], in_=vsrc[:, k0:k0 + 8, g * 64:(g + 1) * 64]),
                              reads=["vsw"], writes=["kgrp"])
                        p.dma("sp", lambda e, g=g, k0=k0: e.dma_start(out=vsw_t[:, k0:k0 + 8, 128:192], in_=vsrc[:, k0:k0 + 8, 256 + g * 64:256 + (g + 1) * 64]),
                              reads=["vsw"], writes=["kgrp"])
                    for i in range(NO):
                        chunks = c_chunks(i)
                        for hp in range(2):
                            pbs_h = {}
                            for hh in (2 * hp, 2 * hp + 1):
                                hd = g * 4 + hh
                                qb = hh % 2
                                p.dma("sp", lambda e, hd=hd, i=i, qb=qb: e.dma_start(out=qa[0][qb][0:64, :], in_=qTs.ap()[hd][:, i * 512:(i + 1) * 512]),
                                      reads=["qTs"], writes=["qa%d" % qb])
                                pbs = []
                                for nk in chunks:
                                    mb = load_mc(hd, i, nk)
                                    pbs.append(score_unit(hd, kc_t, nk * 128, qa[0][qb][0:64, :], "qa%d" % qb, mc_t[mb][:, :], ["mc_t%d" % mb]))
                                pbs_h[hh] = pbs
                            for hh in (2 * hp, 2 * hp + 1):
                                hd = g * 4 + hh
                                pbs = pbs_h[hh]
                                for qq in range(4):
                                    bank = 4 + qq // 2
                                    col = (qq % 2) * (NSB + 1)

                                    def mm(e, qq=qq, bank=bank, col=col, pbs=pbs, chunks=chunks):
                                        for ii, nk in enumerate(chunks):
                                            ins = e.matmul(PS[bank][:, col:col + NSB + 1], p_t[pbs[ii]][:, qq * 128:(qq + 1) * 128],
                                                           ov_t[:, nk * (NSB + 1):(nk + 1) * (NSB + 1)], start=(ii == 0), stop=(ii == len(chunks) - 1))
                                        return ins
                                    p.op("pe", mm, reads=["p_t%d" % b_ for b_ in pbs] + ["ov_t"], writes=[PSN[bank]])
                                    p.op("dve", lambda e, qq=qq, bank=bank, col=col: e.tensor_scalar(out=rl[:, qq:qq + 1], in0=PS[bank][:, col + NSB:col + NSB + 1],
                                                                                                    scalar1=1e-30, scalar2=None, op0=ALU.add),
                                         reads=[PSN[bank]], writes=["rl"])
                                    p.op("dve", lambda e, qq=qq: e.reciprocal(out=rl[:, qq:qq + 1], in_=rl[:, qq:qq + 1]), reads=["rl"], writes=["rl"])
                                    if hh == 0:
                                        p.op("dve", lambda e, qq=qq, bank=bank, col=col, i=i: e.tensor_scalar(out=impacc[:, qq, :], in0=PS[bank][:, col:col + NSB],
                                                                                                             scalar1=rl[:, qq:qq + 1], scalar2=None, op0=ALU.mult),
                                             reads=[PSN[bank], "rl"], writes=["impacc"])
                                    else:
                                        p.op("dve", lambda e, qq=qq, bank=bank, col=col, i=i: e.scalar_tensor_tensor(out=impacc[:, qq, :], in0=PS[bank][:, col:col + NSB],
                                                                                                                    scalar=rl[:, qq:qq + 1], in1=impacc[:, qq, :],
                                                                                                                    op0=ALU.mult, op1=ALU.add),
                                             reads=[PSN[bank], "rl", "impacc"], writes=["impacc"])

                        for qq in range(4):
                            qi = i * 4 + qq
                            p.dma("sp", lambda e, qi=qi: e.dma_start(out=sA[:, :], in_=selA.ap()[qi]), writes=["sA"])
                            p.dma("sp", lambda e, qi=qi: e.dma_start(out=sB[:, :], in_=selB.ap()[qi]), writes=["sB"])
                            p.op("dve", lambda e, qq=qq: e.tensor_tensor(out=sc[:, :], in0=impacc[:, qq, :], in1=sA[:, :], op=ALU.mult), reads=["impacc", "sA"], writes=["sc"])
                            p.op("dve", lambda e: e.tensor_tensor(out=sc[:, :], in0=sc[:, :], in1=sB[:, :], op=ALU.add), reads=["sc", "sB"], writes=["sc"])
                            p.op("dve", lambda e: e.max(out=m8a[:, :], in_=sc[:, :]), reads=["sc"], writes=["m8a"])
                            p.op("dve", lambda e: e.match_replace(out=sc2[:, :], in_to_replace=m8a[:, :], in_values=sc[:, :], imm_value=-3e38),
                                 reads=["sc", "m8a"], writes=["sc2"])
                            p.op("dve", lambda e: e.max(out=m8b[:, :], in_=sc2[:, :]), reads=["sc2"], writes=["m8b"])
                            p.op("dve", lambda e: e.tensor_scalar(out=sc2[:, :], in0=sc[:, :], scalar1=m8b[:, 7:8], scalar2=-MASKV, op0=ALU.is_ge, op1=ALU.mult),
                                 reads=["sc", "m8b"], writes=["sc2"])
                            p.op("dve", lambda e: e.tensor_scalar(out=sc2w[:, 0:128], in0=sc2[:, :], scalar1=MASKV, scalar2=None, op0=ALU.add), reads=["sc2"], writes=["sc2w"])
                            p.op("dve", lambda e: e.tensor_scalar(out=sc2w[:, 128:256], in0=sc2[:, :], scalar1=MASKV, scalar2=None, op0=ALU.add), reads=["sc2"], writes=["sc2w"])

                            def tr2(e):
                                e.transpose(out=PS[6][:, 0:128], in_=sc2w[:, 0:128], identity=ident_t[:, :])
                                return e.transpose(out=PS[6][:, 128:256], in_=sc2w[:, 64:192], identity=ident_t[:, :])
                            p.op("pe", tr2, reads=["sc2w", "ident"], writes=[PSN[6]])
                            p.op("act", lambda e, qi=qi: e.activation(out=negT[64:128, qi * 128:(qi + 1) * 128], in_=PS[6][64:128, 0:128], func=AF.Copy),
                                 reads=[PSN[6]], writes=["negT"])
                            p.op("act", lambda e, qi=qi: e.activation(out=negS[64:128, qi * 128:(qi + 1) * 128], in_=PS[6][64:128, 128:256], func=AF.Copy),
                                 reads=[PSN[6]], writes=["negT"])
                    LA = 6
                    DEFER = 6
                    epi = [0]
                    for hh in range(4):
                        hd = g * 4 + hh
                        src_s = bass.AP(tensor=rep_s, offset=hd * 128 * L_S + 127, ap=[[L_S - 1, 128], [1, W_S]])
                        src_w = bass.AP(tensor=rep_w, offset=hd * 128 * L_W + 127, ap=[[L_W - 1, 128], [1, W_W]])
                        p.dma("sp", lambda e, src_s=src_s: e.dma_start(out=ms_t[:, :], in_=src_s), reads=["rep_s.%d" % hd], writes=["ms_t"])
                        p.dma("sp", lambda e, src_w=src_w: e.dma_start(out=mw_t[:, :], in_=src_w), reads=["rep_w.%d" % hd], writes=["mw_t"])
                        pend = []

                        pend2 = []

                        def emit_pv(un, b):
                            ob = 4 + un["br"]
                            first, last, vap, br, gi, hd_, i_ = un["first"], un["last"], un["vap"], un["br"], un["gi"], un["hd"], un["i"]
                            p.op("pe", lambda e: e.matmul(PS[ob][:, :], vap, p_t[b][:, :], start=first, stop=last),
                                 reads=["p_t%d" % b, "kgrp"], writes=[PSN[ob]])
                            if last:
                                gbn = "gb%d" % gi
                                k_ = epi[0] % 3
                                epi[0] += 1
                                rr2 = rr2b[k_]
                                rn = "rr2.%d" % k_
                                p.op("dve", lambda e: e.tensor_scalar(out=rr2[64:128, :], in0=PS[ob][64:128, :], scalar1=1e-30, scalar2=None, op0=ALU.add), reads=[PSN[ob]], writes=[rn])
                                p.op("dve", lambda e: e.reciprocal(out=rscr2[64:128, :], in_=rr2[64:128, :]), reads=[rn], writes=["rscr2"])
                                p.op("dve", lambda e: e.tensor_tensor(out=rr2[64:128, :], in0=rscr2[64:128, :], in1=gbb[gi][64:128, br, :], op=ALU.mult), reads=["rscr2", gbn], writes=[rn])

                                def stage2():
                                    p.op("pe", lambda e: e.matmul(PS[7][0:64, :], ident_t[:, 64:128], rr2[:, :], start=True, stop=True), reads=[rn, "ident"], writes=[PSN[7]])
                                    p.op("act", lambda e: e.activation(out=rr[:, :], in_=PS[7][0:64, :], func=AF.Copy), reads=[PSN[7]], writes=["rr"])
                                    if br == 0:
                                        p.op("dve", lambda e: e.tensor_tensor(out=acc[:, :], in0=PS[ob][0:64, :], in1=rr[:, :], op=ALU.mult), reads=[PSN[ob], "rr"], writes=["acc"])
                                    else:
                                        p.op("dve", lambda e: e.tensor_tensor(out=rr[:, :], in0=PS[ob][0:64, :], in1=rr[:, :], op=ALU.mult), reads=[PSN[ob], "rr"], writes=["rr"])
                                        p.op("dve", lambda e: e.tensor_tensor(out=acc[:, :], in0=acc[:, :], in1=rr[:, :], op=ALU.add), reads=["acc", "rr"], writes=["acc"])
                                    if br == 2:
                                        p.op("act", lambda e: e.activation(out=ocb[:, :], in_=acc[:, :], func=AF.Copy), reads=["acc"], writes=["ocb"])
                                        p.dma("sp", lambda e: e.dma_start(out=ocT.ap()[hd_][:, i_ * 512:(i_ + 1) * 512], in_=ocb[:, :]), reads=["ocb"], writes=["ocT"])
                                pend2.append([DEFER + 1, stage2])
                            for it in pend2:
                                it[0] -= 1
                            while pend2 and pend2[0][0] <= 0:
                                pend2.pop(0)[1]()

                        for i in range(NO):
                            qb = i % 2
                            gi = i % 2
                            for v_ in range(2):
                                p.dma("sp", lambda e, hd=hd, i=i, qb=qb, v_=v_: e.dma_start(out=qa[v_][qb][0:64, :], in_=qTs.ap()[hd][:, i * 512:(i + 1) * 512]),
                                      reads=["qTs"], writes=["qa%d" % qb])
                            p.op("pool", lambda e, i=i, qb=qb: e.tensor_copy(out=qa[0][qb][64:128, :], in_=negS[64:128, i * 512:(i + 1) * 512]), reads=["negT"], writes=["qa%d" % qb])
                            p.op("pool", lambda e, i=i, qb=qb: e.tensor_copy(out=qa[1][qb][64:128, :], in_=negT[64:128, i * 512:(i + 1) * 512]), reads=["negT"], writes=["qa%d" % qb])
                            p.dma("sp", lambda e, i=i: e.dma_start(out=gs_t[:, :], in_=gsT.ap()[:, i * 512:(i + 1) * 512]), reads=["gsT"], writes=["gs_t"])
                            for br in range(3):
                                p.op("pe", lambda e, br=br, hd=hd: e.matmul(PS[7][:, :], selg_t[:, (hd * 3 + br) * 128:(hd * 3 + br + 1) * 128], gs_t[:, :],
                                                                           start=True, stop=True), reads=["selg_t", "gs_t"], writes=[PSN[7]])
                                p.op("act", lambda e, br=br, gi=gi: e.activation(out=gbb[gi][64:128, br, :], in_=PS[7][64:128, :], func=AF.Copy), reads=[PSN[7]], writes=["gb%d" % gi])
                            units = []
                            for br in range(3):
                                if br == 0:
                                    kts = c_chunks(i)
                                elif br == 1:
                                    kts = list(range((2 * i + 2) * 4))
                                else:
                                    kts = list(range(max(0, (2 * i - 1) * 4), (2 * i + 2) * 4))
                                for ui, kt in enumerate(kts):
                                    units.append(dict(br=br, kt=kt, first=(ui == 0), last=(ui == len(kts) - 1), gi=gi, hd=hd, i=i))
                            for un in units:
                                kt, br = un["kt"], un["br"]
                                if br == 0:
                                    mb = load_mc(hd, i, kt)
                                    b = score_unit(hd, kc_t, kt * 128, qa[0][qb][0:64, :], "qa%d" % qb, mc_t[mb][:, :], ["mc_t%d" % mb])
                                    un["vap"] = vc_t[:, kt, :]
                                elif br == 1:
                                    j0 = min(1024 * i - 128 * kt + OFFS - 127, JSAT_S)
                                    b = score_unit(hd, ks_t, kt * 128, qa[kt // 32][qb][:, :], "qa%d" % qb, ms_t[:, j0:j0 + 512], ["ms_t"])
                                    un["vap"] = vsw_t[:, kt, 0:128]
                                else:
                                    j0 = 1024 * i - 128 * kt + OFFW - 127
                                    b = score_unit(hd, kw_t, kt * 128, qa[0][qb][0:64, :], "qa%d" % qb, mw_t[:, j0:j0 + 512], ["mw_t"])
                                    un["vap"] = vsw_t[:, kt, 128:256]
                                pend.append((un, b))
                                if len(pend) > LA:
                                    emit_pv(*pend.pop(0))
                        while pend:
                            emit_pv(*pend.pop(0))
                        while pend2:
                            pend2.pop(0)[1]()
        _phase6()
        p.barrier()

        def _phase7():
            with contextlib.ExitStack() as st:
                h = T(st, [128, 8, 512], F32, "h")
                sq = T(st, [128, 8, 512], BF16, "sq")
                xn = T(st, [128, 8, 512], BF16, "xn")
                rs = T(st, [128, 512], F32, "rs")
                act = T(st, [128, NJ, 512], BF16, "act")
                sg = [T(st, [128, 512], F32, "sg") for _ in range(2)]
                wpi = [T(st, [128, 2048], BF16, "wpi") for _ in range(4)]
                wpo = [T(st, [128, NJ * 128], BF16, "wpo") for _ in range(3)]
                bufs = (sq, xn, rs, act, sg, wpi, wpo)
                wo_t = T(st, [64, 16, 1024], BF16, "wo_t")
                oc_t = T(st, [64, 16, 512], BF16, "oc_t")
                p.dma("sp", lambda e: e.dma_start(out=wo_t[:, :, :], in_=wo_b.ap().rearrange("h p n -> p h n")), reads=["wo.%d" % i for i in range(16)], writes=["wo_t"])
                h2v = h2T.ap().rearrange("(c p) s -> p c s", p=128)
                ov = outT.ap().rearrange("(c p) s -> p c s", p=128)
                for i in range(NO):
                    p.dma("sp", lambda e, i=i: e.dma_start(out=h[:, :, :], in_=h2v[:, :, i * 512:(i + 1) * 512]), reads=["h2T"], writes=HN)
                    p.dma("sp", lambda e, i=i: e.dma_start(out=oc_t[:, :, :], in_=ocT.ap().rearrange("h p s -> p h s")[:, :, i * 512:(i + 1) * 512]),
                          reads=["ocT"], writes=["oc_t"])
                    for d in range(8):
                        b = d % 2

                        def mm(e, d=d, b=b):
                            for hd in range(16):
                                ins = e.matmul(PS[b][:, :], wo_t[:, hd, d * 128:(d + 1) * 128], oc_t[:, hd, :], start=(hd == 0), stop=(hd == 15))
                            return ins
                        p.op("pe", mm, reads=["wo_t", "oc_t"], writes=[PSN[b]])
                        p.op("dve", lambda e, d=d, b=b: e.tensor_tensor(out=h[:, d, :], in0=PS[b][:, :], in1=h[:, d, :], op=ALU.add), reads=[PSN[b], "h.%d" % d], writes=["h.%d" % d])
                    ffn(h, "h", 3, 48, bufs)
                    p.dma("pool", lambda e, i=i: e.dma_start(out=ov[:, :, i * 512:(i + 1) * 512], in_=h[:, :, :]), reads=HN, writes=["outT"])
        _phase7()
        p.barrier()
        p.run()
    return nc


def _rel_bucket_table(n):
    import jax
    import jax.numpy as jnp
    import math
    with jax.default_device(jax.devices("cpu")[0]):
        rel = jnp.arange(n)
        nn = jnp.maximum(rel, 0)
        nf = jnp.maximum(nn, 1).astype(jnp.float32)
        large = 16 + (jnp.log(nf / 16) / math.log(4096 / 16) * 16).astype(jnp.int32)
        large = jnp.minimum(large, 31)
        return np.asarray(jnp.where(nn < 16, nn, large))


def prep_shared(S, inputs):
    f = lambda a: np.ascontiguousarray(a, dtype=np.float32)
    NSB = S // 64
    NCP = S // 16
    NCK = NCP // 128
    sh = {}

    def col8(g):
        return g.reshape(8, 128).T
    norms = [inputs["ffn1_norm"][0], inputs["mix_norm"][0], inputs["ffn2_norm"][0], inputs["kv_norm"],
             inputs["ffn1_norm"][1], inputs["mix_norm"][1], inputs["ffn2_norm"][1]]
    sh["gains"] = f(np.concatenate([col8(np.asarray(g)) for g in norms], axis=1))

    def win_r(w):
        w = np.asarray(w).reshape(8, 128, 2, NJ, 128)
        return f(w.transpose(3, 1, 2, 0, 4).reshape(NJ, 128, 2048))

    def wout_r(w):
        w = np.asarray(w).reshape(NJ, 128, 8, 128)
        return f(w.transpose(2, 1, 0, 3).reshape(8, 128, NJ * 128))
    ffn_list = [("ffn1_w_in", "ffn1_w_out", 0), ("ffn2_w_in", "ffn2_w_out", 0), ("ffn1_w_in", "ffn1_w_out", 1), ("ffn2_w_in", "ffn2_w_out", 1)]
    for i, (a, b, l) in enumerate(ffn_list):
        sh["win%d" % i] = win_r(inputs[a][l])
        sh["wout%d" % i] = wout_r(inputs[b][l])

    def sq_r(w, ncol):
        w = np.asarray(w).reshape(8, 128, ncol, 128)
        return w.transpose(2, 1, 0, 3)
    ci = np.asarray(inputs["conv_w_in"][0]).reshape(8, 128, 3, 8, 128)
    sh["cin"] = f(ci.transpose(3, 1, 2, 0, 4).reshape(8, 128, 3072))
    sh["cout"] = f(sq_r(inputs["conv_w_out"][0], 8).reshape(8, 128, 1024))
    cw = np.asarray(inputs["conv_w"][0])
    sh["cwk"] = f(cw.reshape(3, 8, 128).transpose(2, 1, 0).reshape(128, 24))
    wkv = np.asarray(inputs["w_kv"])
    units = []
    for c0 in (0, 128, 256, 384):
        units.append(wkv[:, c0:c0 + 128])
    for br in (1, 2):
        for g in range(4):
            cc = wkv[:, br * 512 + g * 64: br * 512 + (g + 1) * 64]
            units.append(np.concatenate([cc, cc], axis=1))
    sh["wkf"] = f(np.stack([u.reshape(8, 128, 128).transpose(1, 0, 2).reshape(128, 1024) for u in units]))
    wv = np.concatenate([wkv[:, 768:1024], wkv[:, 1280:1536]], axis=1)
    sh["wvt"] = f(wv.reshape(8, 128, 512).transpose(1, 0, 2).reshape(1, 128, 4096))
    wqf = np.asarray(inputs["attn_w_q"][0])
    sh["wq"] = f(np.stack([wqf[:, h * 64:(h + 1) * 64].reshape(8, 128, 64).transpose(1, 0, 2).reshape(128, 512) for h in range(16)]))
    sh["wg"] = f(wqf[:, 1024:1072].reshape(8, 128, 48).transpose(1, 0, 2).reshape(1, 128, 384))
    sh["wo"] = f(np.asarray(inputs["attn_w_o"][0]).reshape(16, 64, 1024))
    knm = np.asarray(inputs["k_norm"])
    sh["kn"] = f(np.concatenate([knm.T, knm.T], axis=0))
    qnm = np.asarray(inputs["attn_q_norm"][0])
    sh["qn"] = f(np.concatenate([qnm, qnm])[:, None])

    def w1_r(w):
        w = np.asarray(w).reshape(32, 64, 128).transpose(1, 0, 2).reshape(64, 4096)
        return f(np.concatenate([w, w], axis=0)[None])
    sh["w1k"] = w1_r(inputs["cmp_w1_k"])
    sh["w1v"] = w1_r(inputs["cmp_w1_v"])
    sh["w2k"] = f(inputs["cmp_w2_k"])
    sh["w2v"] = f(inputs["cmp_w2_v"])
    pk = np.asarray(inputs["cmp_pe_k"]).T
    pv = np.asarray(inputs["cmp_pe_v"]).T
    pe = np.concatenate([pk, pv], axis=1)
    sh["peT"] = f(np.concatenate([pe, pe], axis=0))
    assert NSB == 128
    ex = np.zeros((1, 64, S), np.float32)
    for j in range(NSB):
        ex[0, j % 64, j * 64:(j + 1) * 64] = 1.0
    sh["expd"] = ex
    n = np.arange(NCP)
    cs = n * 16
    ss = np.arange(NSB) * 64
    ovl = np.clip(np.minimum(cs[:, None] + 32, ss[None, :] + 64) - np.maximum(cs[:, None], ss[None, :]), 0, None) / 16.0
    ovl[NCP - 1, :] = 0.0
    ovl1 = np.concatenate([ovl, np.ones((NCP, 1))], axis=1)
    sh["ovm"] = f(ovl1.reshape(NCK, 128, NSB + 1).transpose(1, 0, 2).reshape(128, NCK * (NSB + 1)))
    sg_ = np.zeros((1, 48, 48 * 128), np.float32)
    for c in range(48):
        sg_[0, c, c * 128 + 64:(c + 1) * 128] = 1.0
    sh["selg"] = sg_
    sh["ident"] = np.eye(128, dtype=np.float32)
    return sh


def prep_parity(S, inputs, par):
    f = lambda a: np.ascontiguousarray(a, dtype=np.float32)
    NSB = S // 64
    NO = S // 1024
    pp = {}
    pp["par"] = f(np.tile(np.array([[par, 1 - par]], np.float32), (128, 1)))
    rb = np.asarray(inputs["rel_bias"])
    bt = _rel_bucket_table(max(S, 4096) + 8192)

    def vec(L, off, lo, hi):
        rel = np.arange(L) - off + 512 * par
        ok = (rel >= lo) & (rel < hi)
        idx = bt[np.clip(rel, 0, len(bt) - 1)]
        v = rb[idx, :].T.copy()
        v[:, ~ok] = MASKV
        return f(v)
    pp["bvs"] = vec(L_S, OFFS, 0, 1 << 30)
    pp["bvw"] = vec(L_W, OFFW, 0, 512)
    pp["bvc"] = vec(L_C, OFFC, 0, 1 << 30)
    A = np.zeros((NO * 4, 128, NSB), np.float32)
    B = np.zeros((NO * 4, 128, NSB), np.float32)
    jj = np.arange(NSB)[None, :]
    for i in range(NO):
        for qq in range(4):
            t = (2 * i + par) * 512 + qq * 128 + np.arange(128)
            bt_ = (t // 64)[:, None]
            valid = jj <= bt_
            forced = (jj == 0) | (jj == bt_) | (jj == bt_ - 1)
            A[i * 4 + qq] = (valid & ~forced)
            B[i * 4 + qq] = np.where(valid, np.where(forced, 1e30, 0.0), -1e30)
    pp["selA"] = A
    pp["selB"] = B
    return pp


_CACHE = {}


def run_model(S, inputs, x, dbg=()):
    B = x.shape[0]
    key = (S, tuple(dbg))
    if key not in _CACHE:
        _CACHE[key] = build(S, dbg)
    nc = _CACHE[key]
    sh = prep_shared(S, inputs)
    pps = [prep_parity(S, inputs, 0), prep_parity(S, inputs, 1)]
    in_maps = []
    for core in range(2 * B):
        b, par = core // 2, core % 2
        m = dict(sh)
        m.update(pps[par])
        m["xT"] = np.ascontiguousarray(np.asarray(x[b], dtype=np.float32).T)
        in_maps.append(m)
    res = run_bass_kernel_spmd(nc, in_maps, core_ids=list(range(2 * B)))
    out = np.zeros((B, S, D), np.float32)
    for core in range(2 * B):
        b, par = core // 2, core % 2
        oT = np.asarray(res.results[core]["outT"])
        o = oT.T.reshape(S // 1024, 512, D)
        for i in range(S // 1024):
            t0 = (2 * i + par) * 512
            out[b, t0:t0 + 512] = o[i]
    return out, res


def kernel(**inputs):
    x = np.asarray(inputs["x"])
    out, _ = run_model(x.shape[1], inputs, x)
    return out
```

```python
import contextlib
import numpy as np
import concourse.bass as bass
import concourse.mybir as mybir
from concourse.bass_utils import run_bass_kernel_spmd

F32 = mybir.dt.float32
BF16 = mybir.dt.bfloat16
AF = mybir.ActivationFunctionType
ALU = mybir.AluOpType

ENGS = ("pe", "act", "dve", "pool", "sp")
NDSEM = 32
DSEM_Q = {"sp": 24, "pool": 8, "act": 4}
SAME_ENGINE_SYNC = True

D = 1024
DFF = 2816
NJ = 22
EPS = 1e-6
OFFS, OFFW, OFFC = 1023, 1535, 2063
JSAT_S = 2896 + OFFS
W_S = JSAT_S + 512
L_S = W_S + 127
W_W = 2432
L_W = W_W + 127
JSAT_C = 4959
W_C = JSAT_C + 512
L_C = W_C + 2032
MASKV = -30000.0


class Prog:
    def __init__(self, nc):
        self.nc = nc
        self.ops = {e: [] for e in ENGS}
        self.cnt = {e: 0 for e in ENGS}
        self.waited = {e: {} for e in ENGS}
        self.res_w = {}
        self.res_r = {}
        self.ndma = 0
        self.ndma_q = {}

    def _deps(self, reads, writes):
        deps = {}

        def add(d):
            if d is None:
                return
            k, v = d
            if deps.get(k, 0) < v:
                deps[k] = v
        for r in reads:
            add(self.res_w.get(r))
        for w in writes:
            add(self.res_w.get(w))
            for k, v in self.res_r.get(w, {}).items():
                add((k, v))
        return deps

    def _commit(self, reads, writes, me):
        for r in reads:
            d = self.res_r.setdefault(r, {})
            if d.get(me[0], 0) < me[1]:
                d[me[0]] = me[1]
        for w in writes:
            self.res_w[w] = me
            self.res_r[w] = {}

    def op(self, eng, fn, reads=(), writes=()):
        deps = self._deps(reads, writes)
        waits = []
        for k, v in deps.items():
            if k == eng and (eng == "pe" or not SAME_ENGINE_SYNC):
                continue
            if self.waited[eng].get(k, 0) >= v:
                continue
            self.waited[eng][k] = v
            waits.append((k, v))
        self.cnt[eng] += 1
        me = (eng, self.cnt[eng])
        self.ops[eng].append((waits, fn, me))
        self._commit(reads, writes, me)

    def dma(self, q, fn, reads=(), writes=()):
        deps = self._deps(reads, writes)
        k = self.ndma_q.get(q, 0)
        self.ndma_q[q] = k + 1
        self.ndma += 1
        nsl = DSEM_Q[q]
        slot = k % nsl
        val = 16 * (k // nsl + 1)
        key = ("d", q, slot)
        if val > 16:
            deps[key] = max(deps.get(key, 0), val - 16)
        waits = []
        for kk, v in deps.items():
            if self.waited[q].get(kk, 0) >= v:
                continue
            self.waited[q][kk] = v
            waits.append((kk, v))
        me = (key, val)
        self.ops[q].append((waits, fn, me))
        self._commit(reads, writes, me)

    def barrier(self):
        tgt = [(e, self.cnt[e]) for e in ENGS if self.cnt[e] > 0]
        for q, n in self.ndma_q.items():
            nsl = DSEM_Q[q]
            for k in range(max(0, n - nsl), n):
                tgt.append((("d", q, k % nsl), 16 * (k // nsl + 1)))
        for e in ENGS:
            waits = []
            for k, v in tgt:
                if k == e:
                    continue
                if self.waited[e].get(k, 0) >= v:
                    continue
                self.waited[e][k] = v
                waits.append((k, v))
            if waits:
                self.ops[e].append((waits, None, None))

    def run(self):
        nc = self.nc
        def _phase1():
            with contextlib.ExitStack() as st:
                sems = {}
                for e in ENGS:
                    sems[e] = st.enter_context(nc.semaphore("s_" + e))
                for q_, n_ in DSEM_Q.items():
                    for i in range(n_):
                        sems[("d", q_, i)] = st.enter_context(nc.semaphore("d_%s_%d" % (q_, i)))
                block = st.enter_context(nc.Block())

                def body(ename):
                    def f(eng):
                        for waits, fn, me in self.ops[ename]:
                            for k, v in waits:
                                eng.wait_ge(sems[k], v)
                            if fn is None:
                                continue
                            ins = fn(eng)
                            ins.then_inc(sems[me[0]], 16 if isinstance(me[0], tuple) else 1)
                    return f
                block.tensor(body("pe"))
                block.scalar(body("act"))
                block.vector(body("dve"))
                block.gpsimd(body("pool"))
                block.sync(body("sp"))


        _phase1()
def build(S, dbg=()):
    NT = S // 512
    NO = NT // 2
    SO = S // 2
    NKT = S // 128
    NSB = S // 64
    NCP = S // 16
    NCK = NCP // 128
    NCV = NCP - 1
    nc = bass.Bass("TRN2", target_bir_lowering=False)
    IN = {}

    def inp(name, shape):
        IN[name] = nc.dram_tensor(name, list(shape), F32, kind="ExternalInput")
        return IN[name]

    def scr(name, shape, dt):
        kind = "ExternalOutput" if name in dbg else "Internal"
        return nc.dram_tensor(name, list(shape), dt, kind=kind)

    xT = inp("xT", [D, S])
    gains = inp("gains", [128, 56])
    par = inp("par", [128, 2])
    wins = [inp("win%d" % i, [NJ, 128, 2048]) for i in range(4)]
    wouts = [inp("wout%d" % i, [8, 128, NJ * 128]) for i in range(4)]
    cin = inp("cin", [8, 128, 3072])
    cout = inp("cout", [8, 128, 1024])
    cwk = inp("cwk", [128, 24])
    wkf = inp("wkf", [12, 128, 1024])
    wvt = inp("wvt", [1, 128, 4096])
    wq = inp("wq", [16, 128, 512])
    wg = inp("wg", [1, 128, 384])
    wo = inp("wo", [16, 64, 1024])
    kn = inp("kn", [128, 3])
    qn = inp("qn", [128, 1])
    w1k = inp("w1k", [1, 128, 4096])
    w1v = inp("w1v", [1, 128, 4096])
    w2k = inp("w2k", [128, 64])
    w2v = inp("w2v", [128, 64])
    peT = inp("peT", [128, 64])
    bvs = inp("bvs", [16, L_S])
    bvw = inp("bvw", [16, L_W])
    bvc = inp("bvc", [16, L_C])
    expd = inp("expd", [1, 64, S])
    ovm = inp("ovm", [128, NCK * (NSB + 1)])
    selg = inp("selg", [1, 48, 48 * 128])
    selA = inp("selA", [NO * 4, 128, NSB])
    selB = inp("selB", [NO * 4, 128, NSB])
    ident_in = inp("ident", [128, 128])
    outT = nc.dram_tensor("outT", [D, SO], F32, kind="ExternalOutput")

    def bfc(name, src):
        return scr(name + "_bf", list(src.shape), BF16)
    wins_b = [bfc("win%d" % i, wins[i]) for i in range(4)]
    wouts_b = [bfc("wout%d" % i, wouts[i]) for i in range(4)]
    cin_b, cout_b, wkf_b, wvt_b = bfc("cin", cin), bfc("cout", cout), bfc("wkf", wkf), bfc("wvt", wvt)
    wq_b, wg_b, wo_b = bfc("wq", wq), bfc("wg", wg), bfc("wo", wo)
    w1k_b, w1v_b, expd_b, selg_b = bfc("w1k", w1k), bfc("w1v", w1v), bfc("expd", expd), bfc("selg", selg)

    h1T = scr("h1T", [D, S], F32)
    h2T = scr("h2T", [D, SO], F32)
    rawT = scr("rawT", [4, 128, S], BF16)
    ksT = scr("ksT", [4, 64, S], BF16)
    kwT = scr("kwT", [4, 64, S], BF16)
    vsw = scr("vsw", [NKT, 128, 512], BF16)
    kcT = scr("kcT", [4, 64, NCP], BF16)
    vcS = scr("vcS", [4, 128, NCK * 64], BF16)
    qTs = scr("qTs", [16, 64, SO], BF16)
    gsT = scr("gsT", [48, SO], BF16)
    ocT = scr("ocT", [16, 64, SO], BF16)
    rep_s = scr("rep_s", [16, 128, L_S], BF16)
    rep_w = scr("rep_w", [16, 128, L_W], BF16)
    rep_c = scr("rep_c", [16, 128, L_C], BF16)

    p = Prog(nc)
    HN = ["h.%d" % c for c in range(8)]
    sb = nc.sbuf_tensor
    uid = [0]

    def T(st, shape, dt, nm="t"):
        uid[0] += 1
        return st.enter_context(sb("%s_%d" % (nm, uid[0]), list(shape), dt))

    with contextlib.ExitStack() as top:
        PS = [top.enter_context(nc.psum_tensor("ps%d" % i, [128, 512], F32)) for i in range(8)]
        PSN = ["ps%d" % i for i in range(8)]
        gains_t = T(top, [128, 56], F32, "gains")
        par_t = T(top, [128, 2], F32, "par")
        cwk_t = T(top, [128, 24], F32, "cwk")
        kn_t = T(top, [128, 3], F32, "kn")
        qn_t = T(top, [128, 1], F32, "qn")
        eps_t = T(top, [128, 1], F32, "eps")
        ones_bf = T(top, [128, 128], BF16, "ones")
        ones_f = T(top, [128, 128], F32, "onesf")
        ident_t = T(top, [128, 128], F32, "ident")
        for dst, src, nm in ((gains_t, gains, "gains"), (par_t, par, "par"), (cwk_t, cwk, "cwk"),
                             (kn_t, kn, "kn"), (qn_t, qn, "qn"), (ident_t, ident_in, "ident")):
            p.dma("sp", lambda e, dst=dst, src=src: e.dma_start(out=dst[:], in_=src.ap()), writes=[nm])
        p.op("dve", lambda e: e.memset(eps_t[:], EPS), writes=["eps"])
        p.op("dve", lambda e: e.memset(ones_bf[:], 1.0), writes=["ones"])
        p.op("dve", lambda e: e.memset(ones_f[:], 1.0), writes=["onesf"])

        def precast(src, dst, nm):
            n0 = src.shape[0]
            for i in range(n0):
                p.dma("pool", lambda e, i=i: e.dma_start(out=dst.ap()[i], in_=src.ap()[i]), writes=["%s.%d" % (nm, i)])
        order = [(wins[0], wins_b[0], "win0"), (wouts[0], wouts_b[0], "wout0"), (cin, cin_b, "cin"), (cout, cout_b, "cout"),
                 (wins[1], wins_b[1], "win1"), (wouts[1], wouts_b[1], "wout1"), (wkf, wkf_b, "wkf"), (wvt, wvt_b, "wvt"),
                 (w1k, w1k_b, "w1k"), (w1v, w1v_b, "w1v"),
                 (wins[2], wins_b[2], "win2"), (wouts[2], wouts_b[2], "wout2"), (wq, wq_b, "wq"), (wg, wg_b, "wg"),
                 (expd, expd_b, "expd"), (selg, selg_b, "selg"), (wo, wo_b, "wo"),
                 (wins[3], wins_b[3], "win3"), (wouts[3], wouts_b[3], "wout3")]
        early, late = order[:10], order[10:]
        for s_, d_, n_ in early:
            precast(s_, d_, n_)
        late_items = []
        for s_, d_, n_ in late:
            for i_ in range(s_.shape[0]):
                late_items.append((s_, d_, n_, i_))

        def _phase2():
            with contextlib.ExitStack() as st:
                vrow = [T(st, [1, L_C], F32, "vrow") for _ in range(2)]
                ev = [T(st, [128, L_C], BF16, "ev") for _ in range(2)]
                k = 0
                for src, rep, L, nm in ((bvs, rep_s, L_S, "rep_s"), (bvw, rep_w, L_W, "rep_w"), (bvc, rep_c, L_C, "rep_c")):
                    for h in range(16):
                        b = k % 2
                        k += 1
                        p.dma("sp", lambda e, b=b, h=h, src=src, L=L: e.dma_start(out=vrow[b][0:1, 0:L], in_=src.ap()[h:h + 1, :]),
                              writes=["vrow%d" % b])
                        for c0 in range(0, L, 512):
                            c1 = min(L, c0 + 512)
                            pb = (c0 // 512) % 2
                            p.op("pe", lambda e, b=b, c0=c0, c1=c1, pb=pb: e.matmul(PS[pb][:, 0:c1 - c0], ones_f[0:1, :], vrow[b][0:1, c0:c1],
                                                                                  start=True, stop=True),
                                 reads=["vrow%d" % b, "onesf"], writes=[PSN[pb]])
                            p.op("act", lambda e, b=b, c0=c0, c1=c1, pb=pb: e.activation(out=ev[b][:, c0:c1], in_=PS[pb][:, 0:c1 - c0], func=AF.Exp),
                                 reads=[PSN[pb]], writes=["ev%d" % b])
                        p.dma("sp", lambda e, b=b, h=h, rep=rep, L=L: e.dma_start(out=rep.ap()[h], in_=ev[b][:, 0:L]),
                              reads=["ev%d" % b], writes=["%s.%d" % (nm, h)])
        _phase2()
        p.barrier()

        def rstd_from(ps_i, n, rs, rsn, rows=128):
            p.op("act", lambda e: e.activation(out=rs[0:rows, :], in_=PS[ps_i][0:rows, :], func=AF.Ln, bias=eps_t[0:rows, 0:1], scale=1.0 / n),
                 reads=[PSN[ps_i], "eps"], writes=[rsn])
            p.op("act", lambda e: e.activation(out=rs[0:rows, :], in_=rs[0:rows, :], func=AF.Exp, scale=-0.5), reads=[rsn], writes=[rsn])

        def rms_to_xn(h, hn, gcol, sq, xn, rs):
            for c in range(8):
                p.op("act", lambda e, c=c: e.activation(out=sq[:, c, :], in_=h[:, c, :], func=AF.Square), reads=["%s.%d" % (hn, c)], writes=["sq"])

            def mm(e):
                for c in range(8):
                    ins = e.matmul(PS[7][:, :], ones_bf[:, :], sq[:, c, :], start=(c == 0), stop=(c == 7))
                return ins
            p.op("pe", mm, reads=["sq", "ones"], writes=[PSN[7]])
            rstd_from(7, D, rs, "rs")
            for c in range(8):
                p.op("dve", lambda e, c=c: e.scalar_tensor_tensor(out=xn[:, c, :], in0=h[:, c, :], scalar=gains_t[:, gcol + c:gcol + c + 1],
                                                                 in1=rs[:, :], op0=ALU.mult, op1=ALU.mult),
                     reads=["%s.%d" % (hn, c), "rs", "gains"], writes=["xn"])

        def ffn(h, hn, li, gcol, bufs):
            sq, xn, rs, act, sg, wpi, wpo = bufs
            rms_to_xn(h, hn, gcol, sq, xn, rs)
            for j in range(NJ):
                b = j % 2
                wb = j % len(wpi)
                p.dma("sp", lambda e, j=j, wb=wb: e.dma_start(out=wpi[wb][:, :], in_=wins_b[li].ap()[j]),
                      reads=["win%d.%d" % (li, j)], writes=["wpi%d" % wb])
                for s_ in range(2):
                    def mm(e, s_=s_, b=b, wb=wb):
                        for c in range(8):
                            ins = e.matmul(PS[b * 2 + s_][:, :], wpi[wb][:, (s_ * 8 + c) * 128:(s_ * 8 + c + 1) * 128], xn[:, c, :],
                                           start=(c == 0), stop=(c == 7))
                        return ins
                    p.op("pe", mm, reads=["xn", "wpi%d" % wb], writes=[PSN[b * 2 + s_]])
                p.op("act", lambda e, b=b: e.activation(out=sg[b][:, :], in_=PS[b * 2][:, :], func=AF.Silu), reads=[PSN[b * 2]], writes=["sg%d" % b])
                p.op("dve", lambda e, b=b, j=j: e.tensor_tensor(out=act[:, j, :], in0=sg[b][:, :], in1=PS[b * 2 + 1][:, :], op=ALU.mult),
                     reads=["sg%d" % b, PSN[b * 2 + 1]], writes=["act.%d" % j])
            for d in range(8):
                b = d % 2
                wb = d % len(wpo)
                p.dma("sp", lambda e, d=d, wb=wb: e.dma_start(out=wpo[wb][:, :], in_=wouts_b[li].ap()[d]),
                      reads=["wout%d.%d" % (li, d)], writes=["wpo%d" % wb])

                def mm(e, b=b, wb=wb):
                    for j in range(NJ):
                        ins = e.matmul(PS[4 + b][:, :], wpo[wb][:, j * 128:(j + 1) * 128], act[:, j, :], start=(j == 0), stop=(j == NJ - 1))
                    return ins
                p.op("pe", mm, reads=["wpo%d" % wb] + ["act.%d" % j for j in range(NJ)], writes=[PSN[4 + b]])
                p.op("dve", lambda e, d=d, b=b: e.scalar_tensor_tensor(out=h[:, d, :], in0=PS[4 + b][:, :], scalar=0.5, in1=h[:, d, :],
                                                                      op0=ALU.mult, op1=ALU.add),
                     reads=[PSN[4 + b], "%s.%d" % (hn, d)], writes=["%s.%d" % (hn, d)])

        def _phase3():
            with contextlib.ExitStack() as st:
                h = T(st, [128, 8, 512], F32, "h")
                sq = T(st, [128, 8, 512], BF16, "sq")
                xn = T(st, [128, 8, 512], BF16, "xn")
                rs = T(st, [128, 512], F32, "rs")
                act = T(st, [128, NJ, 512], BF16, "act")
                sg = [T(st, [128, 512], F32, "sg") for _ in range(2)]
                wpi = [T(st, [128, 2048], BF16, "wpi") for _ in range(4)]
                wpo = [T(st, [128, NJ * 128], BF16, "wpo") for _ in range(3)]
                bufs = (sq, xn, rs, act, sg, wpi, wpo)
                wci = [T(st, [128, 3072], BF16, "wci") for _ in range(2)]
                wsq = [T(st, [128, 1024], BF16, "wsq") for _ in range(2)]
                u = [T(st, [128, 514], F32, "u") for _ in range(2)]
                ucar = T(st, [128, 8, 2], F32, "ucar")
                yb = [T(st, [128, 512], F32, "yb") for _ in range(2)]
                bgy = T(st, [128, 8, 512], BF16, "bgy")
                wvt_t = T(st, [128, 4096], BF16, "wvt")
                wkf_t = T(st, [128, 12, 1024], BF16, "wkf_t")
                bd64 = T(st, [128, 128], BF16, "bd64")
                ksb = [T(st, [128, 512], BF16, "ksb") for _ in range(4)]
                sgk = [T(st, [128, 512], F32, "sgk") for _ in range(4)]
                vtb = [T(st, [128, 512], BF16, "vtb") for _ in range(2)]
                p.op("dve", lambda e: e.memset(ucar[:], 0.0), writes=["ucar"])
                p.op("dve", lambda e: e.memset(bd64[:], 0.0), writes=["bd64"])
                p.op("dve", lambda e: e.memset(bd64[0:64, 0:64], 1.0), writes=["bd64"])
                p.op("dve", lambda e: e.memset(bd64[64:128, 64:128], 1.0), writes=["bd64"])
                p.dma("sp", lambda e: e.dma_start(out=wvt_t[:, :], in_=wvt_b.ap()[0]), reads=["wvt.0"], writes=["wvtt"])
                p.dma("sp", lambda e: e.dma_start(out=wkf_t[:, :, :], in_=wkf_b.ap().rearrange("u p n -> p u n")), reads=["wkf.%d" % u_ for u_ in range(12)], writes=["wkf_t"])
                xTv = xT.ap().rearrange("(c p) s -> p c s", p=128)
                h1v = h1T.ap().rearrange("(c p) s -> p c s", p=128)
                for ti in range(NT):
                    t0 = ti * 512
                    p.dma("sp", lambda e, t0=t0: e.dma_start(out=h[:, :, :], in_=xTv[:, :, t0:t0 + 512]), writes=HN)
                    if ti >= 1 and late_items:
                        nper = (len(late_items) + max(1, NT - 2) - 1) // max(1, NT - 2) if ti == 1 else nper_keep[0]
                        nper_keep[0] = nper
                        for _ in range(min(nper, len(late_items))):
                            s_, d_, n_, i_ = late_items.pop(0)
                            p.dma("pool", lambda e, s_=s_, d_=d_, i_=i_: e.dma_start(out=d_.ap()[i_], in_=s_.ap()[i_]), writes=["%s.%d" % (n_, i_)])
                    ffn(h, "h", 0, 0, bufs)
                    rms_to_xn(h, "h", 8, sq, xn, rs)
                    for i in range(8):
                        b = i % 2
                        p.dma("sp", lambda e, i=i, b=b: e.dma_start(out=wci[b][:, :], in_=cin_b.ap()[i]), reads=["cin.%d" % i], writes=["wci%d" % b])
                        cb = (0, 1, 2) if i % 2 == 0 else (3, 6, 7)
                        for s_ in range(3):
                            def mm(e, s_=s_, b=b, cb=cb):
                                for c in range(8):
                                    ins = e.matmul(PS[cb[s_]][:, :], wci[b][:, (s_ * 8 + c) * 128:(s_ * 8 + c + 1) * 128], xn[:, c, :],
                                                   start=(c == 0), stop=(c == 7))
                                return ins
                            p.op("pe", mm, reads=["xn", "wci%d" % b], writes=[PSN[cb[s_]]])
                        p.op("dve", lambda e, i=i, b=b: e.tensor_copy(out=u[b][:, 0:2], in_=ucar[:, i, :]), reads=["ucar"], writes=["u%d" % b])
                        p.op("act", lambda e, b=b, cb=cb: e.activation(out=yb[b][:, :], in_=PS[cb[1]][:, :], func=AF.Copy), reads=[PSN[cb[1]]], writes=["yb%d" % b])
                        p.op("dve", lambda e, b=b, cb=cb: e.tensor_tensor(out=u[b][:, 2:514], in0=yb[b][:, :], in1=PS[cb[2]][:, :], op=ALU.mult),
                             reads=["yb%d" % b, PSN[cb[2]]], writes=["u%d" % b])
                        p.op("dve", lambda e, i=i, b=b: e.tensor_copy(out=ucar[:, i, :], in_=u[b][:, 512:514]), reads=["u%d" % b], writes=["ucar"])
                        p.op("dve", lambda e, i=i, b=b: e.tensor_scalar(out=yb[b][:, :], in0=u[b][:, 2:514], scalar1=cwk_t[:, i * 3 + 2:i * 3 + 3], scalar2=None,
                                                                      op0=ALU.mult), reads=["u%d" % b, "cwk"], writes=["yb%d" % b])
                        for w_ in (1, 0):
                            p.op("dve", lambda e, i=i, b=b, w_=w_: e.scalar_tensor_tensor(out=yb[b][:, :], in0=u[b][:, w_:w_ + 512],
                                                                                         scalar=cwk_t[:, i * 3 + w_:i * 3 + w_ + 1], in1=yb[b][:, :],
                                                                                         op0=ALU.mult, op1=ALU.add),
                                 reads=["u%d" % b, "cwk", "yb%d" % b], writes=["yb%d" % b])
                        p.op("dve", lambda e, i=i, b=b, cb=cb: e.tensor_tensor(out=bgy[:, i, :], in0=yb[b][:, :], in1=PS[cb[0]][:, :], op=ALU.mult),
                             reads=["yb%d" % b, PSN[cb[0]]], writes=["bgy.%d" % i])
                    for d in range(8):
                        b = d % 2
                        p.dma("sp", lambda e, d=d, b=b: e.dma_start(out=wsq[b][:, :], in_=cout_b.ap()[d]), reads=["cout.%d" % d], writes=["wsq%d" % b])

                        def mm(e, b=b):
                            for c in range(8):
                                ins = e.matmul(PS[4 + b][:, :], wsq[b][:, c * 128:(c + 1) * 128], bgy[:, c, :], start=(c == 0), stop=(c == 7))
                            return ins
                        p.op("pe", mm, reads=["wsq%d" % b] + ["bgy.%d" % c for c in range(8)], writes=[PSN[4 + b]])
                        p.op("dve", lambda e, d=d, b=b: e.tensor_tensor(out=h[:, d, :], in0=PS[4 + b][:, :], in1=h[:, d, :], op=ALU.add),
                             reads=[PSN[4 + b], "h.%d" % d], writes=["h.%d" % d])
                    ffn(h, "h", 1, 16, bufs)
                    p.dma("pool", lambda e, t0=t0: e.dma_start(out=h1v[:, :, t0:t0 + 512], in_=h[:, :, :]), reads=HN, writes=["h1T"])
                    rms_to_xn(h, "h", 24, sq, xn, rs)
                    for un in range(12):
                        b = un % 2
                        k4 = un % 4
                        def mm(e, un=un, k4=k4):
                            for c in range(8):
                                ins = e.matmul(PS[k4][:, :], wkf_t[:, un, c * 128:(c + 1) * 128], xn[:, c, :], start=(c == 0), stop=(c == 7))
                            return ins
                        p.op("pe", mm, reads=["xn", "wkf_t"], writes=[PSN[k4]])
                        if un < 4:
                            p.op("act", lambda e, k4=k4: e.activation(out=ksb[k4][:, :], in_=PS[k4][:, :], func=AF.Copy), reads=[PSN[k4]], writes=["ksb%d" % k4])
                            p.dma("pool", lambda e, un=un, k4=k4, t0=t0: e.dma_start(out=rawT.ap()[un][:, t0:t0 + 512], in_=ksb[k4][:, :]),
                                  reads=["ksb%d" % k4], writes=["rawT"])
                        else:
                            br = 1 if un < 8 else 2
                            g = (un - 4) % 4
                            dst = ksT if br == 1 else kwT
                            p.op("act", lambda e, k4=k4: e.activation(out=sgk[k4][:, :], in_=PS[k4][:, :], func=AF.Square), reads=[PSN[k4]], writes=["sgk%d" % k4])
                            p.op("dve", lambda e, k4=k4: e.tensor_copy(out=ksb[k4][:, :], in_=sgk[k4][:, :]), reads=["sgk%d" % k4], writes=["ksb%d" % k4])
                            p.op("pe", lambda e, b=b, k4=k4: e.matmul(PS[4 + b][:, :], bd64[:, :], ksb[k4][:, :], start=True, stop=True),
                                 reads=["ksb%d" % k4, "bd64"], writes=[PSN[4 + b]])
                            rstd_from(4 + b, 64, sgk[k4], "sgk%d" % k4)
                            p.op("dve", lambda e, k4=k4, br=br: e.scalar_tensor_tensor(out=ksb[k4][:, :], in0=PS[k4][:, :], scalar=kn_t[:, br:br + 1],
                                                                                      in1=sgk[k4][:, :], op0=ALU.mult, op1=ALU.mult),
                                 reads=[PSN[k4], "sgk%d" % k4, "kn"], writes=["ksb%d" % k4])
                            p.dma("pool", lambda e, g=g, k4=k4, t0=t0, dst=dst: e.dma_start(out=dst.ap()[g][:, t0:t0 + 512], in_=ksb[k4][0:64, :]),
                                  reads=["ksb%d" % k4], writes=["ksT" if br == 1 else "kwT"])
                    for tb in range(4):
                        b = tb % 2

                        def mm(e, tb=tb, b=b):
                            for c in range(8):
                                ins = e.matmul(PS[6 + b][:, :], xn[:, c, tb * 128:(tb + 1) * 128], wvt_t[:, c * 512:(c + 1) * 512],
                                               start=(c == 0), stop=(c == 7))
                            return ins
                        p.op("pe", mm, reads=["xn", "wvtt"], writes=[PSN[6 + b]])
                        p.op("act", lambda e, b=b: e.activation(out=vtb[b][:, :], in_=PS[6 + b][:, :], func=AF.Copy), reads=[PSN[6 + b]], writes=["vtb%d" % b])
                        p.dma("pool", lambda e, tb=tb, b=b, ti=ti: e.dma_start(out=vsw.ap()[ti * 4 + tb], in_=vtb[b][:, :]), reads=["vtb%d" % b], writes=["vsw"])
        nper_keep = [1]
        _phase3()
        while late_items:
            s_, d_, n_, i_ = late_items.pop(0)
            p.dma("pool", lambda e, s_=s_, d_=d_, i_=i_: e.dma_start(out=d_.ap()[i_], in_=s_.ap()[i_]), writes=["%s.%d" % (n_, i_)])
        p.barrier()

        def _phase4():
            with contextlib.ExitStack() as st:
                raw = T(st, [128, S], BF16, "raw")
                w1t = T(st, [128, 4096], BF16, "w1t")
                w2kt = T(st, [128, 64], BF16, "w2kt")
                w2vt = T(st, [128, 64], BF16, "w2vt")
                w2f = T(st, [128, 128], F32, "w2f")
                peTt = T(st, [128, 64], BF16, "peTt")
                peTf = T(st, [128, 64], F32, "peTf")
                c1 = T(st, [128, 2], F32, "c1")
                xx = T(st, [128, 512], F32, "xx")
                tt = T(st, [128, 512], F32, "tt")
                hid = T(st, [128, NCP], BF16, "hid")
                kc_f = T(st, [64, 512], F32, "kc_f")
                kc_b = T(st, [64, 512], BF16, "kc_b")
                vc_b = T(st, [128, 64], BF16, "vc_b")
                bdh = T(st, [64, 64], BF16, "bdh")
                p.dma("sp", lambda e: e.dma_start(out=w2f[:, 0:64], in_=w2k.ap()), writes=["w2f"])
                p.dma("sp", lambda e: e.dma_start(out=w2f[:, 64:128], in_=w2v.ap()), writes=["w2f"])
                p.dma("sp", lambda e: e.dma_start(out=peTf[:, :], in_=peT.ap()), writes=["peTf"])
                p.op("dve", lambda e: e.tensor_copy(out=w2kt[:, :], in_=w2f[:, 0:64]), reads=["w2f"], writes=["w2kt"])
                p.op("dve", lambda e: e.tensor_copy(out=w2vt[:, :], in_=w2f[:, 64:128]), reads=["w2f"], writes=["w2vt"])
                p.op("dve", lambda e: e.tensor_copy(out=peTt[:, :], in_=peTf[:, :]), reads=["peTf"], writes=["peTt"])
                p.op("dve", lambda e: e.memset(bdh[:, :], 1.0), writes=["bdh"])
                p.op("dve", lambda e: e.memset(hid[:, :], 0.0), writes=["hid"])
                for kv in range(2):
                    p.dma("sp", lambda e, kv=kv: e.dma_start(out=w1t[:, :], in_=(w1k_b if kv == 0 else w1v_b).ap()[0]),
                          reads=["w1k.0", "w1v.0"], writes=["w1t"])
                    def mmc(e, kv=kv):
                        for l in range(32):
                            ins = e.matmul(PS[6][:, 0:1], w1t[0:64, l * 128:(l + 1) * 128], peTt[0:64, kv * 32 + l:kv * 32 + l + 1], start=(l == 0), stop=(l == 31))
                        return ins
                    p.op("pe", mmc, reads=["w1t", "peTt"], writes=[PSN[6]])
                    p.op("act", lambda e, kv=kv: e.activation(out=c1[:, kv:kv + 1], in_=PS[6][:, 0:1], func=AF.Copy), reads=[PSN[6]], writes=["c1"])
                    for ch in range(2):
                        p.dma("sp", lambda e, kv=kv, ch=ch: e.dma_start(out=raw[:, :], in_=rawT.ap()[kv * 2 + ch]), reads=["rawT"], writes=["raw"])
                        for gg in range(2):
                            g = ch * 2 + gg
                            r0 = gg * 64
                            for n0 in range(0, NCV, 512):
                                n1 = min(NCV, n0 + 512)
                                nn = n1 - n0

                                def mm(e, r0=r0, n0=n0, nn=nn):
                                    for l in range(32):
                                        a0 = l + 16 * n0
                                        ins = e.matmul(PS[0][:, 0:nn], w1t[r0:r0 + 64, l * 128:(l + 1) * 128], raw[r0:r0 + 64, a0:a0 + 16 * (nn - 1) + 1:16],
                                                       start=(l == 0), stop=(l == 31))
                                    return ins
                                p.op("pe", mm, reads=["raw", "w1t"], writes=[PSN[0]])
                                p.op("act", lambda e, nn=nn, kv=kv: e.activation(out=xx[:, 0:nn], in_=PS[0][:, 0:nn], func=AF.Identity, bias=c1[:, kv:kv + 1], scale=1.0),
                                     reads=[PSN[0], "c1"], writes=["xx"])
                                p.op("dve", lambda e, nn=nn: e.tensor_tensor(out=tt[:, 0:nn], in0=xx[:, 0:nn], in1=xx[:, 0:nn], op=ALU.mult), reads=["xx"], writes=["tt"])
                                p.op("dve", lambda e, nn=nn: e.tensor_scalar(out=tt[:, 0:nn], in0=tt[:, 0:nn], scalar1=0.044715, scalar2=1.0, op0=ALU.mult, op1=ALU.add),
                                     reads=["tt"], writes=["tt"])
                                p.op("dve", lambda e, nn=nn: e.tensor_tensor(out=tt[:, 0:nn], in0=tt[:, 0:nn], in1=xx[:, 0:nn], op=ALU.mult), reads=["tt", "xx"], writes=["tt"])
                                p.op("act", lambda e, nn=nn: e.activation(out=tt[:, 0:nn], in_=tt[:, 0:nn], func=AF.Sigmoid, scale=1.5957691216057308),
                                     reads=["tt"], writes=["tt"])
                                p.op("dve", lambda e, nn=nn, n0=n0: e.tensor_tensor(out=hid[:, n0:n0 + nn], in0=tt[:, 0:nn], in1=xx[:, 0:nn], op=ALU.mult),
                                     reads=["tt", "xx"], writes=["hid"])
                            if kv == 0:
                                CW = min(512, NCP)
                                for n0 in range(0, NCP, CW):
                                    p.op("pe", lambda e, n0=n0: e.matmul(PS[1][0:64, 0:CW], w2kt[:, :], hid[:, n0:n0 + CW], start=True, stop=True),
                                         reads=["hid", "w2kt"], writes=[PSN[1]])
                                    p.op("act", lambda e: e.activation(out=kc_f[:, 0:CW], in_=PS[1][0:64, 0:CW], func=AF.Square), reads=[PSN[1]], writes=["kc_f"])
                                    p.op("dve", lambda e: e.tensor_copy(out=kc_b[:, 0:CW], in_=kc_f[:, 0:CW]), reads=["kc_f"], writes=["kc_b"])
                                    p.op("pe", lambda e: e.matmul(PS[2][0:64, 0:CW], bdh[:, :], kc_b[:, 0:CW], start=True, stop=True), reads=["kc_b", "bdh"], writes=[PSN[2]])
                                    p.op("act", lambda e: e.activation(out=kc_f[:, 0:CW], in_=PS[2][0:64, 0:CW], func=AF.Sqrt, bias=eps_t[0:64, 0:1], scale=1.0 / 64),
                                         reads=[PSN[2], "eps"], writes=["kc_f"])
                                    p.op("dve", lambda e: e.reciprocal(out=kc_f[:, 0:CW], in_=kc_f[:, 0:CW]), reads=["kc_f"], writes=["kc_f"])
                                    p.op("dve", lambda e: e.scalar_tensor_tensor(out=kc_b[:, 0:CW], in0=PS[1][0:64, 0:CW], scalar=kn_t[0:64, 0:1], in1=kc_f[:, 0:CW],
                                                                                op0=ALU.mult, op1=ALU.mult), reads=[PSN[1], "kc_f", "kn"], writes=["kc_b"])
                                    p.dma("sp", lambda e, g=g, n0=n0: e.dma_start(out=kcT.ap()[g][:, n0:n0 + CW], in_=kc_b[:, 0:CW]), reads=["kc_b"], writes=["kcT"])
                            else:
                                for nb in range(NCK):
                                    p.op("pe", lambda e, nb=nb: e.matmul(PS[1][:, 0:64], hid[:, nb * 128:(nb + 1) * 128], w2vt[:, :], start=True, stop=True),
                                         reads=["hid", "w2vt"], writes=[PSN[1]])
                                    p.op("act", lambda e: e.activation(out=vc_b[:, :], in_=PS[1][:, 0:64], func=AF.Copy), reads=[PSN[1]], writes=["vc_b"])
                                    p.dma("sp", lambda e, g=g, nb=nb: e.dma_start(out=vcS.ap()[g][:, nb * 64:(nb + 1) * 64], in_=vc_b[:, :]), reads=["vc_b"], writes=["vcS"])
        _phase4()
        p.barrier()

        def _phase5():
            with contextlib.ExitStack() as st:
                h = T(st, [128, 8, 512], F32, "h")
                hb = T(st, [128, 8, 512], F32, "hb")
                sq = T(st, [128, 8, 512], BF16, "sq")
                xn = T(st, [128, 8, 512], BF16, "xn")
                rs = T(st, [128, 512], F32, "rs")
                act = T(st, [128, NJ, 512], BF16, "act")
                sg = [T(st, [128, 512], F32, "sg") for _ in range(2)]
                wpi = [T(st, [128, 2048], BF16, "wpi") for _ in range(4)]
                wpo = [T(st, [128, NJ * 128], BF16, "wpo") for _ in range(3)]
                bufs = (sq, xn, rs, act, sg, wpi, wpo)
                wqt = [T(st, [128, 512], BF16, "wqt") for _ in range(2)]
                wgt = T(st, [128, 384], BF16, "wgt")
                qb_ = [T(st, [64, 512], BF16, "qb") for _ in range(2)]
                bdh = T(st, [64, 64], BF16, "bdh")
                gsb = T(st, [48, 512], BF16, "gsb")
                p.op("dve", lambda e: e.memset(bdh[:, :], 1.0), writes=["bdh"])
                p.dma("sp", lambda e: e.dma_start(out=wgt[:, :], in_=wg_b.ap()[0]), reads=["wg.0"], writes=["wgt"])
                h1v = h1T.ap().rearrange("(c p) s -> p c s", p=128)
                h2v = h2T.ap().rearrange("(c p) s -> p c s", p=128)
                for i in range(NO):
                    ta, tb_ = (2 * i) * 512, (2 * i + 1) * 512
                    p.dma("sp", lambda e, ta=ta: e.dma_start(out=h[:, :, :], in_=h1v[:, :, ta:ta + 512]), reads=["h1T"], writes=HN)
                    p.dma("sp", lambda e, tb_=tb_: e.dma_start(out=hb[:, :, :], in_=h1v[:, :, tb_:tb_ + 512]), reads=["h1T"], writes=["hb"])
                    p.op("dve", lambda e: e.tensor_scalar(out=h[:, :, :], in0=h[:, :, :], scalar1=par_t[:, 1:2], scalar2=None, op0=ALU.mult),
                         reads=HN + ["par"], writes=HN)
                    p.op("dve", lambda e: e.scalar_tensor_tensor(out=h[:, :, :], in0=hb[:, :, :], scalar=par_t[:, 0:1], in1=h[:, :, :], op0=ALU.mult, op1=ALU.add),
                         reads=HN + ["hb", "par"], writes=HN)
                    ffn(h, "h", 2, 32, bufs)
                    p.dma("pool", lambda e, i=i: e.dma_start(out=h2v[:, :, i * 512:(i + 1) * 512], in_=h[:, :, :]), reads=HN, writes=["h2T"])
                    rms_to_xn(h, "h", 40, sq, xn, rs)
                    if "dbg_xn" in dbg and i == 0:
                        dx = nc.dram_tensor("dbg_xn", [128, 4096], BF16, kind="ExternalOutput")
                        p.dma("sp", lambda e: e.dma_start(out=dx.ap(), in_=xn[:, :, :].rearrange("p c s -> p (c s)")), reads=["xn"], writes=["dbg_xn"])
                        dr = nc.dram_tensor("dbg_rs", [128, 512], F32, kind="ExternalOutput")
                        p.dma("sp", lambda e: e.dma_start(out=dr.ap(), in_=rs[:, :]), reads=["rs"], writes=["dbg_rs"])
                    for hd in range(16):
                        b = hd % 2
                        p.dma("sp", lambda e, hd=hd, b=b: e.dma_start(out=wqt[b][:, :], in_=wq_b.ap()[hd]), reads=["wq.%d" % hd], writes=["wqt%d" % b])

                        def mm(e, b=b):
                            for c in range(8):
                                ins = e.matmul(PS[b][0:64, :], wqt[b][:, c * 64:(c + 1) * 64], xn[:, c, :], start=(c == 0), stop=(c == 7))
                            return ins
                        p.op("pe", mm, reads=["xn", "wqt%d" % b], writes=[PSN[b]])
                        p.op("act", lambda e, b=b: e.activation(out=sg[b][0:64, :], in_=PS[b][0:64, :], func=AF.Square), reads=[PSN[b]], writes=["sg%d" % b])
                        p.op("dve", lambda e, b=b: e.tensor_copy(out=qb_[b][:, :], in_=sg[b][0:64, :]), reads=["sg%d" % b], writes=["qb%d" % b])
                        p.op("pe", lambda e, b=b: e.matmul(PS[2 + b][0:64, :], bdh[:, :], qb_[b][:, :], start=True, stop=True), reads=["qb%d" % b, "bdh"], writes=[PSN[2 + b]])
                        rstd_from(2 + b, 64, sg[b], "sg%d" % b, rows=64)
                        p.op("dve", lambda e, b=b: e.scalar_tensor_tensor(out=qb_[b][:, :], in0=PS[b][0:64, :], scalar=qn_t[0:64, 0:1], in1=sg[b][0:64, :],
                                                                         op0=ALU.mult, op1=ALU.mult), reads=[PSN[b], "sg%d" % b, "qn"], writes=["qb%d" % b])
                        p.dma("pool", lambda e, hd=hd, b=b, i=i: e.dma_start(out=qTs.ap()[hd][:, i * 512:(i + 1) * 512], in_=qb_[b][:, :]),
                              reads=["qb%d" % b], writes=["qTs"])

                    def mmg(e):
                        for c in range(8):
                            ins = e.matmul(PS[4][0:48, :], wgt[:, c * 48:(c + 1) * 48], xn[:, c, :], start=(c == 0), stop=(c == 7))
                        return ins
                    p.op("pe", mmg, reads=["xn", "wgt"], writes=[PSN[4]])
                    p.op("act", lambda e: e.activation(out=gsb[:, :], in_=PS[4][0:48, :], func=AF.Sigmoid), reads=[PSN[4]], writes=["gsb"])
                    p.dma("pool", lambda e, i=i: e.dma_start(out=gsT.ap()[:, i * 512:(i + 1) * 512], in_=gsb[:, :]), reads=["gsb"], writes=["gsT"])
        _phase5()
        p.barrier()

        def _phase6():
            with contextlib.ExitStack() as st:
                ks_t = T(st, [128, S], BF16, "ks_t")
                kw_t = T(st, [64, S], BF16, "kw_t")
                kc_t = T(st, [64, NCP], BF16, "kc_t")
                vsw_t = T(st, [128, NKT, 256], BF16, "vsw_t")
                vc_t = T(st, [128, NCK, 128], BF16, "vc_t")
                ov_t = T(st, [128, NCK * (NSB + 1)], BF16, "ov_t")
                ov_f = T(st, [128, NCK * (NSB + 1)], F32, "ov_f")
                selg_t = T(st, [48, 48 * 128], BF16, "selg_t")
                negT = T(st, [128, SO], BF16, "negT")
                negS = T(st, [128, SO], BF16, "negS")
                sc2w = T(st, [128, 256], F32, "sc2w")
                impacc = T(st, [128, 4, NSB], F32, "impacc")
                qa = [[T(st, [128, 512], BF16, "qa") for _ in range(2)] for _ in range(2)]
                gs_t = T(st, [48, 512], BF16, "gs_t")
                ms_t = T(st, [128, W_S], BF16, "ms_t")
                mw_t = T(st, [128, W_W], BF16, "mw_t")
                mc_t = [T(st, [128, 512], BF16, "mc_t") for _ in range(8)]
                e_t = [T(st, [128, 512], BF16, "e_t") for _ in range(10)]
                p_t = [T(st, [128, 512], BF16, "p_t") for _ in range(10)]
                rl = T(st, [128, 4], F32, "rl")
                sA = T(st, [128, NSB], F32, "sA")
                sB = T(st, [128, NSB], F32, "sB")
                sc = T(st, [128, NSB], F32, "sc")
                sc2 = T(st, [128, NSB], F32, "sc2")
                m8a = T(st, [128, 8], F32, "m8a")
                m8b = T(st, [128, 8], F32, "m8b")
                gbb = [T(st, [128, 3, 512], F32, "gb") for _ in range(2)]
                rr2b = [T(st, [128, 512], F32, "rr2") for _ in range(3)]
                rr = T(st, [64, 512], F32, "rr")
                acc = T(st, [64, 512], F32, "acc")
                ocb = T(st, [64, 512], BF16, "ocb")
                ones64 = ones_bf[:, 0:64]
                p.dma("sp", lambda e: e.dma_start(out=ks_t[64:128, :], in_=expd_b.ap()[0]), reads=["expd.0"], writes=["kpat"])
                p.dma("sp", lambda e: e.dma_start(out=selg_t[:, :], in_=selg_b.ap()[0]), reads=["selg.0"], writes=["selg_t"])
                p.dma("sp", lambda e: e.dma_start(out=ov_f[:, :], in_=ovm.ap()), writes=["ov_f"])
                p.op("dve", lambda e: e.tensor_copy(out=ov_t[:, :], in_=ov_f[:, :]), reads=["ov_f"], writes=["ov_t"])
                cnt = [0]
                p.op("dve", lambda e: e.memset(vsw_t[:, :, :], 1.0), writes=["kgrp"])
                p.op("dve", lambda e: e.memset(vc_t[:, :, :], 1.0), writes=["kgrp"])
                for k_ in range(3):
                    p.op("dve", lambda e, k_=k_: e.memset(rr2b[k_][:, :], 0.0), writes=["rr2.%d" % k_])

                def c_chunks(i):
                    return [nk for nk in range(NCK) if (2 * i + 2) * 512 - 1 >= 2048 * nk + 31]

                mcc = [0]

                def load_mc(hd, i, nk):
                    b = mcc[0] % 8
                    mcc[0] += 1
                    D_ = 1024 * i - 2048 * nk
                    j0 = min(D_, JSAT_C)
                    src = bass.AP(tensor=rep_c, offset=hd * 128 * L_C + 2032 + j0, ap=[[L_C - 16, 128], [1, 512]])
                    p.dma("sp", lambda e, b=b, src=src: e.dma_start(out=mc_t[b][:, :], in_=src), reads=["rep_c.%d" % hd], writes=["mc_t%d" % b])
                    return b

                def score_unit(hd, kmat, kcol0, qbuf, qn_, mask_ap, mask_res, neg_cols=None):
                    b = cnt[0] % 10
                    pb = cnt[0] % 4
                    u3 = cnt[0] % 3
                    cnt[0] += 1

                    p.op("pe", lambda e: e.matmul(PS[pb][:, :], kmat[:, kcol0:kcol0 + 128], qbuf, start=True, stop=True),
                         reads=["kgrp", "kpat", qn_], writes=[PSN[pb]])
                    p.op("act", lambda e: e.activation(out=e_t[b][:, :], in_=PS[pb][:, :], func=AF.Exp, scale=0.125), reads=[PSN[pb]], writes=["e_t%d" % b])
                    eng = "pool" if u3 == 2 else "dve"
                    p.op(eng, lambda e: e.tensor_tensor(out=p_t[b][:, :], in0=e_t[b][:, :], in1=mask_ap, op=ALU.mult),
                         reads=["e_t%d" % b] + mask_res, writes=["p_t%d" % b])
                    return b

                for g in range(4):
                    p.dma("sp", lambda e, g=g: e.dma_start(out=ks_t[0:64, :], in_=ksT.ap()[g]), reads=["ksT"], writes=["kgrp"])
                    p.dma("sp", lambda e, g=g: e.dma_start(out=kw_t[:, :], in_=kwT.ap()[g]), reads=["kwT"], writes=["kgrp"])
                    p.dma("sp", lambda e, g=g: e.dma_start(out=kc_t[:, :], in_=kcT.ap()[g]), reads=["kcT"], writes=["kgrp"])
                    p.dma("sp", lambda e, g=g: e.dma_start(out=vc_t[:, :, 0:64], in_=vcS.ap()[g].rearrange("p (k c) -> p k c", c=64)), reads=["vcS"], writes=["kgrp"])
                    vsrc = vsw.ap().rearrange("k p c -> p k c")
                    for k0 in range(0, NKT, 8):
                        p.dma("sp", lambda e, g=g, k0=k0: e.dma_start(out=vsw_t[:, k0:k0 + 8, 0:64], in_=vsrc[:, k0:k0 + 8, g * 64:(g + 1) * 64]),
                              reads=["vsw"], writes=["kgrp"])
                        p.dma("sp", lambda e, g=g, k0=k0: e.dma_start(out=vsw_t[:, k0:k0 + 8, 128:192], in_=vsrc[:, k0:k0 + 8, 256 + g * 64:256 + (g + 1) * 64]),
                              reads=["vsw"], writes=["kgrp"])
                    for i in range(NO):
                        chunks = c_chunks(i)
                        for hp in range(2):
                            pbs_h = {}
                            for hh in (2 * hp, 2 * hp + 1):
                                hd = g * 4 + hh
                                qb = hh % 2
                                p.dma("sp", lambda e, hd=hd, i=i, qb=qb: e.dma_start(out=qa[0][qb][0:64, :], in_=qTs.ap()[hd][:, i * 512:(i + 1) * 512]),
                                      reads=["qTs"], writes=["qa%d" % qb])
                                pbs = []
                                for nk in chunks:
                                    mb = load_mc(hd, i, nk)
                                    pbs.append(score_unit(hd, kc_t, nk * 128, qa[0][qb][0:64, :], "qa%d" % qb, mc_t[mb][:, :], ["mc_t%d" % mb]))
                                pbs_h[hh] = pbs
                            for hh in (2 * hp, 2 * hp + 1):
                                hd = g * 4 + hh
                                pbs = pbs_h[hh]
                                for qq in range(4):
                                    bank = 4 + qq // 2
                                    col = (qq % 2) * (NSB + 1)

                                    def mm(e, qq=qq, bank=bank, col=col, pbs=pbs, chunks=chunks):
                                        for ii, nk in enumerate(chunks):
                                            ins = e.matmul(PS[bank][:, col:col + NSB + 1], p_t[pbs[ii]][:, qq * 128:(qq + 1) * 128],
                                                           ov_t[:, nk * (NSB + 1):(nk + 1) * (NSB + 1)], start=(ii == 0), stop=(ii == len(chunks) - 1))
                                        return ins
                                    p.op("pe", mm, reads=["p_t%d" % b_ for b_ in pbs] + ["ov_t"], writes=[PSN[bank]])
                                    p.op("dve", lambda e, qq=qq, bank=bank, col=col: e.tensor_scalar(out=rl[:, qq:qq + 1], in0=PS[bank][:, col + NSB:col + NSB + 1],
                                                                                                    scalar1=1e-30, scalar2=None, op0=ALU.add),
                                         reads=[PSN[bank]], writes=["rl"])
                                    p.op("dve", lambda e, qq=qq: e.reciprocal(out=rl[:, qq:qq + 1], in_=rl[:, qq:qq + 1]), reads=["rl"], writes=["rl"])
                                    if hh == 0:
                                        p.op("dve", lambda e, qq=qq, bank=bank, col=col, i=i: e.tensor_scalar(out=impacc[:, qq, :], in0=PS[bank][:, col:col + NSB],
                                                                                                             scalar1=rl[:, qq:qq + 1], scalar2=None, op0=ALU.mult),
                                             reads=[PSN[bank], "rl"], writes=["impacc"])
                                    else:
                                        p.op("dve", lambda e, qq=qq, bank=bank, col=col, i=i: e.scalar_tensor_tensor(out=impacc[:, qq, :], in0=PS[bank][:, col:col + NSB],
                                                                                                                    scalar=rl[:, qq:qq + 1], in1=impacc[:, qq, :],
                                                                                                                    op0=ALU.mult, op1=ALU.add),
                                             reads=[PSN[bank], "rl", "impacc"], writes=["impacc"])

                        for qq in range(4):
                            qi = i * 4 + qq
                            p.dma("sp", lambda e, qi=qi: e.dma_start(out=sA[:, :], in_=selA.ap()[qi]), writes=["sA"])
                            p.dma("sp", lambda e, qi=qi: e.dma_start(out=sB[:, :], in_=selB.ap()[qi]), writes=["sB"])
                            p.op("dve", lambda e, qq=qq: e.tensor_tensor(out=sc[:, :], in0=impacc[:, qq, :], in1=sA[:, :], op=ALU.mult), reads=["impacc", "sA"], writes=["sc"])
                            p.op("dve", lambda e: e.tensor_tensor(out=sc[:, :], in0=sc[:, :], in1=sB[:, :], op=ALU.add), reads=["sc", "sB"], writes=["sc"])
                            p.op("dve", lambda e: e.max(out=m8a[:, :], in_=sc[:, :]), reads=["sc"], writes=["m8a"])
                            p.op("dve", lambda e: e.match_replace(out=sc2[:, :], in_to_replace=m8a[:, :], in_values=sc[:, :], imm_value=-3e38),
                                 reads=["sc", "m8a"], writes=["sc2"])
                            p.op("dve", lambda e: e.max(out=m8b[:, :], in_=sc2[:, :]), reads=["sc2"], writes=["m8b"])
                            p.op("dve", lambda e: e.tensor_scalar(out=sc2[:, :], in0=sc[:, :], scalar1=m8b[:, 7:8], scalar2=-MASKV, op0=ALU.is_ge, op1=ALU.mult),
                                 reads=["sc", "m8b"], writes=["sc2"])
                            p.op("dve", lambda e: e.tensor_scalar(out=sc2w[:, 0:128], in0=sc2[:, :], scalar1=MASKV, scalar2=None, op0=ALU.add), reads=["sc2"], writes=["sc2w"])
                            p.op("dve", lambda e: e.tensor_scalar(out=sc2w[:, 128:256], in0=sc2[:, :], scalar1=MASKV, scalar2=None, op0=ALU.add), reads=["sc2"], writes=["sc2w"])

                            def tr2(e):
                                e.transpose(out=PS[6][:, 0:128], in_=sc2w[:, 0:128], identity=ident_t[:, :])
                                return e.transpose(out=PS[6][:, 128:256], in_=sc2w[:, 64:192], identity=ident_t[:, :])
                            p.op("pe", tr2, reads=["sc2w", "ident"], writes=[PSN[6]])
                            p.op("act", lambda e, qi=qi: e.activation(out=negT[64:128, qi * 128:(qi + 1) * 128], in_=PS[6][64:128, 0:128], func=AF.Copy),
                                 reads=[PSN[6]], writes=["negT"])
                            p.op("act", lambda e, qi=qi: e.activation(out=negS[64:128, qi * 128:(qi + 1) * 128], in_=PS[6][64:128, 128:256], func=AF.Copy),
                                 reads=[PSN[6]], writes=["negT"])
                    LA = 8
                    DEFER = 7
                    epi = [0]
                    for hh in range(4):
                        hd = g * 4 + hh
                        src_s = bass.AP(tensor=rep_s, offset=hd * 128 * L_S + 127, ap=[[L_S - 1, 128], [1, W_S]])
                        src_w = bass.AP(tensor=rep_w, offset=hd * 128 * L_W + 127, ap=[[L_W - 1, 128], [1, W_W]])
                        p.dma("sp", lambda e, src_s=src_s: e.dma_start(out=ms_t[:, :], in_=src_s), reads=["rep_s.%d" % hd], writes=["ms_t"])
                        p.dma("sp", lambda e, src_w=src_w: e.dma_start(out=mw_t[:, :], in_=src_w), reads=["rep_w.%d" % hd], writes=["mw_t"])
                        pend = []

                        pend2 = []

                        def emit_pv(un, b):
                            ob = 4 + un["br"]
                            first, last, vap, br, gi, hd_, i_ = un["first"], un["last"], un["vap"], un["br"], un["gi"], un["hd"], un["i"]
                            p.op("pe", lambda e: e.matmul(PS[ob][:, :], vap, p_t[b][:, :], start=first, stop=last),
                                 reads=["p_t%d" % b, "kgrp"], writes=[PSN[ob]])
                            if last:
                                gbn = "gb%d" % gi
                                k_ = epi[0] % 3
                                epi[0] += 1
                                rr2 = rr2b[k_]
                                rn = "rr2.%d" % k_
                                p.op("dve", lambda e: e.tensor_scalar(out=rr2[64:128, :], in0=PS[ob][64:128, :], scalar1=1e-18, scalar2=None, op0=ALU.max), reads=[PSN[ob]], writes=[rn])
                                p.op("act", lambda e: e.activation(out=rr2[64:128, :], in_=rr2[64:128, :], func=AF.Ln), reads=[rn], writes=[rn])
                                p.op("act", lambda e: e.activation(out=rr2[64:128, :], in_=rr2[64:128, :], func=AF.Exp, scale=-1.0), reads=[rn], writes=[rn])
                                p.op("dve", lambda e: e.tensor_tensor(out=rr2[64:128, :], in0=rr2[64:128, :], in1=gbb[gi][64:128, br, :], op=ALU.mult), reads=[rn, gbn], writes=[rn])

                                def stage2():
                                    p.op("pe", lambda e: e.matmul(PS[7][0:64, :], ident_t[:, 64:128], rr2[:, :], start=True, stop=True), reads=[rn, "ident"], writes=[PSN[7]])
                                    p.op("act", lambda e: e.activation(out=rr[:, :], in_=PS[7][0:64, :], func=AF.Copy), reads=[PSN[7]], writes=["rr"])
                                    if br == 0:
                                        p.op("dve", lambda e: e.tensor_tensor(out=acc[:, :], in0=PS[ob][0:64, :], in1=rr[:, :], op=ALU.mult), reads=[PSN[ob], "rr"], writes=["acc"])
                                    else:
                                        p.op("dve", lambda e: e.tensor_tensor(out=rr[:, :], in0=PS[ob][0:64, :], in1=rr[:, :], op=ALU.mult), reads=[PSN[ob], "rr"], writes=["rr"])
                                        p.op("dve", lambda e: e.tensor_tensor(out=acc[:, :], in0=acc[:, :], in1=rr[:, :], op=ALU.add), reads=["acc", "rr"], writes=["acc"])
                                    if br == 2:
                                        p.op("act", lambda e: e.activation(out=ocb[:, :], in_=acc[:, :], func=AF.Copy), reads=["acc"], writes=["ocb"])
                                        p.dma("sp", lambda e: e.dma_start(out=ocT.ap()[hd_][:, i_ * 512:(i_ + 1) * 512], in_=ocb[:, :]), reads=["ocb"], writes=["ocT"])
                                pend2.append([DEFER + 1, stage2])
                            for it in pend2:
                                it[0] -= 1
                            while pend2 and pend2[0][0] <= 0:
                                pend2.pop(0)[1]()

                        for i in range(NO):
                            qb = i % 2
                            gi = i % 2
                            for v_ in range(2):
                                p.dma("sp", lambda e, hd=hd, i=i, qb=qb, v_=v_: e.dma_start(out=qa[v_][qb][0:64, :], in_=qTs.ap()[hd][:, i * 512:(i + 1) * 512]),
                                      reads=["qTs"], writes=["qa%d" % qb])
                            p.op("pool", lambda e, i=i, qb=qb: e.tensor_copy(out=qa[0][qb][64:128, :], in_=negS[64:128, i * 512:(i + 1) * 512]), reads=["negT"], writes=["qa%d" % qb])
                            p.op("pool", lambda e, i=i, qb=qb: e.tensor_copy(out=qa[1][qb][64:128, :], in_=negT[64:128, i * 512:(i + 1) * 512]), reads=["negT"], writes=["qa%d" % qb])
                            p.dma("sp", lambda e, i=i: e.dma_start(out=gs_t[:, :], in_=gsT.ap()[:, i * 512:(i + 1) * 512]), reads=["gsT"], writes=["gs_t"])
                            for br in range(3):
                                p.op("pe", lambda e, br=br, hd=hd: e.matmul(PS[7][:, :], selg_t[:, (hd * 3 + br) * 128:(hd * 3 + br + 1) * 128], gs_t[:, :],
                                                                           start=True, stop=True), reads=["selg_t", "gs_t"], writes=[PSN[7]])
                                p.op("act", lambda e, br=br, gi=gi: e.activation(out=gbb[gi][64:128, br, :], in_=PS[7][64:128, :], func=AF.Copy), reads=[PSN[7]], writes=["gb%d" % gi])
                            units = []
                            for br in range(3):
                                if br == 0:
                                    kts = c_chunks(i)
                                elif br == 1:
                                    kts = list(range((2 * i + 2) * 4))
                                else:
                                    kts = list(range(max(0, (2 * i - 1) * 4), (2 * i + 2) * 4))
                                for ui, kt in enumerate(kts):
                                    units.append(dict(br=br, kt=kt, first=(ui == 0), last=(ui == len(kts) - 1), gi=gi, hd=hd, i=i))
                            for un in units:
                                kt, br = un["kt"], un["br"]
                                if br == 0:
                                    mb = load_mc(hd, i, kt)
                                    b = score_unit(hd, kc_t, kt * 128, qa[0][qb][0:64, :], "qa%d" % qb, mc_t[mb][:, :], ["mc_t%d" % mb])
                                    un["vap"] = vc_t[:, kt, :]
                                elif br == 1:
                                    j0 = min(1024 * i - 128 * kt + OFFS - 127, JSAT_S)
                                    b = score_unit(hd, ks_t, kt * 128, qa[kt // 32][qb][:, :], "qa%d" % qb, ms_t[:, j0:j0 + 512], ["ms_t"])
                                    un["vap"] = vsw_t[:, kt, 0:128]
                                else:
                                    j0 = 1024 * i - 128 * kt + OFFW - 127
                                    b = score_unit(hd, kw_t, kt * 128, qa[0][qb][0:64, :], "qa%d" % qb, mw_t[:, j0:j0 + 512], ["mw_t"])
                                    un["vap"] = vsw_t[:, kt, 128:256]
                                pend.append((un, b))
                                if len(pend) > LA:
                                    emit_pv(*pend.pop(0))
                        while pend:
                            emit_pv(*pend.pop(0))
                        while pend2:
                            pend2.pop(0)[1]()
        _phase6()
        p.barrier()

        def _phase7():
            with contextlib.ExitStack() as st:
                h = T(st, [128, 8, 512], F32, "h")
                sq = T(st, [128, 8, 512], BF16, "sq")
                xn = T(st, [128, 8, 512], BF16, "xn")
                rs = T(st, [128, 512], F32, "rs")
                act = T(st, [128, NJ, 512], BF16, "act")
                sg = [T(st, [128, 512], F32, "sg") for _ in range(2)]
                wpi = [T(st, [128, 2048], BF16, "wpi") for _ in range(4)]
                wpo = [T(st, [128, NJ * 128], BF16, "wpo") for _ in range(3)]
                bufs = (sq, xn, rs, act, sg, wpi, wpo)
                wo_t = T(st, [64, 16, 1024], BF16, "wo_t")
                oc_t = T(st, [64, 16, 512], BF16, "oc_t")
                p.dma("sp", lambda e: e.dma_start(out=wo_t[:, :, :], in_=wo_b.ap().rearrange("h p n -> p h n")), reads=["wo.%d" % i for i in range(16)], writes=["wo_t"])
                h2v = h2T.ap().rearrange("(c p) s -> p c s", p=128)
                ov = outT.ap().rearrange("(c p) s -> p c s", p=128)
                for i in range(NO):
                    p.dma("sp", lambda e, i=i: e.dma_start(out=h[:, :, :], in_=h2v[:, :, i * 512:(i + 1) * 512]), reads=["h2T"], writes=HN)
                    p.dma("sp", lambda e, i=i: e.dma_start(out=oc_t[:, :, :], in_=ocT.ap().rearrange("h p s -> p h s")[:, :, i * 512:(i + 1) * 512]),
                          reads=["ocT"], writes=["oc_t"])
                    for d in range(8):
                        b = d % 2

                        def mm(e, d=d, b=b):
                            for hd in range(16):
                                ins = e.matmul(PS[b][:, :], wo_t[:, hd, d * 128:(d + 1) * 128], oc_t[:, hd, :], start=(hd == 0), stop=(hd == 15))
                            return ins
                        p.op("pe", mm, reads=["wo_t", "oc_t"], writes=[PSN[b]])
                        p.op("dve", lambda e, d=d, b=b: e.tensor_tensor(out=h[:, d, :], in0=PS[b][:, :], in1=h[:, d, :], op=ALU.add), reads=[PSN[b], "h.%d" % d], writes=["h.%d" % d])
                    ffn(h, "h", 3, 48, bufs)
                    p.dma("pool", lambda e, i=i: e.dma_start(out=ov[:, :, i * 512:(i + 1) * 512], in_=h[:, :, :]), reads=HN, writes=["outT"])
        _phase7()
        p.barrier()
        p.run()
    return nc


def _rel_bucket_table(n):
    import jax
    import jax.numpy as jnp
    import math
    with jax.default_device(jax.devices("cpu")[0]):
        rel = jnp.arange(n)
        nn = jnp.maximum(rel, 0)
        nf = jnp.maximum(nn, 1).astype(jnp.float32)
        large = 16 + (jnp.log(nf / 16) / math.log(4096 / 16) * 16).astype(jnp.int32)
        large = jnp.minimum(large, 31)
        return np.asarray(jnp.where(nn < 16, nn, large))


def prep_shared(S, inputs):
    f = lambda a: np.ascontiguousarray(a, dtype=np.float32)
    NSB = S // 64
    NCP = S // 16
    NCK = NCP // 128
    sh = {}

    def col8(g):
        return g.reshape(8, 128).T
    norms = [inputs["ffn1_norm"][0], inputs["mix_norm"][0], inputs["ffn2_norm"][0], inputs["kv_norm"],
             inputs["ffn1_norm"][1], inputs["mix_norm"][1], inputs["ffn2_norm"][1]]
    sh["gains"] = f(np.concatenate([col8(np.asarray(g)) for g in norms], axis=1))

    def win_r(w):
        w = np.asarray(w).reshape(8, 128, 2, NJ, 128)
        return f(w.transpose(3, 1, 2, 0, 4).reshape(NJ, 128, 2048))

    def wout_r(w):
        w = np.asarray(w).reshape(NJ, 128, 8, 128)
        return f(w.transpose(2, 1, 0, 3).reshape(8, 128, NJ * 128))
    ffn_list = [("ffn1_w_in", "ffn1_w_out", 0), ("ffn2_w_in", "ffn2_w_out", 0), ("ffn1_w_in", "ffn1_w_out", 1), ("ffn2_w_in", "ffn2_w_out", 1)]
    for i, (a, b, l) in enumerate(ffn_list):
        sh["win%d" % i] = win_r(inputs[a][l])
        sh["wout%d" % i] = wout_r(inputs[b][l])

    def sq_r(w, ncol):
        w = np.asarray(w).reshape(8, 128, ncol, 128)
        return w.transpose(2, 1, 0, 3)
    ci = np.asarray(inputs["conv_w_in"][0]).reshape(8, 128, 3, 8, 128)
    sh["cin"] = f(ci.transpose(3, 1, 2, 0, 4).reshape(8, 128, 3072))
    sh["cout"] = f(sq_r(inputs["conv_w_out"][0], 8).reshape(8, 128, 1024))
    cw = np.asarray(inputs["conv_w"][0])
    sh["cwk"] = f(cw.reshape(3, 8, 128).transpose(2, 1, 0).reshape(128, 24))
    wkv = np.asarray(inputs["w_kv"])
    units = []
    for c0 in (0, 128, 256, 384):
        units.append(wkv[:, c0:c0 + 128])
    for br in (1, 2):
        for g in range(4):
            cc = wkv[:, br * 512 + g * 64: br * 512 + (g + 1) * 64]
            units.append(np.concatenate([cc, cc], axis=1))
    sh["wkf"] = f(np.stack([u.reshape(8, 128, 128).transpose(1, 0, 2).reshape(128, 1024) for u in units]))
    wv = np.concatenate([wkv[:, 768:1024], wkv[:, 1280:1536]], axis=1)
    sh["wvt"] = f(wv.reshape(8, 128, 512).transpose(1, 0, 2).reshape(1, 128, 4096))
    wqf = np.asarray(inputs["attn_w_q"][0])
    sh["wq"] = f(np.stack([wqf[:, h * 64:(h + 1) * 64].reshape(8, 128, 64).transpose(1, 0, 2).reshape(128, 512) for h in range(16)]))
    sh["wg"] = f(wqf[:, 1024:1072].reshape(8, 128, 48).transpose(1, 0, 2).reshape(1, 128, 384))
    sh["wo"] = f(np.asarray(inputs["attn_w_o"][0]).reshape(16, 64, 1024))
    knm = np.asarray(inputs["k_norm"])
    sh["kn"] = f(np.concatenate([knm.T, knm.T], axis=0))
    qnm = np.asarray(inputs["attn_q_norm"][0])
    sh["qn"] = f(np.concatenate([qnm, qnm])[:, None])

    def w1_r(w):
        w = np.asarray(w).reshape(32, 64, 128).transpose(1, 0, 2).reshape(64, 4096)
        return f(np.concatenate([w, w], axis=0)[None])
    sh["w1k"] = w1_r(inputs["cmp_w1_k"])
    sh["w1v"] = w1_r(inputs["cmp_w1_v"])
    sh["w2k"] = f(inputs["cmp_w2_k"])
    sh["w2v"] = f(inputs["cmp_w2_v"])
    pk = np.asarray(inputs["cmp_pe_k"]).T
    pv = np.asarray(inputs["cmp_pe_v"]).T
    pe = np.concatenate([pk, pv], axis=1)
    sh["peT"] = f(np.concatenate([pe, pe], axis=0))
    assert NSB == 128
    ex = np.zeros((1, 64, S), np.float32)
    for j in range(NSB):
        ex[0, j % 64, j * 64:(j + 1) * 64] = 1.0
    sh["expd"] = ex
    n = np.arange(NCP)
    cs = n * 16
    ss = np.arange(NSB) * 64
    ovl = np.clip(np.minimum(cs[:, None] + 32, ss[None, :] + 64) - np.maximum(cs[:, None], ss[None, :]), 0, None) / 16.0
    ovl[NCP - 1, :] = 0.0
    ovl1 = np.concatenate([ovl, np.ones((NCP, 1))], axis=1)
    sh["ovm"] = f(ovl1.reshape(NCK, 128, NSB + 1).transpose(1, 0, 2).reshape(128, NCK * (NSB + 1)))
    sg_ = np.zeros((1, 48, 48 * 128), np.float32)
    for c in range(48):
        sg_[0, c, c * 128 + 64:(c + 1) * 128] = 1.0
    sh["selg"] = sg_
    sh["ident"] = np.eye(128, dtype=np.float32)
    return sh


def prep_parity(S, inputs, par):
    f = lambda a: np.ascontiguousarray(a, dtype=np.float32)
    NSB = S // 64
    NO = S // 1024
    pp = {}
    pp["par"] = f(np.tile(np.array([[par, 1 - par]], np.float32), (128, 1)))
    rb = np.asarray(inputs["rel_bias"])
    bt = _rel_bucket_table(max(S, 4096) + 8192)

    def vec(L, off, lo, hi):
        rel = np.arange(L) - off + 512 * par
        ok = (rel >= lo) & (rel < hi)
        idx = bt[np.clip(rel, 0, len(bt) - 1)]
        v = rb[idx, :].T.copy()
        v[:, ~ok] = MASKV
        return f(v)
    pp["bvs"] = vec(L_S, OFFS, 0, 1 << 30)
    pp["bvw"] = vec(L_W, OFFW, 0, 512)
    pp["bvc"] = vec(L_C, OFFC, 0, 1 << 30)
    A = np.zeros((NO * 4, 128, NSB), np.float32)
    B = np.zeros((NO * 4, 128, NSB), np.float32)
    jj = np.arange(NSB)[None, :]
    for i in range(NO):
        for qq in range(4):
            t = (2 * i + par) * 512 + qq * 128 + np.arange(128)
            bt_ = (t // 64)[:, None]
            valid = jj <= bt_
            forced = (jj == 0) | (jj == bt_) | (jj == bt_ - 1)
            A[i * 4 + qq] = (valid & ~forced)
            B[i * 4 + qq] = np.where(valid, np.where(forced, 1e30, 0.0), -1e30)
    pp["selA"] = A
    pp["selB"] = B
    return pp


_CACHE = {}


def run_model(S, inputs, x, dbg=()):
    B = x.shape[0]
    key = (S, tuple(dbg))
    if key not in _CACHE:
        _CACHE[key] = build(S, dbg)
    nc = _CACHE[key]
    sh = prep_shared(S, inputs)
    pps = [prep_parity(S, inputs, 0), prep_parity(S, inputs, 1)]
    in_maps = []
    for core in range(2 * B):
        b, par = core // 2, core % 2
        m = dict(sh)
        m.update(pps[par])
        m["xT"] = np.ascontiguousarray(np.asarray(x[b], dtype=np.float32).T)
        in_maps.append(m)
    res = run_bass_kernel_spmd(nc, in_maps, core_ids=list(range(2 * B)))
    out = np.zeros((B, S, D), np.float32)
    for core in range(2 * B):
        b, par = core // 2, core % 2
        oT = np.asarray(res.results[core]["outT"])
        o = oT.T.reshape(S // 1024, 512, D)
        for i in range(S // 1024):
            t0 = (2 * i + par) * 512
            out[b, t0:t0 + 512] = o[i]
    return out, res


def kernel(**inputs):
    x = np.asarray(inputs["x"])
    out, _ = run_model(x.shape[1], inputs, x)
    return out
```

```python
import contextlib
import numpy as np
import concourse.bass as bass
import concourse.mybir as mybir
from concourse.bass_utils import run_bass_kernel_spmd

F32 = mybir.dt.float32
BF16 = mybir.dt.bfloat16
AF = mybir.ActivationFunctionType
ALU = mybir.AluOpType

ENGS = ("pe", "act", "dve", "pool", "sp")
NDSEM = 32
DSEM_Q = {"sp": 24, "pool": 8, "act": 4}
SAME_ENGINE_SYNC = True

D = 1024
DFF = 2816
NJ = 22
EPS = 1e-6
OFFS, OFFW, OFFC = 1023, 1535, 2063
JSAT_S = 2896 + OFFS
W_S = JSAT_S + 512
L_S = W_S + 127
W_W = 2432
L_W = W_W + 127
JSAT_C = 4959
W_C = JSAT_C + 512
L_C = W_C + 2032
MASKV = -30000.0


class Prog:
    def __init__(self, nc):
        self.nc = nc
        self.ops = {e: [] for e in ENGS}
        self.cnt = {e: 0 for e in ENGS}
        self.waited = {e: {} for e in ENGS}
        self.res_w = {}
        self.res_r = {}
        self.ndma = 0
        self.ndma_q = {}

    def _deps(self, reads, writes):
        deps = {}

        def add(d):
            if d is None:
                return
            k, v = d
            if deps.get(k, 0) < v:
                deps[k] = v
        for r in reads:
            add(self.res_w.get(r))
        for w in writes:
            add(self.res_w.get(w))
            for k, v in self.res_r.get(w, {}).items():
                add((k, v))
        return deps

    def _commit(self, reads, writes, me):
        for r in reads:
            d = self.res_r.setdefault(r, {})
            if d.get(me[0], 0) < me[1]:
                d[me[0]] = me[1]
        for w in writes:
            self.res_w[w] = me
            self.res_r[w] = {}

    def op(self, eng, fn, reads=(), writes=()):
        deps = self._deps(reads, writes)
        waits = []
        for k, v in deps.items():
            if k == eng and (eng == "pe" or not SAME_ENGINE_SYNC):
                continue
            if self.waited[eng].get(k, 0) >= v:
                continue
            self.waited[eng][k] = v
            waits.append((k, v))
        self.cnt[eng] += 1
        me = (eng, self.cnt[eng])
        self.ops[eng].append((waits, fn, me))
        self._commit(reads, writes, me)

    def dma(self, q, fn, reads=(), writes=()):
        deps = self._deps(reads, writes)
        k = self.ndma_q.get(q, 0)
        self.ndma_q[q] = k + 1
        self.ndma += 1
        nsl = DSEM_Q[q]
        slot = k % nsl
        val = 16 * (k // nsl + 1)
        key = ("d", q, slot)
        if val > 16:
            deps[key] = max(deps.get(key, 0), val - 16)
        waits = []
        for kk, v in deps.items():
            if self.waited[q].get(kk, 0) >= v:
                continue
            self.waited[q][kk] = v
            waits.append((kk, v))
        me = (key, val)
        self.ops[q].append((waits, fn, me))
        self._commit(reads, writes, me)

    def barrier(self):
        tgt = [(e, self.cnt[e]) for e in ENGS if self.cnt[e] > 0]
        for q, n in self.ndma_q.items():
            nsl = DSEM_Q[q]
            for k in range(max(0, n - nsl), n):
                tgt.append((("d", q, k % nsl), 16 * (k // nsl + 1)))
        for e in ENGS:
            waits = []
            for k, v in tgt:
                if k == e:
                    continue
                if self.waited[e].get(k, 0) >= v:
                    continue
                self.waited[e][k] = v
                waits.append((k, v))
            if waits:
                self.ops[e].append((waits, None, None))

    def run(self):
        nc = self.nc
        def _phase1():
            with contextlib.ExitStack() as st:
                sems = {}
                for e in ENGS:
                    sems[e] = st.enter_context(nc.semaphore("s_" + e))
                for q_, n_ in DSEM_Q.items():
                    for i in range(n_):
                        sems[("d", q_, i)] = st.enter_context(nc.semaphore("d_%s_%d" % (q_, i)))
                block = st.enter_context(nc.Block())

                def body(ename):
                    def f(eng):
                        for waits, fn, me in self.ops[ename]:
                            for k, v in waits:
                                eng.wait_ge(sems[k], v)
                            if fn is None:
                                continue
                            ins = fn(eng)
                            ins.then_inc(sems[me[0]], 16 if isinstance(me[0], tuple) else 1)
                    return f
                block.tensor(body("pe"))
                block.scalar(body("act"))
                block.vector(body("dve"))
                block.gpsimd(body("pool"))
                block.sync(body("sp"))


        _phase1()
def build(S, dbg=()):
    NT = S // 512
    NO = NT // 2
    SO = S // 2
    NKT = S // 128
    NSB = S // 64
    NCP = S // 16
    NCK = NCP // 128
    NCV = NCP - 1
    nc = bass.Bass("TRN2", target_bir_lowering=False)
    IN = {}

    def inp(name, shape):
        IN[name] = nc.dram_tensor(name, list(shape), F32, kind="ExternalInput")
        return IN[name]

    def scr(name, shape, dt):
        kind = "ExternalOutput" if name in dbg else "Internal"
        return nc.dram_tensor(name, list(shape), dt, kind=kind)

    xT = inp("xT", [D, S])
    gains = inp("gains", [128, 56])
    par = inp("par", [128, 2])
    wins = [inp("win%d" % i, [NJ, 128, 2048]) for i in range(4)]
    wouts = [inp("wout%d" % i, [8, 128, NJ * 128]) for i in range(4)]
    cin = inp("cin", [8, 128, 3072])
    cout = inp("cout", [8, 128, 1024])
    cwk = inp("cwk", [128, 24])
    wkf = inp("wkf", [12, 128, 1024])
    wvt = inp("wvt", [1, 128, 4096])
    wq = inp("wq", [16, 128, 512])
    wg = inp("wg", [1, 128, 384])
    wo = inp("wo", [16, 64, 1024])
    kn = inp("kn", [128, 3])
    qn = inp("qn", [128, 1])
    w1k = inp("w1k", [1, 128, 4096])
    w1v = inp("w1v", [1, 128, 4096])
    w2k = inp("w2k", [128, 64])
    w2v = inp("w2v", [128, 64])
    peT = inp("peT", [128, 64])
    bvs = inp("bvs", [16, L_S])
    bvw = inp("bvw", [16, L_W])
    bvc = inp("bvc", [16, L_C])
    expd = inp("expd", [1, 64, S])
    ovm = inp("ovm", [128, NCK * (NSB + 1)])
    selg = inp("selg", [1, 48, 48 * 128])
    selA = inp("selA", [NO * 4, 128, NSB])
    selB = inp("selB", [NO * 4, 128, NSB])
    ident_in = inp("ident", [128, 128])
    outT = nc.dram_tensor("outT", [D, SO], F32, kind="ExternalOutput")

    def bfc(name, src):
        return scr(name + "_bf", list(src.shape), BF16)
    wins_b = [bfc("win%d" % i, wins[i]) for i in range(4)]
    wouts_b = [bfc("wout%d" % i, wouts[i]) for i in range(4)]
    cin_b, cout_b, wkf_b, wvt_b = bfc("cin", cin), bfc("cout", cout), bfc("wkf", wkf), bfc("wvt", wvt)
    wq_b, wg_b, wo_b = bfc("wq", wq), bfc("wg", wg), bfc("wo", wo)
    w1k_b, w1v_b, expd_b, selg_b = bfc("w1k", w1k), bfc("w1v", w1v), bfc("expd", expd), bfc("selg", selg)

    h1T = scr("h1T", [D, S], F32)
    h2T = scr("h2T", [D, SO], F32)
    rawT = scr("rawT", [4, 128, S], BF16)
    ksT = scr("ksT", [4, 64, S], BF16)
    kwT = scr("kwT", [4, 64, S], BF16)
    vsw = scr("vsw", [NKT, 128, 512], BF16)
    kcT = scr("kcT", [4, 64, NCP], BF16)
    vcS = scr("vcS", [4, 128, NCK * 64], BF16)
    qTs = scr("qTs", [16, 64, SO], BF16)
    gsT = scr("gsT", [48, SO], BF16)
    ocT = scr("ocT", [16, 64, SO], BF16)
    rep_s = scr("rep_s", [16, 128, L_S], BF16)
    rep_w = scr("rep_w", [16, 128, L_W], BF16)
    rep_c = scr("rep_c", [16, 128, L_C], BF16)

    p = Prog(nc)
    HN = ["h.%d" % c for c in range(8)]
    sb = nc.sbuf_tensor
    uid = [0]

    def T(st, shape, dt, nm="t"):
        uid[0] += 1
        return st.enter_context(sb("%s_%d" % (nm, uid[0]), list(shape), dt))

    with contextlib.ExitStack() as top:
        PS = [top.enter_context(nc.psum_tensor("ps%d" % i, [128, 512], F32)) for i in range(8)]
        PSN = ["ps%d" % i for i in range(8)]
        gains_t = T(top, [128, 56], F32, "gains")
        par_t = T(top, [128, 2], F32, "par")
        cwk_t = T(top, [128, 24], F32, "cwk")
        kn_t = T(top, [128, 3], F32, "kn")
        qn_t = T(top, [128, 1], F32, "qn")
        eps_t = T(top, [128, 1], F32, "eps")
        ones_bf = T(top, [128, 128], BF16, "ones")
        ones_f = T(top, [128, 128], F32, "onesf")
        ident_t = T(top, [128, 128], F32, "ident")
        for dst, src, nm in ((gains_t, gains, "gains"), (par_t, par, "par"), (cwk_t, cwk, "cwk"),
                             (kn_t, kn, "kn"), (qn_t, qn, "qn"), (ident_t, ident_in, "ident")):
            p.dma("sp", lambda e, dst=dst, src=src: e.dma_start(out=dst[:], in_=src.ap()), writes=[nm])
        p.op("dve", lambda e: e.memset(eps_t[:], EPS), writes=["eps"])
        p.op("dve", lambda e: e.memset(ones_bf[:], 1.0), writes=["ones"])
        p.op("dve", lambda e: e.memset(ones_f[:], 1.0), writes=["onesf"])

        def precast(src, dst, nm):
            n0 = src.shape[0]
            for i in range(n0):
                p.dma("pool", lambda e, i=i: e.dma_start(out=dst.ap()[i], in_=src.ap()[i]), writes=["%s.%d" % (nm, i)])
        order = [(wins[0], wins_b[0], "win0"), (wouts[0], wouts_b[0], "wout0"), (cin, cin_b, "cin"), (cout, cout_b, "cout"),
                 (wins[1], wins_b[1], "win1"), (wouts[1], wouts_b[1], "wout1"), (wkf, wkf_b, "wkf"), (wvt, wvt_b, "wvt"),
                 (w1k, w1k_b, "w1k"), (w1v, w1v_b, "w1v"),
                 (wins[2], wins_b[2], "win2"), (wouts[2], wouts_b[2], "wout2"), (wq, wq_b, "wq"), (wg, wg_b, "wg"),
                 (expd, expd_b, "expd"), (selg, selg_b, "selg"), (wo, wo_b, "wo"),
                 (wins[3], wins_b[3], "win3"), (wouts[3], wouts_b[3], "wout3")]
        early, late = order[:10], order[10:]
        for s_, d_, n_ in early:
            precast(s_, d_, n_)
        late_items = []
        for s_, d_, n_ in late:
            for i_ in range(s_.shape[0]):
                late_items.append((s_, d_, n_, i_))

        def _phase2():
            with contextlib.ExitStack() as st:
                vrow = [T(st, [1, L_C], F32, "vrow") for _ in range(2)]
                ev = [T(st, [128, L_C], BF16, "ev") for _ in range(2)]
                k = 0
                for src, rep, L, nm in ((bvs, rep_s, L_S, "rep_s"), (bvw, rep_w, L_W, "rep_w"), (bvc, rep_c, L_C, "rep_c")):
                    for h in range(16):
                        b = k % 2
                        k += 1
                        p.dma("sp", lambda e, b=b, h=h, src=src, L=L: e.dma_start(out=vrow[b][0:1, 0:L], in_=src.ap()[h:h + 1, :]),
                              writes=["vrow%d" % b])
                        for c0 in range(0, L, 512):
                            c1 = min(L, c0 + 512)
                            pb = (c0 // 512) % 2
                            p.op("pe", lambda e, b=b, c0=c0, c1=c1, pb=pb: e.matmul(PS[pb][:, 0:c1 - c0], ones_f[0:1, :], vrow[b][0:1, c0:c1],
                                                                                  start=True, stop=True),
                                 reads=["vrow%d" % b, "onesf"], writes=[PSN[pb]])
                            p.op("act", lambda e, b=b, c0=c0, c1=c1, pb=pb: e.activation(out=ev[b][:, c0:c1], in_=PS[pb][:, 0:c1 - c0], func=AF.Exp),
                                 reads=[PSN[pb]], writes=["ev%d" % b])
                        p.dma("sp", lambda e, b=b, h=h, rep=rep, L=L: e.dma_start(out=rep.ap()[h], in_=ev[b][:, 0:L]),
                              reads=["ev%d" % b], writes=["%s.%d" % (nm, h)])
        _phase2()
        p.barrier()

        def rstd_from(ps_i, n, rs, rsn, rows=128):
            p.op("act", lambda e: e.activation(out=rs[0:rows, :], in_=PS[ps_i][0:rows, :], func=AF.Ln, bias=eps_t[0:rows, 0:1], scale=1.0 / n),
                 reads=[PSN[ps_i], "eps"], writes=[rsn])
            p.op("act", lambda e: e.activation(out=rs[0:rows, :], in_=rs[0:rows, :], func=AF.Exp, scale=-0.5), reads=[rsn], writes=[rsn])

        def rms_to_xn(h, hn, gcol, sq, xn, rs):
            for c in range(8):
                p.op("act", lambda e, c=c: e.activation(out=sq[:, c, :], in_=h[:, c, :], func=AF.Square), reads=["%s.%d" % (hn, c)], writes=["sq"])

            def mm(e):
                for c in range(8):
                    ins = e.matmul(PS[7][:, :], ones_bf[:, :], sq[:, c, :], start=(c == 0), stop=(c == 7))
                return ins
            p.op("pe", mm, reads=["sq", "ones"], writes=[PSN[7]])
            rstd_from(7, D, rs, "rs")
            for c in range(8):
                p.op("dve", lambda e, c=c: e.scalar_tensor_tensor(out=xn[:, c, :], in0=h[:, c, :], scalar=gains_t[:, gcol + c:gcol + c + 1],
                                                                 in1=rs[:, :], op0=ALU.mult, op1=ALU.mult),
                     reads=["%s.%d" % (hn, c), "rs", "gains"], writes=["xn"])

        def ffn(h, hn, li, gcol, bufs):
            sq, xn, rs, act, sg, wpi, wpo = bufs
            rms_to_xn(h, hn, gcol, sq, xn, rs)
            for j in range(NJ):
                b = j % 2
                wb = j % len(wpi)
                p.dma("pool" if (li >= 2 and j % 2 == 1) else "sp", lambda e, j=j, wb=wb: e.dma_start(out=wpi[wb][:, :], in_=wins_b[li].ap()[j]),
                      reads=["win%d.%d" % (li, j)], writes=["wpi%d" % wb])
                for s_ in range(2):
                    def mm(e, s_=s_, b=b, wb=wb):
                        for c in range(8):
                            ins = e.matmul(PS[b * 2 + s_][:, :], wpi[wb][:, (s_ * 8 + c) * 128:(s_ * 8 + c + 1) * 128], xn[:, c, :],
                                           start=(c == 0), stop=(c == 7))
                        return ins
                    p.op("pe", mm, reads=["xn", "wpi%d" % wb], writes=[PSN[b * 2 + s_]])
                p.op("act", lambda e, b=b: e.activation(out=sg[b][:, :], in_=PS[b * 2][:, :], func=AF.Silu), reads=[PSN[b * 2]], writes=["sg%d" % b])
                p.op("dve", lambda e, b=b, j=j: e.tensor_tensor(out=act[:, j, :], in0=sg[b][:, :], in1=PS[b * 2 + 1][:, :], op=ALU.mult),
                     reads=["sg%d" % b, PSN[b * 2 + 1]], writes=["act.%d" % j])
            for d in range(8):
                b = d % 2
                wb = d % len(wpo)
                p.dma("pool" if (li >= 2 and d % 2 == 1) else "sp", lambda e, d=d, wb=wb: e.dma_start(out=wpo[wb][:, :], in_=wouts_b[li].ap()[d]),
                      reads=["wout%d.%d" % (li, d)], writes=["wpo%d" % wb])

                def mm(e, b=b, wb=wb):
                    for j in range(NJ):
                        ins = e.matmul(PS[4 + b][:, :], wpo[wb][:, j * 128:(j + 1) * 128], act[:, j, :], start=(j == 0), stop=(j == NJ - 1))
                    return ins
                p.op("pe", mm, reads=["wpo%d" % wb] + ["act.%d" % j for j in range(NJ)], writes=[PSN[4 + b]])
                p.op("dve", lambda e, d=d, b=b: e.scalar_tensor_tensor(out=h[:, d, :], in0=PS[4 + b][:, :], scalar=0.5, in1=h[:, d, :],
                                                                      op0=ALU.mult, op1=ALU.add),
                     reads=[PSN[4 + b], "%s.%d" % (hn, d)], writes=["%s.%d" % (hn, d)])

        def _phase3():
            with contextlib.ExitStack() as st:
                h = T(st, [128, 8, 512], F32, "h")
                sq = T(st, [128, 8, 512], BF16, "sq")
                xn = T(st, [128, 8, 512], BF16, "xn")
                rs = T(st, [128, 512], F32, "rs")
                act = T(st, [128, NJ, 512], BF16, "act")
                sg = [T(st, [128, 512], F32, "sg") for _ in range(2)]
                wpi = [T(st, [128, 2048], BF16, "wpi") for _ in range(4)]
                wpo = [T(st, [128, NJ * 128], BF16, "wpo") for _ in range(3)]
                bufs = (sq, xn, rs, act, sg, wpi, wpo)
                wci = [T(st, [128, 3072], BF16, "wci") for _ in range(2)]
                wsq = [T(st, [128, 1024], BF16, "wsq") for _ in range(2)]
                u = [T(st, [128, 514], F32, "u") for _ in range(2)]
                ucar = T(st, [128, 8, 2], F32, "ucar")
                yb = [T(st, [128, 512], F32, "yb") for _ in range(2)]
                bgy = T(st, [128, 8, 512], BF16, "bgy")
                wvt_t = T(st, [128, 4096], BF16, "wvt")
                wkf_t = T(st, [128, 12, 1024], BF16, "wkf_t")
                bd64 = T(st, [128, 128], BF16, "bd64")
                ksb = [T(st, [128, 512], BF16, "ksb") for _ in range(4)]
                sgk = [T(st, [128, 512], F32, "sgk") for _ in range(4)]
                vtb = [T(st, [128, 512], BF16, "vtb") for _ in range(2)]
                p.op("dve", lambda e: e.memset(ucar[:], 0.0), writes=["ucar"])
                p.op("dve", lambda e: e.memset(bd64[:], 0.0), writes=["bd64"])
                p.op("dve", lambda e: e.memset(bd64[0:64, 0:64], 1.0), writes=["bd64"])
                p.op("dve", lambda e: e.memset(bd64[64:128, 64:128], 1.0), writes=["bd64"])
                p.dma("sp", lambda e: e.dma_start(out=wvt_t[:, :], in_=wvt_b.ap()[0]), reads=["wvt.0"], writes=["wvtt"])
                p.dma("sp", lambda e: e.dma_start(out=wkf_t[:, :, :], in_=wkf_b.ap().rearrange("u p n -> p u n")), reads=["wkf.%d" % u_ for u_ in range(12)], writes=["wkf_t"])
                xTv = xT.ap().rearrange("(c p) s -> p c s", p=128)
                h1v = h1T.ap().rearrange("(c p) s -> p c s", p=128)
                for ti in range(NT):
                    t0 = ti * 512
                    p.dma("sp", lambda e, t0=t0: e.dma_start(out=h[:, :, :], in_=xTv[:, :, t0:t0 + 512]), writes=HN)
                    if ti >= 1 and late_items:
                        nper = (len(late_items) + max(1, NT - 2) - 1) // max(1, NT - 2) if ti == 1 else nper_keep[0]
                        nper_keep[0] = nper
                        for _ in range(min(nper, len(late_items))):
                            s_, d_, n_, i_ = late_items.pop(0)
                            p.dma("pool", lambda e, s_=s_, d_=d_, i_=i_: e.dma_start(out=d_.ap()[i_], in_=s_.ap()[i_]), writes=["%s.%d" % (n_, i_)])
                    ffn(h, "h", 0, 0, bufs)
                    rms_to_xn(h, "h", 8, sq, xn, rs)
                    for i in range(8):
                        b = i % 2
                        p.dma("sp", lambda e, i=i, b=b: e.dma_start(out=wci[b][:, :], in_=cin_b.ap()[i]), reads=["cin.%d" % i], writes=["wci%d" % b])
                        cb = (0, 1, 2) if i % 2 == 0 else (3, 6, 7)
                        for s_ in range(3):
                            def mm(e, s_=s_, b=b, cb=cb):
                                for c in range(8):
                                    ins = e.matmul(PS[cb[s_]][:, :], wci[b][:, (s_ * 8 + c) * 128:(s_ * 8 + c + 1) * 128], xn[:, c, :],
                                                   start=(c == 0), stop=(c == 7))
                                return ins
                            p.op("pe", mm, reads=["xn", "wci%d" % b], writes=[PSN[cb[s_]]])
                        p.op("dve", lambda e, i=i, b=b: e.tensor_copy(out=u[b][:, 0:2], in_=ucar[:, i, :]), reads=["ucar"], writes=["u%d" % b])
                        p.op("act", lambda e, b=b, cb=cb: e.activation(out=yb[b][:, :], in_=PS[cb[1]][:, :], func=AF.Copy), reads=[PSN[cb[1]]], writes=["yb%d" % b])
                        p.op("dve", lambda e, b=b, cb=cb: e.tensor_tensor(out=u[b][:, 2:514], in0=yb[b][:, :], in1=PS[cb[2]][:, :], op=ALU.mult),
                             reads=["yb%d" % b, PSN[cb[2]]], writes=["u%d" % b])
                        p.op("dve", lambda e, i=i, b=b: e.tensor_copy(out=ucar[:, i, :], in_=u[b][:, 512:514]), reads=["u%d" % b], writes=["ucar"])
                        p.op("dve", lambda e, i=i, b=b: e.tensor_scalar(out=yb[b][:, :], in0=u[b][:, 2:514], scalar1=cwk_t[:, i * 3 + 2:i * 3 + 3], scalar2=None,
                                                                      op0=ALU.mult), reads=["u%d" % b, "cwk"], writes=["yb%d" % b])
                        for w_ in (1, 0):
                            p.op("dve", lambda e, i=i, b=b, w_=w_: e.scalar_tensor_tensor(out=yb[b][:, :], in0=u[b][:, w_:w_ + 512],
                                                                                         scalar=cwk_t[:, i * 3 + w_:i * 3 + w_ + 1], in1=yb[b][:, :],
                                                                                         op0=ALU.mult, op1=ALU.add),
                                 reads=["u%d" % b, "cwk", "yb%d" % b], writes=["yb%d" % b])
                        p.op("dve", lambda e, i=i, b=b, cb=cb: e.tensor_tensor(out=bgy[:, i, :], in0=yb[b][:, :], in1=PS[cb[0]][:, :], op=ALU.mult),
                             reads=["yb%d" % b, PSN[cb[0]]], writes=["bgy.%d" % i])
                    for d in range(8):
                        b = d % 2
                        p.dma("sp", lambda e, d=d, b=b: e.dma_start(out=wsq[b][:, :], in_=cout_b.ap()[d]), reads=["cout.%d" % d], writes=["wsq%d" % b])

                        def mm(e, b=b):
                            for c in range(8):
                                ins = e.matmul(PS[4 + b][:, :], wsq[b][:, c * 128:(c + 1) * 128], bgy[:, c, :], start=(c == 0), stop=(c == 7))
                            return ins
                        p.op("pe", mm, reads=["wsq%d" % b] + ["bgy.%d" % c for c in range(8)], writes=[PSN[4 + b]])
                        p.op("dve", lambda e, d=d, b=b: e.tensor_tensor(out=h[:, d, :], in0=PS[4 + b][:, :], in1=h[:, d, :], op=ALU.add),
                             reads=[PSN[4 + b], "h.%d" % d], writes=["h.%d" % d])
                    ffn(h, "h", 1, 16, bufs)
                    p.dma("pool", lambda e, t0=t0: e.dma_start(out=h1v[:, :, t0:t0 + 512], in_=h[:, :, :]), reads=HN, writes=["h1T"])
                    rms_to_xn(h, "h", 24, sq, xn, rs)
                    for un in range(12):
                        b = un % 2
                        k4 = un % 4
                        def mm(e, un=un, k4=k4):
                            for c in range(8):
                                ins = e.matmul(PS[k4][:, :], wkf_t[:, un, c * 128:(c + 1) * 128], xn[:, c, :], start=(c == 0), stop=(c == 7))
                            return ins
                        p.op("pe", mm, reads=["xn", "wkf_t"], writes=[PSN[k4]])
                        if un < 4:
                            p.op("act", lambda e, k4=k4: e.activation(out=ksb[k4][:, :], in_=PS[k4][:, :], func=AF.Copy), reads=[PSN[k4]], writes=["ksb%d" % k4])
                            p.dma("pool", lambda e, un=un, k4=k4, t0=t0: e.dma_start(out=rawT.ap()[un][:, t0:t0 + 512], in_=ksb[k4][:, :]),
                                  reads=["ksb%d" % k4], writes=["rawT"])
                        else:
                            br = 1 if un < 8 else 2
                            g = (un - 4) % 4
                            dst = ksT if br == 1 else kwT
                            p.op("act", lambda e, k4=k4: e.activation(out=sgk[k4][:, :], in_=PS[k4][:, :], func=AF.Square), reads=[PSN[k4]], writes=["sgk%d" % k4])
                            p.op("dve", lambda e, k4=k4: e.tensor_copy(out=ksb[k4][:, :], in_=sgk[k4][:, :]), reads=["sgk%d" % k4], writes=["ksb%d" % k4])
                            p.op("pe", lambda e, b=b, k4=k4: e.matmul(PS[4 + b][:, :], bd64[:, :], ksb[k4][:, :], start=True, stop=True),
                                 reads=["ksb%d" % k4, "bd64"], writes=[PSN[4 + b]])
                            rstd_from(4 + b, 64, sgk[k4], "sgk%d" % k4)
                            p.op("dve", lambda e, k4=k4, br=br: e.scalar_tensor_tensor(out=ksb[k4][:, :], in0=PS[k4][:, :], scalar=kn_t[:, br:br + 1],
                                                                                      in1=sgk[k4][:, :], op0=ALU.mult, op1=ALU.mult),
                                 reads=[PSN[k4], "sgk%d" % k4, "kn"], writes=["ksb%d" % k4])
                            p.dma("pool", lambda e, g=g, k4=k4, t0=t0, dst=dst: e.dma_start(out=dst.ap()[g][:, t0:t0 + 512], in_=ksb[k4][0:64, :]),
                                  reads=["ksb%d" % k4], writes=["ksT" if br == 1 else "kwT"])
                    for tb in range(4):
                        b = tb % 2

                        def mm(e, tb=tb, b=b):
                            for c in range(8):
                                ins = e.matmul(PS[6 + b][:, :], xn[:, c, tb * 128:(tb + 1) * 128], wvt_t[:, c * 512:(c + 1) * 512],
                                               start=(c == 0), stop=(c == 7))
                            return ins
                        p.op("pe", mm, reads=["xn", "wvtt"], writes=[PSN[6 + b]])
                        p.op("act", lambda e, b=b: e.activation(out=vtb[b][:, :], in_=PS[6 + b][:, :], func=AF.Copy), reads=[PSN[6 + b]], writes=["vtb%d" % b])
                        p.dma("pool", lambda e, tb=tb, b=b, ti=ti: e.dma_start(out=vsw.ap()[ti * 4 + tb], in_=vtb[b][:, :]), reads=["vtb%d" % b], writes=["vsw"])
        nper_keep = [1]
        _phase3()
        while late_items:
            s_, d_, n_, i_ = late_items.pop(0)
            p.dma("pool", lambda e, s_=s_, d_=d_, i_=i_: e.dma_start(out=d_.ap()[i_], in_=s_.ap()[i_]), writes=["%s.%d" % (n_, i_)])
        p.barrier()

        def _phase4():
            with contextlib.ExitStack() as st:
                raw = T(st, [128, S], BF16, "raw")
                w1t = T(st, [128, 4096], BF16, "w1t")
                w2kt = T(st, [128, 64], BF16, "w2kt")
                w2vt = T(st, [128, 64], BF16, "w2vt")
                w2f = T(st, [128, 128], F32, "w2f")
                peTt = T(st, [128, 64], BF16, "peTt")
                peTf = T(st, [128, 64], F32, "peTf")
                c1 = T(st, [128, 2], F32, "c1")
                xx = T(st, [128, 512], F32, "xx")
                tt = T(st, [128, 512], F32, "tt")
                hid = T(st, [128, NCP], BF16, "hid")
                kc_f = T(st, [64, 512], F32, "kc_f")
                kc_b = T(st, [64, 512], BF16, "kc_b")
                vc_b = T(st, [128, 64], BF16, "vc_b")
                bdh = T(st, [64, 64], BF16, "bdh")
                p.dma("sp", lambda e: e.dma_start(out=w2f[:, 0:64], in_=w2k.ap()), writes=["w2f"])
                p.dma("sp", lambda e: e.dma_start(out=w2f[:, 64:128], in_=w2v.ap()), writes=["w2f"])
                p.dma("sp", lambda e: e.dma_start(out=peTf[:, :], in_=peT.ap()), writes=["peTf"])
                p.op("dve", lambda e: e.tensor_copy(out=w2kt[:, :], in_=w2f[:, 0:64]), reads=["w2f"], writes=["w2kt"])
                p.op("dve", lambda e: e.tensor_copy(out=w2vt[:, :], in_=w2f[:, 64:128]), reads=["w2f"], writes=["w2vt"])
                p.op("dve", lambda e: e.tensor_copy(out=peTt[:, :], in_=peTf[:, :]), reads=["peTf"], writes=["peTt"])
                p.op("dve", lambda e: e.memset(bdh[:, :], 1.0), writes=["bdh"])
                p.op("dve", lambda e: e.memset(hid[:, :], 0.0), writes=["hid"])
                for kv in range(2):
                    p.dma("sp", lambda e, kv=kv: e.dma_start(out=w1t[:, :], in_=(w1k_b if kv == 0 else w1v_b).ap()[0]),
                          reads=["w1k.0", "w1v.0"], writes=["w1t"])
                    def mmc(e, kv=kv):
                        for l in range(32):
                            ins = e.matmul(PS[6][:, 0:1], w1t[0:64, l * 128:(l + 1) * 128], peTt[0:64, kv * 32 + l:kv * 32 + l + 1], start=(l == 0), stop=(l == 31))
                        return ins
                    p.op("pe", mmc, reads=["w1t", "peTt"], writes=[PSN[6]])
                    p.op("act", lambda e, kv=kv: e.activation(out=c1[:, kv:kv + 1], in_=PS[6][:, 0:1], func=AF.Copy), reads=[PSN[6]], writes=["c1"])
                    for ch in range(2):
                        p.dma("sp", lambda e, kv=kv, ch=ch: e.dma_start(out=raw[:, :], in_=rawT.ap()[kv * 2 + ch]), reads=["rawT"], writes=["raw"])
                        for gg in range(2):
                            g = ch * 2 + gg
                            r0 = gg * 64
                            for n0 in range(0, NCV, 512):
                                n1 = min(NCV, n0 + 512)
                                nn = n1 - n0

                                def mm(e, r0=r0, n0=n0, nn=nn):
                                    for l in range(32):
                                        a0 = l + 16 * n0
                                        ins = e.matmul(PS[0][:, 0:nn], w1t[r0:r0 + 64, l * 128:(l + 1) * 128], raw[r0:r0 + 64, a0:a0 + 16 * (nn - 1) + 1:16],
                                                       start=(l == 0), stop=(l == 31))
                                    return ins
                                p.op("pe", mm, reads=["raw", "w1t"], writes=[PSN[0]])
                                p.op("act", lambda e, nn=nn, kv=kv: e.activation(out=xx[:, 0:nn], in_=PS[0][:, 0:nn], func=AF.Identity, bias=c1[:, kv:kv + 1], scale=1.0),
                                     reads=[PSN[0], "c1"], writes=["xx"])
                                p.op("dve", lambda e, nn=nn: e.tensor_tensor(out=tt[:, 0:nn], in0=xx[:, 0:nn], in1=xx[:, 0:nn], op=ALU.mult), reads=["xx"], writes=["tt"])
                                p.op("dve", lambda e, nn=nn: e.tensor_scalar(out=tt[:, 0:nn], in0=tt[:, 0:nn], scalar1=0.044715, scalar2=1.0, op0=ALU.mult, op1=ALU.add),
                                     reads=["tt"], writes=["tt"])
                                p.op("dve", lambda e, nn=nn: e.tensor_tensor(out=tt[:, 0:nn], in0=tt[:, 0:nn], in1=xx[:, 0:nn], op=ALU.mult), reads=["tt", "xx"], writes=["tt"])
                                p.op("act", lambda e, nn=nn: e.activation(out=tt[:, 0:nn], in_=tt[:, 0:nn], func=AF.Sigmoid, scale=1.5957691216057308),
                                     reads=["tt"], writes=["tt"])
                                p.op("dve", lambda e, nn=nn, n0=n0: e.tensor_tensor(out=hid[:, n0:n0 + nn], in0=tt[:, 0:nn], in1=xx[:, 0:nn], op=ALU.mult),
                                     reads=["tt", "xx"], writes=["hid"])
                            if kv == 0:
                                CW = min(512, NCP)
                                for n0 in range(0, NCP, CW):
                                    p.op("pe", lambda e, n0=n0: e.matmul(PS[1][0:64, 0:CW], w2kt[:, :], hid[:, n0:n0 + CW], start=True, stop=True),
                                         reads=["hid", "w2kt"], writes=[PSN[1]])
                                    p.op("act", lambda e: e.activation(out=kc_f[:, 0:CW], in_=PS[1][0:64, 0:CW], func=AF.Square), reads=[PSN[1]], writes=["kc_f"])
                                    p.op("dve", lambda e: e.tensor_copy(out=kc_b[:, 0:CW], in_=kc_f[:, 0:CW]), reads=["kc_f"], writes=["kc_b"])
                                    p.op("pe", lambda e: e.matmul(PS[2][0:64, 0:CW], bdh[:, :], kc_b[:, 0:CW], start=True, stop=True), reads=["kc_b", "bdh"], writes=[PSN[2]])
                                    p.op("act", lambda e: e.activation(out=kc_f[:, 0:CW], in_=PS[2][0:64, 0:CW], func=AF.Sqrt, bias=eps_t[0:64, 0:1], scale=1.0 / 64),
                                         reads=[PSN[2], "eps"], writes=["kc_f"])
                                    p.op("dve", lambda e: e.reciprocal(out=kc_f[:, 0:CW], in_=kc_f[:, 0:CW]), reads=["kc_f"], writes=["kc_f"])
                                    p.op("dve", lambda e: e.scalar_tensor_tensor(out=kc_b[:, 0:CW], in0=PS[1][0:64, 0:CW], scalar=kn_t[0:64, 0:1], in1=kc_f[:, 0:CW],
                                                                                op0=ALU.mult, op1=ALU.mult), reads=[PSN[1], "kc_f", "kn"], writes=["kc_b"])
                                    p.dma("sp", lambda e, g=g, n0=n0: e.dma_start(out=kcT.ap()[g][:, n0:n0 + CW], in_=kc_b[:, 0:CW]), reads=["kc_b"], writes=["kcT"])
                            else:
                                for nb in range(NCK):
                                    p.op("pe", lambda e, nb=nb: e.matmul(PS[1][:, 0:64], hid[:, nb * 128:(nb + 1) * 128], w2vt[:, :], start=True, stop=True),
                                         reads=["hid", "w2vt"], writes=[PSN[1]])
                                    p.op("act", lambda e: e.activation(out=vc_b[:, :], in_=PS[1][:, 0:64], func=AF.Copy), reads=[PSN[1]], writes=["vc_b"])
                                    p.dma("sp", lambda e, g=g, nb=nb: e.dma_start(out=vcS.ap()[g][:, nb * 64:(nb + 1) * 64], in_=vc_b[:, :]), reads=["vc_b"], writes=["vcS"])
        _phase4()
        p.barrier()

        def _phase5():
            with contextlib.ExitStack() as st:
                h = T(st, [128, 8, 512], F32, "h")
                hb = T(st, [128, 8, 512], F32, "hb")
                sq = T(st, [128, 8, 512], BF16, "sq")
                xn = T(st, [128, 8, 512], BF16, "xn")
                rs = T(st, [128, 512], F32, "rs")
                act = T(st, [128, NJ, 512], BF16, "act")
                sg = [T(st, [128, 512], F32, "sg") for _ in range(2)]
                wpi = [T(st, [128, 2048], BF16, "wpi") for _ in range(4)]
                wpo = [T(st, [128, NJ * 128], BF16, "wpo") for _ in range(3)]
                bufs = (sq, xn, rs, act, sg, wpi, wpo)
                wqt = [T(st, [128, 512], BF16, "wqt") for _ in range(2)]
                wgt = T(st, [128, 384], BF16, "wgt")
                qb_ = [T(st, [64, 512], BF16, "qb") for _ in range(2)]
                bdh = T(st, [64, 64], BF16, "bdh")
                gsb = T(st, [48, 512], BF16, "gsb")
                p.op("dve", lambda e: e.memset(bdh[:, :], 1.0), writes=["bdh"])
                p.dma("sp", lambda e: e.dma_start(out=wgt[:, :], in_=wg_b.ap()[0]), reads=["wg.0"], writes=["wgt"])
                h1v = h1T.ap().rearrange("(c p) s -> p c s", p=128)
                h2v = h2T.ap().rearrange("(c p) s -> p c s", p=128)
                for i in range(NO):
                    ta, tb_ = (2 * i) * 512, (2 * i + 1) * 512
                    p.dma("sp", lambda e, ta=ta: e.dma_start(out=h[:, :, :], in_=h1v[:, :, ta:ta + 512]), reads=["h1T"], writes=HN)
                    p.dma("sp", lambda e, tb_=tb_: e.dma_start(out=hb[:, :, :], in_=h1v[:, :, tb_:tb_ + 512]), reads=["h1T"], writes=["hb"])
                    p.op("dve", lambda e: e.tensor_scalar(out=h[:, :, :], in0=h[:, :, :], scalar1=par_t[:, 1:2], scalar2=None, op0=ALU.mult),
                         reads=HN + ["par"], writes=HN)
                    p.op("dve", lambda e: e.scalar_tensor_tensor(out=h[:, :, :], in0=hb[:, :, :], scalar=par_t[:, 0:1], in1=h[:, :, :], op0=ALU.mult, op1=ALU.add),
                         reads=HN + ["hb", "par"], writes=HN)
                    ffn(h, "h", 2, 32, bufs)
                    p.dma("pool", lambda e, i=i: e.dma_start(out=h2v[:, :, i * 512:(i + 1) * 512], in_=h[:, :, :]), reads=HN, writes=["h2T"])
                    rms_to_xn(h, "h", 40, sq, xn, rs)
                    if "dbg_xn" in dbg and i == 0:
                        dx = nc.dram_tensor("dbg_xn", [128, 4096], BF16, kind="ExternalOutput")
                        p.dma("sp", lambda e: e.dma_start(out=dx.ap(), in_=xn[:, :, :].rearrange("p c s -> p (c s)")), reads=["xn"], writes=["dbg_xn"])
                        dr = nc.dram_tensor("dbg_rs", [128, 512], F32, kind="ExternalOutput")
                        p.dma("sp", lambda e: e.dma_start(out=dr.ap(), in_=rs[:, :]), reads=["rs"], writes=["dbg_rs"])
                    for hd in range(16):
                        b = hd % 2
                        p.dma("sp", lambda e, hd=hd, b=b: e.dma_start(out=wqt[b][:, :], in_=wq_b.ap()[hd]), reads=["wq.%d" % hd], writes=["wqt%d" % b])

                        def mm(e, b=b):
                            for c in range(8):
                                ins = e.matmul(PS[b][0:64, :], wqt[b][:, c * 64:(c + 1) * 64], xn[:, c, :], start=(c == 0), stop=(c == 7))
                            return ins
                        p.op("pe", mm, reads=["xn", "wqt%d" % b], writes=[PSN[b]])
                        p.op("act", lambda e, b=b: e.activation(out=sg[b][0:64, :], in_=PS[b][0:64, :], func=AF.Square), reads=[PSN[b]], writes=["sg%d" % b])
                        p.op("dve", lambda e, b=b: e.tensor_copy(out=qb_[b][:, :], in_=sg[b][0:64, :]), reads=["sg%d" % b], writes=["qb%d" % b])
                        p.op("pe", lambda e, b=b: e.matmul(PS[2 + b][0:64, :], bdh[:, :], qb_[b][:, :], start=True, stop=True), reads=["qb%d" % b, "bdh"], writes=[PSN[2 + b]])
                        rstd_from(2 + b, 64, sg[b], "sg%d" % b, rows=64)
                        p.op("dve", lambda e, b=b: e.scalar_tensor_tensor(out=qb_[b][:, :], in0=PS[b][0:64, :], scalar=qn_t[0:64, 0:1], in1=sg[b][0:64, :],
                                                                         op0=ALU.mult, op1=ALU.mult), reads=[PSN[b], "sg%d" % b, "qn"], writes=["qb%d" % b])
                        p.dma("pool", lambda e, hd=hd, b=b, i=i: e.dma_start(out=qTs.ap()[hd][:, i * 512:(i + 1) * 512], in_=qb_[b][:, :]),
                              reads=["qb%d" % b], writes=["qTs"])

                    def mmg(e):
                        for c in range(8):
                            ins = e.matmul(PS[4][0:48, :], wgt[:, c * 48:(c + 1) * 48], xn[:, c, :], start=(c == 0), stop=(c == 7))
                        return ins
                    p.op("pe", mmg, reads=["xn", "wgt"], writes=[PSN[4]])
                    p.op("act", lambda e: e.activation(out=gsb[:, :], in_=PS[4][0:48, :], func=AF.Sigmoid), reads=[PSN[4]], writes=["gsb"])
                    p.dma("pool", lambda e, i=i: e.dma_start(out=gsT.ap()[:, i * 512:(i + 1) * 512], in_=gsb[:, :]), reads=["gsb"], writes=["gsT"])
        _phase5()
        p.barrier()

        def _phase6():
            with contextlib.ExitStack() as st:
                ks_t = T(st, [128, S], BF16, "ks_t")
                kw_t = T(st, [64, S], BF16, "kw_t")
                kc_t = T(st, [64, NCP], BF16, "kc_t")
                vsw_t = T(st, [128, NKT, 256], BF16, "vsw_t")
                vc_t = T(st, [128, NCK, 128], BF16, "vc_t")
                ov_t = T(st, [128, NCK * (NSB + 1)], BF16, "ov_t")
                ov_f = T(st, [128, NCK * (NSB + 1)], F32, "ov_f")
                selg_t = T(st, [48, 48 * 128], BF16, "selg_t")
                negT = T(st, [128, SO], BF16, "negT")
                negS = T(st, [128, SO], BF16, "negS")
                sc2w = T(st, [128, 256], F32, "sc2w")
                impacc = T(st, [128, 4, NSB], F32, "impacc")
                qa = [[T(st, [128, 512], BF16, "qa") for _ in range(2)] for _ in range(2)]
                gs_t = T(st, [48, 512], BF16, "gs_t")
                ms_t = T(st, [128, W_S], BF16, "ms_t")
                mw_t = T(st, [128, W_W], BF16, "mw_t")
                mc_t = [T(st, [128, 512], BF16, "mc_t") for _ in range(8)]
                e_t = [T(st, [128, 512], BF16, "e_t") for _ in range(10)]
                p_t = [T(st, [128, 512], BF16, "p_t") for _ in range(10)]
                rl = T(st, [128, 4], F32, "rl")
                sA = T(st, [128, NSB], F32, "sA")
                sB = T(st, [128, NSB], F32, "sB")
                sc = T(st, [128, NSB], F32, "sc")
                sc2 = T(st, [128, NSB], F32, "sc2")
                m8a = T(st, [128, 8], F32, "m8a")
                m8b = T(st, [128, 8], F32, "m8b")
                gbb = [T(st, [128, 3, 512], F32, "gb") for _ in range(2)]
                rr2b = [T(st, [128, 512], F32, "rr2") for _ in range(3)]
                rr = T(st, [64, 512], F32, "rr")
                acc = T(st, [64, 512], F32, "acc")
                ocb = T(st, [64, 512], BF16, "ocb")
                ones64 = ones_bf[:, 0:64]
                p.dma("sp", lambda e: e.dma_start(out=ks_t[64:128, :], in_=expd_b.ap()[0]), reads=["expd.0"], writes=["kpat"])
                p.dma("sp", lambda e: e.dma_start(out=selg_t[:, :], in_=selg_b.ap()[0]), reads=["selg.0"], writes=["selg_t"])
                p.dma("sp", lambda e: e.dma_start(out=ov_f[:, :], in_=ovm.ap()), writes=["ov_f"])
                p.op("dve", lambda e: e.tensor_copy(out=ov_t[:, :], in_=ov_f[:, :]), reads=["ov_f"], writes=["ov_t"])
                cnt = [0]
                p.op("dve", lambda e: e.memset(vsw_t[:, :, :], 1.0), writes=["kgrp"])
                p.op("dve", lambda e: e.memset(vc_t[:, :, :], 1.0), writes=["kgrp"])
                for k_ in range(3):
                    p.op("dve", lambda e, k_=k_: e.memset(rr2b[k_][:, :], 0.0), writes=["rr2.%d" % k_])

                def c_chunks(i):
                    return [nk for nk in range(NCK) if (2 * i + 2) * 512 - 1 >= 2048 * nk + 31]

                mcc = [0]

                def load_mc(hd, i, nk):
                    b = mcc[0] % 8
                    mcc[0] += 1
                    D_ = 1024 * i - 2048 * nk
                    j0 = min(D_, JSAT_C)
                    src = bass.AP(tensor=rep_c, offset=hd * 128 * L_C + 2032 + j0, ap=[[L_C - 16, 128], [1, 512]])
                    p.dma("sp", lambda e, b=b, src=src: e.dma_start(out=mc_t[b][:, :], in_=src), reads=["rep_c.%d" % hd], writes=["mc_t%d" % b])
                    return b

                def score_unit(hd, kmat, kcol0, qbuf, qn_, mask_ap, mask_res, neg_cols=None):
                    b = cnt[0] % 10
                    pb = cnt[0] % 4
                    u3 = cnt[0] % 3
                    cnt[0] += 1

                    p.op("pe", lambda e: e.matmul(PS[pb][:, :], kmat[:, kcol0:kcol0 + 128], qbuf, start=True, stop=True),
                         reads=["kgrp", "kpat", qn_], writes=[PSN[pb]])
                    p.op("act", lambda e: e.activation(out=e_t[b][:, :], in_=PS[pb][:, :], func=AF.Exp, scale=0.125), reads=[PSN[pb]], writes=["e_t%d" % b])
                    eng = "pool" if u3 == 2 else "dve"
                    p.op(eng, lambda e: e.tensor_tensor(out=p_t[b][:, :], in0=e_t[b][:, :], in1=mask_ap, op=ALU.mult),
                         reads=["e_t%d" % b] + mask_res, writes=["p_t%d" % b])
                    return b

                for g in range(4):
                    p.dma("sp", lambda e, g=g: e.dma_start(out=ks_t[0:64, :], in_=ksT.ap()[g]), reads=["ksT"], writes=["kgrp"])
                    p.dma("sp", lambda e, g=g: e.dma_start(out=kw_t[:, :], in_=kwT.ap()[g]), reads=["kwT"], writes=["kgrp"])
                    p.dma("sp", lambda e, g=g: e.dma_start(out=kc_t[:, :], in_=kcT.ap()[g]), reads=["kcT"], writes=["kgrp"])
                    p.dma("sp", lambda e, g=g: e.dma_start(out=vc_t[:, :, 0:64], in_=vcS.ap()[g].rearrange("p (k c) -> p k c", c=64)), reads=["vcS"], writes=["kgrp"])
                    vsrc = vsw.ap().rearrange("k p c -> p k c")
                    for k0 in range(0, NKT, 8):
                        p.dma("sp", lambda e, g=g, k0=k0: e.dma_start(out=vsw_t[:, k0:k0 + 8, 0:64], in_=vsrc[:, k0:k0 + 8, g * 64:(g + 1) * 64]),
                              reads=["vsw"], writes=["kgrp"])
                        p.dma("sp", lambda e, g=g, k0=k0: e.dma_start(out=vsw_t[:, k0:k0 + 8, 128:192], in_=vsrc[:, k0:k0 + 8, 256 + g * 64:256 + (g + 1) * 64]),
                              reads=["vsw"], writes=["kgrp"])
                    for i in range(NO):
                        chunks = c_chunks(i)
                        for hp in range(2):
                            pbs_h = {}
                            for hh in (2 * hp, 2 * hp + 1):
                                hd = g * 4 + hh
                                qb = hh % 2
                                p.dma("sp", lambda e, hd=hd, i=i, qb=qb: e.dma_start(out=qa[0][qb][0:64, :], in_=qTs.ap()[hd][:, i * 512:(i + 1) * 512]),
                                      reads=["qTs"], writes=["qa%d" % qb])
                                pbs = []
                                for nk in chunks:
                                    mb = load_mc(hd, i, nk)
                                    pbs.append(score_unit(hd, kc_t, nk * 128, qa[0][qb][0:64, :], "qa%d" % qb, mc_t[mb][:, :], ["mc_t%d" % mb]))
                                pbs_h[hh] = pbs
                            for hh in (2 * hp, 2 * hp + 1):
                                hd = g * 4 + hh
                                pbs = pbs_h[hh]
                                for qq in range(4):
                                    bank = 4 + qq // 2
                                    col = (qq % 2) * (NSB + 1)

                                    def mm(e, qq=qq, bank=bank, col=col, pbs=pbs, chunks=chunks):
                                        for ii, nk in enumerate(chunks):
                                            ins = e.matmul(PS[bank][:, col:col + NSB + 1], p_t[pbs[ii]][:, qq * 128:(qq + 1) * 128],
                                                           ov_t[:, nk * (NSB + 1):(nk + 1) * (NSB + 1)], start=(ii == 0), stop=(ii == len(chunks) - 1))
                                        return ins
                                    p.op("pe", mm, reads=["p_t%d" % b_ for b_ in pbs] + ["ov_t"], writes=[PSN[bank]])
                                    p.op("dve", lambda e, qq=qq, bank=bank, col=col: e.tensor_scalar(out=rl[:, qq:qq + 1], in0=PS[bank][:, col + NSB:col + NSB + 1],
                                                                                                    scalar1=1e-30, scalar2=None, op0=ALU.add),
                                         reads=[PSN[bank]], writes=["rl"])
                                    p.op("dve", lambda e, qq=qq: e.reciprocal(out=rl[:, qq:qq + 1], in_=rl[:, qq:qq + 1]), reads=["rl"], writes=["rl"])
                                    if hh == 0:
                                        p.op("dve", lambda e, qq=qq, bank=bank, col=col, i=i: e.tensor_scalar(out=impacc[:, qq, :], in0=PS[bank][:, col:col + NSB],
                                                                                                             scalar1=rl[:, qq:qq + 1], scalar2=None, op0=ALU.mult),
                                             reads=[PSN[bank], "rl"], writes=["impacc"])
                                    else:
                                        p.op("dve", lambda e, qq=qq, bank=bank, col=col, i=i: e.scalar_tensor_tensor(out=impacc[:, qq, :], in0=PS[bank][:, col:col + NSB],
                                                                                                                    scalar=rl[:, qq:qq + 1], in1=impacc[:, qq, :],
                                                                                                                    op0=ALU.mult, op1=ALU.add),
                                             reads=[PSN[bank], "rl", "impacc"], writes=["impacc"])

                        for qq in range(4):
                            qi = i * 4 + qq
                            p.dma("sp", lambda e, qi=qi: e.dma_start(out=sA[:, :], in_=selA.ap()[qi]), writes=["sA"])
                            p.dma("sp", lambda e, qi=qi: e.dma_start(out=sB[:, :], in_=selB.ap()[qi]), writes=["sB"])
                            p.op("dve", lambda e, qq=qq: e.tensor_tensor(out=sc[:, :], in0=impacc[:, qq, :], in1=sA[:, :], op=ALU.mult), reads=["impacc", "sA"], writes=["sc"])
                            p.op("dve", lambda e: e.tensor_tensor(out=sc[:, :], in0=sc[:, :], in1=sB[:, :], op=ALU.add), reads=["sc", "sB"], writes=["sc"])
                            p.op("dve", lambda e: e.max(out=m8a[:, :], in_=sc[:, :]), reads=["sc"], writes=["m8a"])
                            p.op("dve", lambda e: e.match_replace(out=sc2[:, :], in_to_replace=m8a[:, :], in_values=sc[:, :], imm_value=-3e38),
                                 reads=["sc", "m8a"], writes=["sc2"])
                            p.op("dve", lambda e: e.max(out=m8b[:, :], in_=sc2[:, :]), reads=["sc2"], writes=["m8b"])
                            p.op("dve", lambda e: e.tensor_scalar(out=sc2[:, :], in0=sc[:, :], scalar1=m8b[:, 7:8], scalar2=-MASKV, op0=ALU.is_ge, op1=ALU.mult),
                                 reads=["sc", "m8b"], writes=["sc2"])
                            p.op("dve", lambda e: e.tensor_scalar(out=sc2w[:, 0:128], in0=sc2[:, :], scalar1=MASKV, scalar2=None, op0=ALU.add), reads=["sc2"], writes=["sc2w"])
                            p.op("dve", lambda e: e.tensor_scalar(out=sc2w[:, 128:256], in0=sc2[:, :], scalar1=MASKV, scalar2=None, op0=ALU.add), reads=["sc2"], writes=["sc2w"])

                            def tr2(e):
                                e.transpose(out=PS[6][:, 0:128], in_=sc2w[:, 0:128], identity=ident_t[:, :])
                                return e.transpose(out=PS[6][:, 128:256], in_=sc2w[:, 64:192], identity=ident_t[:, :])
                            p.op("pe", tr2, reads=["sc2w", "ident"], writes=[PSN[6]])
                            p.op("act", lambda e, qi=qi: e.activation(out=negT[64:128, qi * 128:(qi + 1) * 128], in_=PS[6][64:128, 0:128], func=AF.Copy),
                                 reads=[PSN[6]], writes=["negT"])
                            p.op("act", lambda e, qi=qi: e.activation(out=negS[64:128, qi * 128:(qi + 1) * 128], in_=PS[6][64:128, 128:256], func=AF.Copy),
                                 reads=[PSN[6]], writes=["negT"])
                    LA = 8
                    DEFER = 7
                    epi = [0]
                    for hh in range(4):
                        hd = g * 4 + hh
                        src_s = bass.AP(tensor=rep_s, offset=hd * 128 * L_S + 127, ap=[[L_S - 1, 128], [1, W_S]])
                        src_w = bass.AP(tensor=rep_w, offset=hd * 128 * L_W + 127, ap=[[L_W - 1, 128], [1, W_W]])
                        p.dma("sp", lambda e, src_s=src_s: e.dma_start(out=ms_t[:, :], in_=src_s), reads=["rep_s.%d" % hd], writes=["ms_t"])
                        p.dma("sp", lambda e, src_w=src_w: e.dma_start(out=mw_t[:, :], in_=src_w), reads=["rep_w.%d" % hd], writes=["mw_t"])
                        pend = []

                        pend2 = []

                        def emit_pv(un, b):
                            ob = 4 + un["br"]
                            first, last, vap, br, gi, hd_, i_ = un["first"], un["last"], un["vap"], un["br"], un["gi"], un["hd"], un["i"]
                            p.op("pe", lambda e: e.matmul(PS[ob][:, :], vap, p_t[b][:, :], start=first, stop=last),
                                 reads=["p_t%d" % b, "kgrp"], writes=[PSN[ob]])
                            if last:
                                gbn = "gb%d" % gi
                                k_ = epi[0] % 3
                                epi[0] += 1
                                rr2 = rr2b[k_]
                                rn = "rr2.%d" % k_
                                p.op("dve", lambda e: e.tensor_scalar(out=rr2[64:128, :], in0=PS[ob][64:128, :], scalar1=1e-18, scalar2=None, op0=ALU.max), reads=[PSN[ob]], writes=[rn])
                                p.op("act", lambda e: e.activation(out=rr2[64:128, :], in_=rr2[64:128, :], func=AF.Ln), reads=[rn], writes=[rn])
                                p.op("act", lambda e: e.activation(out=rr2[64:128, :], in_=rr2[64:128, :], func=AF.Exp, scale=-1.0), reads=[rn], writes=[rn])
                                p.op("dve", lambda e: e.tensor_tensor(out=rr2[64:128, :], in0=rr2[64:128, :], in1=gbb[gi][64:128, br, :], op=ALU.mult), reads=[rn, gbn], writes=[rn])

                                def stage2():
                                    p.op("pe", lambda e: e.matmul(PS[7][0:64, :], ident_t[:, 64:128], rr2[:, :], start=True, stop=True), reads=[rn, "ident"], writes=[PSN[7]])
                                    p.op("act", lambda e: e.activation(out=rr[:, :], in_=PS[7][0:64, :], func=AF.Copy), reads=[PSN[7]], writes=["rr"])
                                    if br == 0:
                                        p.op("dve", lambda e: e.tensor_tensor(out=acc[:, :], in0=PS[ob][0:64, :], in1=rr[:, :], op=ALU.mult), reads=[PSN[ob], "rr"], writes=["acc"])
                                    else:
                                        p.op("dve", lambda e: e.tensor_tensor(out=rr[:, :], in0=PS[ob][0:64, :], in1=rr[:, :], op=ALU.mult), reads=[PSN[ob], "rr"], writes=["rr"])
                                        p.op("dve", lambda e: e.tensor_tensor(out=acc[:, :], in0=acc[:, :], in1=rr[:, :], op=ALU.add), reads=["acc", "rr"], writes=["acc"])
                                    if br == 2:
                                        p.op("act", lambda e: e.activation(out=ocb[:, :], in_=acc[:, :], func=AF.Copy), reads=["acc"], writes=["ocb"])
                                        p.dma("sp", lambda e: e.dma_start(out=ocT.ap()[hd_][:, i_ * 512:(i_ + 1) * 512], in_=ocb[:, :]), reads=["ocb"], writes=["ocT"])
                                pend2.append([DEFER + 1, stage2])
                            for it in pend2:
                                it[0] -= 1
                            while pend2 and pend2[0][0] <= 0:
                                pend2.pop(0)[1]()

                        for i in range(NO):
                            qb = i % 2
                            gi = i % 2
                            for v_ in range(2):
                                p.dma("sp", lambda e, hd=hd, i=i, qb=qb, v_=v_: e.dma_start(out=qa[v_][qb][0:64, :], in_=qTs.ap()[hd][:, i * 512:(i + 1) * 512]),
                                      reads=["qTs"], writes=["qa%d" % qb])
                            p.op("pool", lambda e, i=i, qb=qb: e.tensor_copy(out=qa[0][qb][64:128, :], in_=negS[64:128, i * 512:(i + 1) * 512]), reads=["negT"], writes=["qa%d" % qb])
                            p.op("pool", lambda e, i=i, qb=qb: e.tensor_copy(out=qa[1][qb][64:128, :], in_=negT[64:128, i * 512:(i + 1) * 512]), reads=["negT"], writes=["qa%d" % qb])
                            p.dma("sp", lambda e, i=i: e.dma_start(out=gs_t[:, :], in_=gsT.ap()[:, i * 512:(i + 1) * 512]), reads=["gsT"], writes=["gs_t"])
                            for br in range(3):
                                p.op("pe", lambda e, br=br, hd=hd: e.matmul(PS[7][:, :], selg_t[:, (hd * 3 + br) * 128:(hd * 3 + br + 1) * 128], gs_t[:, :],
                                                                           start=True, stop=True), reads=["selg_t", "gs_t"], writes=[PSN[7]])
                                p.op("act", lambda e, br=br, gi=gi: e.activation(out=gbb[gi][64:128, br, :], in_=PS[7][64:128, :], func=AF.Copy), reads=[PSN[7]], writes=["gb%d" % gi])
                            units = []
                            for br in range(3):
                                if br == 0:
                                    kts = c_chunks(i)
                                elif br == 1:
                                    kts = list(range((2 * i + 2) * 4))
                                else:
                                    kts = list(range(max(0, (2 * i - 1) * 4), (2 * i + 2) * 4))
                                for ui, kt in enumerate(kts):
                                    units.append(dict(br=br, kt=kt, first=(ui == 0), last=(ui == len(kts) - 1), gi=gi, hd=hd, i=i))
                            for un in units:
                                kt, br = un["kt"], un["br"]
                                if br == 0:
                                    mb = load_mc(hd, i, kt)
                                    b = score_unit(hd, kc_t, kt * 128, qa[0][qb][0:64, :], "qa%d" % qb, mc_t[mb][:, :], ["mc_t%d" % mb])
                                    un["vap"] = vc_t[:, kt, :]
                                elif br == 1:
                                    j0 = min(1024 * i - 128 * kt + OFFS - 127, JSAT_S)
                                    b = score_unit(hd, ks_t, kt * 128, qa[kt // 32][qb][:, :], "qa%d" % qb, ms_t[:, j0:j0 + 512], ["ms_t"])
                                    un["vap"] = vsw_t[:, kt, 0:128]
                                else:
                                    j0 = 1024 * i - 128 * kt + OFFW - 127
                                    b = score_unit(hd, kw_t, kt * 128, qa[0][qb][0:64, :], "qa%d" % qb, mw_t[:, j0:j0 + 512], ["mw_t"])
                                    un["vap"] = vsw_t[:, kt, 128:256]
                                pend.append((un, b))
                                if len(pend) > LA:
                                    emit_pv(*pend.pop(0))
                        while pend:
                            emit_pv(*pend.pop(0))
                        while pend2:
                            pend2.pop(0)[1]()
        _phase6()
        p.barrier()

        def _phase7():
            with contextlib.ExitStack() as st:
                h = T(st, [128, 8, 512], F32, "h")
                sq = T(st, [128, 8, 512], BF16, "sq")
                xn = T(st, [128, 8, 512], BF16, "xn")
                rs = T(st, [128, 512], F32, "rs")
                act = T(st, [128, NJ, 512], BF16, "act")
                sg = [T(st, [128, 512], F32, "sg") for _ in range(2)]
                wpi = [T(st, [128, 2048], BF16, "wpi") for _ in range(4)]
                wpo = [T(st, [128, NJ * 128], BF16, "wpo") for _ in range(3)]
                bufs = (sq, xn, rs, act, sg, wpi, wpo)
                wo_t = T(st, [64, 16, 1024], BF16, "wo_t")
                oc_t = T(st, [64, 16, 512], BF16, "oc_t")
                p.dma("sp", lambda e: e.dma_start(out=wo_t[:, :, :], in_=wo_b.ap().rearrange("h p n -> p h n")), reads=["wo.%d" % i for i in range(16)], writes=["wo_t"])
                h2v = h2T.ap().rearrange("(c p) s -> p c s", p=128)
                ov = outT.ap().rearrange("(c p) s -> p c s", p=128)
                for i in range(NO):
                    p.dma("sp", lambda e, i=i: e.dma_start(out=h[:, :, :], in_=h2v[:, :, i * 512:(i + 1) * 512]), reads=["h2T"], writes=HN)
                    p.dma("sp", lambda e, i=i: e.dma_start(out=oc_t[:, :, :], in_=ocT.ap().rearrange("h p s -> p h s")[:, :, i * 512:(i + 1) * 512]),
                          reads=["ocT"], writes=["oc_t"])
                    for d in range(8):
                        b = d % 2

                        def mm(e, d=d, b=b):
                            for hd in range(16):
                                ins = e.matmul(PS[b][:, :], wo_t[:, hd, d * 128:(d + 1) * 128], oc_t[:, hd, :], start=(hd == 0), stop=(hd == 15))
                            return ins
                        p.op("pe", mm, reads=["wo_t", "oc_t"], writes=[PSN[b]])
                        p.op("dve", lambda e, d=d, b=b: e.tensor_tensor(out=h[:, d, :], in0=PS[b][:, :], in1=h[:, d, :], op=ALU.add), reads=[PSN[b], "h.%d" % d], writes=["h.%d" % d])
                    ffn(h, "h", 3, 48, bufs)
                    p.dma("pool", lambda e, i=i: e.dma_start(out=ov[:, :, i * 512:(i + 1) * 512], in_=h[:, :, :]), reads=HN, writes=["outT"])
        _phase7()
        p.barrier()
        p.run()
    return nc


def _rel_bucket_table(n):
    import jax
    import jax.numpy as jnp
    import math
    with jax.default_device(jax.devices("cpu")[0]):
        rel = jnp.arange(n)
        nn = jnp.maximum(rel, 0)
        nf = jnp.maximum(nn, 1).astype(jnp.float32)
        large = 16 + (jnp.log(nf / 16) / math.log(4096 / 16) * 16).astype(jnp.int32)
        large = jnp.minimum(large, 31)
        return np.asarray(jnp.where(nn < 16, nn, large))


def prep_shared(S, inputs):
    f = lambda a: np.ascontiguousarray(a, dtype=np.float32)
    NSB = S // 64
    NCP = S // 16
    NCK = NCP // 128
    sh = {}

    def col8(g):
        return g.reshape(8, 128).T
    norms = [inputs["ffn1_norm"][0], inputs["mix_norm"][0], inputs["ffn2_norm"][0], inputs["kv_norm"],
             inputs["ffn1_norm"][1], inputs["mix_norm"][1], inputs["ffn2_norm"][1]]
    sh["gains"] = f(np.concatenate([col8(np.asarray(g)) for g in norms], axis=1))

    def win_r(w):
        w = np.asarray(w).reshape(8, 128, 2, NJ, 128)
        return f(w.transpose(3, 1, 2, 0, 4).reshape(NJ, 128, 2048))

    def wout_r(w):
        w = np.asarray(w).reshape(NJ, 128, 8, 128)
        return f(w.transpose(2, 1, 0, 3).reshape(8, 128, NJ * 128))
    ffn_list = [("ffn1_w_in", "ffn1_w_out", 0), ("ffn2_w_in", "ffn2_w_out", 0), ("ffn1_w_in", "ffn1_w_out", 1), ("ffn2_w_in", "ffn2_w_out", 1)]
    for i, (a, b, l) in enumerate(ffn_list):
        sh["win%d" % i] = win_r(inputs[a][l])
        sh["wout%d" % i] = wout_r(inputs[b][l])

    def sq_r(w, ncol):
        w = np.asarray(w).reshape(8, 128, ncol, 128)
        return w.transpose(2, 1, 0, 3)
    ci = np.asarray(inputs["conv_w_in"][0]).reshape(8, 128, 3, 8, 128)
    sh["cin"] = f(ci.transpose(3, 1, 2, 0, 4).reshape(8, 128, 3072))
    sh["cout"] = f(sq_r(inputs["conv_w_out"][0], 8).reshape(8, 128, 1024))
    cw = np.asarray(inputs["conv_w"][0])
    sh["cwk"] = f(cw.reshape(3, 8, 128).transpose(2, 1, 0).reshape(128, 24))
    wkv = np.asarray(inputs["w_kv"])
    units = []
    for c0 in (0, 128, 256, 384):
        units.append(wkv[:, c0:c0 + 128])
    for br in (1, 2):
        for g in range(4):
            cc = wkv[:, br * 512 + g * 64: br * 512 + (g + 1) * 64]
            units.append(np.concatenate([cc, cc], axis=1))
    sh["wkf"] = f(np.stack([u.reshape(8, 128, 128).transpose(1, 0, 2).reshape(128, 1024) for u in units]))
    wv = np.concatenate([wkv[:, 768:1024], wkv[:, 1280:1536]], axis=1)
    sh["wvt"] = f(wv.reshape(8, 128, 512).transpose(1, 0, 2).reshape(1, 128, 4096))
    wqf = np.asarray(inputs["attn_w_q"][0])
    sh["wq"] = f(np.stack([wqf[:, h * 64:(h + 1) * 64].reshape(8, 128, 64).transpose(1, 0, 2).reshape(128, 512) for h in range(16)]))
    sh["wg"] = f(wqf[:, 1024:1072].reshape(8, 128, 48).transpose(1, 0, 2).reshape(1, 128, 384))
    sh["wo"] = f(np.asarray(inputs["attn_w_o"][0]).reshape(16, 64, 1024))
    knm = np.asarray(inputs["k_norm"])
    sh["kn"] = f(np.concatenate([knm.T, knm.T], axis=0))
    qnm = np.asarray(inputs["attn_q_norm"][0])
    sh["qn"] = f(np.concatenate([qnm, qnm])[:, None])

    def w1_r(w):
        w = np.asarray(w).reshape(32, 64, 128).transpose(1, 0, 2).reshape(64, 4096)
        return f(np.concatenate([w, w], axis=0)[None])
    sh["w1k"] = w1_r(inputs["cmp_w1_k"])
    sh["w1v"] = w1_r(inputs["cmp_w1_v"])
    sh["w2k"] = f(inputs["cmp_w2_k"])
    sh["w2v"] = f(inputs["cmp_w2_v"])
    pk = np.asarray(inputs["cmp_pe_k"]).T
    pv = np.asarray(inputs["cmp_pe_v"]).T
    pe = np.concatenate([pk, pv], axis=1)
    sh["peT"] = f(np.concatenate([pe, pe], axis=0))
    assert NSB == 128
    ex = np.zeros((1, 64, S), np.float32)
    for j in range(NSB):
        ex[0, j % 64, j * 64:(j + 1) * 64] = 1.0
    sh["expd"] = ex
    n = np.arange(NCP)
    cs = n * 16
    ss = np.arange(NSB) * 64
    ovl = np.clip(np.minimum(cs[:, None] + 32, ss[None, :] + 64) - np.maximum(cs[:, None], ss[None, :]), 0, None) / 16.0
    ovl[NCP - 1, :] = 0.0
    ovl1 = np.concatenate([ovl, np.ones((NCP, 1))], axis=1)
    sh["ovm"] = f(ovl1.reshape(NCK, 128, NSB + 1).transpose(1, 0, 2).reshape(128, NCK * (NSB + 1)))
    sg_ = np.zeros((1, 48, 48 * 128), np.float32)
    for c in range(48):
        sg_[0, c, c * 128 + 64:(c + 1) * 128] = 1.0
    sh["selg"] = sg_
    sh["ident"] = np.eye(128, dtype=np.float32)
    return sh


def prep_parity(S, inputs, par):
    f = lambda a: np.ascontiguousarray(a, dtype=np.float32)
    NSB = S // 64
    NO = S // 1024
    pp = {}
    pp["par"] = f(np.tile(np.array([[par, 1 - par]], np.float32), (128, 1)))
    rb = np.asarray(inputs["rel_bias"])
    bt = _rel_bucket_table(max(S, 4096) + 8192)

    def vec(L, off, lo, hi):
        rel = np.arange(L) - off + 512 * par
        ok = (rel >= lo) & (rel < hi)
        idx = bt[np.clip(rel, 0, len(bt) - 1)]
        v = rb[idx, :].T.copy()
        v[:, ~ok] = MASKV
        return f(v)
    pp["bvs"] = vec(L_S, OFFS, 0, 1 << 30)
    pp["bvw"] = vec(L_W, OFFW, 0, 512)
    pp["bvc"] = vec(L_C, OFFC, 0, 1 << 30)
    A = np.zeros((NO * 4, 128, NSB), np.float32)
    B = np.zeros((NO * 4, 128, NSB), np.float32)
    jj = np.arange(NSB)[None, :]
    for i in range(NO):
        for qq in range(4):
            t = (2 * i + par) * 512 + qq * 128 + np.arange(128)
            bt_ = (t // 64)[:, None]
            valid = jj <= bt_
            forced = (jj == 0) | (jj == bt_) | (jj == bt_ - 1)
            A[i * 4 + qq] = (valid & ~forced)
            B[i * 4 + qq] = np.where(valid, np.where(forced, 1e30, 0.0), -1e30)
    pp["selA"] = A
    pp["selB"] = B
    return pp


_CACHE = {}


def run_model(S, inputs, x, dbg=()):
    B = x.shape[0]
    key = (S, tuple(dbg))
    if key not in _CACHE:
        _CACHE[key] = build(S, dbg)
    nc = _CACHE[key]
    sh = prep_shared(S, inputs)
    pps = [prep_parity(S, inputs, 0), prep_parity(S, inputs, 1)]
    in_maps = []
    for core in range(2 * B):
        b, par = core // 2, core % 2
        m = dict(sh)
        m.update(pps[par])
        m["xT"] = np.ascontiguousarray(np.asarray(x[b], dtype=np.float32).T)
        in_maps.append(m)
    res = run_bass_kernel_spmd(nc, in_maps, core_ids=list(range(2 * B)))
    out = np.zeros((B, S, D), np.float32)
    for core in range(2 * B):
        b, par = core // 2, core % 2
        oT = np.asarray(res.results[core]["outT"])
        o = oT.T.reshape(S // 1024, 512, D)
        for i in range(S // 1024):
            t0 = (2 * i + par) * 512
            out[b, t0:t0 + 512] = o[i]
    return out, res


def kernel(**inputs):
    x = np.asarray(inputs["x"])
    out, _ = run_model(x.shape[1], inputs, x)
    return out
```
